# Optimizing a Trainium2 kernel written in Bass

```python
import math
import jax, jax.numpy as jnp
from jax import lax
import numpy as np

D_MODEL = 2048
BATCH = 4
SEQ = 2048
DEPTH = 2
DEC_BATCH = 8
DEC_SEQ = 64
PAST_LEN = 1024

CHUNK = 64
Q_BLOCK = 128
EPS = 1e-6
NEG = -1e30
N_BRANCH = 4
BRANCH_W = D_MODEL // 4
HEAD_D = 64
H_A = BRANCH_W // HEAD_D
H_B = BRANCH_W // HEAD_D
N_PREV_CHUNKS = 8
B_REACH = N_PREV_CHUNKS * CHUNK
REL_CLIP = 256
H_C = BRANCH_W // (2 * HEAD_D)
T5_BUCKETS = 32
T5_MAX_DIST = 512
H_D = BRANCH_W // HEAD_D
MLA_NOPE = 64
MLA_ROPE = 32
MLA_V = 64
Q_LORA = 3 * D_MODEL // 16
KV_LORA = D_MODEL // 16
ROPE_THETA = 10000.0
MLA_SCALE = (MLA_NOPE + MLA_ROPE) ** -0.5
IN_WIDTHS = (3 * BRANCH_W, 3 * BRANCH_W, 3 * BRANCH_W, Q_LORA + KV_LORA + MLA_ROPE,
             N_BRANCH * BRANCH_W, N_BRANCH * D_MODEL)
D_IN = sum(IN_WIDTHS)
SPLIT_POINTS = tuple(int(c) for c in np.cumsum(IN_WIDTHS)[:-1])

kernel_name = 'hybrid_streaming_encoder_step'


def rmsnorm(x, g):
    xf = x.astype(jnp.float32)
    y = xf * lax.rsqrt(jnp.mean(xf * xf, axis=-1, keepdims=True) + EPS)
    return (y * g.astype(jnp.float32)).astype(x.dtype)


def rope(x, pos):
    r = x.shape[-1]
    inv = ROPE_THETA ** (-jnp.arange(0, r, 2, dtype=jnp.float32) / r)
    ang = pos.astype(jnp.float32)[:, None] * inv
    if x.ndim == 4:
        ang = ang[:, None, :]
    cos, sin = jnp.cos(ang), jnp.sin(ang)
    xf = x.astype(jnp.float32)
    x1, x2 = xf[..., : r // 2], xf[..., r // 2:]
    return jnp.concatenate([x1 * cos - x2 * sin, x2 * cos + x1 * sin], axis=-1).astype(x.dtype)


def chunk_causal(q_pos, k_pos):
    return (k_pos[None, :] // CHUNK) <= (q_pos[:, None] // CHUNK)


def over_query_blocks(fn, qs, q_pos):
    s_len = q_pos.shape[0]
    if s_len <= Q_BLOCK:
        return fn(qs, q_pos)
    nb = s_len // Q_BLOCK
    qb = tuple(jnp.moveaxis(q.reshape(q.shape[0], nb, Q_BLOCK, *q.shape[2:]), 1, 0) for q in qs)
    out = lax.map(lambda a: fn(a[0], a[1]), (qb, q_pos.reshape(nb, Q_BLOCK)))
    return jnp.moveaxis(out, 0, 1).reshape(out.shape[1], s_len, *out.shape[3:])


def stick_breaking_block(q, q_pos, k, v, k_pos):
    z = jnp.einsum('bqhd,bkhd->bhqk', q, k, preferred_element_type=jnp.float32) * HEAD_D ** -0.5
    mask = k_pos[None, :] < q_pos[:, None]
    log_keep = jnp.where(mask, jax.nn.log_sigmoid(-z), 0.0)
    later = lax.cumsum(log_keep, axis=3, reverse=True) - log_keep
    w = jnp.where(mask, jnp.exp(jax.nn.log_sigmoid(z) + later), 0.0)
    return jnp.einsum('bhqk,bkhd->bqhd', w.astype(v.dtype), v)


def t5_bucket(rel):
    nb = T5_BUCKETS // 2
    max_exact = nb // 2
    ret = jnp.where(rel > 0, nb, 0)
    n = jnp.abs(rel)
    nf = jnp.maximum(n, 1).astype(jnp.float32)
    large = max_exact + (jnp.log(nf / max_exact) / math.log(T5_MAX_DIST / max_exact)
                         * (nb - max_exact)).astype(jnp.int32)
    large = jnp.minimum(large, nb - 1)
    return ret + jnp.where(n < max_exact, n, large)


def t5_bias(q_pos, k_pos, table):
    b = t5_bucket(k_pos[None, :] - q_pos[:, None])
    return jnp.moveaxis(table[b].astype(jnp.float32), -1, 0)


def diff_attn_block(q, q_pos, k, v, k_pos, table, lam):
    s = jnp.einsum('bqhmd,bkhmd->bmhqk', q, k, preferred_element_type=jnp.float32) * HEAD_D ** -0.5
    s = jnp.where(chunk_causal(q_pos, k_pos), s + t5_bias(q_pos, k_pos, table), NEG)
    p = jax.nn.softmax(s, axis=-1)
    w = p[:, 0] - lam * p[:, 1]
    return jnp.einsum('bhqk,bkhe->bqhe', w.astype(v.dtype), v)


def mla_block(q_nope, q_rope, q_pos, k_nope, k_rope, v, k_pos):
    s = (jnp.einsum('bqhd,bkhd->bhqk', q_nope, k_nope, preferred_element_type=jnp.float32)
         + jnp.einsum('bqhr,bkr->bhqk', q_rope, k_rope, preferred_element_type=jnp.float32)) * MLA_SCALE
    p = jax.nn.softmax(jnp.where(chunk_causal(q_pos, k_pos), s, NEG), axis=-1)
    return jnp.einsum('bhqk,bkhd->bqhd', p.astype(v.dtype), v)


def band_rel_bias(rel, table):
    idx = jnp.clip(rel, -REL_CLIP, REL_CLIP) + REL_CLIP
    return jnp.moveaxis(table[idx].astype(jnp.float32), -1, 0)


def band_attend(q, k, v, bias, mask):
    s = jnp.einsum('...qhd,...khd->...hqk', q, k, preferred_element_type=jnp.float32) * HEAD_D ** -0.5
    p = jax.nn.softmax(jnp.where(mask, s + bias, NEG), axis=-1)
    return jnp.einsum('...hqk,...khd->...qhd', p.astype(v.dtype), v)


def band_prompt(q, k, v, table):
    bsz, s_len = q.shape[:2]
    nc = s_len // CHUNK
    n_band = N_PREV_CHUNKS + 1
    band_idx = jnp.arange(nc)[:, None] + jnp.arange(n_band)[None, :]

    def gather_band(t):
        t = t.reshape(bsz, nc, CHUNK, H_B, HEAD_D)
        t = jnp.pad(t, ((0, 0), (N_PREV_CHUNKS, 0), (0, 0), (0, 0), (0, 0)))
        return t[:, band_idx].reshape(bsz, nc, n_band * CHUNK, H_B, HEAD_D)

    valid = jnp.repeat(band_idx >= N_PREV_CHUNKS, CHUNK, axis=1)
    rel = N_PREV_CHUNKS * CHUNK + jnp.arange(CHUNK)[:, None] - jnp.arange(n_band * CHUNK)[None, :]
    out = band_attend(q.reshape(bsz, nc, CHUNK, H_B, HEAD_D), gather_band(k), gather_band(v),
                      band_rel_bias(rel, table), valid[:, None, None, :])
    return out.reshape(bsz, s_len, H_B, HEAD_D)


def band_sample(q, k, v, cache_k, cache_v, pos, table):
    cache_len = cache_k.shape[1]
    k_all = jnp.concatenate([cache_k, k], axis=1)
    v_all = jnp.concatenate([cache_v, v], axis=1)
    k_pos = jnp.concatenate([pos[0] - cache_len + jnp.arange(cache_len), pos])
    q_ch = pos[:, None] // CHUNK
    k_ch = k_pos[None, :] // CHUNK
    mask = (k_ch <= q_ch) & (q_ch - k_ch <= N_PREV_CHUNKS) & (k_pos[None, :] >= 0)
    out = band_attend(q, k_all, v_all, band_rel_bias(pos[:, None] - k_pos[None, :], table), mask)
    return out, k_all[:, -cache_len:], v_all[:, -cache_len:]


def hybrid_layer(x, pos, past, l, p):
    (norm_pre, norm_post, w_in, band_bias, t5_table, diff_lambda, diff_subln,
     mla_q_norm, mla_w_q_up, mla_kv_norm, mla_w_kv_up, w_branch, w_out) = p
    bsz, s_len, _ = x.shape
    h = rmsnorm(x, norm_pre[l])
    proj = jnp.einsum('bsd,de->bse', h, w_in[l])
    a_qkv, b_qkv, c_qkv, d_in, gate_in, merge_in = jnp.split(proj, SPLIT_POINTS, axis=-1)

    a_q, a_k, a_v = (t.reshape(bsz, s_len, H_A, HEAD_D) for t in jnp.split(a_qkv, 3, axis=-1))
    b_q, b_k, b_v = (t.reshape(bsz, s_len, H_B, HEAD_D) for t in jnp.split(b_qkv, 3, axis=-1))
    c_q, c_k = (t.reshape(bsz, s_len, H_C, 2, HEAD_D) for t in jnp.split(c_qkv[..., : 2 * BRANCH_W], 2, axis=-1))
    c_v = c_qkv[..., 2 * BRANCH_W:].reshape(bsz, s_len, H_C, 2 * HEAD_D)
    q_down, kv_down, k_rope_in = jnp.split(d_in, (Q_LORA, Q_LORA + KV_LORA), axis=-1)
    q_mla = jnp.einsum('bsr,re->bse', rmsnorm(q_down, mla_q_norm[l]), mla_w_q_up[l])
    q_mla = q_mla.reshape(bsz, s_len, H_D, MLA_NOPE + MLA_ROPE)
    q_nope = q_mla[..., :MLA_NOPE]
    q_rope = rope(q_mla[..., MLA_NOPE:], pos)
    ckv = rmsnorm(kv_down, mla_kv_norm[l])
    k_rope = rope(k_rope_in, pos)

    if past is None:
        a_k_all, a_v_all, c_k_all, c_v_all, ckv_all, k_rope_all, k_pos = a_k, a_v, c_k, c_v, ckv, k_rope, pos
        out_b = band_prompt(b_q, b_k, b_v, band_bias[l])
        keep = min(B_REACH, s_len)
        band_k_new, band_v_new = b_k[:, s_len - keep:], b_v[:, s_len - keep:]
    else:
        sb_k, sb_v, bd_k, bd_v, df_k, df_v, m_ckv, m_kr = past
        k_pos = jnp.concatenate([jnp.arange(sb_k.shape[1]), pos])
        a_k_all = jnp.concatenate([sb_k, a_k], axis=1)
        a_v_all = jnp.concatenate([sb_v, a_v], axis=1)
        c_k_all = jnp.concatenate([df_k, c_k], axis=1)
        c_v_all = jnp.concatenate([df_v, c_v], axis=1)
        ckv_all = jnp.concatenate([m_ckv, ckv], axis=1)
        k_rope_all = jnp.concatenate([m_kr, k_rope], axis=1)
        out_b, band_k_new, band_v_new = band_sample(b_q, b_k, b_v, bd_k, bd_v, pos, band_bias[l])

    out_a = over_query_blocks(lambda qs, qp: stick_breaking_block(qs[0], qp, a_k_all, a_v_all, k_pos), (a_q,), pos)
    lam_init = 0.8 - 0.6 * math.exp(-0.3 * l)
    lv = diff_lambda[l].astype(jnp.float32)
    lam = jnp.exp(jnp.sum(lv[0] * lv[1])) - jnp.exp(jnp.sum(lv[2] * lv[3])) + lam_init
    out_c = over_query_blocks(lambda qs, qp: diff_attn_block(qs[0], qp, c_k_all, c_v_all, k_pos, t5_table, lam), (c_q,), pos)
    out_c = rmsnorm(out_c, diff_subln[l]) * (1.0 - lam_init)
    kv = jnp.einsum('bkr,re->bke', ckv_all, mla_w_kv_up[l]).reshape(bsz, -1, H_D, MLA_NOPE + MLA_V)
    k_nope, v_mla = kv[..., :MLA_NOPE], kv[..., MLA_NOPE:]
    out_d = over_query_blocks(lambda qs, qp: mla_block(qs[0], qs[1], qp, k_nope, k_rope_all, v_mla, k_pos), (q_nope, q_rope), pos)

    outs = jnp.stack([t.reshape(bsz, s_len, BRANCH_W) for t in (out_a, out_b, out_c, out_d)], axis=2)
    gated = outs * jax.nn.silu(gate_in.reshape(bsz, s_len, N_BRANCH, BRANCH_W))
    y_branch = jnp.einsum('bsnc,ncd->bsnd', gated, w_branch[l])
    merge = jax.nn.sigmoid(merge_in.reshape(bsz, s_len, N_BRANCH, D_MODEL))
    y = jnp.einsum('bsd,de->bse', jnp.sum(merge * y_branch, axis=2), w_out[l])
    x = x + rmsnorm(y, norm_post[l])
    return x, (a_k, a_v, band_k_new, band_v_new, c_k, c_v, ckv, k_rope)


def setup_inputs(seed: int = 0) -> dict:
    key = jax.random.key(seed)
    ks = jax.random.split(key, 24)
    f32 = jnp.float32

    def nrm(k, shape, scale):
        return scale * jax.random.normal(k, shape, f32)

    band_cache = min(B_REACH, PAST_LEN)
    return {
        'x_prompt': nrm(ks[0], (BATCH, SEQ, D_MODEL), 1.0),
        'x_sample': nrm(ks[1], (DEC_BATCH, DEC_SEQ, D_MODEL), 1.0),
        'cache_sb_k': nrm(ks[2], (DEPTH, DEC_BATCH, PAST_LEN, H_A, HEAD_D), 1.0),
        'cache_sb_v': nrm(ks[3], (DEPTH, DEC_BATCH, PAST_LEN, H_A, HEAD_D), 1.0),
        'cache_band_k': nrm(ks[4], (DEPTH, DEC_BATCH, band_cache, H_B, HEAD_D), 1.0),
        'cache_band_v': nrm(ks[5], (DEPTH, DEC_BATCH, band_cache, H_B, HEAD_D), 1.0),
        'cache_diff_k': nrm(ks[6], (DEPTH, DEC_BATCH, PAST_LEN, H_C, 2, HEAD_D), 1.0),
        'cache_diff_v': nrm(ks[7], (DEPTH, DEC_BATCH, PAST_LEN, H_C, 2 * HEAD_D), 1.0),
        'cache_mla_ckv': nrm(ks[8], (DEPTH, DEC_BATCH, PAST_LEN, KV_LORA), 1.0),
        'cache_mla_krope': nrm(ks[9], (DEPTH, DEC_BATCH, PAST_LEN, MLA_ROPE), 1.0),
        'norm_pre': 1.0 + nrm(ks[10], (DEPTH, D_MODEL), 0.02),
        'norm_post': 1.0 + nrm(ks[11], (DEPTH, D_MODEL), 0.02),
        'w_in': nrm(ks[12], (DEPTH, D_MODEL, D_IN), D_MODEL ** -0.5),
        'band_bias': nrm(ks[13], (DEPTH, 2 * REL_CLIP + 1, H_B), 0.1),
        't5_table': nrm(ks[14], (T5_BUCKETS, H_C), 0.1),
        'diff_lambda': nrm(ks[15], (DEPTH, 4, HEAD_D), 0.1),
        'diff_subln': 1.0 + nrm(ks[16], (DEPTH, 2 * HEAD_D), 0.02),
        'mla_q_norm': 1.0 + nrm(ks[17], (DEPTH, Q_LORA), 0.02),
        'mla_w_q_up': nrm(ks[18], (DEPTH, Q_LORA, H_D * (MLA_NOPE + MLA_ROPE)), Q_LORA ** -0.5),
        'mla_kv_norm': 1.0 + nrm(ks[19], (DEPTH, KV_LORA), 0.02),
        'mla_w_kv_up': nrm(ks[20], (DEPTH, KV_LORA, H_D * (MLA_NOPE + MLA_V)), KV_LORA ** -0.5),
        'w_branch': nrm(ks[21], (DEPTH, N_BRANCH, BRANCH_W, D_MODEL), BRANCH_W ** -0.5),
        'w_out': nrm(ks[22], (DEPTH, D_MODEL, D_MODEL), D_MODEL ** -0.5),
    }


def reference(x_prompt, x_sample, cache_sb_k, cache_sb_v, cache_band_k, cache_band_v,
              cache_diff_k, cache_diff_v, cache_mla_ckv, cache_mla_krope,
              norm_pre, norm_post, w_in, band_bias, t5_table, diff_lambda, diff_subln,
              mla_q_norm, mla_w_q_up, mla_kv_norm, mla_w_kv_up, w_branch, w_out):
    params = (norm_pre, norm_post, w_in, band_bias, t5_table, diff_lambda, diff_subln,
              mla_q_norm, mla_w_q_up, mla_kv_norm, mla_w_kv_up, w_branch, w_out)
    pos_p = jnp.arange(x_prompt.shape[1])
    pos_s = PAST_LEN + jnp.arange(x_sample.shape[1])
    xp, xs = x_prompt, x_sample
    st_p, st_s = [], []
    for l in range(DEPTH):
        xp, sp = hybrid_layer(xp, pos_p, None, l, params)
        past = (cache_sb_k[l], cache_sb_v[l], cache_band_k[l], cache_band_v[l],
                cache_diff_k[l], cache_diff_v[l], cache_mla_ckv[l], cache_mla_krope[l])
        xs, ss = hybrid_layer(xs, pos_s, past, l, params)
        st_p.append(sp)
        st_s.append(ss)
    sb_k_p, sb_v_p, band_k_p, band_v_p, diff_k_p, diff_v_p, ckv_p, krope_p = (jnp.stack(t) for t in zip(*st_p))
    sb_k_s, sb_v_s, band_k_s, band_v_s, diff_k_s, diff_v_s, ckv_s, krope_s = (jnp.stack(t) for t in zip(*st_s))
    return (xp, xs, sb_k_p, sb_v_p, band_k_p, band_v_p, diff_k_p, diff_v_p, ckv_p, krope_p,
            sb_k_s, sb_v_s, band_k_s, band_v_s, diff_k_s, diff_v_s, ckv_s, krope_s)
```

```python
import contextlib
import math
import numpy as np
import concourse.bass as bass
import concourse.mybir as mybir
from concourse.bass_utils import run_bass_kernel_spmd

F32 = mybir.dt.float32
BF16 = mybir.dt.bfloat16
AF = mybir.ActivationFunctionType
ALU = mybir.AluOpType

D = 2048
DC = 16
TP = 1024
NS = 1
NPT = TP // 128
T = TP + 128
NT = T // 128
KVW = 3232
KCOL = {'A.k': 0, 'A.v': 512, 'B.k': 1024, 'B.v': 1536, 'C.k': 2048, 'C.v': 2560, 'ckv': 3072, 'kr': 3200}
DIN = 15392
EPS = 1e-6
PAST = 1024
LC = 1408
WC = 1280
LB = 1536
WB = 1408
MLA_SCALE = 96 ** -0.5
DEPTH = 2


class R:
    __slots__ = ("w", "r", "name", "excl")

    def __init__(self, name="", excl=False):
        self.w = None
        self.r = []
        self.name = name
        self.excl = excl


class Sched:
    def __init__(self, nc, es):
        self.nc = nc
        self.eng = {"pe": nc.tensor, "act": nc.scalar, "dve": nc.vector, "pool": nc.gpsimd, "sp": nc.sync}
        self.csem = {}
        self.ccnt = {}
        for e in ("pe", "act", "dve", "pool"):
            self.csem[e] = es.enter_context(nc.semaphore("c_" + e))
            self.ccnt[e] = 0
        self.dsem = {}
        self.dcnt = {}
        self.dnext = {}
        for q, n in (("sp", 40), ("pool", 40)):
            self.dsem[q] = [es.enter_context(nc.semaphore(f"d_{q}{i}")) for i in range(n)]
            self.dcnt[q] = [0] * n
            self.dnext[q] = 0
        self.seen = {e: {} for e in self.eng}
        self.all_dma_tokens = []
        self.cccnt = 0
        self.ccsem = es.enter_context(nc.semaphore("ccsem"))

    def _wait(self, e, tok):
        sem, val, owner = tok
        if e == "pe" and owner == "pe":
            return
        key = id(sem)
        if self.seen[e].get(key, 0) >= val:
            return
        self.eng[e].wait_ge(sem, val)
        self.seen[e][key] = val

    def _deps(self, e, r, w):
        for x in r:
            if x.w is not None:
                self._wait(e, x.w)
            if x.excl:
                for t in x.r:
                    if t[2] != e:
                        self._wait(e, t)
        for x in w:
            if x.w is not None:
                self._wait(e, x.w)
            for t in x.r:
                self._wait(e, t)

    def _commit(self, tok, r, w):
        for x in r:
            x.r.append(tok)
            if len(x.r) > 24:
                x.r = x.r[-24:]
        for x in w:
            x.w = tok
            x.r = []

    def op(self, e, fn, r=(), w=()):
        if _DEAD[0]:
            return None
        self._deps(e, r, w)
        inst = fn(self.eng[e])
        self.ccnt[e] += 1
        tok = (self.csem[e], self.ccnt[e], e)
        inst.then_inc(tok[0], 1)
        self._commit(tok, r, w)
        return tok

    def mm(self, fns, r=(), w=()):
        if _DEAD[0]:
            return None
        self._deps("pe", r, w)
        inst = None
        for fn in fns:
            inst = fn(self.eng["pe"])
        self.ccnt["pe"] += 1
        tok = (self.csem["pe"], self.ccnt["pe"], "pe")
        inst.then_inc(tok[0], 1)
        self._commit(tok, r, w)
        return tok

    def dma(self, q, out, in_, r=(), w=(), **kw):
        if _DEAD[0]:
            return None
        i = self.dnext[q]
        self.dnext[q] = (i + 1) % len(self.dsem[q])
        sem = self.dsem[q][i]
        if self.dcnt[q][i]:
            self._wait(q, (sem, self.dcnt[q][i], "dma"))
        self._deps(q, r, w)
        inst = self.eng[q].dma_start(out=out, in_=in_, **kw)
        self.dcnt[q][i] += 16
        tok = (sem, self.dcnt[q][i], "dma")
        inst.then_inc(sem, 16)
        self._commit(tok, r, w)
        return tok

    def collective(self, ccsem, in_t, out_t, r=(), w=()):
        if _DEAD[0]:
            return None
        self._deps("pool", r, w)
        inst = self.eng["pool"].collective_compute(
            "AllGather", ALU.bypass, replica_groups=[[2 * i_, 2 * i_ + 1] for i_ in range(_NCORES[0] // 2)],
            ins=[in_t.ap().opt()], outs=[out_t.ap().opt()])
        self.cccnt += 1
        tok = (ccsem, self.cccnt, "cc")
        inst.then_inc(ccsem)
        self._commit(tok, r, w)
        return tok

    def barrier(self):
        if _DEAD[0]:
            return
        toks = []
        for e in ("pe", "act", "dve", "pool"):
            if self.ccnt[e]:
                toks.append((self.csem[e], self.ccnt[e], e + "_b"))
        for q in self.dsem:
            for i, s in enumerate(self.dsem[q]):
                if self.dcnt[q][i]:
                    toks.append((s, self.dcnt[q][i], "dma"))
        if self.cccnt:
            toks.append((self.ccsem, self.cccnt, "cc"))
        for e in self.eng:
            for t in toks:
                if e == "pe" and t[2] == "pe_b":
                    continue
                self._wait(e, t)

    def finish(self):
        for q in self.dsem:
            for i, s in enumerate(self.dsem[q]):
                if self.dcnt[q][i]:
                    self._wait("sp", (s, self.dcnt[q][i], "dma"))


_UID = [0]
_STOP = [99]
_NCORES = [8]
WARM = 3


class _StopBuild(Exception):
    pass


_DEAD = [False]


def _chk(n):
    if round(_STOP[0] * 1000) <= round(n * 1000):
        _DEAD[0] = True


class Pool:
    def __init__(self, es, nc, name, n, shape, dtype):
        _UID[0] += 1
        self.t = [es.enter_context(nc.sbuf_tensor(f"{name}{i}_{_UID[0]}", shape, dtype)) for i in range(n)]
        self.res = [R(f"{name}{i}") for i in range(n)]
        self.i = 0

    def get(self):
        i = self.i
        self.i = (i + 1) % len(self.t)
        return self.t[i], self.res[i]


def bc_ap(ap, dims):
    return bass.AP(tensor=ap.tensor, offset=ap.offset, ap=[list(ap.ap[0])] + [list(d) for d in dims])


def build_nc():
    nc = bass.Bass("TRN2", target_bir_lowering=False)
    _DEAD[0] = False
    dt = nc.dram_tensor

    def inp(name, shape, dtype=F32):
        return dt(name, list(shape), dtype, kind="ExternalInput").ap()

    def outp(name, shape):
        return dt(name, list(shape), F32, kind="ExternalOutput").ap()

    def scr(name, shape, dtype=F32):
        return dt(name, list(shape), dtype).ap()

    xp = inp("xp", [TP, D])
    xs = inp("xs", [128, D])
    csbk = inp("csbk", [2, NS, PAST, 512])
    csbv = inp("csbv", [2, NS, PAST, 512])
    cbk = inp("cbk", [2, NS, 512, 512])
    cbv = inp("cbv", [2, NS, 512, 512])
    cdk = inp("cdk", [2, NS, PAST, 512])
    cdv = inp("cdv", [2, NS, PAST, 512])
    cckv = inp("cckv", [2, NS, PAST, 128])
    ckr = inp("ckr", [2, NS, PAST, 32])
    norm_pre = inp("norm_pre", [2, D])
    norm_post = inp("norm_post", [2, D])
    w_in = inp("w_in", [2, D, DIN])
    band_bias = inp("band_bias", [2, 513, 8])
    t5 = inp("t5", [32, 4])
    dlam = inp("dlam", [2, 256])
    subln = inp("subln", [2, 128])
    qnorm = inp("qnorm", [2, 384])
    wqup = inp("wqup", [2, 384, 768])
    kvnorm = inp("kvnorm", [2, 128])
    wkvup = inp("wkvup", [2, 128, 1024])
    wbr = inp("wbr", [2, 4, 512, D])
    wout = inp("wout", [2, D, D])
    c_ident = inp("c_ident", [128, 128])
    c_j = inp("c_j", [128, 128])
    c_tri = inp("c_tri", [3, 128, 128])
    c_masks = inp("c_masks", [16, 128, 512])
    c_oh = inp("c_oh", [32, LC])
    c_cs = inp("c_cs", [T, 32])
    c_vb = inp("c_vb", [128, 1])

    yp = outp("yp", [TP, D])
    ys = outp("ys", [128, D])
    sbk_p = outp("sbk_p", [2, TP, 512])
    sbv_p = outp("sbv_p", [2, TP, 512])
    bk_p = outp("bk_p", [2, 512, 512])
    bv_p = outp("bv_p", [2, 512, 512])
    dk_p = outp("dk_p", [2, TP, 512])
    dv_p = outp("dv_p", [2, TP, 512])
    ckv_p = outp("ckv_p", [2, TP, 128])
    kr_p = outp("kr_p", [2, TP, 32])
    sbk_s = outp("sbk_s", [2, NS, 64, 512])
    sbv_s = outp("sbv_s", [2, NS, 64, 512])
    bk_s = outp("bk_s", [2, NS, 512, 512])
    bv_s = outp("bv_s", [2, NS, 512, 512])
    dk_s = outp("dk_s", [2, NS, 64, 512])
    dv_s = outp("dv_s", [2, NS, 64, 512])
    ckv_s = outp("ckv_s", [2, NS, 64, 128])
    kr_s = outp("kr_s", [2, NS, 64, 32])

    x1 = scr("x1", [T, D])
    kvloc_t = [dt(f"kvloc{t_}", [128, KVW], F32) for t_ in range(NPT)]
    kvloc = [t_.ap() for t_ in kvloc_t]
    kvall_t = [[dt(f"kvall{i}_{t_}", [256, KVW], F32) for t_ in range(NPT)] for i in range(2)]
    kvall = [[t_.ap() for t_ in row] for row in kvall_t]
    qT_scr = scr("qT_scr", [3, 512, T], BF16)
    gT_scr = scr("gT_scr", [2048, T], BF16)
    mT_scr = scr("mT_scr", [8192, T], BF16)
    qn_scr = scr("qn_scr", [T, 384])
    gvC_scr = scr("gvC_scr", [4, LC])
    gvB_scr = scr("gvB_scr", [8, LB])

    with contextlib.ExitStack() as es:
        S = Sched(nc, es)

        try:
            def sb(name, shape, dtype, stack=es):
                _UID[0] += 1
                return stack.enter_context(nc.sbuf_tensor(f"{name}_{_UID[0]}", list(shape), dtype))

            PS = [es.enter_context(nc.psum_tensor(f"ps{i}", [128, 512], F32)) for i in range(7)]
            PSR = [R(f"ps{i}", excl=True) for i in range(7)]
            psT = es.enter_context(nc.psum_tensor("psT", [128, 1024], BF16))
            psTR = R("psT", excl=True)

            identb = sb("identb", [128, 128], BF16)
            jm = sb("jm", [128, 128], F32)
            trib = sb("trib", [128, 3, 128], BF16)
            masks = sb("masks", [128, 16, 512], BF16)
            EGC = sb("EGC", [128, 4, WC], BF16)
            EGB = sb("EGB", [128, 8, WB], BF16)
            satC = sb("satC", [128, 4], F32)
            cR = R("consts")
            EGCR = R("EGC")
            EGBR = R("EGB")
            S.dma("pool", identb[:], c_ident[:, :], w=[cR])
            S.dma("sp", jm[:], c_j[:, :], w=[cR])
            S.dma("pool", trib[:], c_tri.rearrange("m p n -> p m n"), w=[cR])
            for m4 in range(4):
                S.dma("pool", masks[:, 4 * m4:4 * m4 + 4, :],
                      c_masks[4 * m4:4 * m4 + 4].rearrange("m p n -> p m n"), w=[cR])
            S.dma("sp", satC[:], bass.AP(tensor=t5.tensor, offset=15 * 4, ap=[[0, 128], [1, 4]]), w=[cR])
            vbt = sb("vbt", [128, 1], F32)
            satCv = sb("satCv", [128, 4], F32)
            S.dma("sp", vbt[:], c_vb[:, :], w=[cR])
            S.op("dve", lambda v: v.tensor_scalar(out=satCv[:], in0=satC[:], scalar1=vbt[:, 0:1], scalar2=None, op0=ALU.add),
                 r=[cR], w=[cR])
            kvallR = [[R(f"kvall{i}_{t_}") for t_ in range(NPT)] for i in range(2)]
            TRI = trib[:, 0, :]
            TRIC = trib[:, 1, :]
            ONES = trib[:, 2, :]

            def toeplitz_build(gv_scr, nheads, L, W, EG, EGR, stack, defer=None):
                nb = 1 if defer is None else 2
                hks = [sb("hk", [128, W], F32, stack) for _ in range(nb)]
                hkRs = [R("hk") for _ in range(nb)]

                def dma_step(h):
                    src = bass.AP(tensor=gv_scr.tensor, offset=h * L, ap=[[1, 128], [1, W]])
                    S.dma("sp", hks[h % nb][:], src, r=[gvR], w=[hkRs[h % nb]])

                def comp_step(h, bank=None):
                    hk, hkR = hks[h % nb], hkRs[h % nb]
                    c = 0
                    bi = 0
                    while c < W:
                        n = min(512, W - c)
                        if bank is None:
                            ps_, pr_ = PS[bi % 2], PSR[bi % 2]
                        else:
                            ps_, pr_ = bank()
                        S.mm([lambda pe: pe.matmul(ps_[:, 0:n], lhsT=jm[:], rhs=hk[:, c:c + n], start=True, stop=True)],
                             r=[hkR, cR], w=[pr_])
                        S.op("act", lambda a: a.activation(out=EG[:, h, c:c + n], in_=ps_[:, 0:n], func=AF.Exp),
                             r=[pr_], w=[EGR])
                        c += n
                        bi += 1

                if defer is None:
                    for h in range(nheads):
                        dma_step(h)
                        comp_step(h)
                else:
                    defer.append(lambda: dma_step(0))
                    for h in range(nheads):
                        if h + 1 < nheads:
                            defer.append(lambda h=h: dma_step(h + 1))
                        defer.append(lambda h=h: comp_step(h, bank=defer_bank[0]))

            defer_bank = [None]
            gvR = R("gv")
            with contextlib.ExitStack() as st0:
                t5sb = sb("t5sb", [32, 4], F32, st0)
                ohsb = sb("ohsb", [32, LC], F32, st0)
                gvsb = sb("gvsb", [4, LC], F32, st0)
                tr = R("t0")
                S.dma("sp", t5sb[:], t5[:, :], w=[tr])
                S.dma("sp", ohsb[:], c_oh[:, :], w=[tr])
                c = 0
                gvsR = R("gvs")
                while c < LC:
                    n = min(512, LC - c)
                    S.mm([lambda pe, c=c, n=n: pe.matmul(PS[0][0:4, 0:n], lhsT=t5sb[:], rhs=ohsb[:, c:c + n],
                                                         start=True, stop=True)], r=[tr], w=[PSR[0]])
                    S.op("dve", lambda v, c=c, n=n: v.tensor_copy(out=gvsb[:, c:c + n], in_=PS[0][0:4, 0:n]),
                         r=[PSR[0]], w=[gvsR])
                    c += n
                S.dma("sp", gvC_scr[:, :], gvsb[:], r=[gvsR], w=[gvR])
                toeplitz_build(gvC_scr, 4, LC, WC, EGC, EGCR, st0)
                S.barrier()
                _chk(0)

            RX = [R(f"x{t}") for t in range(NT)]
            RKV = {}

            def rkv(name, i):
                k = (name, i)
                if k not in RKV:
                    RKV[k] = R(str(k))
                return RKV[k]

            RFM = {}

            def rfm(name, tb):
                k = (name, tb)
                if k not in RFM:
                    RFM[k] = R(str(k))
                return RFM[k]

            TB = [(0, 512), (512, 512), (1024, 128)]

            for l in range(DEPTH):
                lam_init = 0.8 - 0.6 * math.exp(-0.3 * l)
                last = (l == DEPTH - 1)
                with contextlib.ExitStack() as sl:
                    wqupb = sb("wqupb", [128, 3, 768], BF16, sl)
                    wkvupb = sb("wkvupb", [128, 1024], BF16, sl)
                    lcol = sb("lcol", [128, 8], F32, sl)
                    lR = R("layerconst")
                    S.dma("pool", wqupb[:], wqup[l].rearrange("(c p) n -> p c n", p=128), w=[lR])
                    S.dma("pool", wkvupb[:], wkvup[l], w=[lR])

                    with contextlib.ExitStack() as s2:
                        hT = sb("hT", [128, DC, T], BF16, s2)
                        hTR = [R(f"hT{t}") for t in range(NT)]
                        DEFER = []
                        bb = sb("bb", [8, 513], F32, s2)
                        gvb = sb("gvb", [8, LB], F32, s2)
                        bbR = R("bb")
                        S.dma("sp", bb[:], bass.AP(tensor=band_bias.tensor, offset=l * 513 * 8, ap=[[1, 8], [8, 513]]),
                              w=[bbR], allow_slow_non_contiguous=True)
                        gvbR = R("gvb")
                        S.op("dve", lambda v: v.tensor_copy(out=gvb[:, 0:255], in_=bc_ap(bb[:, 0:1], [[0, 255]])),
                             r=[bbR], w=[gvbR])
                        S.op("dve", lambda v: v.tensor_copy(out=gvb[:, 255:768], in_=bb[:, 0:513]), r=[bbR], w=[gvbR])
                        S.op("dve", lambda v: v.tensor_copy(out=gvb[:, 768:LB], in_=bc_ap(bb[:, 512:513], [[0, LB - 768]])),
                             r=[bbR], w=[gvbR])
                        S.dma("sp", gvB_scr[:, :], gvb[:], r=[gvbR], w=[gvR])
                        toeplitz_build(gvB_scr, 8, LB, WB, EGB, EGBR, s2, defer=DEFER)
                        dl = sb("dl", [128, 256], F32, s2)
                        dj = sb("dj", [128, 128], F32, s2)
                        sg = sb("sg", [128, 128], F32, s2)
                        la = sb("la", [128, 8], F32, s2)
                        dR = R("dl")
                        S.dma("sp", dl[:], bass.AP(tensor=dlam.tensor, offset=l * 256, ap=[[0, 128], [1, 256]]), w=[dR])
                        S.dma("sp", sg[:, 0:1], bass.AP(tensor=subln.tensor, offset=l * 128, ap=[[1, 128], [1, 1]]), w=[dR])
                        S.op("dve", lambda v: v.tensor_tensor(out=dj[:, 0:64], in0=dl[:, 0:64], in1=dl[:, 64:128], op=ALU.mult),
                             r=[dR], w=[dR])
                        S.op("dve", lambda v: v.tensor_tensor(out=dj[:, 64:128], in0=dl[:, 128:192], in1=dl[:, 192:256], op=ALU.mult),
                             r=[dR], w=[dR])
                        S.op("dve", lambda v: v.reduce_sum(out=la[:, 0:1], in_=dj[:, 0:64], axis=mybir.AxisListType.X), r=[dR], w=[dR])
                        S.op("dve", lambda v: v.reduce_sum(out=la[:, 1:2], in_=dj[:, 64:128], axis=mybir.AxisListType.X), r=[dR], w=[dR])
                        S.op("act", lambda a: a.activation(out=la[:, 2:4], in_=la[:, 0:2], func=AF.Exp), r=[dR], w=[dR])
                        S.op("dve", lambda v: v.tensor_tensor(out=la[:, 4:5], in0=la[:, 3:4], in1=la[:, 2:3], op=ALU.subtract),
                             r=[dR], w=[dR])
                        S.op("dve", lambda v: v.tensor_scalar(out=lcol[:, 0:1], in0=la[:, 4:5], scalar1=-lam_init, scalar2=None,
                                                              op0=ALU.add), r=[dR], w=[lR])
                        S.op("dve", lambda v: v.tensor_scalar(out=lcol[:, 1:2], in0=sg[:, 0:1], scalar1=(1.0 - lam_init),
                                                              scalar2=None, op0=ALU.mult), r=[dR], w=[lR])
                        with contextlib.ExitStack() as s1:
                            gpre = sb("gpre", [128, D], F32, s1)
                            gR = R("gpre")
                            S.dma("sp", gpre[:], bass.AP(tensor=norm_pre.tensor, offset=l * D, ap=[[0, 128], [1, D]]), w=[gR])
                            xpool = Pool(s1, nc, "xt", 2, [128, D], F32)
                            hbpool = Pool(s1, nc, "hb", 2, [128, D], BF16)
                            junk = sb("junk", [128, D], BF16, s1)
                            junkR = R("junk")
                            sspool = Pool(s1, nc, "ss", 2, [128, 4], F32)
                            p1st = {}

                            def p1front(tt):
                                xt, xr = xpool.get()
                                if l == 0:
                                    src = xp[tt * 128:(tt + 1) * 128, :] if tt < NPT else xs[:, :]
                                    S.dma("sp", xt[:], src, w=[xr])
                                else:
                                    S.dma("sp", xt[:], x1[tt * 128:(tt + 1) * 128, :], r=[RX[tt]], w=[xr])
                                ss, ssr = sspool.get()
                                S.op("dve", lambda v, ss=ss: v.memset(ss[:], 0.0), w=[ssr])
                                S.op("act", lambda a, xt=xt, ss=ss: a.activation(out=junk[:], in_=xt[:], func=AF.Square,
                                                                                 accum_out=ss[:, 0:1]),
                                     r=[xr], w=[junkR, ssr])
                                S.op("dve", lambda v, ss=ss: v.tensor_scalar(out=ss[:, 1:2], in0=ss[:, 0:1], scalar1=1.0 / D,
                                                                              scalar2=EPS, op0=ALU.mult, op1=ALU.add),
                                     r=[ssr], w=[ssr])
                                S.op("act", lambda a, ss=ss: a.activation(out=ss[:, 2:3], in_=ss[:, 1:2], func=AF.Sqrt),
                                     r=[ssr], w=[ssr])
                                S.op("dve", lambda v, ss=ss: v.reciprocal(out=ss[:, 3:4], in_=ss[:, 2:3]), r=[ssr], w=[ssr])
                                hb, hbr = hbpool.get()
                                S.op("dve", lambda v, hb=hb, xt=xt, ss=ss: v.scalar_tensor_tensor(
                                    out=hb[:], in0=xt[:], scalar=ss[:, 3:4], in1=gpre[:], op0=ALU.mult, op1=ALU.mult),
                                    r=[xr, ssr, gR], w=[hbr])
                                p1st[tt] = (hb, hbr)

                            def p1back(tt):
                                hb, hbr = p1st.pop(tt)
                                for half in range(2):
                                    S.mm([lambda pe, hb=hb, c=c, half=half: pe.transpose(
                                        out=psT[:, c * 128:(c + 1) * 128], in_=hb[:, (half * 8 + c) * 128:(half * 8 + c + 1) * 128],
                                        identity=identb[:]) for c in range(8)], r=[hbr, cR], w=[psTR])
                                    eng = "act" if half == 0 else "dve"
                                    if eng == "act":
                                        S.op("act", lambda a, half=half, tt=tt: a.copy(
                                            out=hT[:, half * 8:half * 8 + 8, tt * 128:(tt + 1) * 128],
                                            in_=psT[:, :].rearrange("p (a b) -> p a b", b=128)), r=[psTR], w=[hTR[tt]])
                                    else:
                                        S.op("dve", lambda v, half=half, tt=tt: v.tensor_copy(
                                            out=hT[:, half * 8:half * 8 + 8, tt * 128:(tt + 1) * 128],
                                            in_=psT[:, :].rearrange("p (a b) -> p a b", b=128)), r=[psTR], w=[hTR[tt]])


                            p1front(0)
                            for tt in range(NT):
                                if tt + 1 < NT:
                                    p1front(tt + 1)
                                p1back(tt)

                        S.barrier()
                        _chk(2)
                        wpool = Pool(s2, nc, "wblk", 3, [128, DC, 512], BF16)
                        w32 = sb("w32", [128, DC, 32], BF16, s2)
                        w32R = R("w32")
                        evf = Pool(s2, nc, "evf", 6, [128, 512], F32)
                        evb = Pool(s2, nc, "evb", 8, [128, 512], BF16)
                        dsb = Pool(s2, nc, "dsb", 2, [128, 544], F32)
                        dwk = Pool(s2, nc, "dwk", 2, [128, 640], F32)
                        csb = Pool(s2, nc, "csb", 2, [128, 32], F32)
                        gq = sb("gq", [128, 384], F32, s2)
                        gkv = sb("gkv", [128, 128], F32, s2)
                        jk2 = sb("jk2", [128, 384], F32, s2)
                        jk2R = R("jk2")
                        g2R = R("g2")
                        S.dma("sp", gq[:], bass.AP(tensor=qnorm.tensor, offset=l * 384, ap=[[0, 128], [1, 384]]), w=[g2R])
                        S.dma("sp", gkv[:], bass.AP(tensor=kvnorm.tensor, offset=l * 128, ap=[[0, 128], [1, 128]]), w=[g2R])
                        w_l = w_in[l].rearrange("(c p) n -> p c n", p=128)
                        psi = [0]

                        def next_ps():
                            i = psi[0]
                            psi[0] = (i + 1) % 4
                            return PS[i], PSR[i]

                        def tm_dests(name, tt):
                            outs = []
                            if tt < NPT:
                                rows = slice(tt * 128, (tt + 1) * 128)
                                kc = KCOL[name]
                                outs.append((kvloc[tt][:, kc:kc + 512], (0, 128), rkv("L" + name, tt)))
                                po = {"A.k": sbk_p, "A.v": sbv_p, "C.k": dk_p, "C.v": dv_p}.get(name)
                                if po is not None:
                                    outs.append((po[l, rows, :], (0, 128), rkv("O" + name, tt)))
                            else:
                                for s in range(NS):
                                    rs = (s * 64, s * 64 + 64)
                                    if name == "A.k":
                                        outs.append((sbk_s[l, s, :, :], rs, rkv("sbk_s", s)))
                                    elif name == "A.v":
                                        outs.append((sbv_s[l, s, :, :], rs, rkv("sbv_s", s)))
                                    elif name == "B.k":
                                        outs.append((bk_s[l, s, 448:512, :], rs, rkv("bk_s", s)))
                                    elif name == "B.v":
                                        outs.append((bv_s[l, s, 448:512, :], rs, rkv("bv_s", s)))
                                    elif name == "C.k":
                                        outs.append((dk_s[l, s, :, :], rs, rkv("dk_s", s)))
                                    elif name == "C.v":
                                        outs.append((dv_s[l, s, :, :], rs, rkv("dv_s", s)))
                            return outs

                        TMB = [("A.k", 512), ("A.v", 1024), ("B.k", 2048), ("B.v", 2560), ("C.k", 3584), ("C.v", 4096)]
                        for name, c0 in TMB:
                            wt, wr = wpool.get()
                            S.dma("pool", wt[:], w_l[:, :, c0:c0 + 512], w=[wr])
                            for tt in range(NT):
                                ps, pr = next_ps()
                                S.mm([lambda pe, ps=ps, wt=wt, dc=dc, tt=tt: pe.matmul(
                                    ps[:, :], lhsT=hT[:, dc, tt * 128:(tt + 1) * 128], rhs=wt[:, dc, :],
                                    start=(dc == 0), stop=(dc == DC - 1)) for dc in range(DC)],
                                    r=[wr, hTR[tt]], w=[pr])
                                ev, er = evf.get()
                                S.op("dve", lambda v, ev=ev, ps=ps: v.tensor_copy(out=ev[:], in_=ps[:, :]), r=[pr], w=[er])
                                for (dap, (ra, rb), dres) in tm_dests(name, tt):
                                    S.dma("sp", dap, ev[ra:rb, :], r=[er], w=[dres])
                        bandR = R("band_out")
                        for t_ in range(4, 8):
                            S.dma("sp", bk_p[l, (t_ - 4) * 128:(t_ - 3) * 128, :], kvloc[t_][:, 1024:1536], r=[rkv("LB.k", t_)], w=[bandR])
                            S.dma("sp", bv_p[l, (t_ - 4) * 128:(t_ - 3) * 128, :], kvloc[t_][:, 1536:2048], r=[rkv("LB.v", t_)], w=[bandR])
                        for s in range(NS):
                            S.dma("sp", bk_s[l, s, 0:448, :], cbk[l, s, 64:512, :], w=[bandR])
                            S.dma("sp", bv_s[l, s, 0:448, :], cbv[l, s, 64:512, :], w=[bandR])

                        wt, wr = wpool.get()
                        S.dma("pool", wt[:], w_l[:, :, 4608:5120], w=[wr])
                        S.dma("pool", w32[:], w_l[:, :, 5120:5152], w=[w32R])
                        for tt in range(NT):
                            ps, pr = next_ps()
                            ps2, pr2 = next_ps()
                            S.mm([lambda pe, ps=ps, wt=wt, dc=dc, tt=tt: pe.matmul(
                                ps[:, :], lhsT=hT[:, dc, tt * 128:(tt + 1) * 128], rhs=wt[:, dc, :],
                                start=(dc == 0), stop=(dc == DC - 1)) for dc in range(DC)], r=[wr, hTR[tt]], w=[pr])
                            S.mm([lambda pe, ps2=ps2, dc=dc, tt=tt: pe.matmul(
                                ps2[:, 0:32], lhsT=hT[:, dc, tt * 128:(tt + 1) * 128], rhs=w32[:, dc, :],
                                start=(dc == 0), stop=(dc == DC - 1)) for dc in range(DC)], r=[w32R, hTR[tt]], w=[pr2])
                            d, dr = dsb.get()
                            S.op("dve", lambda v, d=d, ps=ps: v.tensor_copy(out=d[:, 0:512], in_=ps[:, :]), r=[pr], w=[dr])
                            S.op("dve", lambda v, d=d, ps2=ps2: v.tensor_copy(out=d[:, 512:544], in_=ps2[:, 0:32]), r=[pr2], w=[dr])
                            wk, wkr = dwk.get()
                            cs, csr = csb.get()
                            S.dma("sp", cs[:], c_cs[tt * 128:(tt + 1) * 128, :], w=[csr])
                            st = 600
                            S.op("dve", lambda v, wk=wk: v.memset(wk[:, st:st + 8], 0.0), w=[wkr])
                            S.op("act", lambda a, d=d, wk=wk: a.activation(out=jk2[:, 0:384], in_=d[:, 0:384], func=AF.Square,
                                                                           accum_out=wk[:, st:st + 1]),
                                 r=[dr], w=[wkr, jk2R])
                            S.op("act", lambda a, d=d, wk=wk: a.activation(out=jk2[:, 0:128], in_=d[:, 384:512], func=AF.Square,
                                                                           accum_out=wk[:, st + 1:st + 2]),
                                 r=[dr], w=[wkr, jk2R])
                            S.op("dve", lambda v, wk=wk: v.tensor_scalar(out=wk[:, st + 2:st + 3], in0=wk[:, st:st + 1],
                                                                         scalar1=1.0 / 384, scalar2=EPS, op0=ALU.mult, op1=ALU.add),
                                 r=[wkr], w=[wkr])
                            S.op("dve", lambda v, wk=wk: v.tensor_scalar(out=wk[:, st + 3:st + 4], in0=wk[:, st + 1:st + 2],
                                                                         scalar1=1.0 / 128, scalar2=EPS, op0=ALU.mult, op1=ALU.add),
                                 r=[wkr], w=[wkr])
                            S.op("act", lambda a, wk=wk: a.activation(out=wk[:, st + 4:st + 6], in_=wk[:, st + 2:st + 4], func=AF.Sqrt),
                                 r=[wkr], w=[wkr])
                            S.op("dve", lambda v, wk=wk: v.reciprocal(out=wk[:, st + 6:st + 8], in_=wk[:, st + 4:st + 6]),
                                 r=[wkr], w=[wkr])
                            S.op("dve", lambda v, wk=wk, d=d: v.scalar_tensor_tensor(
                                out=wk[:, 0:384], in0=d[:, 0:384], scalar=wk[:, st + 6:st + 7], in1=gq[:], op0=ALU.mult, op1=ALU.mult),
                                r=[dr, wkr, g2R], w=[wkr])
                            S.op("dve", lambda v, wk=wk, d=d: v.scalar_tensor_tensor(
                                out=wk[:, 384:512], in0=d[:, 384:512], scalar=wk[:, st + 7:st + 8], in1=gkv[:], op0=ALU.mult, op1=ALU.mult),
                                r=[dr, wkr, g2R], w=[wkr])
                            x1a, x2a = d[:, 512:528], d[:, 528:544]
                            cosa, sina = cs[:, 0:16], cs[:, 16:32]
                            S.op("dve", lambda v, wk=wk: v.tensor_tensor(out=wk[:, 544:560], in0=x1a, in1=cosa, op=ALU.mult), r=[dr, csr], w=[wkr])
                            S.op("dve", lambda v, wk=wk: v.tensor_tensor(out=wk[:, 560:576], in0=x2a, in1=sina, op=ALU.mult), r=[dr, csr], w=[wkr])
                            S.op("dve", lambda v, wk=wk: v.tensor_tensor(out=wk[:, 512:528], in0=wk[:, 544:560], in1=wk[:, 560:576], op=ALU.subtract), r=[wkr], w=[wkr])
                            S.op("dve", lambda v, wk=wk: v.tensor_tensor(out=wk[:, 544:560], in0=x2a, in1=cosa, op=ALU.mult), r=[dr, csr], w=[wkr])
                            S.op("dve", lambda v, wk=wk: v.tensor_tensor(out=wk[:, 560:576], in0=x1a, in1=sina, op=ALU.mult), r=[dr, csr], w=[wkr])
                            S.op("dve", lambda v, wk=wk: v.tensor_tensor(out=wk[:, 528:544], in0=wk[:, 544:560], in1=wk[:, 560:576], op=ALU.add), r=[wkr], w=[wkr])
                            S.dma("sp", qn_scr[tt * 128:(tt + 1) * 128, :], wk[:, 0:384], r=[wkr], w=[rkv("qn", tt)])
                            if tt < NPT:
                                S.dma("sp", ckv_p[l, tt * 128:(tt + 1) * 128, :], wk[:, 384:512], r=[wkr], w=[rkv("Ockv", tt)])
                                S.dma("sp", kr_p[l, tt * 128:(tt + 1) * 128, :], wk[:, 512:544], r=[wkr], w=[rkv("Okr", tt)])
                                S.dma("sp", kvloc[tt][:, 3072:3200], wk[:, 384:512], r=[wkr], w=[rkv("Lckv", tt)])
                                S.dma("sp", kvloc[tt][:, 3200:3232], wk[:, 512:544], r=[wkr], w=[rkv("Lkr", tt)])
                            else:
                                for s in range(NS):
                                    S.dma("sp", ckv_s[l, s, :, :], wk[s * 64:s * 64 + 64, 384:512], r=[wkr], w=[rkv("ckv_s", s)])
                                    S.dma("sp", kr_s[l, s, :, :], wk[s * 64:s * 64 + 64, 512:544], r=[wkr], w=[rkv("kr_s", s)])

                        FMB = []
                        for bi, c0 in enumerate((0, 1536, 3072)):
                            FMB.append((c0, AF.Copy, qT_scr[bi], "q%d" % bi))
                        for n in range(4):
                            FMB.append((5152 + 512 * n, AF.Silu, gT_scr[512 * n:512 * n + 512], "g"))
                        for j in range(16):
                            FMB.append((7200 + 512 * j, AF.Sigmoid, mT_scr[512 * j:512 * j + 512], "m"))
                        defer_bank[0] = next_ps
                        for fmi, (c0, func, dst, nm) in enumerate(FMB):
                            if DEFER:
                                DEFER.pop(0)()
                            wt, wr = wpool.get()
                            S.dma("pool", wt[:], w_l[:, :, c0:c0 + 512], w=[wr])
                            if fmi == 2:
                                for t_ in range(NPT):
                                    S.collective(S.ccsem, kvloc_t[t_], kvall_t[l][t_],
                                                 r=[rkv("L" + n_, t_) for n_ in KCOL], w=[kvallR[l][t_]])
                            for ch in range(4):
                                for tbi, (t0, nt) in enumerate(TB):
                                    ps, pr = next_ps()
                                    tts = [hTR[t0 // 128 + i] for i in range(nt // 128)]
                                    S.mm([lambda pe, ps=ps, wt=wt, dc=dc, ch=ch, t0=t0, nt=nt: pe.matmul(
                                        ps[:, 0:nt], lhsT=wt[:, dc, ch * 128:(ch + 1) * 128], rhs=hT[:, dc, t0:t0 + nt],
                                        start=(dc == 0), stop=(dc == DC - 1)) for dc in range(DC)], r=[wr] + tts, w=[pr])
                                    ev, er = evb.get()
                                    S.op("act", lambda a, ev=ev, ps=ps, nt=nt, func=func: a.activation(
                                        out=ev[:, 0:nt], in_=ps[:, 0:nt], func=func), r=[pr], w=[er])
                                    S.dma("sp", dst[ch * 128:(ch + 1) * 128, t0:t0 + nt], ev[:, 0:nt], r=[er], w=[rfm(nm, tbi)])
                        while DEFER:
                            DEFER.pop(0)()
                        S.barrier()
                        _chk(3)

                    for blk in range(2):
                        with contextlib.ExitStack() as s34:
                            gatedT = sb("gatedT", [128, 16, 640], BF16, s34)
                            gatedR = R("gated")
                            groups = [dict(kind="p", qb=blk, tok0=512 * blk, nq=512, col0=0, tbi=blk)]
                            NQB = 512
                            CPS = [(0, 512)]
                            MTB = [blk]
                            if blk == 1:
                                groups += [dict(kind="s", s=s, tok0=TP + 64 * s, nq=64, col0=512 + 64 * s, tbi=2) for s in range(NS)]
                                NQB = 640
                                CPS = [(0, 512), (512, 128)]
                                MTB = [1, 2]
                                S.op("dve", lambda v: v.memset(gatedT[:, :, 576:640], 0.0), w=[gatedR])
                            with contextlib.ExitStack() as s3:
                                KT = sb("KT", [128, 8, 2048], BF16, s3)
                                KTR = R("KT")
                                Vsb = sb("Vsb", [128, 16, 512], BF16, s3)
                                VR = R("V")
                                qT = sb("qT", [128, 8, 512], BF16, s3)
                                qTR = R("qT")
                                gT = sb("gT", [128, 4, 512], BF16, s3)
                                gTR = R("gT")
                                kraw = Pool(s3, nc, "kraw", 3, [128, 512], BF16)
                                kst = Pool(s3, nc, "kst", 3, [128, 512], F32)
                                vst = Pool(s3, nc, "vst", 3, [128, 512], F32)

                                def warm(n):
                                    for _ in range(n):
                                        S.mm([lambda pe: pe.matmul(PS[0][:, 0:512], lhsT=identb[:], rhs=masks[:, 0, :],
                                                                   start=True, stop=True)], r=[cR], w=[PSR[0]])
                                wf = Pool(s3, nc, "wf", 4, [128, 512], F32)
                                t1p = Pool(s3, nc, "t1p", 6, [128, 512], F32)
                                wb_ = Pool(s3, nc, "wb", 16, [128, 512], BF16)
                                small = Pool(s3, nc, "small", 7, [128, 1024], BF16)
                                qmf = Pool(s3, nc, "qmf", 2, [128, 768], F32)
                                csq = Pool(s3, nc, "csq", 2, [128, 32], F32)

                                for G in groups:
                                    nq = G["nq"]
                                    tok0 = G["tok0"]
                                    col0 = G["col0"]

                                    def key_tiles(br):
                                        tl = []
                                        if G["kind"] == "p":
                                            qb = G["qb"]
                                            hi = 8 + 4 * qb + 4
                                            lo = (8 + 4 * qb - 4) if br == "B" else 0
                                            for kt in range(lo, hi):
                                                D0 = 128 * kt - (1024 + 512 * qb)
                                                prv = kt < 8
                                                rows = slice(0, 128)
                                                if prv:
                                                    srcd = kvall[l][kt]
                                                    rs_ = [kvallR[l][kt]]
                                                else:
                                                    srcd = kvloc[kt - 8]
                                                if br != "D":
                                                    kc, vc = KCOL[br + ".k"], KCOL[br + ".v"]
                                                    if not prv:
                                                        rs_ = [rkv("L" + br + ".k", kt - 8), rkv("L" + br + ".v", kt - 8)]
                                                    tl.append((128, srcd[rows, kc:kc + 512], srcd[rows, vc:vc + 512], D0, rs_, prv))
                                                else:
                                                    if not prv:
                                                        rs_ = [rkv("Lckv", kt - 8), rkv("Lkr", kt - 8)]
                                                    tl.append((128, srcd[rows, 3072:3200], srcd[rows, 3200:3232], D0, rs_, prv))
                                        else:
                                            s = G["s"]
                                            if br == "B":
                                                for kt in range(4):
                                                    rows = slice(kt * 128, kt * 128 + 128)
                                                    tl.append((128, cbk[l, s, rows, :], cbv[l, s, rows, :], -512 + 128 * kt, [], False))
                                                tl.append((64, bk_s[l, s, 448:512, :], bv_s[l, s, 448:512, :], 0, [rkv("bk_s", s), rkv("bv_s", s)], False))
                                            else:
                                                for kt in range(8):
                                                    rows = slice(kt * 128, kt * 128 + 128)
                                                    D0 = kt * 128 - 1024
                                                    if br == "A":
                                                        tl.append((128, csbk[l, s, rows, :], csbv[l, s, rows, :], D0, [], False))
                                                    elif br == "C":
                                                        tl.append((128, cdk[l, s, rows, :], cdv[l, s, rows, :], D0, [], False))
                                                    else:
                                                        tl.append((128, cckv[l, s, rows, :], ckr[l, s, rows, :], D0, [], False))
                                                if br == "A":
                                                    tl.append((64, sbk_s[l, s, :, :], sbv_s[l, s, :, :], 0, [rkv("sbk_s", s), rkv("sbv_s", s)], False))
                                                elif br == "C":
                                                    tl.append((64, dk_s[l, s, :, :], dv_s[l, s, :, :], 0, [rkv("dk_s", s), rkv("dv_s", s)], False))
                                                else:
                                                    tl.append((64, ckv_s[l, s, :, :], kr_s[l, s, :, :], 0, [rkv("ckv_s", s), rkv("kr_s", s)], False))
                                        return tl

                                    for bi, br in enumerate("ABCD"):
                                        tiles = key_tiles(br)
                                        S.dma("sp", gT[:, :, 0:nq],
                                              gT_scr[512 * bi:512 * bi + 512, tok0:tok0 + nq].rearrange("(c p) t -> p c t", p=128),
                                              r=[rfm("g", G["tbi"])], w=[gTR])
                                        if br != "D":
                                            S.dma("sp", qT[:, 0:4, 0:nq],
                                                  qT_scr[bi, :, tok0:tok0 + nq].rearrange("(c p) t -> p c t", p=128),
                                                  r=[rfm("q%d" % bi, G["tbi"])], w=[qTR])
                                            for ti, (nk, kap, vap, D0, rs, prv) in enumerate(tiles):
                                                kf, kfr = kst.get()
                                                vf, vfr = vst.get()
                                                S.dma("sp", kf[0:nk, :], kap, r=rs, w=[kfr])
                                                S.dma("sp", vf[0:nk, :], vap, r=rs, w=[vfr])
                                                kr_, krr = kraw.get()
                                                S.op("dve", lambda v: v.tensor_copy(out=kr_[0:nk, :], in_=kf[0:nk, :]), r=[kfr], w=[krr])
                                                S.op("act", lambda a: a.copy(out=Vsb[0:nk, ti, :], in_=vf[0:nk, :]), r=[vfr], w=[VR])
                                                S.mm([lambda pe, kr_=kr_, c=c, nk=nk: pe.transpose(
                                                    out=psT[:, c * 128:c * 128 + nk], in_=kr_[0:nk, c * 128:(c + 1) * 128],
                                                    identity=identb[0:nk, 0:nk]) for c in range(4)], r=[krr, cR], w=[psTR])
                                                S.op("dve", lambda v, ti=ti, nk=nk: v.tensor_copy(
                                                    out=KT[:, 0:4, ti * 128:ti * 128 + nk],
                                                    in_=psT[:, 0:512].rearrange("p (a b) -> p a b", b=128)[:, :, 0:nk]),
                                                    r=[psTR], w=[KTR])
                                        else:
                                            ntt = max(1, nq // 128)
                                            for qi in range(ntt):
                                                nr = min(128, nq)
                                                r0 = tok0 + qi * 128
                                                tt = r0 // 128
                                                qn_, qnr = small.get()
                                                kf, kfr = kst.get()
                                                S.dma("sp", kf[0:nr, 0:384], qn_scr[r0:r0 + nr, :], r=[rkv("qn", tt)], w=[kfr])
                                                S.op("dve", lambda v: v.tensor_copy(out=qn_[0:nr, 0:384], in_=kf[0:nr, 0:384]), r=[kfr], w=[qnr])
                                                cs, csr = csq.get()
                                                S.dma("sp", cs[0:nr, :], c_cs[r0:r0 + nr, :], w=[csr])
                                                S.mm([lambda pe, qn_=qn_, c=c, nr=nr: pe.transpose(
                                                    out=psT[:, c * 128:c * 128 + nr], in_=qn_[0:nr, c * 128:(c + 1) * 128],
                                                    identity=identb[0:nr, 0:nr]) for c in range(3)], r=[qnr, cR], w=[psTR])
                                                qnT, qnTr = small.get()
                                                S.op("dve", lambda v, qnT=qnT: v.tensor_copy(out=qnT[:, 0:384], in_=psT[:, 0:384]),
                                                     r=[psTR], w=[qnTr])
                                                S.mm([lambda pe, qnT=qnT, c=c, nr=nr: pe.matmul(
                                                    PS[5][0:nr, :], lhsT=qnT[:, c * 128:c * 128 + nr], rhs=wqupb[:, c, 0:512],
                                                    start=(c == 0), stop=(c == 2)) for c in range(3)], r=[qnTr, lR], w=[PSR[5]])
                                                S.mm([lambda pe, qnT=qnT, c=c, nr=nr: pe.matmul(
                                                    PS[6][0:nr, 0:256], lhsT=qnT[:, c * 128:c * 128 + nr], rhs=wqupb[:, c, 512:768],
                                                    start=(c == 0), stop=(c == 2)) for c in range(3)], r=[qnTr, lR], w=[PSR[6]])
                                                qf, qfr = qmf.get()
                                                S.op("act", lambda a, qf=qf, nr=nr: a.copy(out=qf[0:nr, 0:512], in_=PS[5][0:nr, :]),
                                                     r=[PSR[5]], w=[qfr])
                                                S.op("act", lambda a, qf=qf, nr=nr: a.copy(out=qf[0:nr, 512:768], in_=PS[6][0:nr, 0:256]),
                                                     r=[PSR[6]], w=[qfr])
                                                qb_, qbr = small.get()
                                                qf3 = qf[0:nr, :].rearrange("p (h e) -> p h e", e=96)
                                                qb3 = qb_[0:nr, 0:768].rearrange("p (h e) -> p h e", e=96)
                                                tmp, tmpr = wf.get()
                                                ta = tmp[0:nr, 0:128].rearrange("p (h e) -> p h e", e=16)
                                                tb_ = tmp[0:nr, 128:256].rearrange("p (h e) -> p h e", e=16)
                                                cosb = bc_ap(cs[0:nr, 0:16], [[0, 8], [1, 16]])
                                                sinb = bc_ap(cs[0:nr, 16:32], [[0, 8], [1, 16]])
                                                S.op("dve", lambda v: v.tensor_copy(out=qb3[:, :, 0:64], in_=qf3[:, :, 0:64]), r=[qfr], w=[qbr])
                                                S.op("dve", lambda v: v.tensor_tensor(out=ta, in0=qf3[:, :, 64:80], in1=cosb, op=ALU.mult), r=[qfr, csr], w=[tmpr])
                                                S.op("dve", lambda v: v.tensor_tensor(out=tb_, in0=qf3[:, :, 80:96], in1=sinb, op=ALU.mult), r=[qfr, csr], w=[tmpr])
                                                S.op("dve", lambda v: v.tensor_tensor(out=qb3[:, :, 64:80], in0=ta, in1=tb_, op=ALU.subtract), r=[tmpr], w=[qbr, tmpr])
                                                S.op("dve", lambda v: v.tensor_tensor(out=ta, in0=qf3[:, :, 80:96], in1=cosb, op=ALU.mult), r=[qfr, csr], w=[tmpr])
                                                S.op("dve", lambda v: v.tensor_tensor(out=tb_, in0=qf3[:, :, 64:80], in1=sinb, op=ALU.mult), r=[qfr, csr], w=[tmpr])
                                                S.op("dve", lambda v: v.tensor_tensor(out=qb3[:, :, 80:96], in0=ta, in1=tb_, op=ALU.add), r=[tmpr], w=[qbr, tmpr])
                                                S.mm([lambda pe, qb_=qb_, h=h, nr=nr: pe.transpose(
                                                    out=psT[0:96, h * 128:h * 128 + nr], in_=qb_[0:nr, h * 96:(h + 1) * 96],
                                                    identity=identb[0:nr, 0:nr]) for h in range(8)], r=[qbr, cR], w=[psTR])
                                                S.op("dve", lambda v, qi=qi, nr=nr: v.tensor_copy(
                                                    out=qT[0:96, :, qi * 128:qi * 128 + nr],
                                                    in_=psT[0:96, :].rearrange("p (a b) -> p a b", b=128)[:, :, 0:nr]),
                                                    r=[psTR], w=[qTR])
                                            _chk(3.61)
                                            dst_ = {}

                                            def stA(ti):
                                                nk, cap, rap, D0, rs, prv = tiles[ti]
                                                b0 = 5 if ti % 2 == 0 else 3
                                                ck, ckr_ = small.get()
                                                kf, kfr = kst.get()
                                                S.dma("sp", kf[0:nk, 0:128], cap, r=rs, w=[kfr])
                                                S.dma("sp", kf[0:nk, 128:160], rap, r=rs, w=[kfr])
                                                S.op("dve", lambda v: v.tensor_copy(out=ck[0:nk, 0:160], in_=kf[0:nk, 0:160]), r=[kfr], w=[ckr_])
                                                S.mm([lambda pe: pe.transpose(
                                                    out=psT[:, 0:nk], in_=ck[0:nk, 0:128], identity=identb[0:nk, 0:nk])],
                                                    r=[ckr_, cR], w=[psTR])
                                                ckT, ckTr = small.get()
                                                S.op("dve", lambda v: v.tensor_copy(out=ckT[:, 0:nk], in_=psT[:, 0:nk]),
                                                     r=[psTR], w=[ckTr])
                                                S.mm([lambda pe: pe.matmul(
                                                    PS[b0][0:nk, :], lhsT=ckT[:, 0:nk], rhs=wkvupb[:, 0:512], start=True, stop=True)],
                                                    r=[ckTr, lR], w=[PSR[b0]])
                                                S.mm([lambda pe: pe.matmul(
                                                    PS[b0 + 1][0:nk, :], lhsT=ckT[:, 0:nk], rhs=wkvupb[:, 512:1024], start=True, stop=True)],
                                                    r=[ckTr, lR], w=[PSR[b0 + 1]])
                                                dst_[ti] = (ck, ckr_)

                                            def stB(ti):
                                                nk, cap, rap, D0, rs, prv = tiles[ti]
                                                b0 = 5 if ti % 2 == 0 else 3
                                                ck, ckr_ = dst_.pop(ti)
                                                kd, kdr = small.get()
                                                kd3 = kd[0:nk, 0:768].rearrange("p (h e) -> p h e", e=96)
                                                for hf in range(2):
                                                    pv = PS[b0 + hf][0:nk, :].rearrange("p (h e) -> p h e", e=128)
                                                    S.op("act", lambda a: a.copy(out=kd3[:, 4 * hf:4 * hf + 4, 0:64], in_=pv[:, :, 0:64]),
                                                         r=[PSR[b0 + hf]], w=[kdr])
                                                    S.op("dve", lambda v: v.tensor_copy(
                                                        out=Vsb[0:nk, ti, 256 * hf:256 * hf + 256].rearrange("p (h e) -> p h e", e=64),
                                                        in_=pv[:, :, 64:128]), r=[PSR[b0 + hf]], w=[VR])
                                                S.op("dve", lambda v: v.tensor_copy(
                                                    out=kd3[:, :, 64:96], in_=bc_ap(ck[0:nk, 128:160], [[0, 8], [1, 32]])), r=[ckr_], w=[kdr])
                                                S.mm([lambda pe, h=h: pe.transpose(
                                                    out=psT[0:96, h * 128:h * 128 + nk], in_=kd[0:nk, h * 96:(h + 1) * 96],
                                                    identity=identb[0:nk, 0:nk]) for h in range(8)], r=[kdr, cR], w=[psTR])
                                                S.op("dve", lambda v: v.tensor_copy(
                                                    out=KT[0:96, :, ti * 128:ti * 128 + nk],
                                                    in_=psT[0:96, :].rearrange("p (a b) -> p a b", b=128)[:, :, 0:nk]),
                                                    r=[psTR], w=[KTR])
                                                warm(WARM)

                                            stA(0)
                                            for ti in range(len(tiles)):
                                                if ti + 1 < len(tiles):
                                                    stA(ti + 1)
                                                stB(ti)

                                        _chk(3.0 + 0.2 * bi + 0.1)
                                        def mask_ap(kind, D0, nk):
                                            d = D0 // 128
                                            if kind == "A":
                                                return masks[0:nk, d, 0:nq]
                                            if kind == "CD":
                                                return masks[0:nk, 4 + d, 0:nq]
                                            return masks[0:nk, 8 + (D0 + 512) // 128, 0:nq]

                                        isP = G["kind"] == "p"
                                        LOOK = 3

                                        def run_pipe(items, s1, s2):
                                            n = len(items)
                                            for i in range(min(LOOK, n)):
                                                s1(i)
                                            for i in range(n):
                                                s2(i)
                                                if i + LOOK < n:
                                                    s1(i + LOOK)

                                        if br == "A":
                                            ntl = len(tiles)
                                            items = [(2 * hp_ + hf_, idx) for hp_ in range(4) for idx in range(ntl) for hf_ in range(2)]
                                            order = list(enumerate(tiles))[::-1]
                                            st = {}
                                            psbanks = [0, 1, 5]

                                            def s1q(i):
                                                h, idx = items[i]
                                                hp, r0 = h // 2, 64 * (h % 2)
                                                ti, (nk, kap, vap, D0, rs, prv) = order[idx]
                                                pS, pSR = PS[psbanks[i % 3]], PSR[psbanks[i % 3]]
                                                S.mm([lambda pe: pe.matmul(
                                                    pS[0:nk, 0:nq], lhsT=KT[r0:r0 + 64, hp, ti * 128:ti * 128 + nk],
                                                    rhs=qT[r0:r0 + 64, hp, 0:nq], start=True, stop=True)], r=[KTR, qTR], w=[pSR])

                                            def s1(i):
                                                h, idx = items[i]
                                                hp, r0 = h // 2, 64 * (h % 2)
                                                ti, (nk, kap, vap, D0, rs, prv) = order[idx]
                                                pS, pSR = PS[psbanks[i % 3]], PSR[psbanks[i % 3]]
                                                diag = (D0 >= 0)
                                                ez, ezr = wf.get()
                                                bkw = dict(bias=vbt[0:nk, 0:1]) if prv else {}
                                                S.op("act", lambda a: a.activation(
                                                    out=ez[0:nk, 0:nq], in_=pS[0:nk, 0:nq], func=AF.Exp, scale=0.125, **bkw), r=[pSR, cR], w=[ezr])
                                                sp, spr = wb_.get()
                                                S.op("act", lambda a: a.activation(
                                                    out=sp[0:nk, 0:nq], in_=ez[0:nk, 0:nq], func=AF.Ln, bias=1.0), r=[ezr], w=[spr])
                                                t1, t1r = t1p.get()
                                                S.op("dve", lambda v: v.scalar_tensor_tensor(
                                                    out=t1[0:nk, 0:nq], in0=pS[0:nk, 0:nq], scalar=0.125, in1=sp[0:nk, 0:nq],
                                                    op0=ALU.mult, op1=ALU.subtract), r=[pSR, spr], w=[t1r])
                                                if diag:
                                                    spm, spmr = wb_.get()
                                                    S.op("dve", lambda v: v.tensor_tensor(
                                                        out=spm[0:nk, 0:nq], in0=sp[0:nk, 0:nq], in1=mask_ap("A", D0, nk), op=ALU.mult),
                                                        r=[spr, cR], w=[spmr])
                                                else:
                                                    spm, spmr = sp, spr
                                                st[i] = (spm, spmr, t1, t1r)

                                            def s2a(i):
                                                h, idx = items[i]
                                                ti, (nk, kap, vap, D0, rs, prv) = order[idx]
                                                lb = 2 if h % 2 == 0 else 6
                                                psL, psLR = PS[lb], PSR[lb]
                                                spm, spmr, t1, t1r = st[i]
                                                fns = []
                                                rr = [spmr, cR]
                                                if idx > 0:
                                                    psp, pspr, pnk = st[("prev", h % 2)]
                                                    fns.append(lambda pe: pe.matmul(
                                                        psL[:, 0:nq], lhsT=TRIC[0:pnk, :], rhs=psp[0:pnk, 0:nq],
                                                        start=False, stop=False, skip_group_check=True))
                                                    rr.append(pspr)
                                                fns.append(lambda pe: pe.matmul(
                                                    psL[:, 0:nq], lhsT=TRI[0:nk, :], rhs=spm[0:nk, 0:nq],
                                                    start=(idx == 0), stop=True, skip_group_check=True))
                                                S.mm(fns, r=rr, w=[psLR])
                                                st[("prev", h % 2)] = (spm, spmr, nk)

                                            def s2t(i):
                                                h, idx = items[i]
                                                ti, (nk, kap, vap, D0, rs, prv) = order[idx]
                                                lb = 2 if h % 2 == 0 else 6
                                                psL, psLR = PS[lb], PSR[lb]
                                                spm, spmr, t1, t1r = st[i]
                                                S.op("dve", lambda v: v.tensor_tensor(
                                                    out=t1[0:nk, 0:nq], in0=t1[0:nk, 0:nq], in1=psL[0:nk, 0:nq], op=ALU.subtract),
                                                    r=[psLR], w=[t1r])

                                            def s2b(i):
                                                h, idx = items[i]
                                                hp, r0 = h // 2, 64 * (h % 2)
                                                ti, (nk, kap, vap, D0, rs, prv) = order[idx]
                                                diag = (D0 >= 0)
                                                lb = 2 if h % 2 == 0 else 6
                                                psL, psLR = PS[lb], PSR[lb]
                                                psO, psOR = PS[3 + (h % 2)], PSR[3 + (h % 2)]
                                                spm, spmr, t1, t1r = st.pop(i)
                                                wt_, wtr = wb_.get()
                                                bkw = dict(bias=vbt[0:nk, 0:1]) if prv else {}
                                                S.op("act", lambda a: a.activation(
                                                    out=wt_[0:nk, 0:nq], in_=t1[0:nk, 0:nq], func=AF.Exp, **bkw), r=[t1r, cR], w=[wtr])
                                                if diag:
                                                    S.op("dve", lambda v: v.tensor_tensor(
                                                        out=wt_[0:nk, 0:nq], in0=wt_[0:nk, 0:nq], in1=mask_ap("A", D0, nk), op=ALU.mult),
                                                        r=[cR], w=[wtr])
                                                S.mm([lambda pe: pe.matmul(
                                                    psO[:, 0:nq], lhsT=Vsb[0:nk, ti, hp * 128:(hp + 1) * 128], rhs=wt_[0:nk, 0:nq],
                                                    start=(idx == 0), stop=(idx == ntl - 1), skip_group_check=True)],
                                                    r=[wtr, VR], w=[psOR])
                                                if idx == ntl - 1:
                                                    S.op("dve", lambda v: v.tensor_tensor(
                                                        out=gatedT[r0:r0 + 64, hp, col0:col0 + nq], in0=psO[r0:r0 + 64, 0:nq],
                                                        in1=gT[r0:r0 + 64, hp, 0:nq], op=ALU.mult), r=[psOR, gTR], w=[gatedR])

                                            nit = len(items)
                                            for i in range(min(3, nit)):
                                                s1q(i)
                                            s1(0)
                                            if nit > 3:
                                                s1q(3)
                                            if nit > 1:
                                                s1(1)
                                            s2a(0)
                                            for i in range(nit):
                                                if i + 4 < nit:
                                                    s1q(i + 4)
                                                s2t(i)
                                                if i + 1 < nit:
                                                    s2a(i + 1)
                                                if i + 2 < nit:
                                                    s1(i + 2)
                                                s2b(i)
                                        elif br in ("B", "D"):
                                            ntl = len(tiles)
                                            items = [(2 * hp_ + hf_, idx) for hp_ in range(4) for idx in range(ntl) for hf_ in range(2)]
                                            st = {}

                                            def s1(i):
                                                h, idx = items[i]
                                                hp, r0 = h // 2, 64 * (h % 2)
                                                nk, kap, vap, D0, rs, prv = tiles[idx]
                                                ti = idx
                                                pS, pSR = PS[i % 3], PSR[i % 3]
                                                if br == "B":
                                                    S.mm([lambda pe: pe.matmul(
                                                        pS[0:nk, 0:nq], lhsT=KT[r0:r0 + 64, hp, ti * 128:ti * 128 + nk],
                                                        rhs=qT[r0:r0 + 64, hp, 0:nq], start=True, stop=True)], r=[KTR, qTR], w=[pSR])
                                                    scale = 0.125
                                                else:
                                                    S.mm([lambda pe: pe.matmul(
                                                        pS[0:nk, 0:nq], lhsT=KT[0:96, h, ti * 128:ti * 128 + nk],
                                                        rhs=qT[0:96, h, 0:nq], start=True, stop=True)], r=[KTR, qTR], w=[pSR])
                                                    scale = MLA_SCALE
                                                e, er = wb_.get()
                                                bkw = dict(bias=vbt[0:nk, 0:1]) if prv else {}
                                                S.op("act", lambda a: a.activation(
                                                    out=e[0:nk, 0:nq], in_=pS[0:nk, 0:nq], func=AF.Exp, scale=scale, **bkw), r=[pSR, cR], w=[er])
                                                if br == "B":
                                                    c0 = 384 - D0
                                                    S.op("dve", lambda v: v.tensor_tensor(
                                                        out=e[0:nk, 0:nq], in0=e[0:nk, 0:nq], in1=EGB[0:nk, h, c0:c0 + nq], op=ALU.mult),
                                                        r=[EGBR], w=[er])
                                                    if isP:
                                                        S.op("dve", lambda g: g.tensor_tensor(
                                                            out=e[0:nk, 0:nq], in0=e[0:nk, 0:nq], in1=mask_ap("B", D0, nk), op=ALU.mult),
                                                            r=[cR], w=[er])
                                                else:
                                                    if isP and D0 >= 0:
                                                        S.op("dve", lambda v: v.tensor_tensor(
                                                            out=e[0:nk, 0:nq], in0=e[0:nk, 0:nq], in1=mask_ap("CD", D0, nk), op=ALU.mult),
                                                            r=[cR], w=[er])
                                                st[i] = (e, er)

                                            def s2(i):
                                                h, idx = items[i]
                                                hp, r0 = h // 2, 64 * (h % 2)
                                                nk, kap, vap, D0, rs, prv = tiles[idx]
                                                ti = idx
                                                psO, psOR = PS[3 + (h % 2)], PSR[3 + (h % 2)]
                                                psD, psDR = PS[5 + (h % 2)], PSR[5 + (h % 2)]
                                                e, er = st.pop(i)
                                                S.mm([lambda pe: pe.matmul(
                                                    psO[:, 0:nq], lhsT=Vsb[0:nk, ti, hp * 128:(hp + 1) * 128], rhs=e[0:nk, 0:nq],
                                                    start=(idx == 0), stop=(idx == ntl - 1), skip_group_check=True)],
                                                    r=[er, VR], w=[psOR])
                                                S.mm([lambda pe: pe.matmul(
                                                    psD[:, 0:nq], lhsT=ONES[0:nk, :], rhs=e[0:nk, 0:nq],
                                                    start=(idx == 0), stop=(idx == ntl - 1), skip_group_check=True)],
                                                    r=[er, cR], w=[psDR])
                                                if idx == ntl - 1:
                                                    rc, rcr = wf.get()
                                                    S.op("act", lambda a: a.activation(out=rc[r0:r0 + 64, 0:nq], in_=psD[r0:r0 + 64, 0:nq], func=AF.Ln),
                                                         r=[psDR], w=[rcr])
                                                    S.op("act", lambda a: a.activation(out=rc[r0:r0 + 64, 0:nq], in_=rc[r0:r0 + 64, 0:nq], func=AF.Exp, scale=-1.0),
                                                         r=[], w=[rcr])
                                                    S.op("dve", lambda v: v.tensor_tensor(
                                                        out=rc[r0:r0 + 64, 0:nq], in0=psO[r0:r0 + 64, 0:nq], in1=rc[r0:r0 + 64, 0:nq], op=ALU.mult),
                                                        r=[psOR], w=[rcr])
                                                    S.op("pool", lambda g: g.tensor_tensor(
                                                        out=gatedT[r0:r0 + 64, 4 * bi + hp, col0:col0 + nq], in0=rc[r0:r0 + 64, 0:nq],
                                                        in1=gT[r0:r0 + 64, hp, 0:nq], op=ALU.mult), r=[rcr, gTR], w=[gatedR])

                                            run_pipe(items, s1, s2)
                                        else:
                                            ntl = len(tiles)
                                            items = [(h, m, idx) for h in range(4) for idx in range(ntl) for m in range(2)]
                                            st = {}

                                            def s1(i):
                                                h, m, idx = items[i]
                                                r0 = 64 * m
                                                nk, kap, vap, D0, rs, prv = tiles[idx]
                                                ti = idx
                                                pS, pSR = PS[i % 3], PSR[i % 3]
                                                S.mm([lambda pe: pe.matmul(
                                                    pS[0:nk, 0:nq], lhsT=KT[r0:r0 + 64, h, ti * 128:ti * 128 + nk],
                                                    rhs=qT[r0:r0 + 64, h, 0:nq], start=True, stop=True)], r=[KTR, qTR], w=[pSR])
                                                e, er = wb_.get()
                                                if D0 <= -512:
                                                    S.op("act", lambda a: a.activation(
                                                        out=e[0:nk, 0:nq], in_=pS[0:nk, 0:nq], func=AF.Exp, scale=0.125,
                                                        bias=(satCv if prv else satC)[0:nk, h:h + 1]), r=[pSR, cR], w=[er])
                                                else:
                                                    bkw = dict(bias=vbt[0:nk, 0:1]) if prv else {}
                                                    S.op("act", lambda a: a.activation(
                                                        out=e[0:nk, 0:nq], in_=pS[0:nk, 0:nq], func=AF.Exp, scale=0.125, **bkw), r=[pSR, cR], w=[er])
                                                    c0 = 384 - D0
                                                    S.op("dve", lambda v: v.tensor_tensor(
                                                        out=e[0:nk, 0:nq], in0=e[0:nk, 0:nq], in1=EGC[0:nk, h, c0:c0 + nq], op=ALU.mult),
                                                        r=[EGCR], w=[er])
                                                    if isP and D0 >= 0:
                                                        S.op("dve", lambda g: g.tensor_tensor(
                                                            out=e[0:nk, 0:nq], in0=e[0:nk, 0:nq], in1=mask_ap("CD", D0, nk), op=ALU.mult),
                                                            r=[cR], w=[er])
                                                st[i] = (e, er)

                                            def s2(i):
                                                h, m, idx = items[i]
                                                nk, kap, vap, D0, rs, prv = tiles[idx]
                                                ti = idx
                                                psO, psOR = PS[3 + m], PSR[3 + m]
                                                psD, psDR = PS[5 + m], PSR[5 + m]
                                                e, er = st.pop(i)
                                                S.mm([lambda pe: pe.matmul(
                                                    psO[:, 0:nq], lhsT=Vsb[0:nk, ti, h * 128:(h + 1) * 128], rhs=e[0:nk, 0:nq],
                                                    start=(idx == 0), stop=(idx == ntl - 1), skip_group_check=True)],
                                                    r=[er, VR], w=[psOR])
                                                S.mm([lambda pe: pe.matmul(
                                                    psD[:, 0:nq], lhsT=ONES[0:nk, :], rhs=e[0:nk, 0:nq],
                                                    start=(idx == 0), stop=(idx == ntl - 1), skip_group_check=True)],
                                                    r=[er, cR], w=[psDR])
                                                if idx == ntl - 1:
                                                    a_, ar = wf.get()
                                                    S.op("act", lambda a: a.activation(out=a_[:, 0:nq], in_=psD[:, 0:nq], func=AF.Ln), r=[psDR], w=[ar])
                                                    S.op("act", lambda a: a.activation(out=a_[:, 0:nq], in_=a_[:, 0:nq], func=AF.Exp, scale=-1.0), r=[], w=[ar])
                                                    S.op("dve", lambda v: v.tensor_tensor(
                                                        out=a_[:, 0:nq], in0=psO[:, 0:nq], in1=a_[:, 0:nq], op=ALU.mult), r=[psOR], w=[ar])
                                                    st[("acc", m)] = (a_, ar)
                                                    if m == 1:
                                                        a0, a0r = st.pop(("acc", 0))
                                                        a1, a1r = st.pop(("acc", 1))
                                                        S.op("dve", lambda v: v.scalar_tensor_tensor(
                                                            out=a0[:, 0:nq], in0=a1[:, 0:nq], scalar=lcol[:, 0:1], in1=a0[:, 0:nq],
                                                            op0=ALU.mult, op1=ALU.add), r=[a1r, lR], w=[a0r])
                                                        sq, sqr = wb_.get()
                                                        S.op("act", lambda a: a.activation(out=sq[:, 0:nq], in_=a0[:, 0:nq], func=AF.Square),
                                                             r=[a0r], w=[sqr])
                                                        S.mm([lambda pe: pe.matmul(PS[2][:, 0:nq], lhsT=ONES, rhs=sq[:, 0:nq],
                                                                                   start=True, stop=True)], r=[sqr, cR], w=[PSR[2]])
                                                        S.op("dve", lambda v: v.tensor_scalar(
                                                            out=a1[:, 0:nq], in0=PS[2][:, 0:nq], scalar1=1.0 / 128, scalar2=EPS,
                                                            op0=ALU.mult, op1=ALU.add), r=[PSR[2]], w=[a1r])
                                                        S.op("act", lambda a: a.activation(out=a1[:, 0:nq], in_=a1[:, 0:nq], func=AF.Ln),
                                                             r=[], w=[a1r])
                                                        S.op("act", lambda a: a.activation(out=a1[:, 0:nq], in_=a1[:, 0:nq], func=AF.Exp, scale=-0.5),
                                                             r=[], w=[a1r])
                                                        S.op("pool", lambda g: g.tensor_tensor(
                                                            out=a0[:, 0:nq], in0=a0[:, 0:nq], in1=a1[:, 0:nq], op=ALU.mult), r=[a1r], w=[a0r])
                                                        S.op("dve", lambda g: g.scalar_tensor_tensor(
                                                            out=gatedT[:, 8 + h, col0:col0 + nq], in0=a0[:, 0:nq], scalar=lcol[:, 1:2],
                                                            in1=gT[:, h, 0:nq], op0=ALU.mult, op1=ALU.mult), r=[a0r, lR, gTR], w=[gatedR])

                                            run_pipe(items, s1, s2)
                                S.barrier()
                                _chk(4)

                            with contextlib.ExitStack() as s4:
                                mergedT = sb("mergedT", [128, DC, 640], BF16, s4)
                                mrgR = R("merged")
                                with contextlib.ExitStack() as s4a:
                                    wbp = Pool(s4a, nc, "wbp", 2, [128, 16, 512], BF16)
                                    mgp = Pool(s4a, nc, "mgp", 3, [128, 4, 512], BF16)
                                    pfb = Pool(s4a, nc, "pfb", 10, [128, 512], BF16)
                                    tok0 = 512 * blk
                                    pend = [None]
                                    acnt = [0]

                                    def flush_acc():
                                        if pend[0] is None:
                                            return
                                        dcp, prods, cc0, ccn, abi = pend[0]
                                        pend[0] = None
                                        ab = 4 + abi % 2
                                        S.mm([lambda pe, p_=p_, n=n: pe.matmul(
                                            PS[ab][:, 0:ccn], lhsT=identb[:], rhs=p_[:, 0:ccn], start=(n == 0), stop=(n == 3))
                                            for n, (p_, p_r) in enumerate(prods)], r=[p_r for (_, p_r) in prods] + [cR], w=[PSR[ab]])
                                        S.op("act", lambda a: a.copy(out=mergedT[:, dcp, cc0:cc0 + ccn], in_=PS[ab][:, 0:ccn]),
                                             r=[PSR[ab]], w=[mrgR])

                                    for g4 in range(4):
                                        wt, wr = wbp.get()
                                        S.dma("pool", wt[:],
                                              wbr[l, :, :, g4 * 512:(g4 + 1) * 512].rearrange("n (c p) d -> p (n c) d", p=128), w=[wr])
                                        for j4 in range(4):
                                            dc = 4 * g4 + j4
                                            for (cc0, ccn) in CPS:
                                                mg, mgr = mgp.get()
                                                S.dma("sp", mg[:, :, 0:ccn],
                                                      mT_scr[:, tok0 + cc0:tok0 + cc0 + ccn].rearrange("(n c p) t -> p n c t", p=128, c=16)[:, :, dc, :],
                                                      r=[rfm("m", t_) for t_ in MTB], w=[mgr])
                                                prods = []
                                                for n in range(4):
                                                    S.mm([lambda pe, n=n, c=c: pe.matmul(
                                                        PS[n][:, 0:ccn], lhsT=wt[:, 4 * n + c, j4 * 128:(j4 + 1) * 128],
                                                        rhs=gatedT[:, 4 * n + c, cc0:cc0 + ccn],
                                                        start=(c == 0), stop=(c == 3)) for c in range(4)], r=[wr, gatedR], w=[PSR[n]])
                                                    p_, p_r = pfb.get()
                                                    S.op("dve", lambda v, p_=p_, n=n: v.tensor_tensor(
                                                        out=p_[:, 0:ccn], in0=PS[n][:, 0:ccn], in1=mg[:, n, 0:ccn], op=ALU.mult),
                                                        r=[PSR[n], mgr], w=[p_r])
                                                    prods.append((p_, p_r))
                                                flush_acc()
                                                acnt[0] += 1
                                                pend[0] = (dc, prods, cc0, ccn, acnt[0])
                                    flush_acc()
                                    S.barrier()
                                    _chk(5)
                                with contextlib.ExitStack() as s4b:
                                    wop = Pool(s4b, nc, "wop", 2, [128, DC, 512], BF16)
                                    ntk = NQB // 128
                                    ysb = [sb(f"ysb{i}", [128, D], F32, s4b) for i in range(ntk)]
                                    ysR = [R(f"ysb{i}") for i in range(ntk)]
                                    gpost = sb("gpost", [128, D], F32, s4b)
                                    gpR = R("gpost")
                                    S.dma("sp", gpost[:], bass.AP(tensor=norm_post.tensor, offset=l * D, ap=[[0, 128], [1, D]]), w=[gpR])
                                    xin = Pool(s4b, nc, "xin", 2, [128, D], F32)
                                    jk4 = sb("jk4", [128, D], BF16, s4b)
                                    jk4R = R("jk4")
                                    st4 = Pool(s4b, nc, "st4", 2, [128, 4], F32)
                                    for eb in range(4):
                                        wt, wr = wop.get()
                                        S.dma("pool", wt[:], wout[l, :, eb * 512:(eb + 1) * 512].rearrange("(c p) n -> p c n", p=128), w=[wr])
                                        for tk in range(ntk):
                                            pi = 4 + (eb * ntk + tk) % 3
                                            S.mm([lambda pe, wt=wt, c=c, tk=tk, pi=pi: pe.matmul(
                                                PS[pi][:, :], lhsT=mergedT[:, c, tk * 128:(tk + 1) * 128], rhs=wt[:, c, :],
                                                start=(c == 0), stop=(c == DC - 1)) for c in range(DC)], r=[wr, mrgR], w=[PSR[pi]])
                                            S.op("act", lambda a, tk=tk, eb=eb, pi=pi: a.copy(out=ysb[tk][:, eb * 512:(eb + 1) * 512], in_=PS[pi][:, :]),
                                                 r=[PSR[pi]], w=[ysR[tk]])
                                    for tk in range(ntk):
                                        tt = (512 * blk) // 128 + tk
                                        xt, xr = xin.get()
                                        if l == 0:
                                            src = xp[tt * 128:(tt + 1) * 128, :] if tt < NPT else xs[:, :]
                                            S.dma("sp", xt[:], src, w=[xr])
                                        else:
                                            S.dma("sp", xt[:], x1[tt * 128:(tt + 1) * 128, :], r=[RX[tt]], w=[xr])
                                        ss, ssr = st4.get()
                                        S.op("dve", lambda v, ss=ss: v.memset(ss[:], 0.0), w=[ssr])
                                        S.op("act", lambda a, tk=tk, ss=ss: a.activation(out=jk4[:], in_=ysb[tk][:], func=AF.Square,
                                                                                        accum_out=ss[:, 0:1]), r=[ysR[tk]], w=[jk4R, ssr])
                                        S.op("dve", lambda v, ss=ss: v.tensor_scalar(out=ss[:, 1:2], in0=ss[:, 0:1], scalar1=1.0 / D,
                                                                                      scalar2=EPS, op0=ALU.mult, op1=ALU.add), r=[ssr], w=[ssr])
                                        S.op("act", lambda a, ss=ss: a.activation(out=ss[:, 2:3], in_=ss[:, 1:2], func=AF.Sqrt), r=[ssr], w=[ssr])
                                        S.op("dve", lambda v, ss=ss: v.reciprocal(out=ss[:, 3:4], in_=ss[:, 2:3]), r=[ssr], w=[ssr])
                                        S.op("dve", lambda v, tk=tk, ss=ss: v.scalar_tensor_tensor(
                                            out=ysb[tk][:], in0=ysb[tk][:], scalar=ss[:, 3:4], in1=gpost[:], op0=ALU.mult, op1=ALU.mult),
                                            r=[ssr, gpR], w=[ysR[tk]])
                                        S.op("pool", lambda g, tk=tk, xt=xt: g.tensor_tensor(out=ysb[tk][:], in0=ysb[tk][:], in1=xt[:], op=ALU.add),
                                             r=[xr], w=[ysR[tk]])
                                        if last:
                                            dst = yp[tt * 128:(tt + 1) * 128, :] if tt < NPT else ys[:, :]
                                            S.dma("sp", dst, ysb[tk][:], r=[ysR[tk]], w=[R("yo")])
                                        else:
                                            S.dma("sp", x1[tt * 128:(tt + 1) * 128, :], ysb[tk][:], r=[ysR[tk]], w=[RX[tt]])
                                    S.barrier()
                                    _chk(6)
                    S.barrier()
                    _chk(7)
        except _StopBuild:
            pass
        S.finish()
    return nc


def _t5_bucket(rel):
    rel = np.asarray(rel, dtype=np.int64)
    nb = 16
    max_exact = 8
    ret = np.where(rel > 0, nb, 0)
    n = np.abs(rel)
    nf = np.maximum(n, 1).astype(np.float32)
    large = max_exact + (np.log(nf / np.float32(max_exact)) / np.float32(math.log(512 / max_exact))
                         * np.float32(nb - max_exact)).astype(np.int32)
    large = np.minimum(large, nb - 1)
    return ret + np.where(n < max_exact, n, large)


def _consts():
    ident = np.eye(128, dtype=np.float32)
    jm = np.ascontiguousarray(ident[::-1])
    j = np.arange(128)[:, None]
    s = np.arange(128)[None, :]
    tri = np.stack([(j > s), (j <= s), np.ones((128, 128), bool)]).astype(np.float32)
    i = np.arange(128)[:, None]
    q = np.arange(512)[None, :]
    masks = np.zeros((16, 128, 512), np.float32)
    for d in range(4):
        D0 = 128 * d
        masks[d] = (D0 + i < q)
        masks[4 + d] = ((D0 + i) // 64 <= q // 64)
    for m in range(8):
        D0 = -512 + 128 * m
        dd = q // 64 - (D0 + i) // 64
        masks[8 + m] = (dd >= 0) & (dd <= 8)
    t = np.arange(LC)
    b = _t5_bucket(511 - t)
    oh = (b[None, :] == np.arange(32)[:, None]).astype(np.float32)
    inv = (10000.0 ** (-np.arange(0, 32, 2, dtype=np.float32) / np.float32(32))).astype(np.float32)
    css = []
    for p in range(2):
        pos = np.concatenate([TP * p + np.arange(TP), PAST + np.arange(64), np.zeros(64)]).astype(np.float32)
        ang = pos[:, None] * inv[None, :]
        css.append(np.concatenate([np.cos(ang), np.sin(ang)], axis=1).astype(np.float32))
    return dict(c_ident=ident, c_j=jm, c_tri=tri, c_masks=masks, c_oh=oh), css


_NC = None


def kernel(x_prompt, x_sample, cache_sb_k, cache_sb_v, cache_band_k, cache_band_v,
           cache_diff_k, cache_diff_v, cache_mla_ckv, cache_mla_krope,
           norm_pre, norm_post, w_in, band_bias, t5_table, diff_lambda, diff_subln,
           mla_q_norm, mla_w_q_up, mla_kv_norm, mla_w_kv_up, w_branch, w_out):
    global _NC
    f = lambda a: np.ascontiguousarray(np.asarray(a, dtype=np.float32))
    if _NC is None:
        _NC = build_nc()
    nc = _NC
    consts, css = _consts()
    shared = dict(norm_pre=f(norm_pre), norm_post=f(norm_post), w_in=f(w_in), band_bias=f(band_bias), t5=f(t5_table),
                  dlam=f(diff_lambda).reshape(2, 256), subln=f(diff_subln), qnorm=f(mla_q_norm), wqup=f(mla_w_q_up),
                  kvnorm=f(mla_kv_norm), wkvup=f(mla_w_kv_up), wbr=f(w_branch), wout=f(w_out), **consts)
    in_maps = []
    for c in range(8):
        b, p = c // 2, c % 2
        m = dict(shared)
        m["c_cs"] = css[p]
        m["c_vb"] = np.full((128, 1), 0.0 if p == 1 else -30000.0, np.float32)
        m["xp"] = f(x_prompt[b, TP * p:TP * (p + 1)])
        xs_ = np.zeros((128, D), np.float32)
        xs_[:64] = np.asarray(x_sample[c], dtype=np.float32)
        m["xs"] = xs_
        cs_ = slice(c, c + 1)
        m["csbk"] = f(cache_sb_k[:, cs_]).reshape(2, 1, PAST, 512)
        m["csbv"] = f(cache_sb_v[:, cs_]).reshape(2, 1, PAST, 512)
        m["cbk"] = f(cache_band_k[:, cs_]).reshape(2, 1, 512, 512)
        m["cbv"] = f(cache_band_v[:, cs_]).reshape(2, 1, 512, 512)
        m["cdk"] = f(cache_diff_k[:, cs_]).reshape(2, 1, PAST, 512)
        m["cdv"] = f(cache_diff_v[:, cs_]).reshape(2, 1, PAST, 512)
        m["cckv"] = f(cache_mla_ckv[:, cs_])
        m["ckr"] = f(cache_mla_krope[:, cs_])
        in_maps.append(m)
    ncores = _NCORES[0]
    res = run_bass_kernel_spmd(nc, in_maps[:ncores], core_ids=list(range(ncores)))
    rs = [res.results[i if i < ncores else i % 2] for i in range(8)]

    def cat_p(name, shape_tail):
        a = np.stack([np.concatenate([rs[2 * b][name], rs[2 * b + 1][name]], axis=1) for b in range(4)], axis=1)
        return a.reshape((2, 4) + shape_tail).astype(np.float32)

    def cat_b(name, shape_tail):
        a = np.stack([rs[2 * b + 1][name] for b in range(4)], axis=1)
        return a.reshape((2, 4) + shape_tail).astype(np.float32)

    def cat_s(name, shape_tail):
        return np.concatenate([r[name] for r in rs], axis=1).reshape((2, 8) + shape_tail).astype(np.float32)

    y_p = np.stack([np.concatenate([rs[2 * b]["yp"], rs[2 * b + 1]["yp"]], axis=0) for b in range(4)], axis=0).astype(np.float32)
    y_s = np.stack([r["ys"][:64] for r in rs], axis=0).astype(np.float32)
    S2 = 2 * TP
    return (y_p, y_s,
            cat_p("sbk_p", (S2, 8, 64)), cat_p("sbv_p", (S2, 8, 64)),
            cat_b("bk_p", (512, 8, 64)), cat_b("bv_p", (512, 8, 64)),
            cat_p("dk_p", (S2, 4, 2, 64)), cat_p("dv_p", (S2, 4, 128)),
            cat_p("ckv_p", (S2, 128)), cat_p("kr_p", (S2, 32)),
            cat_s("sbk_s", (64, 8, 64)), cat_s("sbv_s", (64, 8, 64)),
            cat_s("bk_s", (512, 8, 64)), cat_s("bv_s", (512, 8, 64)),
            cat_s("dk_s", (64, 4, 2, 64)), cat_s("dv_s", (64, 4, 128)),
            cat_s("ckv_s", (64, 128)), cat_s("kr_s", (64, 32)))
```

```python
import contextlib
import math
import numpy as np
import concourse.bass as bass
import concourse.mybir as mybir
from concourse.bass_utils import run_bass_kernel_spmd

F32 = mybir.dt.float32
BF16 = mybir.dt.bfloat16
AF = mybir.ActivationFunctionType
ALU = mybir.AluOpType

D = 2048
DC = 16
TP = 1024
NS = 1
NPT = TP // 128
T = TP + 128
NT = T // 128
KVW = 3232
KCOL = {'A.k': 0, 'A.v': 512, 'B.k': 1024, 'B.v': 1536, 'C.k': 2048, 'C.v': 2560, 'ckv': 3072, 'kr': 3200}
DIN = 15392
EPS = 1e-6
PAST = 1024
LC = 1408
WC = 1280
LB = 1536
WB = 1408
MLA_SCALE = 96 ** -0.5
DEPTH = 2


class R:
    __slots__ = ("w", "r", "name", "excl")

    def __init__(self, name="", excl=False):
        self.w = None
        self.r = []
        self.name = name
        self.excl = excl


class Sched:
    def __init__(self, nc, es):
        self.nc = nc
        self.eng = {"pe": nc.tensor, "act": nc.scalar, "dve": nc.vector, "pool": nc.gpsimd, "sp": nc.sync}
        self.csem = {}
        self.ccnt = {}
        for e in ("pe", "act", "dve", "pool"):
            self.csem[e] = es.enter_context(nc.semaphore("c_" + e))
            self.ccnt[e] = 0
        self.dsem = {}
        self.dcnt = {}
        self.dnext = {}
        for q, n in (("sp", 40), ("pool", 40)):
            self.dsem[q] = [es.enter_context(nc.semaphore(f"d_{q}{i}")) for i in range(n)]
            self.dcnt[q] = [0] * n
            self.dnext[q] = 0
        self.seen = {e: {} for e in self.eng}
        self.all_dma_tokens = []
        self.cccnt = 0
        self.ccsem = es.enter_context(nc.semaphore("ccsem"))

    def _wait(self, e, tok):
        sem, val, owner = tok
        if e == "pe" and owner == "pe":
            return
        key = id(sem)
        if self.seen[e].get(key, 0) >= val:
            return
        self.eng[e].wait_ge(sem, val)
        self.seen[e][key] = val

    def _deps(self, e, r, w):
        for x in r:
            if x.w is not None:
                self._wait(e, x.w)
            if x.excl:
                for t in x.r:
                    if t[2] != e:
                        self._wait(e, t)
        for x in w:
            if x.w is not None:
                self._wait(e, x.w)
            for t in x.r:
                self._wait(e, t)

    def _commit(self, tok, r, w):
        for x in r:
            x.r.append(tok)
            if len(x.r) > 24:
                x.r = x.r[-24:]
        for x in w:
            x.w = tok
            x.r = []

    def op(self, e, fn, r=(), w=()):
        if _DEAD[0]:
            return None
        self._deps(e, r, w)
        inst = fn(self.eng[e])
        self.ccnt[e] += 1
        tok = (self.csem[e], self.ccnt[e], e)
        inst.then_inc(tok[0], 1)
        self._commit(tok, r, w)
        return tok

    def mm(self, fns, r=(), w=()):
        if _DEAD[0]:
            return None
        self._deps("pe", r, w)
        inst = None
        for fn in fns:
            inst = fn(self.eng["pe"])
        self.ccnt["pe"] += 1
        tok = (self.csem["pe"], self.ccnt["pe"], "pe")
        inst.then_inc(tok[0], 1)
        self._commit(tok, r, w)
        return tok

    def dma(self, q, out, in_, r=(), w=(), **kw):
        if _DEAD[0]:
            return None
        i = self.dnext[q]
        self.dnext[q] = (i + 1) % len(self.dsem[q])
        sem = self.dsem[q][i]
        if self.dcnt[q][i]:
            self._wait(q, (sem, self.dcnt[q][i], "dma"))
        self._deps(q, r, w)
        inst = self.eng[q].dma_start(out=out, in_=in_, **kw)
        self.dcnt[q][i] += 16
        tok = (sem, self.dcnt[q][i], "dma")
        inst.then_inc(sem, 16)
        self._commit(tok, r, w)
        return tok

    def collective(self, ccsem, in_t, out_t, r=(), w=()):
        if _DEAD[0]:
            return None
        self._deps("pool", r, w)
        inst = self.eng["pool"].collective_compute(
            "AllGather", ALU.bypass, replica_groups=[[2 * i_, 2 * i_ + 1] for i_ in range(_NCORES[0] // 2)],
            ins=[in_t.ap().opt()], outs=[out_t.ap().opt()])
        self.cccnt += 1
        tok = (ccsem, self.cccnt, "cc")
        inst.then_inc(ccsem)
        self._commit(tok, r, w)
        return tok

    def barrier(self):
        if _DEAD[0]:
            return
        toks = []
        for e in ("pe", "act", "dve", "pool"):
            if self.ccnt[e]:
                toks.append((self.csem[e], self.ccnt[e], e + "_b"))
        for q in self.dsem:
            for i, s in enumerate(self.dsem[q]):
                if self.dcnt[q][i]:
                    toks.append((s, self.dcnt[q][i], "dma"))
        if self.cccnt:
            toks.append((self.ccsem, self.cccnt, "cc"))
        for e in self.eng:
            for t in toks:
                if e == "pe" and t[2] == "pe_b":
                    continue
                self._wait(e, t)

    def finish(self):
        for q in self.dsem:
            for i, s in enumerate(self.dsem[q]):
                if self.dcnt[q][i]:
                    self._wait("sp", (s, self.dcnt[q][i], "dma"))


_UID = [0]
_STOP = [99]
_NCORES = [8]
WARM = 3


class _StopBuild(Exception):
    pass


_DEAD = [False]


def _chk(n):
    if round(_STOP[0] * 1000) <= round(n * 1000):
        _DEAD[0] = True


class Pool:
    def __init__(self, es, nc, name, n, shape, dtype):
        _UID[0] += 1
        self.t = [es.enter_context(nc.sbuf_tensor(f"{name}{i}_{_UID[0]}", shape, dtype)) for i in range(n)]
        self.res = [R(f"{name}{i}") for i in range(n)]
        self.i = 0

    def get(self):
        i = self.i
        self.i = (i + 1) % len(self.t)
        return self.t[i], self.res[i]


def bc_ap(ap, dims):
    return bass.AP(tensor=ap.tensor, offset=ap.offset, ap=[list(ap.ap[0])] + [list(d) for d in dims])


def build_nc():
    nc = bass.Bass("TRN2", target_bir_lowering=False)
    _DEAD[0] = False
    dt = nc.dram_tensor

    def inp(name, shape, dtype=F32):
        return dt(name, list(shape), dtype, kind="ExternalInput").ap()

    def outp(name, shape):
        return dt(name, list(shape), F32, kind="ExternalOutput").ap()

    def scr(name, shape, dtype=F32):
        return dt(name, list(shape), dtype).ap()

    xp = inp("xp", [TP, D])
    xs = inp("xs", [128, D])
    csbk = inp("csbk", [2, NS, PAST, 512])
    csbv = inp("csbv", [2, NS, PAST, 512])
    cbk = inp("cbk", [2, NS, 512, 512])
    cbv = inp("cbv", [2, NS, 512, 512])
    cdk = inp("cdk", [2, NS, PAST, 512])
    cdv = inp("cdv", [2, NS, PAST, 512])
    cckv = inp("cckv", [2, NS, PAST, 128])
    ckr = inp("ckr", [2, NS, PAST, 32])
    norm_pre = inp("norm_pre", [2, D])
    norm_post = inp("norm_post", [2, D])
    w_in = inp("w_in", [2, D, DIN])
    band_bias = inp("band_bias", [2, 513, 8])
    t5 = inp("t5", [32, 4])
    dlam = inp("dlam", [2, 256])
    subln = inp("subln", [2, 128])
    qnorm = inp("qnorm", [2, 384])
    wqup = inp("wqup", [2, 384, 768])
    kvnorm = inp("kvnorm", [2, 128])
    wkvup = inp("wkvup", [2, 128, 1024])
    wbr = inp("wbr", [2, 4, 512, D])
    wout = inp("wout", [2, D, D])
    c_ident = inp("c_ident", [128, 128])
    c_j = inp("c_j", [128, 128])
    c_tri = inp("c_tri", [3, 128, 128])
    c_masks = inp("c_masks", [16, 128, 512])
    c_oh = inp("c_oh", [32, LC])
    c_cs = inp("c_cs", [T, 32])
    c_vb = inp("c_vb", [128, 1])

    yp = outp("yp", [TP, D])
    ys = outp("ys", [128, D])
    sbk_p = outp("sbk_p", [2, TP, 512])
    sbv_p = outp("sbv_p", [2, TP, 512])
    bk_p = outp("bk_p", [2, 512, 512])
    bv_p = outp("bv_p", [2, 512, 512])
    dk_p = outp("dk_p", [2, TP, 512])
    dv_p = outp("dv_p", [2, TP, 512])
    ckv_p = outp("ckv_p", [2, TP, 128])
    kr_p = outp("kr_p", [2, TP, 32])
    sbk_s = outp("sbk_s", [2, NS, 64, 512])
    sbv_s = outp("sbv_s", [2, NS, 64, 512])
    bk_s = outp("bk_s", [2, NS, 512, 512])
    bv_s = outp("bv_s", [2, NS, 512, 512])
    dk_s = outp("dk_s", [2, NS, 64, 512])
    dv_s = outp("dv_s", [2, NS, 64, 512])
    ckv_s = outp("ckv_s", [2, NS, 64, 128])
    kr_s = outp("kr_s", [2, NS, 64, 32])

    x1 = scr("x1", [T, D])
    kvloc_t = [dt(f"kvloc{t_}", [128, KVW], F32) for t_ in range(NPT)]
    kvloc = [t_.ap() for t_ in kvloc_t]
    kvall_t = [[dt(f"kvall{i}_{t_}", [256, KVW], F32) for t_ in range(NPT)] for i in range(2)]
    kvall = [[t_.ap() for t_ in row] for row in kvall_t]
    qT_scr = scr("qT_scr", [3, 512, T], BF16)
    gT_scr = scr("gT_scr", [2048, T], BF16)
    mT_scr = scr("mT_scr", [8192, T], BF16)
    qn_scr = scr("qn_scr", [T, 384])
    gvC_scr = scr("gvC_scr", [4, LC])
    gvB_scr = scr("gvB_scr", [8, LB])

    with contextlib.ExitStack() as es:
        S = Sched(nc, es)

        try:
            def sb(name, shape, dtype, stack=es):
                _UID[0] += 1
                return stack.enter_context(nc.sbuf_tensor(f"{name}_{_UID[0]}", list(shape), dtype))

            PS = [es.enter_context(nc.psum_tensor(f"ps{i}", [128, 512], F32)) for i in range(7)]
            PSR = [R(f"ps{i}", excl=True) for i in range(7)]
            psT = es.enter_context(nc.psum_tensor("psT", [128, 1024], BF16))
            psTR = R("psT", excl=True)

            identb = sb("identb", [128, 128], BF16)
            jm = sb("jm", [128, 128], F32)
            trib = sb("trib", [128, 3, 128], BF16)
            masks = sb("masks", [128, 16, 512], BF16)
            EGC = sb("EGC", [128, 4, WC], BF16)
            EGB = sb("EGB", [128, 8, WB], BF16)
            satC = sb("satC", [128, 4], F32)
            cR = R("consts")
            EGCR = R("EGC")
            EGBR = R("EGB")
            S.dma("pool", identb[:], c_ident[:, :], w=[cR])
            S.dma("sp", jm[:], c_j[:, :], w=[cR])
            S.dma("pool", trib[:], c_tri.rearrange("m p n -> p m n"), w=[cR])
            for m4 in range(4):
                S.dma("pool", masks[:, 4 * m4:4 * m4 + 4, :],
                      c_masks[4 * m4:4 * m4 + 4].rearrange("m p n -> p m n"), w=[cR])
            S.dma("sp", satC[:], bass.AP(tensor=t5.tensor, offset=15 * 4, ap=[[0, 128], [1, 4]]), w=[cR])
            vbt = sb("vbt", [128, 1], F32)
            satCv = sb("satCv", [128, 4], F32)
            S.dma("sp", vbt[:], c_vb[:, :], w=[cR])
            S.op("dve", lambda v: v.tensor_scalar(out=satCv[:], in0=satC[:], scalar1=vbt[:, 0:1], scalar2=None, op0=ALU.add),
                 r=[cR], w=[cR])
            kvallR = [[R(f"kvall{i}_{t_}") for t_ in range(NPT)] for i in range(2)]
            TRI = trib[:, 0, :]
            TRIC = trib[:, 1, :]
            ONES = trib[:, 2, :]

            def toeplitz_build(gv_scr, nheads, L, W, EG, EGR, stack, defer=None):
                nb = 1 if defer is None else 2
                hks = [sb("hk", [128, W], F32, stack) for _ in range(nb)]
                hkRs = [R("hk") for _ in range(nb)]

                def dma_step(h):
                    src = bass.AP(tensor=gv_scr.tensor, offset=h * L, ap=[[1, 128], [1, W]])
                    S.dma("sp", hks[h % nb][:], src, r=[gvR], w=[hkRs[h % nb]])

                def comp_step(h, bank=None):
                    hk, hkR = hks[h % nb], hkRs[h % nb]
                    c = 0
                    bi = 0
                    while c < W:
                        n = min(512, W - c)
                        if bank is None:
                            ps_, pr_ = PS[bi % 2], PSR[bi % 2]
                        else:
                            ps_, pr_ = bank()
                        S.mm([lambda pe: pe.matmul(ps_[:, 0:n], lhsT=jm[:], rhs=hk[:, c:c + n], start=True, stop=True)],
                             r=[hkR, cR], w=[pr_])
                        S.op("act", lambda a: a.activation(out=EG[:, h, c:c + n], in_=ps_[:, 0:n], func=AF.Exp),
                             r=[pr_], w=[EGR])
                        c += n
                        bi += 1

                if defer is None:
                    for h in range(nheads):
                        dma_step(h)
                        comp_step(h)
                else:
                    defer.append(lambda: dma_step(0))
                    for h in range(nheads):
                        if h + 1 < nheads:
                            defer.append(lambda h=h: dma_step(h + 1))
                        defer.append(lambda h=h: comp_step(h, bank=defer_bank[0]))

            defer_bank = [None]
            gvR = R("gv")
            with contextlib.ExitStack() as st0:
                t5sb = sb("t5sb", [32, 4], F32, st0)
                ohsb = sb("ohsb", [32, LC], F32, st0)
                gvsb = sb("gvsb", [4, LC], F32, st0)
                tr = R("t0")
                S.dma("sp", t5sb[:], t5[:, :], w=[tr])
                S.dma("sp", ohsb[:], c_oh[:, :], w=[tr])
                c = 0
                gvsR = R("gvs")
                while c < LC:
                    n = min(512, LC - c)
                    S.mm([lambda pe, c=c, n=n: pe.matmul(PS[0][0:4, 0:n], lhsT=t5sb[:], rhs=ohsb[:, c:c + n],
                                                         start=True, stop=True)], r=[tr], w=[PSR[0]])
                    S.op("dve", lambda v, c=c, n=n: v.tensor_copy(out=gvsb[:, c:c + n], in_=PS[0][0:4, 0:n]),
                         r=[PSR[0]], w=[gvsR])
                    c += n
                S.dma("sp", gvC_scr[:, :], gvsb[:], r=[gvsR], w=[gvR])
                toeplitz_build(gvC_scr, 4, LC, WC, EGC, EGCR, st0)
                S.barrier()
                _chk(0)

            RX = [R(f"x{t}") for t in range(NT)]
            RKV = {}

            def rkv(name, i):
                k = (name, i)
                if k not in RKV:
                    RKV[k] = R(str(k))
                return RKV[k]

            RFM = {}

            def rfm(name, tb):
                k = (name, tb)
                if k not in RFM:
                    RFM[k] = R(str(k))
                return RFM[k]

            TB = [(0, 512), (512, 512), (1024, 128)]

            for l in range(DEPTH):
                lam_init = 0.8 - 0.6 * math.exp(-0.3 * l)
                last = (l == DEPTH - 1)
                with contextlib.ExitStack() as sl:
                    wqupb = sb("wqupb", [128, 3, 768], BF16, sl)
                    wkvupb = sb("wkvupb", [128, 1024], BF16, sl)
                    lcol = sb("lcol", [128, 8], F32, sl)
                    lR = R("layerconst")
                    S.dma("pool", wqupb[:], wqup[l].rearrange("(c p) n -> p c n", p=128), w=[lR])
                    S.dma("pool", wkvupb[:], wkvup[l], w=[lR])

                    with contextlib.ExitStack() as s2:
                        hT = sb("hT", [128, DC, T], BF16, s2)
                        hTR = [R(f"hT{t}") for t in range(NT)]
                        DEFER = []
                        bb = sb("bb", [8, 513], F32, s2)
                        gvb = sb("gvb", [8, LB], F32, s2)
                        bbR = R("bb")
                        S.dma("sp", bb[:], bass.AP(tensor=band_bias.tensor, offset=l * 513 * 8, ap=[[1, 8], [8, 513]]),
                              w=[bbR], allow_slow_non_contiguous=True)
                        gvbR = R("gvb")
                        S.op("dve", lambda v: v.tensor_copy(out=gvb[:, 0:255], in_=bc_ap(bb[:, 0:1], [[0, 255]])),
                             r=[bbR], w=[gvbR])
                        S.op("dve", lambda v: v.tensor_copy(out=gvb[:, 255:768], in_=bb[:, 0:513]), r=[bbR], w=[gvbR])
                        S.op("dve", lambda v: v.tensor_copy(out=gvb[:, 768:LB], in_=bc_ap(bb[:, 512:513], [[0, LB - 768]])),
                             r=[bbR], w=[gvbR])
                        S.dma("sp", gvB_scr[:, :], gvb[:], r=[gvbR], w=[gvR])
                        toeplitz_build(gvB_scr, 8, LB, WB, EGB, EGBR, s2, defer=DEFER)
                        dl = sb("dl", [128, 256], F32, s2)
                        dj = sb("dj", [128, 128], F32, s2)
                        sg = sb("sg", [128, 128], F32, s2)
                        la = sb("la", [128, 8], F32, s2)
                        dR = R("dl")
                        S.dma("sp", dl[:], bass.AP(tensor=dlam.tensor, offset=l * 256, ap=[[0, 128], [1, 256]]), w=[dR])
                        S.dma("sp", sg[:, 0:1], bass.AP(tensor=subln.tensor, offset=l * 128, ap=[[1, 128], [1, 1]]), w=[dR])
                        S.op("dve", lambda v: v.tensor_tensor(out=dj[:, 0:64], in0=dl[:, 0:64], in1=dl[:, 64:128], op=ALU.mult),
                             r=[dR], w=[dR])
                        S.op("dve", lambda v: v.tensor_tensor(out=dj[:, 64:128], in0=dl[:, 128:192], in1=dl[:, 192:256], op=ALU.mult),
                             r=[dR], w=[dR])
                        S.op("dve", lambda v: v.reduce_sum(out=la[:, 0:1], in_=dj[:, 0:64], axis=mybir.AxisListType.X), r=[dR], w=[dR])
                        S.op("dve", lambda v: v.reduce_sum(out=la[:, 1:2], in_=dj[:, 64:128], axis=mybir.AxisListType.X), r=[dR], w=[dR])
                        S.op("act", lambda a: a.activation(out=la[:, 2:4], in_=la[:, 0:2], func=AF.Exp), r=[dR], w=[dR])
                        S.op("dve", lambda v: v.tensor_tensor(out=la[:, 4:5], in0=la[:, 3:4], in1=la[:, 2:3], op=ALU.subtract),
                             r=[dR], w=[dR])
                        S.op("dve", lambda v: v.tensor_scalar(out=lcol[:, 0:1], in0=la[:, 4:5], scalar1=-lam_init, scalar2=None,
                                                              op0=ALU.add), r=[dR], w=[lR])
                        S.op("dve", lambda v: v.tensor_scalar(out=lcol[:, 1:2], in0=sg[:, 0:1], scalar1=(1.0 - lam_init),
                                                              scalar2=None, op0=ALU.mult), r=[dR], w=[lR])
                        with contextlib.ExitStack() as s1:
                            gpre = sb("gpre", [128, D], F32, s1)
                            gR = R("gpre")
                            S.dma("sp", gpre[:], bass.AP(tensor=norm_pre.tensor, offset=l * D, ap=[[0, 128], [1, D]]), w=[gR])
                            xpool = Pool(s1, nc, "xt", 2, [128, D], F32)
                            hbpool = Pool(s1, nc, "hb", 2, [128, D], BF16)
                            junk = sb("junk", [128, D], BF16, s1)
                            junkR = R("junk")
                            sspool = Pool(s1, nc, "ss", 2, [128, 4], F32)
                            p1st = {}

                            def p1front(tt):
                                xt, xr = xpool.get()
                                if l == 0:
                                    src = xp[tt * 128:(tt + 1) * 128, :] if tt < NPT else xs[:, :]
                                    S.dma("sp", xt[:], src, w=[xr])
                                else:
                                    S.dma("sp", xt[:], x1[tt * 128:(tt + 1) * 128, :], r=[RX[tt]], w=[xr])
                                ss, ssr = sspool.get()
                                S.op("dve", lambda v, ss=ss: v.memset(ss[:], 0.0), w=[ssr])
                                S.op("act", lambda a, xt=xt, ss=ss: a.activation(out=junk[:], in_=xt[:], func=AF.Square,
                                                                                 accum_out=ss[:, 0:1]),
                                     r=[xr], w=[junkR, ssr])
                                S.op("dve", lambda v, ss=ss: v.tensor_scalar(out=ss[:, 1:2], in0=ss[:, 0:1], scalar1=1.0 / D,
                                                                              scalar2=EPS, op0=ALU.mult, op1=ALU.add),
                                     r=[ssr], w=[ssr])
                                S.op("act", lambda a, ss=ss: a.activation(out=ss[:, 2:3], in_=ss[:, 1:2], func=AF.Sqrt),
                                     r=[ssr], w=[ssr])
                                S.op("dve", lambda v, ss=ss: v.reciprocal(out=ss[:, 3:4], in_=ss[:, 2:3]), r=[ssr], w=[ssr])
                                hb, hbr = hbpool.get()
                                S.op("dve", lambda v, hb=hb, xt=xt, ss=ss: v.scalar_tensor_tensor(
                                    out=hb[:], in0=xt[:], scalar=ss[:, 3:4], in1=gpre[:], op0=ALU.mult, op1=ALU.mult),
                                    r=[xr, ssr, gR], w=[hbr])
                                p1st[tt] = (hb, hbr)

                            def p1back(tt):
                                hb, hbr = p1st.pop(tt)
                                for half in range(2):
                                    S.mm([lambda pe, hb=hb, c=c, half=half: pe.transpose(
                                        out=psT[:, c * 128:(c + 1) * 128], in_=hb[:, (half * 8 + c) * 128:(half * 8 + c + 1) * 128],
                                        identity=identb[:]) for c in range(8)], r=[hbr, cR], w=[psTR])
                                    eng = "act" if half == 0 else "dve"
                                    if eng == "act":
                                        S.op("act", lambda a, half=half, tt=tt: a.copy(
                                            out=hT[:, half * 8:half * 8 + 8, tt * 128:(tt + 1) * 128],
                                            in_=psT[:, :].rearrange("p (a b) -> p a b", b=128)), r=[psTR], w=[hTR[tt]])
                                    else:
                                        S.op("dve", lambda v, half=half, tt=tt: v.tensor_copy(
                                            out=hT[:, half * 8:half * 8 + 8, tt * 128:(tt + 1) * 128],
                                            in_=psT[:, :].rearrange("p (a b) -> p a b", b=128)), r=[psTR], w=[hTR[tt]])


                            p1front(0)
                            for tt in range(NT):
                                if tt + 1 < NT:
                                    p1front(tt + 1)
                                p1back(tt)

                        S.barrier()
                        _chk(2)
                        wpool = Pool(s2, nc, "wblk", 3, [128, DC, 512], BF16)
                        w32 = sb("w32", [128, DC, 32], BF16, s2)
                        w32R = R("w32")
                        evf = Pool(s2, nc, "evf", 6, [128, 512], F32)
                        evb = Pool(s2, nc, "evb", 8, [128, 512], BF16)
                        dsb = Pool(s2, nc, "dsb", 2, [128, 544], F32)
                        dwk = Pool(s2, nc, "dwk", 2, [128, 640], F32)
                        csb = Pool(s2, nc, "csb", 2, [128, 32], F32)
                        gq = sb("gq", [128, 384], F32, s2)
                        gkv = sb("gkv", [128, 128], F32, s2)
                        jk2 = sb("jk2", [128, 384], F32, s2)
                        jk2R = R("jk2")
                        g2R = R("g2")
                        S.dma("sp", gq[:], bass.AP(tensor=qnorm.tensor, offset=l * 384, ap=[[0, 128], [1, 384]]), w=[g2R])
                        S.dma("sp", gkv[:], bass.AP(tensor=kvnorm.tensor, offset=l * 128, ap=[[0, 128], [1, 128]]), w=[g2R])
                        w_l = w_in[l].rearrange("(c p) n -> p c n", p=128)
                        psi = [0]

                        def next_ps():
                            i = psi[0]
                            psi[0] = (i + 1) % 6
                            return PS[i], PSR[i]

                        def tm_dests(name, tt):
                            outs = []
                            if tt < NPT:
                                rows = slice(tt * 128, (tt + 1) * 128)
                                kc = KCOL[name]
                                outs.append((kvloc[tt][:, kc:kc + 512], (0, 128), rkv("L" + name, tt)))
                                po = {"A.k": sbk_p, "A.v": sbv_p, "C.k": dk_p, "C.v": dv_p}.get(name)
                                if po is not None:
                                    outs.append((po[l, rows, :], (0, 128), rkv("O" + name, tt)))
                            else:
                                for s in range(NS):
                                    rs = (s * 64, s * 64 + 64)
                                    if name == "A.k":
                                        outs.append((sbk_s[l, s, :, :], rs, rkv("sbk_s", s)))
                                    elif name == "A.v":
                                        outs.append((sbv_s[l, s, :, :], rs, rkv("sbv_s", s)))
                                    elif name == "B.k":
                                        outs.append((bk_s[l, s, 448:512, :], rs, rkv("bk_s", s)))
                                    elif name == "B.v":
                                        outs.append((bv_s[l, s, 448:512, :], rs, rkv("bv_s", s)))
                                    elif name == "C.k":
                                        outs.append((dk_s[l, s, :, :], rs, rkv("dk_s", s)))
                                    elif name == "C.v":
                                        outs.append((dv_s[l, s, :, :], rs, rkv("dv_s", s)))
                            return outs

                        TMB = [("A.k", 512), ("A.v", 1024), ("B.k", 2048), ("B.v", 2560), ("C.k", 3584), ("C.v", 4096)]
                        for name, c0 in TMB:
                            wt, wr = wpool.get()
                            S.dma("pool", wt[:], w_l[:, :, c0:c0 + 512], w=[wr])
                            for tt in range(NT):
                                ps, pr = next_ps()
                                S.mm([lambda pe, ps=ps, wt=wt, dc=dc, tt=tt: pe.matmul(
                                    ps[:, :], lhsT=hT[:, dc, tt * 128:(tt + 1) * 128], rhs=wt[:, dc, :],
                                    start=(dc == 0), stop=(dc == DC - 1)) for dc in range(DC)],
                                    r=[wr, hTR[tt]], w=[pr])
                                ev, er = evf.get()
                                S.op("dve", lambda v, ev=ev, ps=ps: v.tensor_copy(out=ev[:], in_=ps[:, :]), r=[pr], w=[er])
                                for (dap, (ra, rb), dres) in tm_dests(name, tt):
                                    S.dma("sp", dap, ev[ra:rb, :], r=[er], w=[dres])
                        bandR = R("band_out")
                        for t_ in range(4, 8):
                            S.dma("sp", bk_p[l, (t_ - 4) * 128:(t_ - 3) * 128, :], kvloc[t_][:, 1024:1536], r=[rkv("LB.k", t_)], w=[bandR])
                            S.dma("sp", bv_p[l, (t_ - 4) * 128:(t_ - 3) * 128, :], kvloc[t_][:, 1536:2048], r=[rkv("LB.v", t_)], w=[bandR])
                        for s in range(NS):
                            S.dma("sp", bk_s[l, s, 0:448, :], cbk[l, s, 64:512, :], w=[bandR])
                            S.dma("sp", bv_s[l, s, 0:448, :], cbv[l, s, 64:512, :], w=[bandR])

                        wt, wr = wpool.get()
                        S.dma("pool", wt[:], w_l[:, :, 4608:5120], w=[wr])
                        S.dma("pool", w32[:], w_l[:, :, 5120:5152], w=[w32R])
                        for tt in range(NT):
                            ps, pr = next_ps()
                            ps2, pr2 = next_ps()
                            S.mm([lambda pe, ps=ps, wt=wt, dc=dc, tt=tt: pe.matmul(
                                ps[:, :], lhsT=hT[:, dc, tt * 128:(tt + 1) * 128], rhs=wt[:, dc, :],
                                start=(dc == 0), stop=(dc == DC - 1)) for dc in range(DC)], r=[wr, hTR[tt]], w=[pr])
                            S.mm([lambda pe, ps2=ps2, dc=dc, tt=tt: pe.matmul(
                                ps2[:, 0:32], lhsT=hT[:, dc, tt * 128:(tt + 1) * 128], rhs=w32[:, dc, :],
                                start=(dc == 0), stop=(dc == DC - 1)) for dc in range(DC)], r=[w32R, hTR[tt]], w=[pr2])
                            d, dr = dsb.get()
                            S.op("dve", lambda v, d=d, ps=ps: v.tensor_copy(out=d[:, 0:512], in_=ps[:, :]), r=[pr], w=[dr])
                            S.op("dve", lambda v, d=d, ps2=ps2: v.tensor_copy(out=d[:, 512:544], in_=ps2[:, 0:32]), r=[pr2], w=[dr])
                            wk, wkr = dwk.get()
                            cs, csr = csb.get()
                            S.dma("sp", cs[:], c_cs[tt * 128:(tt + 1) * 128, :], w=[csr])
                            st = 600
                            S.op("dve", lambda v, wk=wk: v.memset(wk[:, st:st + 8], 0.0), w=[wkr])
                            S.op("act", lambda a, d=d, wk=wk: a.activation(out=jk2[:, 0:384], in_=d[:, 0:384], func=AF.Square,
                                                                           accum_out=wk[:, st:st + 1]),
                                 r=[dr], w=[wkr, jk2R])
                            S.op("act", lambda a, d=d, wk=wk: a.activation(out=jk2[:, 0:128], in_=d[:, 384:512], func=AF.Square,
                                                                           accum_out=wk[:, st + 1:st + 2]),
                                 r=[dr], w=[wkr, jk2R])
                            S.op("dve", lambda v, wk=wk: v.tensor_scalar(out=wk[:, st + 2:st + 3], in0=wk[:, st:st + 1],
                                                                         scalar1=1.0 / 384, scalar2=EPS, op0=ALU.mult, op1=ALU.add),
                                 r=[wkr], w=[wkr])
                            S.op("dve", lambda v, wk=wk: v.tensor_scalar(out=wk[:, st + 3:st + 4], in0=wk[:, st + 1:st + 2],
                                                                         scalar1=1.0 / 128, scalar2=EPS, op0=ALU.mult, op1=ALU.add),
                                 r=[wkr], w=[wkr])
                            S.op("act", lambda a, wk=wk: a.activation(out=wk[:, st + 4:st + 6], in_=wk[:, st + 2:st + 4], func=AF.Sqrt),
                                 r=[wkr], w=[wkr])
                            S.op("dve", lambda v, wk=wk: v.reciprocal(out=wk[:, st + 6:st + 8], in_=wk[:, st + 4:st + 6]),
                                 r=[wkr], w=[wkr])
                            S.op("dve", lambda v, wk=wk, d=d: v.scalar_tensor_tensor(
                                out=wk[:, 0:384], in0=d[:, 0:384], scalar=wk[:, st + 6:st + 7], in1=gq[:], op0=ALU.mult, op1=ALU.mult),
                                r=[dr, wkr, g2R], w=[wkr])
                            S.op("dve", lambda v, wk=wk, d=d: v.scalar_tensor_tensor(
                                out=wk[:, 384:512], in0=d[:, 384:512], scalar=wk[:, st + 7:st + 8], in1=gkv[:], op0=ALU.mult, op1=ALU.mult),
                                r=[dr, wkr, g2R], w=[wkr])
                            x1a, x2a = d[:, 512:528], d[:, 528:544]
                            cosa, sina = cs[:, 0:16], cs[:, 16:32]
                            S.op("dve", lambda v, wk=wk: v.tensor_tensor(out=wk[:, 544:560], in0=x1a, in1=cosa, op=ALU.mult), r=[dr, csr], w=[wkr])
                            S.op("dve", lambda v, wk=wk: v.tensor_tensor(out=wk[:, 560:576], in0=x2a, in1=sina, op=ALU.mult), r=[dr, csr], w=[wkr])
                            S.op("dve", lambda v, wk=wk: v.tensor_tensor(out=wk[:, 512:528], in0=wk[:, 544:560], in1=wk[:, 560:576], op=ALU.subtract), r=[wkr], w=[wkr])
                            S.op("dve", lambda v, wk=wk: v.tensor_tensor(out=wk[:, 544:560], in0=x2a, in1=cosa, op=ALU.mult), r=[dr, csr], w=[wkr])
                            S.op("dve", lambda v, wk=wk: v.tensor_tensor(out=wk[:, 560:576], in0=x1a, in1=sina, op=ALU.mult), r=[dr, csr], w=[wkr])
                            S.op("dve", lambda v, wk=wk: v.tensor_tensor(out=wk[:, 528:544], in0=wk[:, 544:560], in1=wk[:, 560:576], op=ALU.add), r=[wkr], w=[wkr])
                            S.dma("sp", qn_scr[tt * 128:(tt + 1) * 128, :], wk[:, 0:384], r=[wkr], w=[rkv("qn", tt)])
                            if tt < NPT:
                                S.dma("sp", ckv_p[l, tt * 128:(tt + 1) * 128, :], wk[:, 384:512], r=[wkr], w=[rkv("Ockv", tt)])
                                S.dma("sp", kr_p[l, tt * 128:(tt + 1) * 128, :], wk[:, 512:544], r=[wkr], w=[rkv("Okr", tt)])
                                S.dma("sp", kvloc[tt][:, 3072:3200], wk[:, 384:512], r=[wkr], w=[rkv("Lckv", tt)])
                                S.dma("sp", kvloc[tt][:, 3200:3232], wk[:, 512:544], r=[wkr], w=[rkv("Lkr", tt)])
                            else:
                                for s in range(NS):
                                    S.dma("sp", ckv_s[l, s, :, :], wk[s * 64:s * 64 + 64, 384:512], r=[wkr], w=[rkv("ckv_s", s)])
                                    S.dma("sp", kr_s[l, s, :, :], wk[s * 64:s * 64 + 64, 512:544], r=[wkr], w=[rkv("kr_s", s)])

                        FMB = []
                        for bi, c0 in enumerate((0, 1536, 3072)):
                            FMB.append((c0, AF.Copy, qT_scr[bi], "q%d" % bi))
                        for n in range(4):
                            FMB.append((5152 + 512 * n, AF.Silu, gT_scr[512 * n:512 * n + 512], "g"))
                        for j in range(16):
                            FMB.append((7200 + 512 * j, AF.Sigmoid, mT_scr[512 * j:512 * j + 512], "m"))
                        defer_bank[0] = next_ps
                        for fmi, (c0, func, dst, nm) in enumerate(FMB):
                            if DEFER:
                                DEFER.pop(0)()
                            wt, wr = wpool.get()
                            S.dma("pool", wt[:], w_l[:, :, c0:c0 + 512], w=[wr])
                            if fmi == 2:
                                for t_ in range(NPT):
                                    S.collective(S.ccsem, kvloc_t[t_], kvall_t[l][t_],
                                                 r=[rkv("L" + n_, t_) for n_ in KCOL], w=[kvallR[l][t_]])
                            for ch in range(4):
                                for tbi, (t0, nt) in enumerate(TB):
                                    ps, pr = next_ps()
                                    tts = [hTR[t0 // 128 + i] for i in range(nt // 128)]
                                    S.mm([lambda pe, ps=ps, wt=wt, dc=dc, ch=ch, t0=t0, nt=nt: pe.matmul(
                                        ps[:, 0:nt], lhsT=wt[:, dc, ch * 128:(ch + 1) * 128], rhs=hT[:, dc, t0:t0 + nt],
                                        start=(dc == 0), stop=(dc == DC - 1)) for dc in range(DC)], r=[wr] + tts, w=[pr])
                                    ev, er = evb.get()
                                    S.op("act", lambda a, ev=ev, ps=ps, nt=nt, func=func: a.activation(
                                        out=ev[:, 0:nt], in_=ps[:, 0:nt], func=func), r=[pr], w=[er])
                                    S.dma("sp", dst[ch * 128:(ch + 1) * 128, t0:t0 + nt], ev[:, 0:nt], r=[er], w=[rfm(nm, tbi)])
                        while DEFER:
                            DEFER.pop(0)()
                        S.barrier()
                        _chk(3)

                    for blk in range(2):
                        with contextlib.ExitStack() as s34:
                            gatedT = sb("gatedT", [128, 16, 640], BF16, s34)
                            gatedR = R("gated")
                            groups = [dict(kind="p", qb=blk, tok0=512 * blk, nq=512, col0=0, tbi=blk)]
                            NQB = 512
                            CPS = [(0, 512)]
                            MTB = [blk]
                            if blk == 1:
                                groups += [dict(kind="s", s=s, tok0=TP + 64 * s, nq=64, col0=512 + 64 * s, tbi=2) for s in range(NS)]
                                NQB = 640
                                CPS = [(0, 512), (512, 128)]
                                MTB = [1, 2]
                                S.op("dve", lambda v: v.memset(gatedT[:, :, 576:640], 0.0), w=[gatedR])
                            with contextlib.ExitStack() as s3:
                                KT = sb("KT", [128, 8, 2048], BF16, s3)
                                KTR = R("KT")
                                Vsb = sb("Vsb", [128, 16, 512], BF16, s3)
                                VR = R("V")
                                qT = sb("qT", [128, 8, 512], BF16, s3)
                                qTR = R("qT")
                                gT = sb("gT", [128, 4, 512], BF16, s3)
                                gTR = R("gT")
                                kraw = Pool(s3, nc, "kraw", 3, [128, 512], BF16)
                                kst = Pool(s3, nc, "kst", 3, [128, 512], F32)
                                vst = Pool(s3, nc, "vst", 3, [128, 512], F32)

                                def warm(n):
                                    for _ in range(n):
                                        S.mm([lambda pe: pe.matmul(PS[0][:, 0:512], lhsT=identb[:], rhs=masks[:, 0, :],
                                                                   start=True, stop=True)], r=[cR], w=[PSR[0]])
                                wf = Pool(s3, nc, "wf", 4, [128, 512], F32)
                                t1p = Pool(s3, nc, "t1p", 6, [128, 512], F32)
                                wb_ = Pool(s3, nc, "wb", 16, [128, 512], BF16)
                                small = Pool(s3, nc, "small", 7, [128, 1024], BF16)
                                qmf = Pool(s3, nc, "qmf", 2, [128, 768], F32)
                                csq = Pool(s3, nc, "csq", 2, [128, 32], F32)

                                for G in groups:
                                    nq = G["nq"]
                                    tok0 = G["tok0"]
                                    col0 = G["col0"]

                                    def key_tiles(br):
                                        tl = []
                                        if G["kind"] == "p":
                                            qb = G["qb"]
                                            hi = 8 + 4 * qb + 4
                                            lo = (8 + 4 * qb - 4) if br == "B" else 0
                                            for kt in range(lo, hi):
                                                D0 = 128 * kt - (1024 + 512 * qb)
                                                prv = kt < 8
                                                rows = slice(0, 128)
                                                if prv:
                                                    srcd = kvall[l][kt]
                                                    rs_ = [kvallR[l][kt]]
                                                else:
                                                    srcd = kvloc[kt - 8]
                                                if br != "D":
                                                    kc, vc = KCOL[br + ".k"], KCOL[br + ".v"]
                                                    if not prv:
                                                        rs_ = [rkv("L" + br + ".k", kt - 8), rkv("L" + br + ".v", kt - 8)]
                                                    tl.append((128, srcd[rows, kc:kc + 512], srcd[rows, vc:vc + 512], D0, rs_, prv))
                                                else:
                                                    if not prv:
                                                        rs_ = [rkv("Lckv", kt - 8), rkv("Lkr", kt - 8)]
                                                    tl.append((128, srcd[rows, 3072:3200], srcd[rows, 3200:3232], D0, rs_, prv))
                                        else:
                                            s = G["s"]
                                            if br == "B":
                                                for kt in range(4):
                                                    rows = slice(kt * 128, kt * 128 + 128)
                                                    tl.append((128, cbk[l, s, rows, :], cbv[l, s, rows, :], -512 + 128 * kt, [], False))
                                                tl.append((64, bk_s[l, s, 448:512, :], bv_s[l, s, 448:512, :], 0, [rkv("bk_s", s), rkv("bv_s", s)], False))
                                            else:
                                                for kt in range(8):
                                                    rows = slice(kt * 128, kt * 128 + 128)
                                                    D0 = kt * 128 - 1024
                                                    if br == "A":
                                                        tl.append((128, csbk[l, s, rows, :], csbv[l, s, rows, :], D0, [], False))
                                                    elif br == "C":
                                                        tl.append((128, cdk[l, s, rows, :], cdv[l, s, rows, :], D0, [], False))
                                                    else:
                                                        tl.append((128, cckv[l, s, rows, :], ckr[l, s, rows, :], D0, [], False))
                                                if br == "A":
                                                    tl.append((64, sbk_s[l, s, :, :], sbv_s[l, s, :, :], 0, [rkv("sbk_s", s), rkv("sbv_s", s)], False))
                                                elif br == "C":
                                                    tl.append((64, dk_s[l, s, :, :], dv_s[l, s, :, :], 0, [rkv("dk_s", s), rkv("dv_s", s)], False))
                                                else:
                                                    tl.append((64, ckv_s[l, s, :, :], kr_s[l, s, :, :], 0, [rkv("ckv_s", s), rkv("kr_s", s)], False))
                                        return tl

                                    for bi, br in enumerate("ABCD"):
                                        tiles = key_tiles(br)
                                        S.dma("sp", gT[:, :, 0:nq],
                                              gT_scr[512 * bi:512 * bi + 512, tok0:tok0 + nq].rearrange("(c p) t -> p c t", p=128),
                                              r=[rfm("g", G["tbi"])], w=[gTR])
                                        if br != "D":
                                            S.dma("sp", qT[:, 0:4, 0:nq],
                                                  qT_scr[bi, :, tok0:tok0 + nq].rearrange("(c p) t -> p c t", p=128),
                                                  r=[rfm("q%d" % bi, G["tbi"])], w=[qTR])
                                            for ti, (nk, kap, vap, D0, rs, prv) in enumerate(tiles):
                                                kf, kfr = kst.get()
                                                vf, vfr = vst.get()
                                                S.dma("sp", kf[0:nk, :], kap, r=rs, w=[kfr])
                                                S.dma("sp", vf[0:nk, :], vap, r=rs, w=[vfr])
                                                kr_, krr = kraw.get()
                                                S.op("dve", lambda v: v.tensor_copy(out=kr_[0:nk, :], in_=kf[0:nk, :]), r=[kfr], w=[krr])
                                                S.op("act", lambda a: a.copy(out=Vsb[0:nk, ti, :], in_=vf[0:nk, :]), r=[vfr], w=[VR])
                                                S.mm([lambda pe, kr_=kr_, c=c, nk=nk: pe.transpose(
                                                    out=psT[:, c * 128:c * 128 + nk], in_=kr_[0:nk, c * 128:(c + 1) * 128],
                                                    identity=identb[0:nk, 0:nk]) for c in range(4)], r=[krr, cR], w=[psTR])
                                                S.op("dve", lambda v, ti=ti, nk=nk: v.tensor_copy(
                                                    out=KT[:, 0:4, ti * 128:ti * 128 + nk],
                                                    in_=psT[:, 0:512].rearrange("p (a b) -> p a b", b=128)[:, :, 0:nk]),
                                                    r=[psTR], w=[KTR])
                                        else:
                                            ntt = max(1, nq // 128)
                                            for qi in range(ntt):
                                                nr = min(128, nq)
                                                r0 = tok0 + qi * 128
                                                tt = r0 // 128
                                                qn_, qnr = small.get()
                                                kf, kfr = kst.get()
                                                S.dma("sp", kf[0:nr, 0:384], qn_scr[r0:r0 + nr, :], r=[rkv("qn", tt)], w=[kfr])
                                                S.op("dve", lambda v: v.tensor_copy(out=qn_[0:nr, 0:384], in_=kf[0:nr, 0:384]), r=[kfr], w=[qnr])
                                                cs, csr = csq.get()
                                                S.dma("sp", cs[0:nr, :], c_cs[r0:r0 + nr, :], w=[csr])
                                                S.mm([lambda pe, qn_=qn_, c=c, nr=nr: pe.transpose(
                                                    out=psT[:, c * 128:c * 128 + nr], in_=qn_[0:nr, c * 128:(c + 1) * 128],
                                                    identity=identb[0:nr, 0:nr]) for c in range(3)], r=[qnr, cR], w=[psTR])
                                                qnT, qnTr = small.get()
                                                S.op("dve", lambda v, qnT=qnT: v.tensor_copy(out=qnT[:, 0:384], in_=psT[:, 0:384]),
                                                     r=[psTR], w=[qnTr])
                                                S.mm([lambda pe, qnT=qnT, c=c, nr=nr: pe.matmul(
                                                    PS[5][0:nr, :], lhsT=qnT[:, c * 128:c * 128 + nr], rhs=wqupb[:, c, 0:512],
                                                    start=(c == 0), stop=(c == 2)) for c in range(3)], r=[qnTr, lR], w=[PSR[5]])
                                                S.mm([lambda pe, qnT=qnT, c=c, nr=nr: pe.matmul(
                                                    PS[6][0:nr, 0:256], lhsT=qnT[:, c * 128:c * 128 + nr], rhs=wqupb[:, c, 512:768],
                                                    start=(c == 0), stop=(c == 2)) for c in range(3)], r=[qnTr, lR], w=[PSR[6]])
                                                qf, qfr = qmf.get()
                                                S.op("act", lambda a, qf=qf, nr=nr: a.copy(out=qf[0:nr, 0:512], in_=PS[5][0:nr, :]),
                                                     r=[PSR[5]], w=[qfr])
                                                S.op("act", lambda a, qf=qf, nr=nr: a.copy(out=qf[0:nr, 512:768], in_=PS[6][0:nr, 0:256]),
                                                     r=[PSR[6]], w=[qfr])
                                                qb_, qbr = small.get()
                                                qf3 = qf[0:nr, :].rearrange("p (h e) -> p h e", e=96)
                                                qb3 = qb_[0:nr, 0:768].rearrange("p (h e) -> p h e", e=96)
                                                tmp, tmpr = wf.get()
                                                ta = tmp[0:nr, 0:128].rearrange("p (h e) -> p h e", e=16)
                                                tb_ = tmp[0:nr, 128:256].rearrange("p (h e) -> p h e", e=16)
                                                cosb = bc_ap(cs[0:nr, 0:16], [[0, 8], [1, 16]])
                                                sinb = bc_ap(cs[0:nr, 16:32], [[0, 8], [1, 16]])
                                                S.op("dve", lambda v: v.tensor_copy(out=qb3[:, :, 0:64], in_=qf3[:, :, 0:64]), r=[qfr], w=[qbr])
                                                S.op("dve", lambda v: v.tensor_tensor(out=ta, in0=qf3[:, :, 64:80], in1=cosb, op=ALU.mult), r=[qfr, csr], w=[tmpr])
                                                S.op("dve", lambda v: v.tensor_tensor(out=tb_, in0=qf3[:, :, 80:96], in1=sinb, op=ALU.mult), r=[qfr, csr], w=[tmpr])
                                                S.op("dve", lambda v: v.tensor_tensor(out=qb3[:, :, 64:80], in0=ta, in1=tb_, op=ALU.subtract), r=[tmpr], w=[qbr, tmpr])
                                                S.op("dve", lambda v: v.tensor_tensor(out=ta, in0=qf3[:, :, 80:96], in1=cosb, op=ALU.mult), r=[qfr, csr], w=[tmpr])
                                                S.op("dve", lambda v: v.tensor_tensor(out=tb_, in0=qf3[:, :, 64:80], in1=sinb, op=ALU.mult), r=[qfr, csr], w=[tmpr])
                                                S.op("dve", lambda v: v.tensor_tensor(out=qb3[:, :, 80:96], in0=ta, in1=tb_, op=ALU.add), r=[tmpr], w=[qbr, tmpr])
                                                S.mm([lambda pe, qb_=qb_, h=h, nr=nr: pe.transpose(
                                                    out=psT[0:96, h * 128:h * 128 + nr], in_=qb_[0:nr, h * 96:(h + 1) * 96],
                                                    identity=identb[0:nr, 0:nr]) for h in range(8)], r=[qbr, cR], w=[psTR])
                                                S.op("dve", lambda v, qi=qi, nr=nr: v.tensor_copy(
                                                    out=qT[0:96, :, qi * 128:qi * 128 + nr],
                                                    in_=psT[0:96, :].rearrange("p (a b) -> p a b", b=128)[:, :, 0:nr]),
                                                    r=[psTR], w=[qTR])
                                            _chk(3.61)
                                            dst_ = {}

                                            def stA(ti):
                                                nk, cap, rap, D0, rs, prv = tiles[ti]
                                                b0 = 5 if ti % 2 == 0 else 3
                                                ck, ckr_ = small.get()
                                                kf, kfr = kst.get()
                                                S.dma("sp", kf[0:nk, 0:128], cap, r=rs, w=[kfr])
                                                S.dma("sp", kf[0:nk, 128:160], rap, r=rs, w=[kfr])
                                                S.op("dve", lambda v: v.tensor_copy(out=ck[0:nk, 0:160], in_=kf[0:nk, 0:160]), r=[kfr], w=[ckr_])
                                                S.mm([lambda pe: pe.transpose(
                                                    out=psT[:, 0:nk], in_=ck[0:nk, 0:128], identity=identb[0:nk, 0:nk])],
                                                    r=[ckr_, cR], w=[psTR])
                                                ckT, ckTr = small.get()
                                                S.op("dve", lambda v: v.tensor_copy(out=ckT[:, 0:nk], in_=psT[:, 0:nk]),
                                                     r=[psTR], w=[ckTr])
                                                S.mm([lambda pe: pe.matmul(
                                                    PS[b0][0:nk, :], lhsT=ckT[:, 0:nk], rhs=wkvupb[:, 0:512], start=True, stop=True)],
                                                    r=[ckTr, lR], w=[PSR[b0]])
                                                S.mm([lambda pe: pe.matmul(
                                                    PS[b0 + 1][0:nk, :], lhsT=ckT[:, 0:nk], rhs=wkvupb[:, 512:1024], start=True, stop=True)],
                                                    r=[ckTr, lR], w=[PSR[b0 + 1]])
                                                dst_[ti] = (ck, ckr_)

                                            def stB(ti):
                                                nk, cap, rap, D0, rs, prv = tiles[ti]
                                                b0 = 5 if ti % 2 == 0 else 3
                                                ck, ckr_ = dst_.pop(ti)
                                                kd, kdr = small.get()
                                                kd3 = kd[0:nk, 0:768].rearrange("p (h e) -> p h e", e=96)
                                                for hf in range(2):
                                                    pv = PS[b0 + hf][0:nk, :].rearrange("p (h e) -> p h e", e=128)
                                                    S.op("act", lambda a: a.copy(out=kd3[:, 4 * hf:4 * hf + 4, 0:64], in_=pv[:, :, 0:64]),
                                                         r=[PSR[b0 + hf]], w=[kdr])
                                                    S.op("dve", lambda v: v.tensor_copy(
                                                        out=Vsb[0:nk, ti, 256 * hf:256 * hf + 256].rearrange("p (h e) -> p h e", e=64),
                                                        in_=pv[:, :, 64:128]), r=[PSR[b0 + hf]], w=[VR])
                                                S.op("dve", lambda v: v.tensor_copy(
                                                    out=kd3[:, :, 64:96], in_=bc_ap(ck[0:nk, 128:160], [[0, 8], [1, 32]])), r=[ckr_], w=[kdr])
                                                S.mm([lambda pe, h=h: pe.transpose(
                                                    out=psT[0:96, h * 128:h * 128 + nk], in_=kd[0:nk, h * 96:(h + 1) * 96],
                                                    identity=identb[0:nk, 0:nk]) for h in range(8)], r=[kdr, cR], w=[psTR])
                                                S.op("dve", lambda v: v.tensor_copy(
                                                    out=KT[0:96, :, ti * 128:ti * 128 + nk],
                                                    in_=psT[0:96, :].rearrange("p (a b) -> p a b", b=128)[:, :, 0:nk]),
                                                    r=[psTR], w=[KTR])
                                                warm(WARM)

                                            stA(0)
                                            for ti in range(len(tiles)):
                                                if ti + 1 < len(tiles):
                                                    stA(ti + 1)
                                                stB(ti)

                                        _chk(3.0 + 0.2 * bi + 0.1)
                                        def mask_ap(kind, D0, nk):
                                            d = D0 // 128
                                            if kind == "A":
                                                return masks[0:nk, d, 0:nq]
                                            if kind == "CD":
                                                return masks[0:nk, 4 + d, 0:nq]
                                            return masks[0:nk, 8 + (D0 + 512) // 128, 0:nq]

                                        isP = G["kind"] == "p"
                                        LOOK = 3

                                        def run_pipe(items, s1, s2):
                                            n = len(items)
                                            for i in range(min(LOOK, n)):
                                                s1(i)
                                            for i in range(n):
                                                s2(i)
                                                if i + LOOK < n:
                                                    s1(i + LOOK)

                                        if br == "A":
                                            ntl = len(tiles)
                                            items = [(2 * hp_ + hf_, idx) for hp_ in range(4) for idx in range(ntl) for hf_ in range(2)]
                                            order = list(enumerate(tiles))[::-1]
                                            st = {}
                                            psbanks = [0, 1, 5]

                                            def s1q(i):
                                                h, idx = items[i]
                                                hp, r0 = h // 2, 64 * (h % 2)
                                                ti, (nk, kap, vap, D0, rs, prv) = order[idx]
                                                pS, pSR = PS[psbanks[i % 3]], PSR[psbanks[i % 3]]
                                                S.mm([lambda pe: pe.matmul(
                                                    pS[0:nk, 0:nq], lhsT=KT[r0:r0 + 64, hp, ti * 128:ti * 128 + nk],
                                                    rhs=qT[r0:r0 + 64, hp, 0:nq], start=True, stop=True)], r=[KTR, qTR], w=[pSR])

                                            def s1(i):
                                                h, idx = items[i]
                                                hp, r0 = h // 2, 64 * (h % 2)
                                                ti, (nk, kap, vap, D0, rs, prv) = order[idx]
                                                pS, pSR = PS[psbanks[i % 3]], PSR[psbanks[i % 3]]
                                                diag = (D0 >= 0)
                                                ez, ezr = wf.get()
                                                bkw = dict(bias=vbt[0:nk, 0:1]) if prv else {}
                                                S.op("act", lambda a: a.activation(
                                                    out=ez[0:nk, 0:nq], in_=pS[0:nk, 0:nq], func=AF.Exp, scale=0.125, **bkw), r=[pSR, cR], w=[ezr])
                                                sp, spr = wb_.get()
                                                S.op("act", lambda a: a.activation(
                                                    out=sp[0:nk, 0:nq], in_=ez[0:nk, 0:nq], func=AF.Ln, bias=1.0), r=[ezr], w=[spr])
                                                t1, t1r = t1p.get()
                                                S.op("dve", lambda v: v.scalar_tensor_tensor(
                                                    out=t1[0:nk, 0:nq], in0=pS[0:nk, 0:nq], scalar=0.125, in1=sp[0:nk, 0:nq],
                                                    op0=ALU.mult, op1=ALU.subtract), r=[pSR, spr], w=[t1r])
                                                if diag:
                                                    spm, spmr = wb_.get()
                                                    S.op("dve", lambda v: v.tensor_tensor(
                                                        out=spm[0:nk, 0:nq], in0=sp[0:nk, 0:nq], in1=mask_ap("A", D0, nk), op=ALU.mult),
                                                        r=[spr, cR], w=[spmr])
                                                else:
                                                    spm, spmr = sp, spr
                                                st[i] = (spm, spmr, t1, t1r)

                                            def s2a(i):
                                                h, idx = items[i]
                                                ti, (nk, kap, vap, D0, rs, prv) = order[idx]
                                                lb = 2 if h % 2 == 0 else 6
                                                psL, psLR = PS[lb], PSR[lb]
                                                spm, spmr, t1, t1r = st[i]
                                                fns = []
                                                rr = [spmr, cR]
                                                if idx > 0:
                                                    psp, pspr, pnk = st[("prev", h % 2)]
                                                    fns.append(lambda pe: pe.matmul(
                                                        psL[:, 0:nq], lhsT=TRIC[0:pnk, :], rhs=psp[0:pnk, 0:nq],
                                                        start=False, stop=False, skip_group_check=True))
                                                    rr.append(pspr)
                                                fns.append(lambda pe: pe.matmul(
                                                    psL[:, 0:nq], lhsT=TRI[0:nk, :], rhs=spm[0:nk, 0:nq],
                                                    start=(idx == 0), stop=True, skip_group_check=True))
                                                S.mm(fns, r=rr, w=[psLR])
                                                st[("prev", h % 2)] = (spm, spmr, nk)

                                            def s2t(i):
                                                h, idx = items[i]
                                                ti, (nk, kap, vap, D0, rs, prv) = order[idx]
                                                lb = 2 if h % 2 == 0 else 6
                                                psL, psLR = PS[lb], PSR[lb]
                                                spm, spmr, t1, t1r = st[i]
                                                S.op("dve", lambda v: v.tensor_tensor(
                                                    out=t1[0:nk, 0:nq], in0=t1[0:nk, 0:nq], in1=psL[0:nk, 0:nq], op=ALU.subtract),
                                                    r=[psLR], w=[t1r])

                                            def s2b(i):
                                                h, idx = items[i]
                                                hp, r0 = h // 2, 64 * (h % 2)
                                                ti, (nk, kap, vap, D0, rs, prv) = order[idx]
                                                diag = (D0 >= 0)
                                                lb = 2 if h % 2 == 0 else 6
                                                psL, psLR = PS[lb], PSR[lb]
                                                psO, psOR = PS[3 + (h % 2)], PSR[3 + (h % 2)]
                                                spm, spmr, t1, t1r = st.pop(i)
                                                wt_, wtr = wb_.get()
                                                bkw = dict(bias=vbt[0:nk, 0:1]) if prv else {}
                                                S.op("act", lambda a: a.activation(
                                                    out=wt_[0:nk, 0:nq], in_=t1[0:nk, 0:nq], func=AF.Exp, **bkw), r=[t1r, cR], w=[wtr])
                                                if diag:
                                                    S.op("dve", lambda v: v.tensor_tensor(
                                                        out=wt_[0:nk, 0:nq], in0=wt_[0:nk, 0:nq], in1=mask_ap("A", D0, nk), op=ALU.mult),
                                                        r=[cR], w=[wtr])
                                                S.mm([lambda pe: pe.matmul(
                                                    psO[:, 0:nq], lhsT=Vsb[0:nk, ti, hp * 128:(hp + 1) * 128], rhs=wt_[0:nk, 0:nq],
                                                    start=(idx == 0), stop=(idx == ntl - 1), skip_group_check=True)],
                                                    r=[wtr, VR], w=[psOR])
                                                if idx == ntl - 1:
                                                    S.op("dve", lambda v: v.tensor_tensor(
                                                        out=gatedT[r0:r0 + 64, hp, col0:col0 + nq], in0=psO[r0:r0 + 64, 0:nq],
                                                        in1=gT[r0:r0 + 64, hp, 0:nq], op=ALU.mult), r=[psOR, gTR], w=[gatedR])

                                            nit = len(items)
                                            for i in range(min(3, nit)):
                                                s1q(i)
                                            s1(0)
                                            if nit > 3:
                                                s1q(3)
                                            if nit > 1:
                                                s1(1)
                                            s2a(0)
                                            for i in range(nit):
                                                if i + 4 < nit:
                                                    s1q(i + 4)
                                                s2t(i)
                                                if i + 1 < nit:
                                                    s2a(i + 1)
                                                if i + 2 < nit:
                                                    s1(i + 2)
                                                s2b(i)
                                        elif br in ("B", "D"):
                                            ntl = len(tiles)
                                            items = [(2 * hp_ + hf_, idx) for hp_ in range(4) for idx in range(ntl) for hf_ in range(2)]
                                            st = {}

                                            def s1(i):
                                                h, idx = items[i]
                                                hp, r0 = h // 2, 64 * (h % 2)
                                                nk, kap, vap, D0, rs, prv = tiles[idx]
                                                ti = idx
                                                pS, pSR = PS[i % 3], PSR[i % 3]
                                                if br == "B":
                                                    S.mm([lambda pe: pe.matmul(
                                                        pS[0:nk, 0:nq], lhsT=KT[r0:r0 + 64, hp, ti * 128:ti * 128 + nk],
                                                        rhs=qT[r0:r0 + 64, hp, 0:nq], start=True, stop=True)], r=[KTR, qTR], w=[pSR])
                                                    scale = 0.125
                                                else:
                                                    S.mm([lambda pe: pe.matmul(
                                                        pS[0:nk, 0:nq], lhsT=KT[0:96, h, ti * 128:ti * 128 + nk],
                                                        rhs=qT[0:96, h, 0:nq], start=True, stop=True)], r=[KTR, qTR], w=[pSR])
                                                    scale = MLA_SCALE
                                                e, er = wb_.get()
                                                bkw = dict(bias=vbt[0:nk, 0:1]) if prv else {}
                                                S.op("act", lambda a: a.activation(
                                                    out=e[0:nk, 0:nq], in_=pS[0:nk, 0:nq], func=AF.Exp, scale=scale, **bkw), r=[pSR, cR], w=[er])
                                                if br == "B":
                                                    c0 = 384 - D0
                                                    S.op("dve", lambda v: v.tensor_tensor(
                                                        out=e[0:nk, 0:nq], in0=e[0:nk, 0:nq], in1=EGB[0:nk, h, c0:c0 + nq], op=ALU.mult),
                                                        r=[EGBR], w=[er])
                                                    if isP:
                                                        S.op("dve", lambda g: g.tensor_tensor(
                                                            out=e[0:nk, 0:nq], in0=e[0:nk, 0:nq], in1=mask_ap("B", D0, nk), op=ALU.mult),
                                                            r=[cR], w=[er])
                                                else:
                                                    if isP and D0 >= 0:
                                                        S.op("dve", lambda v: v.tensor_tensor(
                                                            out=e[0:nk, 0:nq], in0=e[0:nk, 0:nq], in1=mask_ap("CD", D0, nk), op=ALU.mult),
                                                            r=[cR], w=[er])
                                                st[i] = (e, er)

                                            def s2(i):
                                                h, idx = items[i]
                                                hp, r0 = h // 2, 64 * (h % 2)
                                                nk, kap, vap, D0, rs, prv = tiles[idx]
                                                ti = idx
                                                psO, psOR = PS[3 + (h % 2)], PSR[3 + (h % 2)]
                                                psD, psDR = PS[5 + (h % 2)], PSR[5 + (h % 2)]
                                                e, er = st.pop(i)
                                                S.mm([lambda pe: pe.matmul(
                                                    psO[:, 0:nq], lhsT=Vsb[0:nk, ti, hp * 128:(hp + 1) * 128], rhs=e[0:nk, 0:nq],
                                                    start=(idx == 0), stop=(idx == ntl - 1), skip_group_check=True)],
                                                    r=[er, VR], w=[psOR])
                                                S.mm([lambda pe: pe.matmul(
                                                    psD[:, 0:nq], lhsT=ONES[0:nk, :], rhs=e[0:nk, 0:nq],
                                                    start=(idx == 0), stop=(idx == ntl - 1), skip_group_check=True)],
                                                    r=[er, cR], w=[psDR])
                                                if idx == ntl - 1:
                                                    rc, rcr = wf.get()
                                                    S.op("act", lambda a: a.activation(out=rc[r0:r0 + 64, 0:nq], in_=psD[r0:r0 + 64, 0:nq], func=AF.Ln),
                                                         r=[psDR], w=[rcr])
                                                    S.op("act", lambda a: a.activation(out=rc[r0:r0 + 64, 0:nq], in_=rc[r0:r0 + 64, 0:nq], func=AF.Exp, scale=-1.0),
                                                         r=[], w=[rcr])
                                                    S.op("dve", lambda v: v.tensor_tensor(
                                                        out=rc[r0:r0 + 64, 0:nq], in0=psO[r0:r0 + 64, 0:nq], in1=rc[r0:r0 + 64, 0:nq], op=ALU.mult),
                                                        r=[psOR], w=[rcr])
                                                    S.op("pool", lambda g: g.tensor_tensor(
                                                        out=gatedT[r0:r0 + 64, 4 * bi + hp, col0:col0 + nq], in0=rc[r0:r0 + 64, 0:nq],
                                                        in1=gT[r0:r0 + 64, hp, 0:nq], op=ALU.mult), r=[rcr, gTR], w=[gatedR])

                                            run_pipe(items, s1, s2)
                                        else:
                                            ntl = len(tiles)
                                            items = [(h, m, idx) for h in range(4) for idx in range(ntl) for m in range(2)]
                                            st = {}

                                            def s1(i):
                                                h, m, idx = items[i]
                                                r0 = 64 * m
                                                nk, kap, vap, D0, rs, prv = tiles[idx]
                                                ti = idx
                                                pS, pSR = PS[i % 3], PSR[i % 3]
                                                S.mm([lambda pe: pe.matmul(
                                                    pS[0:nk, 0:nq], lhsT=KT[r0:r0 + 64, h, ti * 128:ti * 128 + nk],
                                                    rhs=qT[r0:r0 + 64, h, 0:nq], start=True, stop=True)], r=[KTR, qTR], w=[pSR])
                                                e, er = wb_.get()
                                                if D0 <= -512:
                                                    S.op("act", lambda a: a.activation(
                                                        out=e[0:nk, 0:nq], in_=pS[0:nk, 0:nq], func=AF.Exp, scale=0.125,
                                                        bias=(satCv if prv else satC)[0:nk, h:h + 1]), r=[pSR, cR], w=[er])
                                                else:
                                                    bkw = dict(bias=vbt[0:nk, 0:1]) if prv else {}
                                                    S.op("act", lambda a: a.activation(
                                                        out=e[0:nk, 0:nq], in_=pS[0:nk, 0:nq], func=AF.Exp, scale=0.125, **bkw), r=[pSR, cR], w=[er])
                                                    c0 = 384 - D0
                                                    S.op("dve", lambda v: v.tensor_tensor(
                                                        out=e[0:nk, 0:nq], in0=e[0:nk, 0:nq], in1=EGC[0:nk, h, c0:c0 + nq], op=ALU.mult),
                                                        r=[EGCR], w=[er])
                                                    if isP and D0 >= 0:
                                                        S.op("dve", lambda g: g.tensor_tensor(
                                                            out=e[0:nk, 0:nq], in0=e[0:nk, 0:nq], in1=mask_ap("CD", D0, nk), op=ALU.mult),
                                                            r=[cR], w=[er])
                                                st[i] = (e, er)

                                            def s2(i):
                                                h, m, idx = items[i]
                                                nk, kap, vap, D0, rs, prv = tiles[idx]
                                                ti = idx
                                                psO, psOR = PS[3 + m], PSR[3 + m]
                                                psD, psDR = PS[5 + m], PSR[5 + m]
                                                e, er = st.pop(i)
                                                S.mm([lambda pe: pe.matmul(
                                                    psO[:, 0:nq], lhsT=Vsb[0:nk, ti, h * 128:(h + 1) * 128], rhs=e[0:nk, 0:nq],
                                                    start=(idx == 0), stop=(idx == ntl - 1), skip_group_check=True)],
                                                    r=[er, VR], w=[psOR])
                                                S.mm([lambda pe: pe.matmul(
                                                    psD[:, 0:nq], lhsT=ONES[0:nk, :], rhs=e[0:nk, 0:nq],
                                                    start=(idx == 0), stop=(idx == ntl - 1), skip_group_check=True)],
                                                    r=[er, cR], w=[psDR])
                                                if idx == ntl - 1:
                                                    a_, ar = wf.get()
                                                    S.op("act", lambda a: a.activation(out=a_[:, 0:nq], in_=psD[:, 0:nq], func=AF.Ln), r=[psDR], w=[ar])
                                                    S.op("act", lambda a: a.activation(out=a_[:, 0:nq], in_=a_[:, 0:nq], func=AF.Exp, scale=-1.0), r=[], w=[ar])
                                                    S.op("dve", lambda v: v.tensor_tensor(
                                                        out=a_[:, 0:nq], in0=psO[:, 0:nq], in1=a_[:, 0:nq], op=ALU.mult), r=[psOR], w=[ar])
                                                    st[("acc", m)] = (a_, ar)
                                                    if m == 1:
                                                        a0, a0r = st.pop(("acc", 0))
                                                        a1, a1r = st.pop(("acc", 1))
                                                        S.op("dve", lambda v: v.scalar_tensor_tensor(
                                                            out=a0[:, 0:nq], in0=a1[:, 0:nq], scalar=lcol[:, 0:1], in1=a0[:, 0:nq],
                                                            op0=ALU.mult, op1=ALU.add), r=[a1r, lR], w=[a0r])
                                                        sq, sqr = wb_.get()
                                                        S.op("act", lambda a: a.activation(out=sq[:, 0:nq], in_=a0[:, 0:nq], func=AF.Square),
                                                             r=[a0r], w=[sqr])
                                                        S.mm([lambda pe: pe.matmul(PS[2][:, 0:nq], lhsT=ONES, rhs=sq[:, 0:nq],
                                                                                   start=True, stop=True)], r=[sqr, cR], w=[PSR[2]])
                                                        S.op("dve", lambda v: v.tensor_scalar(
                                                            out=a1[:, 0:nq], in0=PS[2][:, 0:nq], scalar1=1.0 / 128, scalar2=EPS,
                                                            op0=ALU.mult, op1=ALU.add), r=[PSR[2]], w=[a1r])
                                                        S.op("act", lambda a: a.activation(out=a1[:, 0:nq], in_=a1[:, 0:nq], func=AF.Ln),
                                                             r=[], w=[a1r])
                                                        S.op("act", lambda a: a.activation(out=a1[:, 0:nq], in_=a1[:, 0:nq], func=AF.Exp, scale=-0.5),
                                                             r=[], w=[a1r])
                                                        S.op("pool", lambda g: g.tensor_tensor(
                                                            out=a0[:, 0:nq], in0=a0[:, 0:nq], in1=a1[:, 0:nq], op=ALU.mult), r=[a1r], w=[a0r])
                                                        S.op("dve", lambda g: g.scalar_tensor_tensor(
                                                            out=gatedT[:, 8 + h, col0:col0 + nq], in0=a0[:, 0:nq], scalar=lcol[:, 1:2],
                                                            in1=gT[:, h, 0:nq], op0=ALU.mult, op1=ALU.mult), r=[a0r, lR, gTR], w=[gatedR])

                                            run_pipe(items, s1, s2)
                                S.barrier()
                                _chk(4)

                            with contextlib.ExitStack() as s4:
                                mergedT = sb("mergedT", [128, DC, 640], BF16, s4)
                                mrgR = R("merged")
                                with contextlib.ExitStack() as s4a:
                                    wbp = Pool(s4a, nc, "wbp", 2, [128, 16, 512], BF16)
                                    mgp = Pool(s4a, nc, "mgp", 3, [128, 4, 512], BF16)
                                    pfb = Pool(s4a, nc, "pfb", 10, [128, 512], BF16)
                                    tok0 = 512 * blk
                                    pend = [None]
                                    acnt = [0]

                                    def flush_acc():
                                        if pend[0] is None:
                                            return
                                        dcp, prods, cc0, ccn, abi = pend[0]
                                        pend[0] = None
                                        ab = 4 + abi % 2
                                        S.mm([lambda pe, p_=p_, n=n: pe.matmul(
                                            PS[ab][:, 0:ccn], lhsT=identb[:], rhs=p_[:, 0:ccn], start=(n == 0), stop=(n == 3))
                                            for n, (p_, p_r) in enumerate(prods)], r=[p_r for (_, p_r) in prods] + [cR], w=[PSR[ab]])
                                        S.op("act", lambda a: a.copy(out=mergedT[:, dcp, cc0:cc0 + ccn], in_=PS[ab][:, 0:ccn]),
                                             r=[PSR[ab]], w=[mrgR])

                                    for g4 in range(4):
                                        wt, wr = wbp.get()
                                        S.dma("pool", wt[:],
                                              wbr[l, :, :, g4 * 512:(g4 + 1) * 512].rearrange("n (c p) d -> p (n c) d", p=128), w=[wr])
                                        for j4 in range(4):
                                            dc = 4 * g4 + j4
                                            for (cc0, ccn) in CPS:
                                                mg, mgr = mgp.get()
                                                S.dma("sp", mg[:, :, 0:ccn],
                                                      mT_scr[:, tok0 + cc0:tok0 + cc0 + ccn].rearrange("(n c p) t -> p n c t", p=128, c=16)[:, :, dc, :],
                                                      r=[rfm("m", t_) for t_ in MTB], w=[mgr])
                                                prods = []
                                                for n in range(4):
                                                    S.mm([lambda pe, n=n, c=c: pe.matmul(
                                                        PS[n][:, 0:ccn], lhsT=wt[:, 4 * n + c, j4 * 128:(j4 + 1) * 128],
                                                        rhs=gatedT[:, 4 * n + c, cc0:cc0 + ccn],
                                                        start=(c == 0), stop=(c == 3)) for c in range(4)], r=[wr, gatedR], w=[PSR[n]])
                                                    p_, p_r = pfb.get()
                                                    S.op("dve", lambda v, p_=p_, n=n: v.tensor_tensor(
                                                        out=p_[:, 0:ccn], in0=PS[n][:, 0:ccn], in1=mg[:, n, 0:ccn], op=ALU.mult),
                                                        r=[PSR[n], mgr], w=[p_r])
                                                    prods.append((p_, p_r))
                                                flush_acc()
                                                acnt[0] += 1
                                                pend[0] = (dc, prods, cc0, ccn, acnt[0])
                                    flush_acc()
                                    S.barrier()
                                    _chk(5)
                                with contextlib.ExitStack() as s4b:
                                    wop = Pool(s4b, nc, "wop", 2, [128, DC, 512], BF16)
                                    ntk = NQB // 128
                                    ysb = [sb(f"ysb{i}", [128, D], F32, s4b) for i in range(ntk)]
                                    ysR = [R(f"ysb{i}") for i in range(ntk)]
                                    gpost = sb("gpost", [128, D], F32, s4b)
                                    gpR = R("gpost")
                                    S.dma("sp", gpost[:], bass.AP(tensor=norm_post.tensor, offset=l * D, ap=[[0, 128], [1, D]]), w=[gpR])
                                    xin = Pool(s4b, nc, "xin", 2, [128, D], F32)
                                    jk4 = sb("jk4", [128, D], BF16, s4b)
                                    jk4R = R("jk4")
                                    st4 = Pool(s4b, nc, "st4", 2, [128, 4], F32)
                                    for eb in range(4):
                                        wt, wr = wop.get()
                                        S.dma("pool", wt[:], wout[l, :, eb * 512:(eb + 1) * 512].rearrange("(c p) n -> p c n", p=128), w=[wr])
                                        for tk in range(ntk):
                                            pi = 4 + (eb * ntk + tk) % 3
                                            S.mm([lambda pe, wt=wt, c=c, tk=tk, pi=pi: pe.matmul(
                                                PS[pi][:, :], lhsT=mergedT[:, c, tk * 128:(tk + 1) * 128], rhs=wt[:, c, :],
                                                start=(c == 0), stop=(c == DC - 1)) for c in range(DC)], r=[wr, mrgR], w=[PSR[pi]])
                                            S.op("act", lambda a, tk=tk, eb=eb, pi=pi: a.copy(out=ysb[tk][:, eb * 512:(eb + 1) * 512], in_=PS[pi][:, :]),
                                                 r=[PSR[pi]], w=[ysR[tk]])
                                    for tk in range(ntk):
                                        tt = (512 * blk) // 128 + tk
                                        xt, xr = xin.get()
                                        if l == 0:
                                            src = xp[tt * 128:(tt + 1) * 128, :] if tt < NPT else xs[:, :]
                                            S.dma("sp", xt[:], src, w=[xr])
                                        else:
                                            S.dma("sp", xt[:], x1[tt * 128:(tt + 1) * 128, :], r=[RX[tt]], w=[xr])
                                        ss, ssr = st4.get()
                                        S.op("dve", lambda v, ss=ss: v.memset(ss[:], 0.0), w=[ssr])
                                        S.op("act", lambda a, tk=tk, ss=ss: a.activation(out=jk4[:], in_=ysb[tk][:], func=AF.Square,
                                                                                        accum_out=ss[:, 0:1]), r=[ysR[tk]], w=[jk4R, ssr])
                                        S.op("dve", lambda v, ss=ss: v.tensor_scalar(out=ss[:, 1:2], in0=ss[:, 0:1], scalar1=1.0 / D,
                                                                                      scalar2=EPS, op0=ALU.mult, op1=ALU.add), r=[ssr], w=[ssr])
                                        S.op("act", lambda a, ss=ss: a.activation(out=ss[:, 2:3], in_=ss[:, 1:2], func=AF.Sqrt), r=[ssr], w=[ssr])
                                        S.op("dve", lambda v, ss=ss: v.reciprocal(out=ss[:, 3:4], in_=ss[:, 2:3]), r=[ssr], w=[ssr])
                                        S.op("dve", lambda v, tk=tk, ss=ss: v.scalar_tensor_tensor(
                                            out=ysb[tk][:], in0=ysb[tk][:], scalar=ss[:, 3:4], in1=gpost[:], op0=ALU.mult, op1=ALU.mult),
                                            r=[ssr, gpR], w=[ysR[tk]])
                                        S.op("pool", lambda g, tk=tk, xt=xt: g.tensor_tensor(out=ysb[tk][:], in0=ysb[tk][:], in1=xt[:], op=ALU.add),
                                             r=[xr], w=[ysR[tk]])
                                        if last:
                                            dst = yp[tt * 128:(tt + 1) * 128, :] if tt < NPT else ys[:, :]
                                            S.dma("sp", dst, ysb[tk][:], r=[ysR[tk]], w=[R("yo")])
                                        else:
                                            S.dma("sp", x1[tt * 128:(tt + 1) * 128, :], ysb[tk][:], r=[ysR[tk]], w=[RX[tt]])
                                    S.barrier()
                                    _chk(6)
                    S.barrier()
                    _chk(7)
        except _StopBuild:
            pass
        S.finish()
    return nc


def _t5_bucket(rel):
    rel = np.asarray(rel, dtype=np.int64)
    nb = 16
    max_exact = 8
    ret = np.where(rel > 0, nb, 0)
    n = np.abs(rel)
    nf = np.maximum(n, 1).astype(np.float32)
    large = max_exact + (np.log(nf / np.float32(max_exact)) / np.float32(math.log(512 / max_exact))
                         * np.float32(nb - max_exact)).astype(np.int32)
    large = np.minimum(large, nb - 1)
    return ret + np.where(n < max_exact, n, large)


def _consts():
    ident = np.eye(128, dtype=np.float32)
    jm = np.ascontiguousarray(ident[::-1])
    j = np.arange(128)[:, None]
    s = np.arange(128)[None, :]
    tri = np.stack([(j > s), (j <= s), np.ones((128, 128), bool)]).astype(np.float32)
    i = np.arange(128)[:, None]
    q = np.arange(512)[None, :]
    masks = np.zeros((16, 128, 512), np.float32)
    for d in range(4):
        D0 = 128 * d
        masks[d] = (D0 + i < q)
        masks[4 + d] = ((D0 + i) // 64 <= q // 64)
    for m in range(8):
        D0 = -512 + 128 * m
        dd = q // 64 - (D0 + i) // 64
        masks[8 + m] = (dd >= 0) & (dd <= 8)
    t = np.arange(LC)
    b = _t5_bucket(511 - t)
    oh = (b[None, :] == np.arange(32)[:, None]).astype(np.float32)
    inv = (10000.0 ** (-np.arange(0, 32, 2, dtype=np.float32) / np.float32(32))).astype(np.float32)
    css = []
    for p in range(2):
        pos = np.concatenate([TP * p + np.arange(TP), PAST + np.arange(64), np.zeros(64)]).astype(np.float32)
        ang = pos[:, None] * inv[None, :]
        css.append(np.concatenate([np.cos(ang), np.sin(ang)], axis=1).astype(np.float32))
    return dict(c_ident=ident, c_j=jm, c_tri=tri, c_masks=masks, c_oh=oh), css


_NC = None


def kernel(x_prompt, x_sample, cache_sb_k, cache_sb_v, cache_band_k, cache_band_v,
           cache_diff_k, cache_diff_v, cache_mla_ckv, cache_mla_krope,
           norm_pre, norm_post, w_in, band_bias, t5_table, diff_lambda, diff_subln,
           mla_q_norm, mla_w_q_up, mla_kv_norm, mla_w_kv_up, w_branch, w_out):
    global _NC
    f = lambda a: np.ascontiguousarray(np.asarray(a, dtype=np.float32))
    if _NC is None:
        _NC = build_nc()
    nc = _NC
    consts, css = _consts()
    shared = dict(norm_pre=f(norm_pre), norm_post=f(norm_post), w_in=f(w_in), band_bias=f(band_bias), t5=f(t5_table),
                  dlam=f(diff_lambda).reshape(2, 256), subln=f(diff_subln), qnorm=f(mla_q_norm), wqup=f(mla_w_q_up),
                  kvnorm=f(mla_kv_norm), wkvup=f(mla_w_kv_up), wbr=f(w_branch), wout=f(w_out), **consts)
    in_maps = []
    for c in range(8):
        b, p = c // 2, c % 2
        m = dict(shared)
        m["c_cs"] = css[p]
        m["c_vb"] = np.full((128, 1), 0.0 if p == 1 else -30000.0, np.float32)
        m["xp"] = f(x_prompt[b, TP * p:TP * (p + 1)])
        xs_ = np.zeros((128, D), np.float32)
        xs_[:64] = np.asarray(x_sample[c], dtype=np.float32)
        m["xs"] = xs_
        cs_ = slice(c, c + 1)
        m["csbk"] = f(cache_sb_k[:, cs_]).reshape(2, 1, PAST, 512)
        m["csbv"] = f(cache_sb_v[:, cs_]).reshape(2, 1, PAST, 512)
        m["cbk"] = f(cache_band_k[:, cs_]).reshape(2, 1, 512, 512)
        m["cbv"] = f(cache_band_v[:, cs_]).reshape(2, 1, 512, 512)
        m["cdk"] = f(cache_diff_k[:, cs_]).reshape(2, 1, PAST, 512)
        m["cdv"] = f(cache_diff_v[:, cs_]).reshape(2, 1, PAST, 512)
        m["cckv"] = f(cache_mla_ckv[:, cs_])
        m["ckr"] = f(cache_mla_krope[:, cs_])
        in_maps.append(m)
    ncores = _NCORES[0]
    res = run_bass_kernel_spmd(nc, in_maps[:ncores], core_ids=list(range(ncores)))
    rs = [res.results[i if i < ncores else i % 2] for i in range(8)]

    def cat_p(name, shape_tail):
        a = np.stack([np.concatenate([rs[2 * b][name], rs[2 * b + 1][name]], axis=1) for b in range(4)], axis=1)
        return a.reshape((2, 4) + shape_tail).astype(np.float32)

    def cat_b(name, shape_tail):
        a = np.stack([rs[2 * b + 1][name] for b in range(4)], axis=1)
        return a.reshape((2, 4) + shape_tail).astype(np.float32)

    def cat_s(name, shape_tail):
        return np.concatenate([r[name] for r in rs], axis=1).reshape((2, 8) + shape_tail).astype(np.float32)

    y_p = np.stack([np.concatenate([rs[2 * b]["yp"], rs[2 * b + 1]["yp"]], axis=0) for b in range(4)], axis=0).astype(np.float32)
    y_s = np.stack([r["ys"][:64] for r in rs], axis=0).astype(np.float32)
    S2 = 2 * TP
    return (y_p, y_s,
            cat_p("sbk_p", (S2, 8, 64)), cat_p("sbv_p", (S2, 8, 64)),
            cat_b("bk_p", (512, 8, 64)), cat_b("bv_p", (512, 8, 64)),
            cat_p("dk_p", (S2, 4, 2, 64)), cat_p("dv_p", (S2, 4, 128)),
            cat_p("ckv_p", (S2, 128)), cat_p("kr_p", (S2, 32)),
            cat_s("sbk_s", (64, 8, 64)), cat_s("sbv_s", (64, 8, 64)),
            cat_s("bk_s", (512, 8, 64)), cat_s("bv_s", (512, 8, 64)),
            cat_s("dk_s", (64, 4, 2, 64)), cat_s("dv_s", (64, 4, 128)),
            cat_s("ckv_s", (64, 128)), cat_s("kr_s", (64, 32)))
```

```python
import contextlib
import math
import numpy as np
import concourse.bass as bass
import concourse.mybir as mybir
from concourse.bass_utils import run_bass_kernel_spmd

F32 = mybir.dt.float32
BF16 = mybir.dt.bfloat16
AF = mybir.ActivationFunctionType
ALU = mybir.AluOpType

D = 2048
DC = 16
TP = 1024
NS = 1
NPT = TP // 128
T = TP + 128
NT = T // 128
KVW = 3232
KCOL = {'A.k': 0, 'A.v': 512, 'B.k': 1024, 'B.v': 1536, 'C.k': 2048, 'C.v': 2560, 'ckv': 3072, 'kr': 3200}
DIN = 15392
EPS = 1e-6
PAST = 1024
LC = 1408
WC = 1280
LB = 1536
WB = 1408
MLA_SCALE = 96 ** -0.5
DEPTH = 2


class R:
    __slots__ = ("w", "r", "name", "excl")

    def __init__(self, name="", excl=False):
        self.w = None
        self.r = []
        self.name = name
        self.excl = excl


class Sched:
    def __init__(self, nc, es):
        self.nc = nc
        self.eng = {"pe": nc.tensor, "act": nc.scalar, "dve": nc.vector, "pool": nc.gpsimd, "sp": nc.sync}
        self.csem = {}
        self.ccnt = {}
        for e in ("pe", "act", "dve", "pool"):
            self.csem[e] = es.enter_context(nc.semaphore("c_" + e))
            self.ccnt[e] = 0
        self.dsem = {}
        self.dcnt = {}
        self.dnext = {}
        for q, n in (("sp", 40), ("pool", 40)):
            self.dsem[q] = [es.enter_context(nc.semaphore(f"d_{q}{i}")) for i in range(n)]
            self.dcnt[q] = [0] * n
            self.dnext[q] = 0
        self.seen = {e: {} for e in self.eng}
        self.all_dma_tokens = []
        self.cccnt = 0
        self.ccsem = es.enter_context(nc.semaphore("ccsem"))

    def _wait(self, e, tok):
        sem, val, owner = tok
        if e == "pe" and owner == "pe":
            return
        key = id(sem)
        if self.seen[e].get(key, 0) >= val:
            return
        self.eng[e].wait_ge(sem, val)
        self.seen[e][key] = val

    def _deps(self, e, r, w):
        for x in r:
            if x.w is not None:
                self._wait(e, x.w)
            if x.excl:
                for t in x.r:
                    if t[2] != e:
                        self._wait(e, t)
        for x in w:
            if x.w is not None:
                self._wait(e, x.w)
            for t in x.r:
                self._wait(e, t)

    def _commit(self, tok, r, w):
        for x in r:
            x.r.append(tok)
            if len(x.r) > 24:
                x.r = x.r[-24:]
        for x in w:
            x.w = tok
            x.r = []

    def op(self, e, fn, r=(), w=()):
        if _DEAD[0]:
            return None
        self._deps(e, r, w)
        inst = fn(self.eng[e])
        self.ccnt[e] += 1
        tok = (self.csem[e], self.ccnt[e], e)
        inst.then_inc(tok[0], 1)
        self._commit(tok, r, w)
        return tok

    def mm(self, fns, r=(), w=()):
        if _DEAD[0]:
            return None
        self._deps("pe", r, w)
        inst = None
        for fn in fns:
            inst = fn(self.eng["pe"])
        self.ccnt["pe"] += 1
        tok = (self.csem["pe"], self.ccnt["pe"], "pe")
        inst.then_inc(tok[0], 1)
        self._commit(tok, r, w)
        return tok

    def dma(self, q, out, in_, r=(), w=(), **kw):
        if _DEAD[0]:
            return None
        i = self.dnext[q]
        self.dnext[q] = (i + 1) % len(self.dsem[q])
        sem = self.dsem[q][i]
        if self.dcnt[q][i]:
            self._wait(q, (sem, self.dcnt[q][i], "dma"))
        self._deps(q, r, w)
        inst = self.eng[q].dma_start(out=out, in_=in_, **kw)
        self.dcnt[q][i] += 16
        tok = (sem, self.dcnt[q][i], "dma")
        inst.then_inc(sem, 16)
        self._commit(tok, r, w)
        return tok

    def collective(self, ccsem, in_t, out_t, r=(), w=()):
        if _DEAD[0]:
            return None
        self._deps("pool", r, w)
        inst = self.eng["pool"].collective_compute(
            "AllGather", ALU.bypass, replica_groups=[[2 * i_, 2 * i_ + 1] for i_ in range(_NCORES[0] // 2)],
            ins=[in_t.ap().opt()], outs=[out_t.ap().opt()])
        self.cccnt += 1
        tok = (ccsem, self.cccnt, "cc")
        inst.then_inc(ccsem)
        self._commit(tok, r, w)
        return tok

    def barrier(self):
        if _DEAD[0]:
            return
        toks = []
        for e in ("pe", "act", "dve", "pool"):
            if self.ccnt[e]:
                toks.append((self.csem[e], self.ccnt[e], e + "_b"))
        for q in self.dsem:
            for i, s in enumerate(self.dsem[q]):
                if self.dcnt[q][i]:
                    toks.append((s, self.dcnt[q][i], "dma"))
        if self.cccnt:
            toks.append((self.ccsem, self.cccnt, "cc"))
        for e in self.eng:
            for t in toks:
                if e == "pe" and t[2] == "pe_b":
                    continue
                self._wait(e, t)

    def finish(self):
        for q in self.dsem:
            for i, s in enumerate(self.dsem[q]):
                if self.dcnt[q][i]:
                    self._wait("sp", (s, self.dcnt[q][i], "dma"))


_UID = [0]
_STOP = [99]
_NCORES = [8]
WARM = 3


class _StopBuild(Exception):
    pass


_DEAD = [False]


def _chk(n):
    if round(_STOP[0] * 1000) <= round(n * 1000):
        _DEAD[0] = True


class Pool:
    def __init__(self, es, nc, name, n, shape, dtype):
        _UID[0] += 1
        self.t = [es.enter_context(nc.sbuf_tensor(f"{name}{i}_{_UID[0]}", shape, dtype)) for i in range(n)]
        self.res = [R(f"{name}{i}") for i in range(n)]
        self.i = 0

    def get(self):
        i = self.i
        self.i = (i + 1) % len(self.t)
        return self.t[i], self.res[i]


def bc_ap(ap, dims):
    return bass.AP(tensor=ap.tensor, offset=ap.offset, ap=[list(ap.ap[0])] + [list(d) for d in dims])


def build_nc():
    nc = bass.Bass("TRN2", target_bir_lowering=False)
    _DEAD[0] = False
    dt = nc.dram_tensor

    def inp(name, shape, dtype=F32):
        return dt(name, list(shape), dtype, kind="ExternalInput").ap()

    def outp(name, shape):
        return dt(name, list(shape), F32, kind="ExternalOutput").ap()

    def scr(name, shape, dtype=F32):
        return dt(name, list(shape), dtype).ap()

    xp = inp("xp", [TP, D])
    xs = inp("xs", [128, D])
    csbk = inp("csbk", [2, NS, PAST, 512])
    csbv = inp("csbv", [2, NS, PAST, 512])
    cbk = inp("cbk", [2, NS, 512, 512])
    cbv = inp("cbv", [2, NS, 512, 512])
    cdk = inp("cdk", [2, NS, PAST, 512])
    cdv = inp("cdv", [2, NS, PAST, 512])
    cckv = inp("cckv", [2, NS, PAST, 128])
    ckr = inp("ckr", [2, NS, PAST, 32])
    norm_pre = inp("norm_pre", [2, D])
    norm_post = inp("norm_post", [2, D])
    w_in = inp("w_in", [2, D, DIN])
    band_bias = inp("band_bias", [2, 513, 8])
    t5 = inp("t5", [32, 4])
    dlam = inp("dlam", [2, 256])
    subln = inp("subln", [2, 128])
    qnorm = inp("qnorm", [2, 384])
    wqup = inp("wqup", [2, 384, 768])
    kvnorm = inp("kvnorm", [2, 128])
    wkvup = inp("wkvup", [2, 128, 1024])
    wbr = inp("wbr", [2, 4, 512, D])
    wout = inp("wout", [2, D, D])
    c_ident = inp("c_ident", [128, 128])
    c_j = inp("c_j", [128, 128])
    c_tri = inp("c_tri", [3, 128, 128])
    c_masks = inp("c_masks", [16, 128, 512])
    c_oh = inp("c_oh", [32, LC])
    c_cs = inp("c_cs", [T, 32])
    c_vb = inp("c_vb", [128, 1])

    yp = outp("yp", [TP, D])
    ys = outp("ys", [128, D])
    sbk_p = outp("sbk_p", [2, TP, 512])
    sbv_p = outp("sbv_p", [2, TP, 512])
    bk_p = outp("bk_p", [2, 512, 512])
    bv_p = outp("bv_p", [2, 512, 512])
    dk_p = outp("dk_p", [2, TP, 512])
    dv_p = outp("dv_p", [2, TP, 512])
    ckv_p = outp("ckv_p", [2, TP, 128])
    kr_p = outp("kr_p", [2, TP, 32])
    sbk_s = outp("sbk_s", [2, NS, 64, 512])
    sbv_s = outp("sbv_s", [2, NS, 64, 512])
    bk_s = outp("bk_s", [2, NS, 512, 512])
    bv_s = outp("bv_s", [2, NS, 512, 512])
    dk_s = outp("dk_s", [2, NS, 64, 512])
    dv_s = outp("dv_s", [2, NS, 64, 512])
    ckv_s = outp("ckv_s", [2, NS, 64, 128])
    kr_s = outp("kr_s", [2, NS, 64, 32])

    x1 = scr("x1", [T, D])
    kvloc_t = [dt(f"kvloc{t_}", [128, KVW], F32) for t_ in range(NPT)]
    kvloc = [t_.ap() for t_ in kvloc_t]
    kvall_t = [[dt(f"kvall{i}_{t_}", [256, KVW], F32) for t_ in range(NPT)] for i in range(2)]
    kvall = [[t_.ap() for t_ in row] for row in kvall_t]
    qT_scr = scr("qT_scr", [3, 512, T], BF16)
    gT_scr = scr("gT_scr", [2048, T], BF16)
    mT_scr = scr("mT_scr", [8192, T], BF16)
    qn_scr = scr("qn_scr", [T, 384])
    gvC_scr = scr("gvC_scr", [4, LC])
    gvB_scr = scr("gvB_scr", [8, LB])

    with contextlib.ExitStack() as es:
        S = Sched(nc, es)

        try:
            def sb(name, shape, dtype, stack=es):
                _UID[0] += 1
                return stack.enter_context(nc.sbuf_tensor(f"{name}_{_UID[0]}", list(shape), dtype))

            PS = [es.enter_context(nc.psum_tensor(f"ps{i}", [128, 512], F32)) for i in range(7)]
            PSR = [R(f"ps{i}", excl=True) for i in range(7)]
            psT = es.enter_context(nc.psum_tensor("psT", [128, 1024], BF16))
            psTR = R("psT", excl=True)

            identb = sb("identb", [128, 128], BF16)
            jm = sb("jm", [128, 128], F32)
            trib = sb("trib", [128, 3, 128], BF16)
            masks = sb("masks", [128, 16, 512], BF16)
            EGC = sb("EGC", [128, 4, WC], BF16)
            EGB = sb("EGB", [128, 8, WB], BF16)
            satC = sb("satC", [128, 4], F32)
            cR = R("consts")
            EGCR = R("EGC")
            EGBR = R("EGB")
            S.dma("pool", identb[:], c_ident[:, :], w=[cR])
            S.dma("sp", jm[:], c_j[:, :], w=[cR])
            S.dma("pool", trib[:], c_tri.rearrange("m p n -> p m n"), w=[cR])
            for m4 in range(4):
                S.dma("pool", masks[:, 4 * m4:4 * m4 + 4, :],
                      c_masks[4 * m4:4 * m4 + 4].rearrange("m p n -> p m n"), w=[cR])
            S.dma("sp", satC[:], bass.AP(tensor=t5.tensor, offset=15 * 4, ap=[[0, 128], [1, 4]]), w=[cR])
            vbt = sb("vbt", [128, 1], F32)
            satCv = sb("satCv", [128, 4], F32)
            S.dma("sp", vbt[:], c_vb[:, :], w=[cR])
            S.op("dve", lambda v: v.tensor_scalar(out=satCv[:], in0=satC[:], scalar1=vbt[:, 0:1], scalar2=None, op0=ALU.add),
                 r=[cR], w=[cR])
            kvallR = [[R(f"kvall{i}_{t_}") for t_ in range(NPT)] for i in range(2)]
            TRI = trib[:, 0, :]
            TRIC = trib[:, 1, :]
            ONES = trib[:, 2, :]

            def toeplitz_build(gv_scr, nheads, L, W, EG, EGR, stack, defer=None):
                nb = 1 if defer is None else 2
                hks = [sb("hk", [128, W], F32, stack) for _ in range(nb)]
                hkRs = [R("hk") for _ in range(nb)]

                def dma_step(h):
                    src = bass.AP(tensor=gv_scr.tensor, offset=h * L, ap=[[1, 128], [1, W]])
                    S.dma("sp", hks[h % nb][:], src, r=[gvR], w=[hkRs[h % nb]])

                def comp_step(h, bank=None):
                    hk, hkR = hks[h % nb], hkRs[h % nb]
                    c = 0
                    bi = 0
                    while c < W:
                        n = min(512, W - c)
                        if bank is None:
                            ps_, pr_ = PS[bi % 2], PSR[bi % 2]
                        else:
                            ps_, pr_ = bank()
                        S.mm([lambda pe: pe.matmul(ps_[:, 0:n], lhsT=jm[:], rhs=hk[:, c:c + n], start=True, stop=True)],
                             r=[hkR, cR], w=[pr_])
                        S.op("act", lambda a: a.activation(out=EG[:, h, c:c + n], in_=ps_[:, 0:n], func=AF.Exp),
                             r=[pr_], w=[EGR])
                        c += n
                        bi += 1

                if defer is None:
                    for h in range(nheads):
                        dma_step(h)
                        comp_step(h)
                else:
                    defer.append(lambda: dma_step(0))
                    for h in range(nheads):
                        if h + 1 < nheads:
                            defer.append(lambda h=h: dma_step(h + 1))
                        defer.append(lambda h=h: comp_step(h, bank=defer_bank[0]))

            defer_bank = [None]
            gvR = R("gv")
            with contextlib.ExitStack() as st0:
                t5sb = sb("t5sb", [32, 4], F32, st0)
                ohsb = sb("ohsb", [32, LC], F32, st0)
                gvsb = sb("gvsb", [4, LC], F32, st0)
                tr = R("t0")
                S.dma("sp", t5sb[:], t5[:, :], w=[tr])
                S.dma("sp", ohsb[:], c_oh[:, :], w=[tr])
                c = 0
                gvsR = R("gvs")
                while c < LC:
                    n = min(512, LC - c)
                    S.mm([lambda pe, c=c, n=n: pe.matmul(PS[0][0:4, 0:n], lhsT=t5sb[:], rhs=ohsb[:, c:c + n],
                                                         start=True, stop=True)], r=[tr], w=[PSR[0]])
                    S.op("dve", lambda v, c=c, n=n: v.tensor_copy(out=gvsb[:, c:c + n], in_=PS[0][0:4, 0:n]),
                         r=[PSR[0]], w=[gvsR])
                    c += n
                S.dma("sp", gvC_scr[:, :], gvsb[:], r=[gvsR], w=[gvR])
                toeplitz_build(gvC_scr, 4, LC, WC, EGC, EGCR, st0)
                S.barrier()
                _chk(0)

            RX = [R(f"x{t}") for t in range(NT)]
            RKV = {}

            def rkv(name, i):
                k = (name, i)
                if k not in RKV:
                    RKV[k] = R(str(k))
                return RKV[k]

            RFM = {}

            def rfm(name, tb):
                k = (name, tb)
                if k not in RFM:
                    RFM[k] = R(str(k))
                return RFM[k]

            TB = [(0, 512), (512, 512), (1024, 128)]

            for l in range(DEPTH):
                lam_init = 0.8 - 0.6 * math.exp(-0.3 * l)
                last = (l == DEPTH - 1)
                with contextlib.ExitStack() as sl:
                    wqupb = sb("wqupb", [128, 3, 768], BF16, sl)
                    wkvupb = sb("wkvupb", [128, 1024], BF16, sl)
                    lcol = sb("lcol", [128, 8], F32, sl)
                    lR = R("layerconst")
                    S.dma("pool", wqupb[:], wqup[l].rearrange("(c p) n -> p c n", p=128), w=[lR])
                    S.dma("pool", wkvupb[:], wkvup[l], w=[lR])

                    with contextlib.ExitStack() as s2:
                        hT = sb("hT", [128, DC, T], BF16, s2)
                        hTR = [R(f"hT{t}") for t in range(NT)]
                        DEFER = []
                        bb = sb("bb", [8, 513], F32, s2)
                        gvb = sb("gvb", [8, LB], F32, s2)
                        bbR = R("bb")
                        S.dma("sp", bb[:], bass.AP(tensor=band_bias.tensor, offset=l * 513 * 8, ap=[[1, 8], [8, 513]]),
                              w=[bbR], allow_slow_non_contiguous=True)
                        gvbR = R("gvb")
                        S.op("dve", lambda v: v.tensor_copy(out=gvb[:, 0:255], in_=bc_ap(bb[:, 0:1], [[0, 255]])),
                             r=[bbR], w=[gvbR])
                        S.op("dve", lambda v: v.tensor_copy(out=gvb[:, 255:768], in_=bb[:, 0:513]), r=[bbR], w=[gvbR])
                        S.op("dve", lambda v: v.tensor_copy(out=gvb[:, 768:LB], in_=bc_ap(bb[:, 512:513], [[0, LB - 768]])),
                             r=[bbR], w=[gvbR])
                        S.dma("sp", gvB_scr[:, :], gvb[:], r=[gvbR], w=[gvR])
                        toeplitz_build(gvB_scr, 8, LB, WB, EGB, EGBR, s2, defer=DEFER)
                        dl = sb("dl", [128, 256], F32, s2)
                        dj = sb("dj", [128, 128], F32, s2)
                        sg = sb("sg", [128, 128], F32, s2)
                        la = sb("la", [128, 8], F32, s2)
                        dR = R("dl")
                        S.dma("sp", dl[:], bass.AP(tensor=dlam.tensor, offset=l * 256, ap=[[0, 128], [1, 256]]), w=[dR])
                        S.dma("sp", sg[:, 0:1], bass.AP(tensor=subln.tensor, offset=l * 128, ap=[[1, 128], [1, 1]]), w=[dR])
                        S.op("dve", lambda v: v.tensor_tensor(out=dj[:, 0:64], in0=dl[:, 0:64], in1=dl[:, 64:128], op=ALU.mult),
                             r=[dR], w=[dR])
                        S.op("dve", lambda v: v.tensor_tensor(out=dj[:, 64:128], in0=dl[:, 128:192], in1=dl[:, 192:256], op=ALU.mult),
                             r=[dR], w=[dR])
                        S.op("dve", lambda v: v.reduce_sum(out=la[:, 0:1], in_=dj[:, 0:64], axis=mybir.AxisListType.X), r=[dR], w=[dR])
                        S.op("dve", lambda v: v.reduce_sum(out=la[:, 1:2], in_=dj[:, 64:128], axis=mybir.AxisListType.X), r=[dR], w=[dR])
                        S.op("act", lambda a: a.activation(out=la[:, 2:4], in_=la[:, 0:2], func=AF.Exp), r=[dR], w=[dR])
                        S.op("dve", lambda v: v.tensor_tensor(out=la[:, 4:5], in0=la[:, 3:4], in1=la[:, 2:3], op=ALU.subtract),
                             r=[dR], w=[dR])
                        S.op("dve", lambda v: v.tensor_scalar(out=lcol[:, 0:1], in0=la[:, 4:5], scalar1=-lam_init, scalar2=None,
                                                              op0=ALU.add), r=[dR], w=[lR])
                        S.op("dve", lambda v: v.tensor_scalar(out=lcol[:, 1:2], in0=sg[:, 0:1], scalar1=(1.0 - lam_init),
                                                              scalar2=None, op0=ALU.mult), r=[dR], w=[lR])
                        with contextlib.ExitStack() as s1:
                            gpre = sb("gpre", [128, D], F32, s1)
                            gR = R("gpre")
                            S.dma("sp", gpre[:], bass.AP(tensor=norm_pre.tensor, offset=l * D, ap=[[0, 128], [1, D]]), w=[gR])
                            xpool = Pool(s1, nc, "xt", 2, [128, D], F32)
                            hbpool = Pool(s1, nc, "hb", 2, [128, D], BF16)
                            junk = sb("junk", [128, D], BF16, s1)
                            junkR = R("junk")
                            sspool = Pool(s1, nc, "ss", 2, [128, 4], F32)
                            p1st = {}

                            def p1front(tt):
                                xt, xr = xpool.get()
                                if l == 0:
                                    src = xp[tt * 128:(tt + 1) * 128, :] if tt < NPT else xs[:, :]
                                    S.dma("sp", xt[:], src, w=[xr])
                                else:
                                    S.dma("sp", xt[:], x1[tt * 128:(tt + 1) * 128, :], r=[RX[tt]], w=[xr])
                                ss, ssr = sspool.get()
                                S.op("dve", lambda v, ss=ss: v.memset(ss[:], 0.0), w=[ssr])
                                S.op("act", lambda a, xt=xt, ss=ss: a.activation(out=junk[:], in_=xt[:], func=AF.Square,
                                                                                 accum_out=ss[:, 0:1]),
                                     r=[xr], w=[junkR, ssr])
                                S.op("dve", lambda v, ss=ss: v.tensor_scalar(out=ss[:, 1:2], in0=ss[:, 0:1], scalar1=1.0 / D,
                                                                              scalar2=EPS, op0=ALU.mult, op1=ALU.add),
                                     r=[ssr], w=[ssr])
                                S.op("act", lambda a, ss=ss: a.activation(out=ss[:, 2:3], in_=ss[:, 1:2], func=AF.Sqrt),
                                     r=[ssr], w=[ssr])
                                S.op("dve", lambda v, ss=ss: v.reciprocal(out=ss[:, 3:4], in_=ss[:, 2:3]), r=[ssr], w=[ssr])
                                hb, hbr = hbpool.get()
                                S.op("dve", lambda v, hb=hb, xt=xt, ss=ss: v.scalar_tensor_tensor(
                                    out=hb[:], in0=xt[:], scalar=ss[:, 3:4], in1=gpre[:], op0=ALU.mult, op1=ALU.mult),
                                    r=[xr, ssr, gR], w=[hbr])
                                p1st[tt] = (hb, hbr)

                            def p1back(tt):
                                hb, hbr = p1st.pop(tt)
                                for half in range(2):
                                    S.mm([lambda pe, hb=hb, c=c, half=half: pe.transpose(
                                        out=psT[:, c * 128:(c + 1) * 128], in_=hb[:, (half * 8 + c) * 128:(half * 8 + c + 1) * 128],
                                        identity=identb[:]) for c in range(8)], r=[hbr, cR], w=[psTR])
                                    eng = "act" if half == 0 else "dve"
                                    if eng == "act":
                                        S.op("act", lambda a, half=half, tt=tt: a.copy(
                                            out=hT[:, half * 8:half * 8 + 8, tt * 128:(tt + 1) * 128],
                                            in_=psT[:, :].rearrange("p (a b) -> p a b", b=128)), r=[psTR], w=[hTR[tt]])
                                    else:
                                        S.op("dve", lambda v, half=half, tt=tt: v.tensor_copy(
                                            out=hT[:, half * 8:half * 8 + 8, tt * 128:(tt + 1) * 128],
                                            in_=psT[:, :].rearrange("p (a b) -> p a b", b=128)), r=[psTR], w=[hTR[tt]])


                            p1front(0)
                            for tt in range(NT):
                                if tt + 1 < NT:
                                    p1front(tt + 1)
                                p1back(tt)

                        S.barrier()
                        _chk(2)
                        wpool = Pool(s2, nc, "wblk", 3, [128, DC, 512], BF16)
                        w32 = sb("w32", [128, DC, 32], BF16, s2)
                        w32R = R("w32")
                        evf = Pool(s2, nc, "evf", 6, [128, 512], F32)
                        evb = Pool(s2, nc, "evb", 8, [128, 512], BF16)
                        dsb = Pool(s2, nc, "dsb", 2, [128, 544], F32)
                        dwk = Pool(s2, nc, "dwk", 2, [128, 640], F32)
                        csb = Pool(s2, nc, "csb", 2, [128, 32], F32)
                        gq = sb("gq", [128, 384], F32, s2)
                        gkv = sb("gkv", [128, 128], F32, s2)
                        jk2 = sb("jk2", [128, 384], F32, s2)
                        jk2R = R("jk2")
                        g2R = R("g2")
                        S.dma("sp", gq[:], bass.AP(tensor=qnorm.tensor, offset=l * 384, ap=[[0, 128], [1, 384]]), w=[g2R])
                        S.dma("sp", gkv[:], bass.AP(tensor=kvnorm.tensor, offset=l * 128, ap=[[0, 128], [1, 128]]), w=[g2R])
                        w_l = w_in[l].rearrange("(c p) n -> p c n", p=128)
                        psi = [0]

                        def next_ps():
                            i = psi[0]
                            psi[0] = (i + 1) % 7
                            return PS[i], PSR[i]

                        def tm_dests(name, tt):
                            outs = []
                            if tt < NPT:
                                rows = slice(tt * 128, (tt + 1) * 128)
                                kc = KCOL[name]
                                outs.append((kvloc[tt][:, kc:kc + 512], (0, 128), rkv("L" + name, tt)))
                                po = {"A.k": sbk_p, "A.v": sbv_p, "C.k": dk_p, "C.v": dv_p}.get(name)
                                if po is not None:
                                    outs.append((po[l, rows, :], (0, 128), rkv("O" + name, tt)))
                            else:
                                for s in range(NS):
                                    rs = (s * 64, s * 64 + 64)
                                    if name == "A.k":
                                        outs.append((sbk_s[l, s, :, :], rs, rkv("sbk_s", s)))
                                    elif name == "A.v":
                                        outs.append((sbv_s[l, s, :, :], rs, rkv("sbv_s", s)))
                                    elif name == "B.k":
                                        outs.append((bk_s[l, s, 448:512, :], rs, rkv("bk_s", s)))
                                    elif name == "B.v":
                                        outs.append((bv_s[l, s, 448:512, :], rs, rkv("bv_s", s)))
                                    elif name == "C.k":
                                        outs.append((dk_s[l, s, :, :], rs, rkv("dk_s", s)))
                                    elif name == "C.v":
                                        outs.append((dv_s[l, s, :, :], rs, rkv("dv_s", s)))
                            return outs

                        TMB = [("A.k", 512), ("A.v", 1024), ("B.k", 2048), ("B.v", 2560), ("C.k", 3584), ("C.v", 4096)]
                        for name, c0 in TMB:
                            wt, wr = wpool.get()
                            S.dma("pool", wt[:], w_l[:, :, c0:c0 + 512], w=[wr])
                            for tt in range(NT):
                                ps, pr = next_ps()
                                S.mm([lambda pe, ps=ps, wt=wt, dc=dc, tt=tt: pe.matmul(
                                    ps[:, :], lhsT=hT[:, dc, tt * 128:(tt + 1) * 128], rhs=wt[:, dc, :],
                                    start=(dc == 0), stop=(dc == DC - 1)) for dc in range(DC)],
                                    r=[wr, hTR[tt]], w=[pr])
                                ev, er = evf.get()
                                S.op("dve", lambda v, ev=ev, ps=ps: v.tensor_copy(out=ev[:], in_=ps[:, :]), r=[pr], w=[er])
                                for (dap, (ra, rb), dres) in tm_dests(name, tt):
                                    S.dma("sp", dap, ev[ra:rb, :], r=[er], w=[dres])
                        bandR = R("band_out")
                        for t_ in range(4, 8):
                            S.dma("sp", bk_p[l, (t_ - 4) * 128:(t_ - 3) * 128, :], kvloc[t_][:, 1024:1536], r=[rkv("LB.k", t_)], w=[bandR])
                            S.dma("sp", bv_p[l, (t_ - 4) * 128:(t_ - 3) * 128, :], kvloc[t_][:, 1536:2048], r=[rkv("LB.v", t_)], w=[bandR])
                        for s in range(NS):
                            S.dma("sp", bk_s[l, s, 0:448, :], cbk[l, s, 64:512, :], w=[bandR])
                            S.dma("sp", bv_s[l, s, 0:448, :], cbv[l, s, 64:512, :], w=[bandR])

                        wt, wr = wpool.get()
                        S.dma("pool", wt[:], w_l[:, :, 4608:5120], w=[wr])
                        S.dma("pool", w32[:], w_l[:, :, 5120:5152], w=[w32R])
                        for tt in range(NT):
                            ps, pr = next_ps()
                            ps2, pr2 = next_ps()
                            S.mm([lambda pe, ps=ps, wt=wt, dc=dc, tt=tt: pe.matmul(
                                ps[:, :], lhsT=hT[:, dc, tt * 128:(tt + 1) * 128], rhs=wt[:, dc, :],
                                start=(dc == 0), stop=(dc == DC - 1)) for dc in range(DC)], r=[wr, hTR[tt]], w=[pr])
                            S.mm([lambda pe, ps2=ps2, dc=dc, tt=tt: pe.matmul(
                                ps2[:, 0:32], lhsT=hT[:, dc, tt * 128:(tt + 1) * 128], rhs=w32[:, dc, :],
                                start=(dc == 0), stop=(dc == DC - 1)) for dc in range(DC)], r=[w32R, hTR[tt]], w=[pr2])
                            d, dr = dsb.get()
                            S.op("dve", lambda v, d=d, ps=ps: v.tensor_copy(out=d[:, 0:512], in_=ps[:, :]), r=[pr], w=[dr])
                            S.op("dve", lambda v, d=d, ps2=ps2: v.tensor_copy(out=d[:, 512:544], in_=ps2[:, 0:32]), r=[pr2], w=[dr])
                            wk, wkr = dwk.get()
                            cs, csr = csb.get()
                            S.dma("sp", cs[:], c_cs[tt * 128:(tt + 1) * 128, :], w=[csr])
                            st = 600
                            S.op("dve", lambda v, wk=wk: v.memset(wk[:, st:st + 8], 0.0), w=[wkr])
                            S.op("act", lambda a, d=d, wk=wk: a.activation(out=jk2[:, 0:384], in_=d[:, 0:384], func=AF.Square,
                                                                           accum_out=wk[:, st:st + 1]),
                                 r=[dr], w=[wkr, jk2R])
                            S.op("act", lambda a, d=d, wk=wk: a.activation(out=jk2[:, 0:128], in_=d[:, 384:512], func=AF.Square,
                                                                           accum_out=wk[:, st + 1:st + 2]),
                                 r=[dr], w=[wkr, jk2R])
                            S.op("dve", lambda v, wk=wk: v.tensor_scalar(out=wk[:, st + 2:st + 3], in0=wk[:, st:st + 1],
                                                                         scalar1=1.0 / 384, scalar2=EPS, op0=ALU.mult, op1=ALU.add),
                                 r=[wkr], w=[wkr])
                            S.op("dve", lambda v, wk=wk: v.tensor_scalar(out=wk[:, st + 3:st + 4], in0=wk[:, st + 1:st + 2],
                                                                         scalar1=1.0 / 128, scalar2=EPS, op0=ALU.mult, op1=ALU.add),
                                 r=[wkr], w=[wkr])
                            S.op("act", lambda a, wk=wk: a.activation(out=wk[:, st + 4:st + 6], in_=wk[:, st + 2:st + 4], func=AF.Sqrt),
                                 r=[wkr], w=[wkr])
                            S.op("dve", lambda v, wk=wk: v.reciprocal(out=wk[:, st + 6:st + 8], in_=wk[:, st + 4:st + 6]),
                                 r=[wkr], w=[wkr])
                            S.op("dve", lambda v, wk=wk, d=d: v.scalar_tensor_tensor(
                                out=wk[:, 0:384], in0=d[:, 0:384], scalar=wk[:, st + 6:st + 7], in1=gq[:], op0=ALU.mult, op1=ALU.mult),
                                r=[dr, wkr, g2R], w=[wkr])
                            S.op("dve", lambda v, wk=wk, d=d: v.scalar_tensor_tensor(
                                out=wk[:, 384:512], in0=d[:, 384:512], scalar=wk[:, st + 7:st + 8], in1=gkv[:], op0=ALU.mult, op1=ALU.mult),
                                r=[dr, wkr, g2R], w=[wkr])
                            x1a, x2a = d[:, 512:528], d[:, 528:544]
                            cosa, sina = cs[:, 0:16], cs[:, 16:32]
                            S.op("dve", lambda v, wk=wk: v.tensor_tensor(out=wk[:, 544:560], in0=x1a, in1=cosa, op=ALU.mult), r=[dr, csr], w=[wkr])
                            S.op("dve", lambda v, wk=wk: v.tensor_tensor(out=wk[:, 560:576], in0=x2a, in1=sina, op=ALU.mult), r=[dr, csr], w=[wkr])
                            S.op("dve", lambda v, wk=wk: v.tensor_tensor(out=wk[:, 512:528], in0=wk[:, 544:560], in1=wk[:, 560:576], op=ALU.subtract), r=[wkr], w=[wkr])
                            S.op("dve", lambda v, wk=wk: v.tensor_tensor(out=wk[:, 544:560], in0=x2a, in1=cosa, op=ALU.mult), r=[dr, csr], w=[wkr])
                            S.op("dve", lambda v, wk=wk: v.tensor_tensor(out=wk[:, 560:576], in0=x1a, in1=sina, op=ALU.mult), r=[dr, csr], w=[wkr])
                            S.op("dve", lambda v, wk=wk: v.tensor_tensor(out=wk[:, 528:544], in0=wk[:, 544:560], in1=wk[:, 560:576], op=ALU.add), r=[wkr], w=[wkr])
                            S.dma("sp", qn_scr[tt * 128:(tt + 1) * 128, :], wk[:, 0:384], r=[wkr], w=[rkv("qn", tt)])
                            if tt < NPT:
                                S.dma("sp", ckv_p[l, tt * 128:(tt + 1) * 128, :], wk[:, 384:512], r=[wkr], w=[rkv("Ockv", tt)])
                                S.dma("sp", kr_p[l, tt * 128:(tt + 1) * 128, :], wk[:, 512:544], r=[wkr], w=[rkv("Okr", tt)])
                                S.dma("sp", kvloc[tt][:, 3072:3200], wk[:, 384:512], r=[wkr], w=[rkv("Lckv", tt)])
                                S.dma("sp", kvloc[tt][:, 3200:3232], wk[:, 512:544], r=[wkr], w=[rkv("Lkr", tt)])
                            else:
                                for s in range(NS):
                                    S.dma("sp", ckv_s[l, s, :, :], wk[s * 64:s * 64 + 64, 384:512], r=[wkr], w=[rkv("ckv_s", s)])
                                    S.dma("sp", kr_s[l, s, :, :], wk[s * 64:s * 64 + 64, 512:544], r=[wkr], w=[rkv("kr_s", s)])

                        FMB = []
                        for bi, c0 in enumerate((0, 1536, 3072)):
                            FMB.append((c0, AF.Copy, qT_scr[bi], "q%d" % bi))
                        for n in range(4):
                            FMB.append((5152 + 512 * n, AF.Silu, gT_scr[512 * n:512 * n + 512], "g"))
                        for j in range(16):
                            FMB.append((7200 + 512 * j, AF.Sigmoid, mT_scr[512 * j:512 * j + 512], "m"))
                        defer_bank[0] = next_ps
                        for fmi, (c0, func, dst, nm) in enumerate(FMB):
                            if DEFER:
                                DEFER.pop(0)()
                            wt, wr = wpool.get()
                            S.dma("pool", wt[:], w_l[:, :, c0:c0 + 512], w=[wr])
                            if fmi == 2:
                                for t_ in range(NPT):
                                    S.collective(S.ccsem, kvloc_t[t_], kvall_t[l][t_],
                                                 r=[rkv("L" + n_, t_) for n_ in KCOL], w=[kvallR[l][t_]])
                            for ch in range(4):
                                for tbi, (t0, nt) in enumerate(TB):
                                    ps, pr = next_ps()
                                    tts = [hTR[t0 // 128 + i] for i in range(nt // 128)]
                                    S.mm([lambda pe, ps=ps, wt=wt, dc=dc, ch=ch, t0=t0, nt=nt: pe.matmul(
                                        ps[:, 0:nt], lhsT=wt[:, dc, ch * 128:(ch + 1) * 128], rhs=hT[:, dc, t0:t0 + nt],
                                        start=(dc == 0), stop=(dc == DC - 1)) for dc in range(DC)], r=[wr] + tts, w=[pr])
                                    ev, er = evb.get()
                                    S.op("act", lambda a, ev=ev, ps=ps, nt=nt, func=func: a.activation(
                                        out=ev[:, 0:nt], in_=ps[:, 0:nt], func=func), r=[pr], w=[er])
                                    S.dma("sp", dst[ch * 128:(ch + 1) * 128, t0:t0 + nt], ev[:, 0:nt], r=[er], w=[rfm(nm, tbi)])
                        while DEFER:
                            DEFER.pop(0)()
                        S.barrier()
                        _chk(3)

                    for blk in range(2):
                        with contextlib.ExitStack() as s34:
                            gatedT = sb("gatedT", [128, 16, 640], BF16, s34)
                            gatedR = R("gated")
                            groups = [dict(kind="p", qb=blk, tok0=512 * blk, nq=512, col0=0, tbi=blk)]
                            NQB = 512
                            CPS = [(0, 512)]
                            MTB = [blk]
                            if blk == 1:
                                groups += [dict(kind="s", s=s, tok0=TP + 64 * s, nq=64, col0=512 + 64 * s, tbi=2) for s in range(NS)]
                                NQB = 640
                                CPS = [(0, 512), (512, 128)]
                                MTB = [1, 2]
                                S.op("dve", lambda v: v.memset(gatedT[:, :, 576:640], 0.0), w=[gatedR])
                            with contextlib.ExitStack() as s3:
                                KT = sb("KT", [128, 8, 2048], BF16, s3)
                                KTR = R("KT")
                                Vsb = sb("Vsb", [128, 16, 512], BF16, s3)
                                VR = R("V")
                                qT = sb("qT", [128, 8, 512], BF16, s3)
                                qTR = R("qT")
                                gT = sb("gT", [128, 4, 512], BF16, s3)
                                gTR = R("gT")
                                kraw = Pool(s3, nc, "kraw", 3, [128, 512], BF16)
                                kst = Pool(s3, nc, "kst", 3, [128, 512], F32)
                                vst = Pool(s3, nc, "vst", 3, [128, 512], F32)

                                def warm(n):
                                    for _ in range(n):
                                        S.mm([lambda pe: pe.matmul(PS[0][:, 0:512], lhsT=identb[:], rhs=masks[:, 0, :],
                                                                   start=True, stop=True)], r=[cR], w=[PSR[0]])
                                wf = Pool(s3, nc, "wf", 4, [128, 512], F32)
                                t1p = Pool(s3, nc, "t1p", 6, [128, 512], F32)
                                wb_ = Pool(s3, nc, "wb", 16, [128, 512], BF16)
                                small = Pool(s3, nc, "small", 7, [128, 1024], BF16)
                                qmf = Pool(s3, nc, "qmf", 2, [128, 768], F32)
                                csq = Pool(s3, nc, "csq", 2, [128, 32], F32)

                                for G in groups:
                                    nq = G["nq"]
                                    tok0 = G["tok0"]
                                    col0 = G["col0"]

                                    def key_tiles(br):
                                        tl = []
                                        if G["kind"] == "p":
                                            qb = G["qb"]
                                            hi = 8 + 4 * qb + 4
                                            lo = (8 + 4 * qb - 4) if br == "B" else 0
                                            for kt in range(lo, hi):
                                                D0 = 128 * kt - (1024 + 512 * qb)
                                                prv = kt < 8
                                                rows = slice(0, 128)
                                                if prv:
                                                    srcd = kvall[l][kt]
                                                    rs_ = [kvallR[l][kt]]
                                                else:
                                                    srcd = kvloc[kt - 8]
                                                if br != "D":
                                                    kc, vc = KCOL[br + ".k"], KCOL[br + ".v"]
                                                    if not prv:
                                                        rs_ = [rkv("L" + br + ".k", kt - 8), rkv("L" + br + ".v", kt - 8)]
                                                    tl.append((128, srcd[rows, kc:kc + 512], srcd[rows, vc:vc + 512], D0, rs_, prv))
                                                else:
                                                    if not prv:
                                                        rs_ = [rkv("Lckv", kt - 8), rkv("Lkr", kt - 8)]
                                                    tl.append((128, srcd[rows, 3072:3200], srcd[rows, 3200:3232], D0, rs_, prv))
                                        else:
                                            s = G["s"]
                                            if br == "B":
                                                for kt in range(4):
                                                    rows = slice(kt * 128, kt * 128 + 128)
                                                    tl.append((128, cbk[l, s, rows, :], cbv[l, s, rows, :], -512 + 128 * kt, [], False))
                                                tl.append((64, bk_s[l, s, 448:512, :], bv_s[l, s, 448:512, :], 0, [rkv("bk_s", s), rkv("bv_s", s)], False))
                                            else:
                                                for kt in range(8):
                                                    rows = slice(kt * 128, kt * 128 + 128)
                                                    D0 = kt * 128 - 1024
                                                    if br == "A":
                                                        tl.append((128, csbk[l, s, rows, :], csbv[l, s, rows, :], D0, [], False))
                                                    elif br == "C":
                                                        tl.append((128, cdk[l, s, rows, :], cdv[l, s, rows, :], D0, [], False))
                                                    else:
                                                        tl.append((128, cckv[l, s, rows, :], ckr[l, s, rows, :], D0, [], False))
                                                if br == "A":
                                                    tl.append((64, sbk_s[l, s, :, :], sbv_s[l, s, :, :], 0, [rkv("sbk_s", s), rkv("sbv_s", s)], False))
                                                elif br == "C":
                                                    tl.append((64, dk_s[l, s, :, :], dv_s[l, s, :, :], 0, [rkv("dk_s", s), rkv("dv_s", s)], False))
                                                else:
                                                    tl.append((64, ckv_s[l, s, :, :], kr_s[l, s, :, :], 0, [rkv("ckv_s", s), rkv("kr_s", s)], False))
                                        return tl

                                    for bi, br in enumerate("ABCD"):
                                        tiles = key_tiles(br)
                                        S.dma("sp", gT[:, :, 0:nq],
                                              gT_scr[512 * bi:512 * bi + 512, tok0:tok0 + nq].rearrange("(c p) t -> p c t", p=128),
                                              r=[rfm("g", G["tbi"])], w=[gTR])
                                        if br != "D":
                                            S.dma("sp", qT[:, 0:4, 0:nq],
                                                  qT_scr[bi, :, tok0:tok0 + nq].rearrange("(c p) t -> p c t", p=128),
                                                  r=[rfm("q%d" % bi, G["tbi"])], w=[qTR])
                                            for ti, (nk, kap, vap, D0, rs, prv) in enumerate(tiles):
                                                kf, kfr = kst.get()
                                                vf, vfr = vst.get()
                                                S.dma("sp", kf[0:nk, :], kap, r=rs, w=[kfr])
                                                S.dma("sp", vf[0:nk, :], vap, r=rs, w=[vfr])
                                                kr_, krr = kraw.get()
                                                S.op("dve", lambda v: v.tensor_copy(out=kr_[0:nk, :], in_=kf[0:nk, :]), r=[kfr], w=[krr])
                                                S.op("act", lambda a: a.copy(out=Vsb[0:nk, ti, :], in_=vf[0:nk, :]), r=[vfr], w=[VR])
                                                S.mm([lambda pe, kr_=kr_, c=c, nk=nk: pe.transpose(
                                                    out=psT[:, c * 128:c * 128 + nk], in_=kr_[0:nk, c * 128:(c + 1) * 128],
                                                    identity=identb[0:nk, 0:nk]) for c in range(4)], r=[krr, cR], w=[psTR])
                                                S.op("dve", lambda v, ti=ti, nk=nk: v.tensor_copy(
                                                    out=KT[:, 0:4, ti * 128:ti * 128 + nk],
                                                    in_=psT[:, 0:512].rearrange("p (a b) -> p a b", b=128)[:, :, 0:nk]),
                                                    r=[psTR], w=[KTR])
                                        else:
                                            ntt = max(1, nq // 128)
                                            for qi in range(ntt):
                                                nr = min(128, nq)
                                                r0 = tok0 + qi * 128
                                                tt = r0 // 128
                                                qn_, qnr = small.get()
                                                kf, kfr = kst.get()
                                                S.dma("sp", kf[0:nr, 0:384], qn_scr[r0:r0 + nr, :], r=[rkv("qn", tt)], w=[kfr])
                                                S.op("dve", lambda v: v.tensor_copy(out=qn_[0:nr, 0:384], in_=kf[0:nr, 0:384]), r=[kfr], w=[qnr])
                                                cs, csr = csq.get()
                                                S.dma("sp", cs[0:nr, :], c_cs[r0:r0 + nr, :], w=[csr])
                                                S.mm([lambda pe, qn_=qn_, c=c, nr=nr: pe.transpose(
                                                    out=psT[:, c * 128:c * 128 + nr], in_=qn_[0:nr, c * 128:(c + 1) * 128],
                                                    identity=identb[0:nr, 0:nr]) for c in range(3)], r=[qnr, cR], w=[psTR])
                                                qnT, qnTr = small.get()
                                                S.op("dve", lambda v, qnT=qnT: v.tensor_copy(out=qnT[:, 0:384], in_=psT[:, 0:384]),
                                                     r=[psTR], w=[qnTr])
                                                S.mm([lambda pe, qnT=qnT, c=c, nr=nr: pe.matmul(
                                                    PS[5][0:nr, :], lhsT=qnT[:, c * 128:c * 128 + nr], rhs=wqupb[:, c, 0:512],
                                                    start=(c == 0), stop=(c == 2)) for c in range(3)], r=[qnTr, lR], w=[PSR[5]])
                                                S.mm([lambda pe, qnT=qnT, c=c, nr=nr: pe.matmul(
                                                    PS[6][0:nr, 0:256], lhsT=qnT[:, c * 128:c * 128 + nr], rhs=wqupb[:, c, 512:768],
                                                    start=(c == 0), stop=(c == 2)) for c in range(3)], r=[qnTr, lR], w=[PSR[6]])
                                                qf, qfr = qmf.get()
                                                S.op("act", lambda a, qf=qf, nr=nr: a.copy(out=qf[0:nr, 0:512], in_=PS[5][0:nr, :]),
                                                     r=[PSR[5]], w=[qfr])
                                                S.op("act", lambda a, qf=qf, nr=nr: a.copy(out=qf[0:nr, 512:768], in_=PS[6][0:nr, 0:256]),
                                                     r=[PSR[6]], w=[qfr])
                                                qb_, qbr = small.get()
                                                qf3 = qf[0:nr, :].rearrange("p (h e) -> p h e", e=96)
                                                qb3 = qb_[0:nr, 0:768].rearrange("p (h e) -> p h e", e=96)
                                                tmp, tmpr = wf.get()
                                                ta = tmp[0:nr, 0:128].rearrange("p (h e) -> p h e", e=16)
                                                tb_ = tmp[0:nr, 128:256].rearrange("p (h e) -> p h e", e=16)
                                                cosb = bc_ap(cs[0:nr, 0:16], [[0, 8], [1, 16]])
                                                sinb = bc_ap(cs[0:nr, 16:32], [[0, 8], [1, 16]])
                                                S.op("dve", lambda v: v.tensor_copy(out=qb3[:, :, 0:64], in_=qf3[:, :, 0:64]), r=[qfr], w=[qbr])
                                                S.op("dve", lambda v: v.tensor_tensor(out=ta, in0=qf3[:, :, 64:80], in1=cosb, op=ALU.mult), r=[qfr, csr], w=[tmpr])
                                                S.op("dve", lambda v: v.tensor_tensor(out=tb_, in0=qf3[:, :, 80:96], in1=sinb, op=ALU.mult), r=[qfr, csr], w=[tmpr])
                                                S.op("dve", lambda v: v.tensor_tensor(out=qb3[:, :, 64:80], in0=ta, in1=tb_, op=ALU.subtract), r=[tmpr], w=[qbr, tmpr])
                                                S.op("dve", lambda v: v.tensor_tensor(out=ta, in0=qf3[:, :, 80:96], in1=cosb, op=ALU.mult), r=[qfr, csr], w=[tmpr])
                                                S.op("dve", lambda v: v.tensor_tensor(out=tb_, in0=qf3[:, :, 64:80], in1=sinb, op=ALU.mult), r=[qfr, csr], w=[tmpr])
                                                S.op("dve", lambda v: v.tensor_tensor(out=qb3[:, :, 80:96], in0=ta, in1=tb_, op=ALU.add), r=[tmpr], w=[qbr, tmpr])
                                                S.mm([lambda pe, qb_=qb_, h=h, nr=nr: pe.transpose(
                                                    out=psT[0:96, h * 128:h * 128 + nr], in_=qb_[0:nr, h * 96:(h + 1) * 96],
                                                    identity=identb[0:nr, 0:nr]) for h in range(8)], r=[qbr, cR], w=[psTR])
                                                S.op("dve", lambda v, qi=qi, nr=nr: v.tensor_copy(
                                                    out=qT[0:96, :, qi * 128:qi * 128 + nr],
                                                    in_=psT[0:96, :].rearrange("p (a b) -> p a b", b=128)[:, :, 0:nr]),
                                                    r=[psTR], w=[qTR])
                                            _chk(3.61)
                                            dst_ = {}

                                            def stA(ti):
                                                nk, cap, rap, D0, rs, prv = tiles[ti]
                                                b0 = 5 if ti % 2 == 0 else 3
                                                ck, ckr_ = small.get()
                                                kf, kfr = kst.get()
                                                S.dma("sp", kf[0:nk, 0:128], cap, r=rs, w=[kfr])
                                                S.dma("sp", kf[0:nk, 128:160], rap, r=rs, w=[kfr])
                                                S.op("dve", lambda v: v.tensor_copy(out=ck[0:nk, 0:160], in_=kf[0:nk, 0:160]), r=[kfr], w=[ckr_])
                                                S.mm([lambda pe: pe.transpose(
                                                    out=psT[:, 0:nk], in_=ck[0:nk, 0:128], identity=identb[0:nk, 0:nk])],
                                                    r=[ckr_, cR], w=[psTR])
                                                ckT, ckTr = small.get()
                                                S.op("dve", lambda v: v.tensor_copy(out=ckT[:, 0:nk], in_=psT[:, 0:nk]),
                                                     r=[psTR], w=[ckTr])
                                                S.mm([lambda pe: pe.matmul(
                                                    PS[b0][0:nk, :], lhsT=ckT[:, 0:nk], rhs=wkvupb[:, 0:512], start=True, stop=True)],
                                                    r=[ckTr, lR], w=[PSR[b0]])
                                                S.mm([lambda pe: pe.matmul(
                                                    PS[b0 + 1][0:nk, :], lhsT=ckT[:, 0:nk], rhs=wkvupb[:, 512:1024], start=True, stop=True)],
                                                    r=[ckTr, lR], w=[PSR[b0 + 1]])
                                                dst_[ti] = (ck, ckr_)

                                            def stB(ti):
                                                nk, cap, rap, D0, rs, prv = tiles[ti]
                                                b0 = 5 if ti % 2 == 0 else 3
                                                ck, ckr_ = dst_.pop(ti)
                                                kd, kdr = small.get()
                                                kd3 = kd[0:nk, 0:768].rearrange("p (h e) -> p h e", e=96)
                                                for hf in range(2):
                                                    pv = PS[b0 + hf][0:nk, :].rearrange("p (h e) -> p h e", e=128)
                                                    S.op("act", lambda a: a.copy(out=kd3[:, 4 * hf:4 * hf + 4, 0:64], in_=pv[:, :, 0:64]),
                                                         r=[PSR[b0 + hf]], w=[kdr])
                                                    S.op("dve", lambda v: v.tensor_copy(
                                                        out=Vsb[0:nk, ti, 256 * hf:256 * hf + 256].rearrange("p (h e) -> p h e", e=64),
                                                        in_=pv[:, :, 64:128]), r=[PSR[b0 + hf]], w=[VR])
                                                S.op("dve", lambda v: v.tensor_copy(
                                                    out=kd3[:, :, 64:96], in_=bc_ap(ck[0:nk, 128:160], [[0, 8], [1, 32]])), r=[ckr_], w=[kdr])
                                                S.mm([lambda pe, h=h: pe.transpose(
                                                    out=psT[0:96, h * 128:h * 128 + nk], in_=kd[0:nk, h * 96:(h + 1) * 96],
                                                    identity=identb[0:nk, 0:nk]) for h in range(8)], r=[kdr, cR], w=[psTR])
                                                S.op("dve", lambda v: v.tensor_copy(
                                                    out=KT[0:96, :, ti * 128:ti * 128 + nk],
                                                    in_=psT[0:96, :].rearrange("p (a b) -> p a b", b=128)[:, :, 0:nk]),
                                                    r=[psTR], w=[KTR])
                                                warm(WARM)

                                            stA(0)
                                            for ti in range(len(tiles)):
                                                if ti + 1 < len(tiles):
                                                    stA(ti + 1)
                                                stB(ti)

                                        _chk(3.0 + 0.2 * bi + 0.1)
                                        def mask_ap(kind, D0, nk):
                                            d = D0 // 128
                                            if kind == "A":
                                                return masks[0:nk, d, 0:nq]
                                            if kind == "CD":
                                                return masks[0:nk, 4 + d, 0:nq]
                                            return masks[0:nk, 8 + (D0 + 512) // 128, 0:nq]

                                        isP = G["kind"] == "p"
                                        LOOK = 3

                                        def run_pipe(items, s1, s2):
                                            n = len(items)
                                            for i in range(min(LOOK, n)):
                                                s1(i)
                                            for i in range(n):
                                                s2(i)
                                                if i + LOOK < n:
                                                    s1(i + LOOK)

                                        if br == "A":
                                            ntl = len(tiles)
                                            items = [(2 * hp_ + hf_, idx) for hp_ in range(4) for idx in range(ntl) for hf_ in range(2)]
                                            order = list(enumerate(tiles))[::-1]
                                            st = {}
                                            psbanks = [0, 1, 5]

                                            def s1q(i):
                                                h, idx = items[i]
                                                hp, r0 = h // 2, 64 * (h % 2)
                                                ti, (nk, kap, vap, D0, rs, prv) = order[idx]
                                                pS, pSR = PS[psbanks[i % 3]], PSR[psbanks[i % 3]]
                                                S.mm([lambda pe: pe.matmul(
                                                    pS[0:nk, 0:nq], lhsT=KT[r0:r0 + 64, hp, ti * 128:ti * 128 + nk],
                                                    rhs=qT[r0:r0 + 64, hp, 0:nq], start=True, stop=True)], r=[KTR, qTR], w=[pSR])

                                            def s1(i):
                                                h, idx = items[i]
                                                hp, r0 = h // 2, 64 * (h % 2)
                                                ti, (nk, kap, vap, D0, rs, prv) = order[idx]
                                                pS, pSR = PS[psbanks[i % 3]], PSR[psbanks[i % 3]]
                                                diag = (D0 >= 0)
                                                ez, ezr = wf.get()
                                                bkw = dict(bias=vbt[0:nk, 0:1]) if prv else {}
                                                S.op("act", lambda a: a.activation(
                                                    out=ez[0:nk, 0:nq], in_=pS[0:nk, 0:nq], func=AF.Exp, scale=0.125, **bkw), r=[pSR, cR], w=[ezr])
                                                sp, spr = wb_.get()
                                                S.op("act", lambda a: a.activation(
                                                    out=sp[0:nk, 0:nq], in_=ez[0:nk, 0:nq], func=AF.Ln, bias=1.0), r=[ezr], w=[spr])
                                                t1, t1r = t1p.get()
                                                S.op("dve", lambda v: v.scalar_tensor_tensor(
                                                    out=t1[0:nk, 0:nq], in0=pS[0:nk, 0:nq], scalar=0.125, in1=sp[0:nk, 0:nq],
                                                    op0=ALU.mult, op1=ALU.subtract), r=[pSR, spr], w=[t1r])
                                                if diag:
                                                    spm, spmr = wb_.get()
                                                    S.op("dve", lambda v: v.tensor_tensor(
                                                        out=spm[0:nk, 0:nq], in0=sp[0:nk, 0:nq], in1=mask_ap("A", D0, nk), op=ALU.mult),
                                                        r=[spr, cR], w=[spmr])
                                                else:
                                                    spm, spmr = sp, spr
                                                st[i] = (spm, spmr, t1, t1r)

                                            def s2a(i):
                                                h, idx = items[i]
                                                ti, (nk, kap, vap, D0, rs, prv) = order[idx]
                                                lb = 2 if h % 2 == 0 else 6
                                                psL, psLR = PS[lb], PSR[lb]
                                                spm, spmr, t1, t1r = st[i]
                                                fns = []
                                                rr = [spmr, cR]
                                                if idx > 0:
                                                    psp, pspr, pnk = st[("prev", h % 2)]
                                                    fns.append(lambda pe: pe.matmul(
                                                        psL[:, 0:nq], lhsT=TRIC[0:pnk, :], rhs=psp[0:pnk, 0:nq],
                                                        start=False, stop=False, skip_group_check=True))
                                                    rr.append(pspr)
                                                fns.append(lambda pe: pe.matmul(
                                                    psL[:, 0:nq], lhsT=TRI[0:nk, :], rhs=spm[0:nk, 0:nq],
                                                    start=(idx == 0), stop=True, skip_group_check=True))
                                                S.mm(fns, r=rr, w=[psLR])
                                                st[("prev", h % 2)] = (spm, spmr, nk)

                                            def s2t(i):
                                                h, idx = items[i]
                                                ti, (nk, kap, vap, D0, rs, prv) = order[idx]
                                                lb = 2 if h % 2 == 0 else 6
                                                psL, psLR = PS[lb], PSR[lb]
                                                spm, spmr, t1, t1r = st[i]
                                                S.op("dve", lambda v: v.tensor_tensor(
                                                    out=t1[0:nk, 0:nq], in0=t1[0:nk, 0:nq], in1=psL[0:nk, 0:nq], op=ALU.subtract),
                                                    r=[psLR], w=[t1r])

                                            def s2b(i):
                                                h, idx = items[i]
                                                hp, r0 = h // 2, 64 * (h % 2)
                                                ti, (nk, kap, vap, D0, rs, prv) = order[idx]
                                                diag = (D0 >= 0)
                                                lb = 2 if h % 2 == 0 else 6
                                                psL, psLR = PS[lb], PSR[lb]
                                                psO, psOR = PS[3 + (h % 2)], PSR[3 + (h % 2)]
                                                spm, spmr, t1, t1r = st.pop(i)
                                                wt_, wtr = wb_.get()
                                                bkw = dict(bias=vbt[0:nk, 0:1]) if prv else {}
                                                S.op("act", lambda a: a.activation(
                                                    out=wt_[0:nk, 0:nq], in_=t1[0:nk, 0:nq], func=AF.Exp, **bkw), r=[t1r, cR], w=[wtr])
                                                if diag:
                                                    S.op("dve", lambda v: v.tensor_tensor(
                                                        out=wt_[0:nk, 0:nq], in0=wt_[0:nk, 0:nq], in1=mask_ap("A", D0, nk), op=ALU.mult),
                                                        r=[cR], w=[wtr])
                                                S.mm([lambda pe: pe.matmul(
                                                    psO[:, 0:nq], lhsT=Vsb[0:nk, ti, hp * 128:(hp + 1) * 128], rhs=wt_[0:nk, 0:nq],
                                                    start=(idx == 0), stop=(idx == ntl - 1), skip_group_check=True)],
                                                    r=[wtr, VR], w=[psOR])
                                                if idx == ntl - 1:
                                                    S.op("dve", lambda v: v.tensor_tensor(
                                                        out=gatedT[r0:r0 + 64, hp, col0:col0 + nq], in0=psO[r0:r0 + 64, 0:nq],
                                                        in1=gT[r0:r0 + 64, hp, 0:nq], op=ALU.mult), r=[psOR, gTR], w=[gatedR])

                                            nit = len(items)
                                            for i in range(min(3, nit)):
                                                s1q(i)
                                            s1(0)
                                            if nit > 3:
                                                s1q(3)
                                            if nit > 1:
                                                s1(1)
                                            s2a(0)
                                            for i in range(nit):
                                                if i + 4 < nit:
                                                    s1q(i + 4)
                                                s2t(i)
                                                if i + 1 < nit:
                                                    s2a(i + 1)
                                                if i + 2 < nit:
                                                    s1(i + 2)
                                                s2b(i)
                                        elif br in ("B", "D"):
                                            ntl = len(tiles)
                                            items = [(2 * hp_ + hf_, idx) for hp_ in range(4) for idx in range(ntl) for hf_ in range(2)]
                                            st = {}

                                            def s1(i):
                                                h, idx = items[i]
                                                hp, r0 = h // 2, 64 * (h % 2)
                                                nk, kap, vap, D0, rs, prv = tiles[idx]
                                                ti = idx
                                                pS, pSR = PS[i % 3], PSR[i % 3]
                                                if br == "B":
                                                    S.mm([lambda pe: pe.matmul(
                                                        pS[0:nk, 0:nq], lhsT=KT[r0:r0 + 64, hp, ti * 128:ti * 128 + nk],
                                                        rhs=qT[r0:r0 + 64, hp, 0:nq], start=True, stop=True)], r=[KTR, qTR], w=[pSR])
                                                    scale = 0.125
                                                else:
                                                    S.mm([lambda pe: pe.matmul(
                                                        pS[0:nk, 0:nq], lhsT=KT[0:96, h, ti * 128:ti * 128 + nk],
                                                        rhs=qT[0:96, h, 0:nq], start=True, stop=True)], r=[KTR, qTR], w=[pSR])
                                                    scale = MLA_SCALE
                                                e, er = wb_.get()
                                                bkw = dict(bias=vbt[0:nk, 0:1]) if prv else {}
                                                S.op("act", lambda a: a.activation(
                                                    out=e[0:nk, 0:nq], in_=pS[0:nk, 0:nq], func=AF.Exp, scale=scale, **bkw), r=[pSR, cR], w=[er])
                                                if br == "B":
                                                    c0 = 384 - D0
                                                    S.op("dve", lambda v: v.tensor_tensor(
                                                        out=e[0:nk, 0:nq], in0=e[0:nk, 0:nq], in1=EGB[0:nk, h, c0:c0 + nq], op=ALU.mult),
                                                        r=[EGBR], w=[er])
                                                    if isP:
                                                        S.op("dve", lambda g: g.tensor_tensor(
                                                            out=e[0:nk, 0:nq], in0=e[0:nk, 0:nq], in1=mask_ap("B", D0, nk), op=ALU.mult),
                                                            r=[cR], w=[er])
                                                else:
                                                    if isP and D0 >= 0:
                                                        S.op("dve", lambda v: v.tensor_tensor(
                                                            out=e[0:nk, 0:nq], in0=e[0:nk, 0:nq], in1=mask_ap("CD", D0, nk), op=ALU.mult),
                                                            r=[cR], w=[er])
                                                st[i] = (e, er)

                                            def s2(i):
                                                h, idx = items[i]
                                                hp, r0 = h // 2, 64 * (h % 2)
                                                nk, kap, vap, D0, rs, prv = tiles[idx]
                                                ti = idx
                                                psO, psOR = PS[3 + (h % 2)], PSR[3 + (h % 2)]
                                                psD, psDR = PS[5 + (h % 2)], PSR[5 + (h % 2)]
                                                e, er = st.pop(i)
                                                S.mm([lambda pe: pe.matmul(
                                                    psO[:, 0:nq], lhsT=Vsb[0:nk, ti, hp * 128:(hp + 1) * 128], rhs=e[0:nk, 0:nq],
                                                    start=(idx == 0), stop=(idx == ntl - 1), skip_group_check=True)],
                                                    r=[er, VR], w=[psOR])
                                                S.mm([lambda pe: pe.matmul(
                                                    psD[:, 0:nq], lhsT=ONES[0:nk, :], rhs=e[0:nk, 0:nq],
                                                    start=(idx == 0), stop=(idx == ntl - 1), skip_group_check=True)],
                                                    r=[er, cR], w=[psDR])
                                                if idx == ntl - 1:
                                                    rc, rcr = wf.get()
                                                    S.op("act", lambda a: a.activation(out=rc[r0:r0 + 64, 0:nq], in_=psD[r0:r0 + 64, 0:nq], func=AF.Ln),
                                                         r=[psDR], w=[rcr])
                                                    S.op("act", lambda a: a.activation(out=rc[r0:r0 + 64, 0:nq], in_=rc[r0:r0 + 64, 0:nq], func=AF.Exp, scale=-1.0),
                                                         r=[], w=[rcr])
                                                    S.op("dve", lambda v: v.tensor_tensor(
                                                        out=rc[r0:r0 + 64, 0:nq], in0=psO[r0:r0 + 64, 0:nq], in1=rc[r0:r0 + 64, 0:nq], op=ALU.mult),
                                                        r=[psOR], w=[rcr])
                                                    S.op("pool", lambda g: g.tensor_tensor(
                                                        out=gatedT[r0:r0 + 64, 4 * bi + hp, col0:col0 + nq], in0=rc[r0:r0 + 64, 0:nq],
                                                        in1=gT[r0:r0 + 64, hp, 0:nq], op=ALU.mult), r=[rcr, gTR], w=[gatedR])

                                            run_pipe(items, s1, s2)
                                        else:
                                            ntl = len(tiles)
                                            items = [(h, m, idx) for h in range(4) for idx in range(ntl) for m in range(2)]
                                            st = {}

                                            def s1(i):
                                                h, m, idx = items[i]
                                                r0 = 64 * m
                                                nk, kap, vap, D0, rs, prv = tiles[idx]
                                                ti = idx
                                                pS, pSR = PS[i % 3], PSR[i % 3]
                                                S.mm([lambda pe: pe.matmul(
                                                    pS[0:nk, 0:nq], lhsT=KT[r0:r0 + 64, h, ti * 128:ti * 128 + nk],
                                                    rhs=qT[r0:r0 + 64, h, 0:nq], start=True, stop=True)], r=[KTR, qTR], w=[pSR])
                                                e, er = wb_.get()
                                                if D0 <= -512:
                                                    S.op("act", lambda a: a.activation(
                                                        out=e[0:nk, 0:nq], in_=pS[0:nk, 0:nq], func=AF.Exp, scale=0.125,
                                                        bias=(satCv if prv else satC)[0:nk, h:h + 1]), r=[pSR, cR], w=[er])
                                                else:
                                                    bkw = dict(bias=vbt[0:nk, 0:1]) if prv else {}
                                                    S.op("act", lambda a: a.activation(
                                                        out=e[0:nk, 0:nq], in_=pS[0:nk, 0:nq], func=AF.Exp, scale=0.125, **bkw), r=[pSR, cR], w=[er])
                                                    c0 = 384 - D0
                                                    S.op("dve", lambda v: v.tensor_tensor(
                                                        out=e[0:nk, 0:nq], in0=e[0:nk, 0:nq], in1=EGC[0:nk, h, c0:c0 + nq], op=ALU.mult),
                                                        r=[EGCR], w=[er])
                                                    if isP and D0 >= 0:
                                                        S.op("dve", lambda g: g.tensor_tensor(
                                                            out=e[0:nk, 0:nq], in0=e[0:nk, 0:nq], in1=mask_ap("CD", D0, nk), op=ALU.mult),
                                                            r=[cR], w=[er])
                                                st[i] = (e, er)

                                            def s2(i):
                                                h, m, idx = items[i]
                                                nk, kap, vap, D0, rs, prv = tiles[idx]
                                                ti = idx
                                                psO, psOR = PS[3 + m], PSR[3 + m]
                                                psD, psDR = PS[5 + m], PSR[5 + m]
                                                e, er = st.pop(i)
                                                S.mm([lambda pe: pe.matmul(
                                                    psO[:, 0:nq], lhsT=Vsb[0:nk, ti, h * 128:(h + 1) * 128], rhs=e[0:nk, 0:nq],
                                                    start=(idx == 0), stop=(idx == ntl - 1), skip_group_check=True)],
                                                    r=[er, VR], w=[psOR])
                                                S.mm([lambda pe: pe.matmul(
                                                    psD[:, 0:nq], lhsT=ONES[0:nk, :], rhs=e[0:nk, 0:nq],
                                                    start=(idx == 0), stop=(idx == ntl - 1), skip_group_check=True)],
                                                    r=[er, cR], w=[psDR])
                                                if idx == ntl - 1:
                                                    a_, ar = wf.get()
                                                    S.op("act", lambda a: a.activation(out=a_[:, 0:nq], in_=psD[:, 0:nq], func=AF.Ln), r=[psDR], w=[ar])
                                                    S.op("act", lambda a: a.activation(out=a_[:, 0:nq], in_=a_[:, 0:nq], func=AF.Exp, scale=-1.0), r=[], w=[ar])
                                                    S.op("dve", lambda v: v.tensor_tensor(
                                                        out=a_[:, 0:nq], in0=psO[:, 0:nq], in1=a_[:, 0:nq], op=ALU.mult), r=[psOR], w=[ar])
                                                    st[("acc", m)] = (a_, ar)
                                                    if m == 1:
                                                        a0, a0r = st.pop(("acc", 0))
                                                        a1, a1r = st.pop(("acc", 1))
                                                        S.op("dve", lambda v: v.scalar_tensor_tensor(
                                                            out=a0[:, 0:nq], in0=a1[:, 0:nq], scalar=lcol[:, 0:1], in1=a0[:, 0:nq],
                                                            op0=ALU.mult, op1=ALU.add), r=[a1r, lR], w=[a0r])
                                                        sq, sqr = wb_.get()
                                                        S.op("act", lambda a: a.activation(out=sq[:, 0:nq], in_=a0[:, 0:nq], func=AF.Square),
                                                             r=[a0r], w=[sqr])
                                                        S.mm([lambda pe: pe.matmul(PS[2][:, 0:nq], lhsT=ONES, rhs=sq[:, 0:nq],
                                                                                   start=True, stop=True)], r=[sqr, cR], w=[PSR[2]])
                                                        S.op("dve", lambda v: v.tensor_scalar(
                                                            out=a1[:, 0:nq], in0=PS[2][:, 0:nq], scalar1=1.0 / 128, scalar2=EPS,
                                                            op0=ALU.mult, op1=ALU.add), r=[PSR[2]], w=[a1r])
                                                        S.op("act", lambda a: a.activation(out=a1[:, 0:nq], in_=a1[:, 0:nq], func=AF.Ln),
                                                             r=[], w=[a1r])
                                                        S.op("act", lambda a: a.activation(out=a1[:, 0:nq], in_=a1[:, 0:nq], func=AF.Exp, scale=-0.5),
                                                             r=[], w=[a1r])
                                                        S.op("pool", lambda g: g.tensor_tensor(
                                                            out=a0[:, 0:nq], in0=a0[:, 0:nq], in1=a1[:, 0:nq], op=ALU.mult), r=[a1r], w=[a0r])
                                                        S.op("dve", lambda g: g.scalar_tensor_tensor(
                                                            out=gatedT[:, 8 + h, col0:col0 + nq], in0=a0[:, 0:nq], scalar=lcol[:, 1:2],
                                                            in1=gT[:, h, 0:nq], op0=ALU.mult, op1=ALU.mult), r=[a0r, lR, gTR], w=[gatedR])

                                            run_pipe(items, s1, s2)
                                S.barrier()
                                _chk(4)

                            with contextlib.ExitStack() as s4:
                                mergedT = sb("mergedT", [128, DC, 640], BF16, s4)
                                mrgR = R("merged")
                                with contextlib.ExitStack() as s4a:
                                    wbp = Pool(s4a, nc, "wbp", 2, [128, 16, 512], BF16)
                                    mgp = Pool(s4a, nc, "mgp", 3, [128, 4, 512], BF16)
                                    pfb = Pool(s4a, nc, "pfb", 10, [128, 512], BF16)
                                    tok0 = 512 * blk
                                    pend = [None]
                                    acnt = [0]

                                    def flush_acc():
                                        if pend[0] is None:
                                            return
                                        dcp, prods, cc0, ccn, abi = pend[0]
                                        pend[0] = None
                                        ab = 4 + abi % 2
                                        S.mm([lambda pe, p_=p_, n=n: pe.matmul(
                                            PS[ab][:, 0:ccn], lhsT=identb[:], rhs=p_[:, 0:ccn], start=(n == 0), stop=(n == 3))
                                            for n, (p_, p_r) in enumerate(prods)], r=[p_r for (_, p_r) in prods] + [cR], w=[PSR[ab]])
                                        S.op("act", lambda a: a.copy(out=mergedT[:, dcp, cc0:cc0 + ccn], in_=PS[ab][:, 0:ccn]),
                                             r=[PSR[ab]], w=[mrgR])

                                    for g4 in range(4):
                                        wt, wr = wbp.get()
                                        S.dma("pool", wt[:],
                                              wbr[l, :, :, g4 * 512:(g4 + 1) * 512].rearrange("n (c p) d -> p (n c) d", p=128), w=[wr])
                                        for j4 in range(4):
                                            dc = 4 * g4 + j4
                                            for (cc0, ccn) in CPS:
                                                mg, mgr = mgp.get()
                                                S.dma("sp", mg[:, :, 0:ccn],
                                                      mT_scr[:, tok0 + cc0:tok0 + cc0 + ccn].rearrange("(n c p) t -> p n c t", p=128, c=16)[:, :, dc, :],
                                                      r=[rfm("m", t_) for t_ in MTB], w=[mgr])
                                                prods = []
                                                for n in range(4):
                                                    S.mm([lambda pe, n=n, c=c: pe.matmul(
                                                        PS[n][:, 0:ccn], lhsT=wt[:, 4 * n + c, j4 * 128:(j4 + 1) * 128],
                                                        rhs=gatedT[:, 4 * n + c, cc0:cc0 + ccn],
                                                        start=(c == 0), stop=(c == 3)) for c in range(4)], r=[wr, gatedR], w=[PSR[n]])
                                                    p_, p_r = pfb.get()
                                                    S.op("dve", lambda v, p_=p_, n=n: v.tensor_tensor(
                                                        out=p_[:, 0:ccn], in0=PS[n][:, 0:ccn], in1=mg[:, n, 0:ccn], op=ALU.mult),
                                                        r=[PSR[n], mgr], w=[p_r])
                                                    prods.append((p_, p_r))
                                                flush_acc()
                                                acnt[0] += 1
                                                pend[0] = (dc, prods, cc0, ccn, acnt[0])
                                    flush_acc()
                                    S.barrier()
                                    _chk(5)
                                with contextlib.ExitStack() as s4b:
                                    wop = Pool(s4b, nc, "wop", 2, [128, DC, 512], BF16)
                                    ntk = NQB // 128
                                    ysb = [sb(f"ysb{i}", [128, D], F32, s4b) for i in range(ntk)]
                                    ysR = [R(f"ysb{i}") for i in range(ntk)]
                                    gpost = sb("gpost", [128, D], F32, s4b)
                                    gpR = R("gpost")
                                    S.dma("sp", gpost[:], bass.AP(tensor=norm_post.tensor, offset=l * D, ap=[[0, 128], [1, D]]), w=[gpR])
                                    xin = Pool(s4b, nc, "xin", 2, [128, D], F32)
                                    jk4 = sb("jk4", [128, D], BF16, s4b)
                                    jk4R = R("jk4")
                                    st4 = Pool(s4b, nc, "st4", 2, [128, 4], F32)
                                    for eb in range(4):
                                        wt, wr = wop.get()
                                        S.dma("pool", wt[:], wout[l, :, eb * 512:(eb + 1) * 512].rearrange("(c p) n -> p c n", p=128), w=[wr])
                                        for tk in range(ntk):
                                            pi = 4 + (eb * ntk + tk) % 3
                                            S.mm([lambda pe, wt=wt, c=c, tk=tk, pi=pi: pe.matmul(
                                                PS[pi][:, :], lhsT=mergedT[:, c, tk * 128:(tk + 1) * 128], rhs=wt[:, c, :],
                                                start=(c == 0), stop=(c == DC - 1)) for c in range(DC)], r=[wr, mrgR], w=[PSR[pi]])
                                            S.op("act", lambda a, tk=tk, eb=eb, pi=pi: a.copy(out=ysb[tk][:, eb * 512:(eb + 1) * 512], in_=PS[pi][:, :]),
                                                 r=[PSR[pi]], w=[ysR[tk]])
                                    for tk in range(ntk):
                                        tt = (512 * blk) // 128 + tk
                                        xt, xr = xin.get()
                                        if l == 0:
                                            src = xp[tt * 128:(tt + 1) * 128, :] if tt < NPT else xs[:, :]
                                            S.dma("sp", xt[:], src, w=[xr])
                                        else:
                                            S.dma("sp", xt[:], x1[tt * 128:(tt + 1) * 128, :], r=[RX[tt]], w=[xr])
                                        ss, ssr = st4.get()
                                        S.op("dve", lambda v, ss=ss: v.memset(ss[:], 0.0), w=[ssr])
                                        S.op("act", lambda a, tk=tk, ss=ss: a.activation(out=jk4[:], in_=ysb[tk][:], func=AF.Square,
                                                                                        accum_out=ss[:, 0:1]), r=[ysR[tk]], w=[jk4R, ssr])
                                        S.op("dve", lambda v, ss=ss: v.tensor_scalar(out=ss[:, 1:2], in0=ss[:, 0:1], scalar1=1.0 / D,
                                                                                      scalar2=EPS, op0=ALU.mult, op1=ALU.add), r=[ssr], w=[ssr])
                                        S.op("act", lambda a, ss=ss: a.activation(out=ss[:, 2:3], in_=ss[:, 1:2], func=AF.Sqrt), r=[ssr], w=[ssr])
                                        S.op("dve", lambda v, ss=ss: v.reciprocal(out=ss[:, 3:4], in_=ss[:, 2:3]), r=[ssr], w=[ssr])
                                        S.op("dve", lambda v, tk=tk, ss=ss: v.scalar_tensor_tensor(
                                            out=ysb[tk][:], in0=ysb[tk][:], scalar=ss[:, 3:4], in1=gpost[:], op0=ALU.mult, op1=ALU.mult),
                                            r=[ssr, gpR], w=[ysR[tk]])
                                        S.op("pool", lambda g, tk=tk, xt=xt: g.tensor_tensor(out=ysb[tk][:], in0=ysb[tk][:], in1=xt[:], op=ALU.add),
                                             r=[xr], w=[ysR[tk]])
                                        if last:
                                            dst = yp[tt * 128:(tt + 1) * 128, :] if tt < NPT else ys[:, :]
                                            S.dma("sp", dst, ysb[tk][:], r=[ysR[tk]], w=[R("yo")])
                                        else:
                                            S.dma("sp", x1[tt * 128:(tt + 1) * 128, :], ysb[tk][:], r=[ysR[tk]], w=[RX[tt]])
                                    S.barrier()
                                    _chk(6)
                    S.barrier()
                    _chk(7)
        except _StopBuild:
            pass
        S.finish()
    return nc


def _t5_bucket(rel):
    rel = np.asarray(rel, dtype=np.int64)
    nb = 16
    max_exact = 8
    ret = np.where(rel > 0, nb, 0)
    n = np.abs(rel)
    nf = np.maximum(n, 1).astype(np.float32)
    large = max_exact + (np.log(nf / np.float32(max_exact)) / np.float32(math.log(512 / max_exact))
                         * np.float32(nb - max_exact)).astype(np.int32)
    large = np.minimum(large, nb - 1)
    return ret + np.where(n < max_exact, n, large)


def _consts():
    ident = np.eye(128, dtype=np.float32)
    jm = np.ascontiguousarray(ident[::-1])
    j = np.arange(128)[:, None]
    s = np.arange(128)[None, :]
    tri = np.stack([(j > s), (j <= s), np.ones((128, 128), bool)]).astype(np.float32)
    i = np.arange(128)[:, None]
    q = np.arange(512)[None, :]
    masks = np.zeros((16, 128, 512), np.float32)
    for d in range(4):
        D0 = 128 * d
        masks[d] = (D0 + i < q)
        masks[4 + d] = ((D0 + i) // 64 <= q // 64)
    for m in range(8):
        D0 = -512 + 128 * m
        dd = q // 64 - (D0 + i) // 64
        masks[8 + m] = (dd >= 0) & (dd <= 8)
    t = np.arange(LC)
    b = _t5_bucket(511 - t)
    oh = (b[None, :] == np.arange(32)[:, None]).astype(np.float32)
    inv = (10000.0 ** (-np.arange(0, 32, 2, dtype=np.float32) / np.float32(32))).astype(np.float32)
    css = []
    for p in range(2):
        pos = np.concatenate([TP * p + np.arange(TP), PAST + np.arange(64), np.zeros(64)]).astype(np.float32)
        ang = pos[:, None] * inv[None, :]
        css.append(np.concatenate([np.cos(ang), np.sin(ang)], axis=1).astype(np.float32))
    return dict(c_ident=ident, c_j=jm, c_tri=tri, c_masks=masks, c_oh=oh), css


_NC = None


def kernel(x_prompt, x_sample, cache_sb_k, cache_sb_v, cache_band_k, cache_band_v,
           cache_diff_k, cache_diff_v, cache_mla_ckv, cache_mla_krope,
           norm_pre, norm_post, w_in, band_bias, t5_table, diff_lambda, diff_subln,
           mla_q_norm, mla_w_q_up, mla_kv_norm, mla_w_kv_up, w_branch, w_out):
    global _NC
    f = lambda a: np.ascontiguousarray(np.asarray(a, dtype=np.float32))
    if _NC is None:
        _NC = build_nc()
    nc = _NC
    consts, css = _consts()
    shared = dict(norm_pre=f(norm_pre), norm_post=f(norm_post), w_in=f(w_in), band_bias=f(band_bias), t5=f(t5_table),
                  dlam=f(diff_lambda).reshape(2, 256), subln=f(diff_subln), qnorm=f(mla_q_norm), wqup=f(mla_w_q_up),
                  kvnorm=f(mla_kv_norm), wkvup=f(mla_w_kv_up), wbr=f(w_branch), wout=f(w_out), **consts)
    in_maps = []
    for c in range(8):
        b, p = c // 2, c % 2
        m = dict(shared)
        m["c_cs"] = css[p]
        m["c_vb"] = np.full((128, 1), 0.0 if p == 1 else -30000.0, np.float32)
        m["xp"] = f(x_prompt[b, TP * p:TP * (p + 1)])
        xs_ = np.zeros((128, D), np.float32)
        xs_[:64] = np.asarray(x_sample[c], dtype=np.float32)
        m["xs"] = xs_
        cs_ = slice(c, c + 1)
        m["csbk"] = f(cache_sb_k[:, cs_]).reshape(2, 1, PAST, 512)
        m["csbv"] = f(cache_sb_v[:, cs_]).reshape(2, 1, PAST, 512)
        m["cbk"] = f(cache_band_k[:, cs_]).reshape(2, 1, 512, 512)
        m["cbv"] = f(cache_band_v[:, cs_]).reshape(2, 1, 512, 512)
        m["cdk"] = f(cache_diff_k[:, cs_]).reshape(2, 1, PAST, 512)
        m["cdv"] = f(cache_diff_v[:, cs_]).reshape(2, 1, PAST, 512)
        m["cckv"] = f(cache_mla_ckv[:, cs_])
        m["ckr"] = f(cache_mla_krope[:, cs_])
        in_maps.append(m)
    ncores = _NCORES[0]
    res = run_bass_kernel_spmd(nc, in_maps[:ncores], core_ids=list(range(ncores)))
    rs = [res.results[i if i < ncores else i % 2] for i in range(8)]

    def cat_p(name, shape_tail):
        a = np.stack([np.concatenate([rs[2 * b][name], rs[2 * b + 1][name]], axis=1) for b in range(4)], axis=1)
        return a.reshape((2, 4) + shape_tail).astype(np.float32)

    def cat_b(name, shape_tail):
        a = np.stack([rs[2 * b + 1][name] for b in range(4)], axis=1)
        return a.reshape((2, 4) + shape_tail).astype(np.float32)

    def cat_s(name, shape_tail):
        return np.concatenate([r[name] for r in rs], axis=1).reshape((2, 8) + shape_tail).astype(np.float32)

    y_p = np.stack([np.concatenate([rs[2 * b]["yp"], rs[2 * b + 1]["yp"]], axis=0) for b in range(4)], axis=0).astype(np.float32)
    y_s = np.stack([r["ys"][:64] for r in rs], axis=0).astype(np.float32)
    S2 = 2 * TP
    return (y_p, y_s,
            cat_p("sbk_p", (S2, 8, 64)), cat_p("sbv_p", (S2, 8, 64)),
            cat_b("bk_p", (512, 8, 64)), cat_b("bv_p", (512, 8, 64)),
            cat_p("dk_p", (S2, 4, 2, 64)), cat_p("dv_p", (S2, 4, 128)),
            cat_p("ckv_p", (S2, 128)), cat_p("kr_p", (S2, 32)),
            cat_s("sbk_s", (64, 8, 64)), cat_s("sbv_s", (64, 8, 64)),
            cat_s("bk_s", (512, 8, 64)), cat_s("bv_s", (512, 8, 64)),
            cat_s("dk_s", (64, 4, 2, 64)), cat_s("dv_s", (64, 4, 128)),
            cat_s("ckv_s", (64, 128)), cat_s("kr_s", (64, 32)))
```

```python
import contextlib
import math
import numpy as np
import concourse.bass as bass
import concourse.mybir as mybir
from concourse.bass_utils import run_bass_kernel_spmd

F32 = mybir.dt.float32
BF16 = mybir.dt.bfloat16
AF = mybir.ActivationFunctionType
ALU = mybir.AluOpType

D = 2048
DC = 16
TP = 1024
NS = 1
NPT = TP // 128
T = TP + 128
NT = T // 128
KVW = 3232
KCOL = {'A.k': 0, 'A.v': 512, 'B.k': 1024, 'B.v': 1536, 'C.k': 2048, 'C.v': 2560, 'ckv': 3072, 'kr': 3200}
DIN = 15392
EPS = 1e-6
PAST = 1024
LC = 1408
WC = 1280
LB = 1536
WB = 1408
MLA_SCALE = 96 ** -0.5
DEPTH = 2


class R:
    __slots__ = ("w", "r", "name", "excl")

    def __init__(self, name="", excl=False):
        self.w = None
        self.r = []
        self.name = name
        self.excl = excl


class Sched:
    def __init__(self, nc, es):
        self.nc = nc
        self.eng = {"pe": nc.tensor, "act": nc.scalar, "dve": nc.vector, "pool": nc.gpsimd, "sp": nc.sync}
        self.csem = {}
        self.ccnt = {}
        for e in ("pe", "act", "dve", "pool"):
            self.csem[e] = es.enter_context(nc.semaphore("c_" + e))
            self.ccnt[e] = 0
        self.dsem = {}
        self.dcnt = {}
        self.dnext = {}
        for q, n in (("sp", 40), ("pool", 40)):
            self.dsem[q] = [es.enter_context(nc.semaphore(f"d_{q}{i}")) for i in range(n)]
            self.dcnt[q] = [0] * n
            self.dnext[q] = 0
        self.seen = {e: {} for e in self.eng}
        self.all_dma_tokens = []
        self.cccnt = 0
        self.ccsem = es.enter_context(nc.semaphore("ccsem"))

    def _wait(self, e, tok):
        sem, val, owner = tok
        if e == "pe" and owner == "pe":
            return
        key = id(sem)
        if self.seen[e].get(key, 0) >= val:
            return
        self.eng[e].wait_ge(sem, val)
        self.seen[e][key] = val

    def _deps(self, e, r, w):
        for x in r:
            if x.w is not None:
                self._wait(e, x.w)
            if x.excl:
                for t in x.r:
                    if t[2] != e:
                        self._wait(e, t)
        for x in w:
            if x.w is not None:
                self._wait(e, x.w)
            for t in x.r:
                self._wait(e, t)

    def _commit(self, tok, r, w):
        for x in r:
            x.r.append(tok)
            if len(x.r) > 24:
                x.r = x.r[-24:]
        for x in w:
            x.w = tok
            x.r = []

    def op(self, e, fn, r=(), w=()):
        if _DEAD[0]:
            return None
        self._deps(e, r, w)
        inst = fn(self.eng[e])
        self.ccnt[e] += 1
        tok = (self.csem[e], self.ccnt[e], e)
        inst.then_inc(tok[0], 1)
        self._commit(tok, r, w)
        return tok

    def mm(self, fns, r=(), w=()):
        if _DEAD[0]:
            return None
        self._deps("pe", r, w)
        inst = None
        for fn in fns:
            inst = fn(self.eng["pe"])
        self.ccnt["pe"] += 1
        tok = (self.csem["pe"], self.ccnt["pe"], "pe")
        inst.then_inc(tok[0], 1)
        self._commit(tok, r, w)
        return tok

    def dma(self, q, out, in_, r=(), w=(), **kw):
        if _DEAD[0]:
            return None
        i = self.dnext[q]
        self.dnext[q] = (i + 1) % len(self.dsem[q])
        sem = self.dsem[q][i]
        if self.dcnt[q][i]:
            self._wait(q, (sem, self.dcnt[q][i], "dma"))
        self._deps(q, r, w)
        inst = self.eng[q].dma_start(out=out, in_=in_, **kw)
        self.dcnt[q][i] += 16
        tok = (sem, self.dcnt[q][i], "dma")
        inst.then_inc(sem, 16)
        self._commit(tok, r, w)
        return tok

    def collective(self, ccsem, in_t, out_t, r=(), w=()):
        if _DEAD[0]:
            return None
        self._deps("pool", r, w)
        inst = self.eng["pool"].collective_compute(
            "AllGather", ALU.bypass, replica_groups=[[2 * i_, 2 * i_ + 1] for i_ in range(_NCORES[0] // 2)],
            ins=[in_t.ap().opt()], outs=[out_t.ap().opt()])
        self.cccnt += 1
        tok = (ccsem, self.cccnt, "cc")
        inst.then_inc(ccsem)
        self._commit(tok, r, w)
        return tok

    def barrier(self):
        if _DEAD[0]:
            return
        toks = []
        for e in ("pe", "act", "dve", "pool"):
            if self.ccnt[e]:
                toks.append((self.csem[e], self.ccnt[e], e + "_b"))
        for q in self.dsem:
            for i, s in enumerate(self.dsem[q]):
                if self.dcnt[q][i]:
                    toks.append((s, self.dcnt[q][i], "dma"))
        if self.cccnt:
            toks.append((self.ccsem, self.cccnt, "cc"))
        for e in self.eng:
            for t in toks:
                if e == "pe" and t[2] == "pe_b":
                    continue
                self._wait(e, t)

    def finish(self):
        for q in self.dsem:
            for i, s in enumerate(self.dsem[q]):
                if self.dcnt[q][i]:
                    self._wait("sp", (s, self.dcnt[q][i], "dma"))


_UID = [0]
_STOP = [99]
_NCORES = [8]
WARM = 0


class _StopBuild(Exception):
    pass


_DEAD = [False]


def _chk(n):
    if round(_STOP[0] * 1000) <= round(n * 1000):
        _DEAD[0] = True


class Pool:
    def __init__(self, es, nc, name, n, shape, dtype):
        _UID[0] += 1
        self.t = [es.enter_context(nc.sbuf_tensor(f"{name}{i}_{_UID[0]}", shape, dtype)) for i in range(n)]
        self.res = [R(f"{name}{i}") for i in range(n)]
        self.i = 0

    def get(self):
        i = self.i
        self.i = (i + 1) % len(self.t)
        return self.t[i], self.res[i]


def bc_ap(ap, dims):
    return bass.AP(tensor=ap.tensor, offset=ap.offset, ap=[list(ap.ap[0])] + [list(d) for d in dims])


def build_nc():
    nc = bass.Bass("TRN2", target_bir_lowering=False)
    _DEAD[0] = False
    dt = nc.dram_tensor

    def inp(name, shape, dtype=F32):
        return dt(name, list(shape), dtype, kind="ExternalInput").ap()

    def outp(name, shape):
        return dt(name, list(shape), F32, kind="ExternalOutput").ap()

    def scr(name, shape, dtype=F32):
        return dt(name, list(shape), dtype).ap()

    xp = inp("xp", [TP, D])
    xs = inp("xs", [128, D])
    csbk = inp("csbk", [2, NS, PAST, 512])
    csbv = inp("csbv", [2, NS, PAST, 512])
    cbk = inp("cbk", [2, NS, 512, 512])
    cbv = inp("cbv", [2, NS, 512, 512])
    cdk = inp("cdk", [2, NS, PAST, 512])
    cdv = inp("cdv", [2, NS, PAST, 512])
    cckv = inp("cckv", [2, NS, PAST, 128])
    ckr = inp("ckr", [2, NS, PAST, 32])
    norm_pre = inp("norm_pre", [2, D])
    norm_post = inp("norm_post", [2, D])
    w_in = inp("w_in", [2, D, DIN])
    band_bias = inp("band_bias", [2, 513, 8])
    t5 = inp("t5", [32, 4])
    dlam = inp("dlam", [2, 256])
    subln = inp("subln", [2, 128])
    qnorm = inp("qnorm", [2, 384])
    wqup = inp("wqup", [2, 384, 768])
    kvnorm = inp("kvnorm", [2, 128])
    wkvup = inp("wkvup", [2, 128, 1024])
    wbr = inp("wbr", [2, 4, 512, D])
    wout = inp("wout", [2, D, D])
    c_ident = inp("c_ident", [128, 128])
    c_j = inp("c_j", [128, 128])
    c_tri = inp("c_tri", [3, 128, 128])
    c_masks = inp("c_masks", [16, 128, 512])
    c_oh = inp("c_oh", [32, LC])
    c_cs = inp("c_cs", [T, 32])
    c_vb = inp("c_vb", [128, 1])

    yp = outp("yp", [TP, D])
    ys = outp("ys", [128, D])
    sbk_p = outp("sbk_p", [2, TP, 512])
    sbv_p = outp("sbv_p", [2, TP, 512])
    bk_p = outp("bk_p", [2, 512, 512])
    bv_p = outp("bv_p", [2, 512, 512])
    dk_p = outp("dk_p", [2, TP, 512])
    dv_p = outp("dv_p", [2, TP, 512])
    ckv_p = outp("ckv_p", [2, TP, 128])
    kr_p = outp("kr_p", [2, TP, 32])
    sbk_s = outp("sbk_s", [2, NS, 64, 512])
    sbv_s = outp("sbv_s", [2, NS, 64, 512])
    bk_s = outp("bk_s", [2, NS, 512, 512])
    bv_s = outp("bv_s", [2, NS, 512, 512])
    dk_s = outp("dk_s", [2, NS, 64, 512])
    dv_s = outp("dv_s", [2, NS, 64, 512])
    ckv_s = outp("ckv_s", [2, NS, 64, 128])
    kr_s = outp("kr_s", [2, NS, 64, 32])

    x1 = scr("x1", [T, D])
    kvloc_t = [dt(f"kvloc{t_}", [128, KVW], F32) for t_ in range(NPT)]
    kvloc = [t_.ap() for t_ in kvloc_t]
    kvall_t = [[dt(f"kvall{i}_{t_}", [256, KVW], F32) for t_ in range(NPT)] for i in range(2)]
    kvall = [[t_.ap() for t_ in row] for row in kvall_t]
    qT_scr = scr("qT_scr", [3, 512, T], BF16)
    gT_scr = scr("gT_scr", [2048, T], BF16)
    mT_scr = scr("mT_scr", [8192, T], BF16)
    qn_scr = scr("qn_scr", [T, 384])
    gvC_scr = scr("gvC_scr", [4, LC])
    gvB_scr = scr("gvB_scr", [8, LB])

    with contextlib.ExitStack() as es:
        S = Sched(nc, es)

        try:
            def sb(name, shape, dtype, stack=es):
                _UID[0] += 1
                return stack.enter_context(nc.sbuf_tensor(f"{name}_{_UID[0]}", list(shape), dtype))

            PS = [es.enter_context(nc.psum_tensor(f"ps{i}", [128, 512], F32)) for i in range(7)]
            PSR = [R(f"ps{i}", excl=True) for i in range(7)]
            psT = es.enter_context(nc.psum_tensor("psT", [128, 1024], BF16))
            psTR = R("psT", excl=True)

            identb = sb("identb", [128, 128], BF16)
            jm = sb("jm", [128, 128], F32)
            trib = sb("trib", [128, 3, 128], BF16)
            masks = sb("masks", [128, 16, 512], BF16)
            EGC = sb("EGC", [128, 4, WC], BF16)
            EGB = sb("EGB", [128, 8, WB], BF16)
            satC = sb("satC", [128, 4], F32)
            cR = R("consts")
            EGCR = R("EGC")
            EGBR = R("EGB")
            S.dma("pool", identb[:], c_ident[:, :], w=[cR])
            S.dma("sp", jm[:], c_j[:, :], w=[cR])
            S.dma("pool", trib[:], c_tri.rearrange("m p n -> p m n"), w=[cR])
            for m4 in range(4):
                S.dma("pool", masks[:, 4 * m4:4 * m4 + 4, :],
                      c_masks[4 * m4:4 * m4 + 4].rearrange("m p n -> p m n"), w=[cR])
            S.dma("sp", satC[:], bass.AP(tensor=t5.tensor, offset=15 * 4, ap=[[0, 128], [1, 4]]), w=[cR])
            vbt = sb("vbt", [128, 1], F32)
            satCv = sb("satCv", [128, 4], F32)
            S.dma("sp", vbt[:], c_vb[:, :], w=[cR])
            S.op("dve", lambda v: v.tensor_scalar(out=satCv[:], in0=satC[:], scalar1=vbt[:, 0:1], scalar2=None, op0=ALU.add),
                 r=[cR], w=[cR])
            kvallR = [[R(f"kvall{i}_{t_}") for t_ in range(NPT)] for i in range(2)]
            TRI = trib[:, 0, :]
            TRIC = trib[:, 1, :]
            ONES = trib[:, 2, :]

            def toeplitz_build(gv_scr, nheads, L, W, EG, EGR, stack, defer=None):
                nb = 1 if defer is None else 2
                hks = [sb("hk", [128, W], F32, stack) for _ in range(nb)]
                hkRs = [R("hk") for _ in range(nb)]

                def dma_step(h):
                    src = bass.AP(tensor=gv_scr.tensor, offset=h * L, ap=[[1, 128], [1, W]])
                    S.dma("sp", hks[h % nb][:], src, r=[gvR], w=[hkRs[h % nb]])

                def comp_step(h, bank=None):
                    hk, hkR = hks[h % nb], hkRs[h % nb]
                    c = 0
                    bi = 0
                    while c < W:
                        n = min(512, W - c)
                        if bank is None:
                            ps_, pr_ = PS[bi % 2], PSR[bi % 2]
                        else:
                            ps_, pr_ = bank()
                        S.mm([lambda pe: pe.matmul(ps_[:, 0:n], lhsT=jm[:], rhs=hk[:, c:c + n], start=True, stop=True)],
                             r=[hkR, cR], w=[pr_])
                        S.op("act", lambda a: a.activation(out=EG[:, h, c:c + n], in_=ps_[:, 0:n], func=AF.Exp),
                             r=[pr_], w=[EGR])
                        c += n
                        bi += 1

                if defer is None:
                    for h in range(nheads):
                        dma_step(h)
                        comp_step(h)
                else:
                    defer.append(lambda: dma_step(0))
                    for h in range(nheads):
                        if h + 1 < nheads:
                            defer.append(lambda h=h: dma_step(h + 1))
                        defer.append(lambda h=h: comp_step(h, bank=defer_bank[0]))

            defer_bank = [None]
            gvR = R("gv")
            with contextlib.ExitStack() as st0:
                t5sb = sb("t5sb", [32, 4], F32, st0)
                ohsb = sb("ohsb", [32, LC], F32, st0)
                gvsb = sb("gvsb", [4, LC], F32, st0)
                tr = R("t0")
                S.dma("sp", t5sb[:], t5[:, :], w=[tr])
                S.dma("sp", ohsb[:], c_oh[:, :], w=[tr])
                c = 0
                gvsR = R("gvs")
                while c < LC:
                    n = min(512, LC - c)
                    S.mm([lambda pe, c=c, n=n: pe.matmul(PS[0][0:4, 0:n], lhsT=t5sb[:], rhs=ohsb[:, c:c + n],
                                                         start=True, stop=True)], r=[tr], w=[PSR[0]])
                    S.op("dve", lambda v, c=c, n=n: v.tensor_copy(out=gvsb[:, c:c + n], in_=PS[0][0:4, 0:n]),
                         r=[PSR[0]], w=[gvsR])
                    c += n
                S.dma("sp", gvC_scr[:, :], gvsb[:], r=[gvsR], w=[gvR])
                toeplitz_build(gvC_scr, 4, LC, WC, EGC, EGCR, st0)
                S.barrier()
                _chk(0)

            RX = [R(f"x{t}") for t in range(NT)]
            RKV = {}

            def rkv(name, i):
                k = (name, i)
                if k not in RKV:
                    RKV[k] = R(str(k))
                return RKV[k]

            RFM = {}

            def rfm(name, tb):
                k = (name, tb)
                if k not in RFM:
                    RFM[k] = R(str(k))
                return RFM[k]

            TB = [(0, 512), (512, 512), (1024, 128)]

            for l in range(DEPTH):
                lam_init = 0.8 - 0.6 * math.exp(-0.3 * l)
                last = (l == DEPTH - 1)
                with contextlib.ExitStack() as sl:
                    wqupb = sb("wqupb", [128, 3, 768], BF16, sl)
                    wkvupb = sb("wkvupb", [128, 1024], BF16, sl)
                    lcol = sb("lcol", [128, 8], F32, sl)
                    lR = R("layerconst")
                    S.dma("pool", wqupb[:], wqup[l].rearrange("(c p) n -> p c n", p=128), w=[lR])
                    S.dma("pool", wkvupb[:], wkvup[l], w=[lR])

                    with contextlib.ExitStack() as s2:
                        hT = sb("hT", [128, DC, T], BF16, s2)
                        hTR = [R(f"hT{t}") for t in range(NT)]
                        DEFER = []
                        bb = sb("bb", [8, 513], F32, s2)
                        gvb = sb("gvb", [8, LB], F32, s2)
                        bbR = R("bb")
                        S.dma("sp", bb[:], bass.AP(tensor=band_bias.tensor, offset=l * 513 * 8, ap=[[1, 8], [8, 513]]),
                              w=[bbR], allow_slow_non_contiguous=True)
                        gvbR = R("gvb")
                        S.op("dve", lambda v: v.tensor_copy(out=gvb[:, 0:255], in_=bc_ap(bb[:, 0:1], [[0, 255]])),
                             r=[bbR], w=[gvbR])
                        S.op("dve", lambda v: v.tensor_copy(out=gvb[:, 255:768], in_=bb[:, 0:513]), r=[bbR], w=[gvbR])
                        S.op("dve", lambda v: v.tensor_copy(out=gvb[:, 768:LB], in_=bc_ap(bb[:, 512:513], [[0, LB - 768]])),
                             r=[bbR], w=[gvbR])
                        S.dma("sp", gvB_scr[:, :], gvb[:], r=[gvbR], w=[gvR])
                        toeplitz_build(gvB_scr, 8, LB, WB, EGB, EGBR, s2, defer=DEFER)
                        dl = sb("dl", [128, 256], F32, s2)
                        dj = sb("dj", [128, 128], F32, s2)
                        sg = sb("sg", [128, 128], F32, s2)
                        la = sb("la", [128, 8], F32, s2)
                        dR = R("dl")
                        S.dma("sp", dl[:], bass.AP(tensor=dlam.tensor, offset=l * 256, ap=[[0, 128], [1, 256]]), w=[dR])
                        S.dma("sp", sg[:, 0:1], bass.AP(tensor=subln.tensor, offset=l * 128, ap=[[1, 128], [1, 1]]), w=[dR])
                        S.op("dve", lambda v: v.tensor_tensor(out=dj[:, 0:64], in0=dl[:, 0:64], in1=dl[:, 64:128], op=ALU.mult),
                             r=[dR], w=[dR])
                        S.op("dve", lambda v: v.tensor_tensor(out=dj[:, 64:128], in0=dl[:, 128:192], in1=dl[:, 192:256], op=ALU.mult),
                             r=[dR], w=[dR])
                        S.op("dve", lambda v: v.reduce_sum(out=la[:, 0:1], in_=dj[:, 0:64], axis=mybir.AxisListType.X), r=[dR], w=[dR])
                        S.op("dve", lambda v: v.reduce_sum(out=la[:, 1:2], in_=dj[:, 64:128], axis=mybir.AxisListType.X), r=[dR], w=[dR])
                        S.op("act", lambda a: a.activation(out=la[:, 2:4], in_=la[:, 0:2], func=AF.Exp), r=[dR], w=[dR])
                        S.op("dve", lambda v: v.tensor_tensor(out=la[:, 4:5], in0=la[:, 3:4], in1=la[:, 2:3], op=ALU.subtract),
                             r=[dR], w=[dR])
                        S.op("dve", lambda v: v.tensor_scalar(out=lcol[:, 0:1], in0=la[:, 4:5], scalar1=-lam_init, scalar2=None,
                                                              op0=ALU.add), r=[dR], w=[lR])
                        S.op("dve", lambda v: v.tensor_scalar(out=lcol[:, 1:2], in0=sg[:, 0:1], scalar1=(1.0 - lam_init),
                                                              scalar2=None, op0=ALU.mult), r=[dR], w=[lR])
                        with contextlib.ExitStack() as s1:
                            gpre = sb("gpre", [128, D], F32, s1)
                            gR = R("gpre")
                            S.dma("sp", gpre[:], bass.AP(tensor=norm_pre.tensor, offset=l * D, ap=[[0, 128], [1, D]]), w=[gR])
                            xpool = Pool(s1, nc, "xt", 2, [128, D], F32)
                            hbpool = Pool(s1, nc, "hb", 2, [128, D], BF16)
                            junk = sb("junk", [128, D], BF16, s1)
                            junkR = R("junk")
                            sspool = Pool(s1, nc, "ss", 2, [128, 4], F32)
                            p1st = {}

                            def p1front(tt):
                                xt, xr = xpool.get()
                                if l == 0:
                                    src = xp[tt * 128:(tt + 1) * 128, :] if tt < NPT else xs[:, :]
                                    S.dma("sp", xt[:], src, w=[xr])
                                else:
                                    S.dma("sp", xt[:], x1[tt * 128:(tt + 1) * 128, :], r=[RX[tt]], w=[xr])
                                ss, ssr = sspool.get()
                                S.op("dve", lambda v, ss=ss: v.memset(ss[:], 0.0), w=[ssr])
                                S.op("act", lambda a, xt=xt, ss=ss: a.activation(out=junk[:], in_=xt[:], func=AF.Square,
                                                                                 accum_out=ss[:, 0:1]),
                                     r=[xr], w=[junkR, ssr])
                                S.op("dve", lambda v, ss=ss: v.tensor_scalar(out=ss[:, 1:2], in0=ss[:, 0:1], scalar1=1.0 / D,
                                                                              scalar2=EPS, op0=ALU.mult, op1=ALU.add),
                                     r=[ssr], w=[ssr])
                                S.op("act", lambda a, ss=ss: a.activation(out=ss[:, 2:3], in_=ss[:, 1:2], func=AF.Sqrt),
                                     r=[ssr], w=[ssr])
                                S.op("dve", lambda v, ss=ss: v.reciprocal(out=ss[:, 3:4], in_=ss[:, 2:3]), r=[ssr], w=[ssr])
                                hb, hbr = hbpool.get()
                                S.op("dve", lambda v, hb=hb, xt=xt, ss=ss: v.scalar_tensor_tensor(
                                    out=hb[:], in0=xt[:], scalar=ss[:, 3:4], in1=gpre[:], op0=ALU.mult, op1=ALU.mult),
                                    r=[xr, ssr, gR], w=[hbr])
                                p1st[tt] = (hb, hbr)

                            def p1back(tt):
                                hb, hbr = p1st.pop(tt)
                                for half in range(2):
                                    S.mm([lambda pe, hb=hb, c=c, half=half: pe.transpose(
                                        out=psT[:, c * 128:(c + 1) * 128], in_=hb[:, (half * 8 + c) * 128:(half * 8 + c + 1) * 128],
                                        identity=identb[:]) for c in range(8)], r=[hbr, cR], w=[psTR])
                                    eng = "act" if half == 0 else "dve"
                                    if eng == "act":
                                        S.op("act", lambda a, half=half, tt=tt: a.copy(
                                            out=hT[:, half * 8:half * 8 + 8, tt * 128:(tt + 1) * 128],
                                            in_=psT[:, :].rearrange("p (a b) -> p a b", b=128)), r=[psTR], w=[hTR[tt]])
                                    else:
                                        S.op("dve", lambda v, half=half, tt=tt: v.tensor_copy(
                                            out=hT[:, half * 8:half * 8 + 8, tt * 128:(tt + 1) * 128],
                                            in_=psT[:, :].rearrange("p (a b) -> p a b", b=128)), r=[psTR], w=[hTR[tt]])


                            p1front(0)
                            for tt in range(NT):
                                if tt + 1 < NT:
                                    p1front(tt + 1)
                                p1back(tt)

                        S.barrier()
                        _chk(2)
                        wpool = Pool(s2, nc, "wblk", 3, [128, DC, 512], BF16)
                        w32 = sb("w32", [128, DC, 32], BF16, s2)
                        w32R = R("w32")
                        evf = Pool(s2, nc, "evf", 6, [128, 512], F32)
                        evb = Pool(s2, nc, "evb", 8, [128, 512], BF16)
                        dsb = Pool(s2, nc, "dsb", 2, [128, 544], F32)
                        dwk = Pool(s2, nc, "dwk", 2, [128, 640], F32)
                        csb = Pool(s2, nc, "csb", 2, [128, 32], F32)
                        gq = sb("gq", [128, 384], F32, s2)
                        gkv = sb("gkv", [128, 128], F32, s2)
                        jk2 = sb("jk2", [128, 384], F32, s2)
                        jk2R = R("jk2")
                        g2R = R("g2")
                        S.dma("sp", gq[:], bass.AP(tensor=qnorm.tensor, offset=l * 384, ap=[[0, 128], [1, 384]]), w=[g2R])
                        S.dma("sp", gkv[:], bass.AP(tensor=kvnorm.tensor, offset=l * 128, ap=[[0, 128], [1, 128]]), w=[g2R])
                        w_l = w_in[l].rearrange("(c p) n -> p c n", p=128)
                        psi = [0]

                        def next_ps():
                            i = psi[0]
                            psi[0] = (i + 1) % 6
                            return PS[i], PSR[i]

                        def tm_dests(name, tt):
                            outs = []
                            if tt < NPT:
                                rows = slice(tt * 128, (tt + 1) * 128)
                                kc = KCOL[name]
                                outs.append((kvloc[tt][:, kc:kc + 512], (0, 128), rkv("L" + name, tt)))
                                po = {"A.k": sbk_p, "A.v": sbv_p, "C.k": dk_p, "C.v": dv_p}.get(name)
                                if po is not None:
                                    outs.append((po[l, rows, :], (0, 128), rkv("O" + name, tt)))
                            else:
                                for s in range(NS):
                                    rs = (s * 64, s * 64 + 64)
                                    if name == "A.k":
                                        outs.append((sbk_s[l, s, :, :], rs, rkv("sbk_s", s)))
                                    elif name == "A.v":
                                        outs.append((sbv_s[l, s, :, :], rs, rkv("sbv_s", s)))
                                    elif name == "B.k":
                                        outs.append((bk_s[l, s, 448:512, :], rs, rkv("bk_s", s)))
                                    elif name == "B.v":
                                        outs.append((bv_s[l, s, 448:512, :], rs, rkv("bv_s", s)))
                                    elif name == "C.k":
                                        outs.append((dk_s[l, s, :, :], rs, rkv("dk_s", s)))
                                    elif name == "C.v":
                                        outs.append((dv_s[l, s, :, :], rs, rkv("dv_s", s)))
                            return outs

                        TMB = [("A.k", 512), ("A.v", 1024), ("B.k", 2048), ("B.v", 2560), ("C.k", 3584), ("C.v", 4096)]
                        for name, c0 in TMB:
                            wt, wr = wpool.get()
                            S.dma("pool", wt[:], w_l[:, :, c0:c0 + 512], w=[wr])
                            for tt in range(NT):
                                ps, pr = next_ps()
                                S.mm([lambda pe, ps=ps, wt=wt, dc=dc, tt=tt: pe.matmul(
                                    ps[:, :], lhsT=hT[:, dc, tt * 128:(tt + 1) * 128], rhs=wt[:, dc, :],
                                    start=(dc == 0), stop=(dc == DC - 1)) for dc in range(DC)],
                                    r=[wr, hTR[tt]], w=[pr])
                                ev, er = evf.get()
                                S.op("dve", lambda v, ev=ev, ps=ps: v.tensor_copy(out=ev[:], in_=ps[:, :]), r=[pr], w=[er])
                                for (dap, (ra, rb), dres) in tm_dests(name, tt):
                                    S.dma("sp", dap, ev[ra:rb, :], r=[er], w=[dres])
                        bandR = R("band_out")
                        for t_ in range(4, 8):
                            S.dma("sp", bk_p[l, (t_ - 4) * 128:(t_ - 3) * 128, :], kvloc[t_][:, 1024:1536], r=[rkv("LB.k", t_)], w=[bandR])
                            S.dma("sp", bv_p[l, (t_ - 4) * 128:(t_ - 3) * 128, :], kvloc[t_][:, 1536:2048], r=[rkv("LB.v", t_)], w=[bandR])
                        for s in range(NS):
                            S.dma("sp", bk_s[l, s, 0:448, :], cbk[l, s, 64:512, :], w=[bandR])
                            S.dma("sp", bv_s[l, s, 0:448, :], cbv[l, s, 64:512, :], w=[bandR])

                        wt, wr = wpool.get()
                        S.dma("pool", wt[:], w_l[:, :, 4608:5120], w=[wr])
                        S.dma("pool", w32[:], w_l[:, :, 5120:5152], w=[w32R])
                        for tt in range(NT):
                            ps, pr = next_ps()
                            ps2, pr2 = next_ps()
                            S.mm([lambda pe, ps=ps, wt=wt, dc=dc, tt=tt: pe.matmul(
                                ps[:, :], lhsT=hT[:, dc, tt * 128:(tt + 1) * 128], rhs=wt[:, dc, :],
                                start=(dc == 0), stop=(dc == DC - 1)) for dc in range(DC)], r=[wr, hTR[tt]], w=[pr])
                            S.mm([lambda pe, ps2=ps2, dc=dc, tt=tt: pe.matmul(
                                ps2[:, 0:32], lhsT=hT[:, dc, tt * 128:(tt + 1) * 128], rhs=w32[:, dc, :],
                                start=(dc == 0), stop=(dc == DC - 1)) for dc in range(DC)], r=[w32R, hTR[tt]], w=[pr2])
                            d, dr = dsb.get()
                            S.op("act", lambda a, d=d, ps=ps: a.copy(out=d[:, 0:512], in_=ps[:, :]), r=[pr], w=[dr])
                            S.op("act", lambda a, d=d, ps2=ps2: a.copy(out=d[:, 512:544], in_=ps2[:, 0:32]), r=[pr2], w=[dr])
                            wk, wkr = dwk.get()
                            cs, csr = csb.get()
                            S.dma("sp", cs[:], c_cs[tt * 128:(tt + 1) * 128, :], w=[csr])
                            st = 600
                            S.op("dve", lambda v, wk=wk: v.memset(wk[:, st:st + 8], 0.0), w=[wkr])
                            S.op("act", lambda a, d=d, wk=wk: a.activation(out=jk2[:, 0:384], in_=d[:, 0:384], func=AF.Square,
                                                                           accum_out=wk[:, st:st + 1]),
                                 r=[dr], w=[wkr, jk2R])
                            S.op("act", lambda a, d=d, wk=wk: a.activation(out=jk2[:, 0:128], in_=d[:, 384:512], func=AF.Square,
                                                                           accum_out=wk[:, st + 1:st + 2]),
                                 r=[dr], w=[wkr, jk2R])
                            S.op("dve", lambda v, wk=wk: v.tensor_scalar(out=wk[:, st + 2:st + 3], in0=wk[:, st:st + 1],
                                                                         scalar1=1.0 / 384, scalar2=EPS, op0=ALU.mult, op1=ALU.add),
                                 r=[wkr], w=[wkr])
                            S.op("dve", lambda v, wk=wk: v.tensor_scalar(out=wk[:, st + 3:st + 4], in0=wk[:, st + 1:st + 2],
                                                                         scalar1=1.0 / 128, scalar2=EPS, op0=ALU.mult, op1=ALU.add),
                                 r=[wkr], w=[wkr])
                            S.op("act", lambda a, wk=wk: a.activation(out=wk[:, st + 4:st + 6], in_=wk[:, st + 2:st + 4], func=AF.Sqrt),
                                 r=[wkr], w=[wkr])
                            S.op("dve", lambda v, wk=wk: v.reciprocal(out=wk[:, st + 6:st + 8], in_=wk[:, st + 4:st + 6]),
                                 r=[wkr], w=[wkr])
                            S.op("dve", lambda v, wk=wk, d=d: v.scalar_tensor_tensor(
                                out=wk[:, 0:384], in0=d[:, 0:384], scalar=wk[:, st + 6:st + 7], in1=gq[:], op0=ALU.mult, op1=ALU.mult),
                                r=[dr, wkr, g2R], w=[wkr])
                            S.op("dve", lambda v, wk=wk, d=d: v.scalar_tensor_tensor(
                                out=wk[:, 384:512], in0=d[:, 384:512], scalar=wk[:, st + 7:st + 8], in1=gkv[:], op0=ALU.mult, op1=ALU.mult),
                                r=[dr, wkr, g2R], w=[wkr])
                            x1a, x2a = d[:, 512:528], d[:, 528:544]
                            cosa, sina = cs[:, 0:16], cs[:, 16:32]
                            S.op("dve", lambda v, wk=wk: v.tensor_tensor(out=wk[:, 544:560], in0=x1a, in1=cosa, op=ALU.mult), r=[dr, csr], w=[wkr])
                            S.op("dve", lambda v, wk=wk: v.tensor_tensor(out=wk[:, 560:576], in0=x2a, in1=sina, op=ALU.mult), r=[dr, csr], w=[wkr])
                            S.op("dve", lambda v, wk=wk: v.tensor_tensor(out=wk[:, 512:528], in0=wk[:, 544:560], in1=wk[:, 560:576], op=ALU.subtract), r=[wkr], w=[wkr])
                            S.op("dve", lambda v, wk=wk: v.tensor_tensor(out=wk[:, 544:560], in0=x2a, in1=cosa, op=ALU.mult), r=[dr, csr], w=[wkr])
                            S.op("dve", lambda v, wk=wk: v.tensor_tensor(out=wk[:, 560:576], in0=x1a, in1=sina, op=ALU.mult), r=[dr, csr], w=[wkr])
                            S.op("dve", lambda v, wk=wk: v.tensor_tensor(out=wk[:, 528:544], in0=wk[:, 544:560], in1=wk[:, 560:576], op=ALU.add), r=[wkr], w=[wkr])
                            S.dma("sp", qn_scr[tt * 128:(tt + 1) * 128, :], wk[:, 0:384], r=[wkr], w=[rkv("qn", tt)])
                            if tt < NPT:
                                S.dma("sp", ckv_p[l, tt * 128:(tt + 1) * 128, :], wk[:, 384:512], r=[wkr], w=[rkv("Ockv", tt)])
                                S.dma("sp", kr_p[l, tt * 128:(tt + 1) * 128, :], wk[:, 512:544], r=[wkr], w=[rkv("Okr", tt)])
                                S.dma("sp", kvloc[tt][:, 3072:3200], wk[:, 384:512], r=[wkr], w=[rkv("Lckv", tt)])
                                S.dma("sp", kvloc[tt][:, 3200:3232], wk[:, 512:544], r=[wkr], w=[rkv("Lkr", tt)])
                            else:
                                for s in range(NS):
                                    S.dma("sp", ckv_s[l, s, :, :], wk[s * 64:s * 64 + 64, 384:512], r=[wkr], w=[rkv("ckv_s", s)])
                                    S.dma("sp", kr_s[l, s, :, :], wk[s * 64:s * 64 + 64, 512:544], r=[wkr], w=[rkv("kr_s", s)])

                        FMB = []
                        for bi, c0 in enumerate((0, 1536, 3072)):
                            FMB.append((c0, AF.Copy, qT_scr[bi], "q%d" % bi))
                        for n in range(4):
                            FMB.append((5152 + 512 * n, AF.Silu, gT_scr[512 * n:512 * n + 512], "g"))
                        for j in range(16):
                            FMB.append((7200 + 512 * j, AF.Sigmoid, mT_scr[512 * j:512 * j + 512], "m"))
                        defer_bank[0] = next_ps
                        for fmi, (c0, func, dst, nm) in enumerate(FMB):
                            if DEFER:
                                DEFER.pop(0)()
                            wt, wr = wpool.get()
                            S.dma("pool", wt[:], w_l[:, :, c0:c0 + 512], w=[wr])
                            if fmi == 2:
                                for t_ in range(NPT):
                                    S.collective(S.ccsem, kvloc_t[t_], kvall_t[l][t_],
                                                 r=[rkv("L" + n_, t_) for n_ in KCOL], w=[kvallR[l][t_]])
                            for ch in range(4):
                                for tbi, (t0, nt) in enumerate(TB):
                                    ps, pr = next_ps()
                                    tts = [hTR[t0 // 128 + i] for i in range(nt // 128)]
                                    S.mm([lambda pe, ps=ps, wt=wt, dc=dc, ch=ch, t0=t0, nt=nt: pe.matmul(
                                        ps[:, 0:nt], lhsT=wt[:, dc, ch * 128:(ch + 1) * 128], rhs=hT[:, dc, t0:t0 + nt],
                                        start=(dc == 0), stop=(dc == DC - 1)) for dc in range(DC)], r=[wr] + tts, w=[pr])
                                    ev, er = evb.get()
                                    S.op("act", lambda a, ev=ev, ps=ps, nt=nt, func=func: a.activation(
                                        out=ev[:, 0:nt], in_=ps[:, 0:nt], func=func), r=[pr], w=[er])
                                    S.dma("sp", dst[ch * 128:(ch + 1) * 128, t0:t0 + nt], ev[:, 0:nt], r=[er], w=[rfm(nm, tbi)])
                        while DEFER:
                            DEFER.pop(0)()
                        S.barrier()
                        _chk(3)

                    for blk in range(2):
                        with contextlib.ExitStack() as s34:
                            gatedT = sb("gatedT", [128, 16, 640], BF16, s34)
                            gatedR = R("gated")
                            groups = [dict(kind="p", qb=blk, tok0=512 * blk, nq=512, col0=0, tbi=blk)]
                            NQB = 512
                            CPS = [(0, 512)]
                            MTB = [blk]
                            if blk == 1:
                                groups += [dict(kind="s", s=s, tok0=TP + 64 * s, nq=64, col0=512 + 64 * s, tbi=2) for s in range(NS)]
                                NQB = 640
                                CPS = [(0, 512), (512, 128)]
                                MTB = [1, 2]
                                S.op("dve", lambda v: v.memset(gatedT[:, :, 576:640], 0.0), w=[gatedR])
                            with contextlib.ExitStack() as s3:
                                KT = sb("KT", [128, 8, 2048], BF16, s3)
                                KTR = R("KT")
                                Vsb = sb("Vsb", [128, 16, 512], BF16, s3)
                                VR = R("V")
                                qT = sb("qT", [128, 8, 512], BF16, s3)
                                qTR = R("qT")
                                gT = sb("gT", [128, 4, 512], BF16, s3)
                                gTR = R("gT")
                                kraw = Pool(s3, nc, "kraw", 3, [128, 512], BF16)
                                kst = Pool(s3, nc, "kst", 3, [128, 512], F32)
                                vst = Pool(s3, nc, "vst", 3, [128, 512], F32)

                                def warm(n):
                                    for _ in range(n):
                                        S.mm([lambda pe: pe.matmul(PS[0][:, 0:512], lhsT=identb[:], rhs=masks[:, 0, :],
                                                                   start=True, stop=True)], r=[cR], w=[PSR[0]])
                                wf = Pool(s3, nc, "wf", 4, [128, 512], F32)
                                t1p = Pool(s3, nc, "t1p", 6, [128, 512], F32)
                                wb_ = Pool(s3, nc, "wb", 16, [128, 512], BF16)
                                small = Pool(s3, nc, "small", 7, [128, 1024], BF16)
                                qmf = Pool(s3, nc, "qmf", 2, [128, 768], F32)
                                csq = Pool(s3, nc, "csq", 2, [128, 32], F32)

                                for G in groups:
                                    nq = G["nq"]
                                    tok0 = G["tok0"]
                                    col0 = G["col0"]

                                    def key_tiles(br):
                                        tl = []
                                        if G["kind"] == "p":
                                            qb = G["qb"]
                                            hi = 8 + 4 * qb + 4
                                            lo = (8 + 4 * qb - 4) if br == "B" else 0
                                            for kt in range(lo, hi):
                                                D0 = 128 * kt - (1024 + 512 * qb)
                                                prv = kt < 8
                                                rows = slice(0, 128)
                                                if prv:
                                                    srcd = kvall[l][kt]
                                                    rs_ = [kvallR[l][kt]]
                                                else:
                                                    srcd = kvloc[kt - 8]
                                                if br != "D":
                                                    kc, vc = KCOL[br + ".k"], KCOL[br + ".v"]
                                                    if not prv:
                                                        rs_ = [rkv("L" + br + ".k", kt - 8), rkv("L" + br + ".v", kt - 8)]
                                                    tl.append((128, srcd[rows, kc:kc + 512], srcd[rows, vc:vc + 512], D0, rs_, prv))
                                                else:
                                                    if not prv:
                                                        rs_ = [rkv("Lckv", kt - 8), rkv("Lkr", kt - 8)]
                                                    tl.append((128, srcd[rows, 3072:3200], srcd[rows, 3200:3232], D0, rs_, prv))
                                        else:
                                            s = G["s"]
                                            if br == "B":
                                                for kt in range(4):
                                                    rows = slice(kt * 128, kt * 128 + 128)
                                                    tl.append((128, cbk[l, s, rows, :], cbv[l, s, rows, :], -512 + 128 * kt, [], False))
                                                tl.append((64, bk_s[l, s, 448:512, :], bv_s[l, s, 448:512, :], 0, [rkv("bk_s", s), rkv("bv_s", s)], False))
                                            else:
                                                for kt in range(8):
                                                    rows = slice(kt * 128, kt * 128 + 128)
                                                    D0 = kt * 128 - 1024
                                                    if br == "A":
                                                        tl.append((128, csbk[l, s, rows, :], csbv[l, s, rows, :], D0, [], False))
                                                    elif br == "C":
                                                        tl.append((128, cdk[l, s, rows, :], cdv[l, s, rows, :], D0, [], False))
                                                    else:
                                                        tl.append((128, cckv[l, s, rows, :], ckr[l, s, rows, :], D0, [], False))
                                                if br == "A":
                                                    tl.append((64, sbk_s[l, s, :, :], sbv_s[l, s, :, :], 0, [rkv("sbk_s", s), rkv("sbv_s", s)], False))
                                                elif br == "C":
                                                    tl.append((64, dk_s[l, s, :, :], dv_s[l, s, :, :], 0, [rkv("dk_s", s), rkv("dv_s", s)], False))
                                                else:
                                                    tl.append((64, ckv_s[l, s, :, :], kr_s[l, s, :, :], 0, [rkv("ckv_s", s), rkv("kr_s", s)], False))
                                        return tl

                                    for bi, br in enumerate("ABCD"):
                                        tiles = key_tiles(br)
                                        S.dma("sp", gT[:, :, 0:nq],
                                              gT_scr[512 * bi:512 * bi + 512, tok0:tok0 + nq].rearrange("(c p) t -> p c t", p=128),
                                              r=[rfm("g", G["tbi"])], w=[gTR])
                                        if br != "D":
                                            S.dma("sp", qT[:, 0:4, 0:nq],
                                                  qT_scr[bi, :, tok0:tok0 + nq].rearrange("(c p) t -> p c t", p=128),
                                                  r=[rfm("q%d" % bi, G["tbi"])], w=[qTR])
                                            for ti, (nk, kap, vap, D0, rs, prv) in enumerate(tiles):
                                                kf, kfr = kst.get()
                                                vf, vfr = vst.get()
                                                S.dma("sp", kf[0:nk, :], kap, r=rs, w=[kfr])
                                                S.dma("sp", vf[0:nk, :], vap, r=rs, w=[vfr])
                                                kr_, krr = kraw.get()
                                                S.op("dve", lambda v: v.tensor_copy(out=kr_[0:nk, :], in_=kf[0:nk, :]), r=[kfr], w=[krr])
                                                S.op("act", lambda a: a.copy(out=Vsb[0:nk, ti, :], in_=vf[0:nk, :]), r=[vfr], w=[VR])
                                                S.mm([lambda pe, kr_=kr_, c=c, nk=nk: pe.transpose(
                                                    out=psT[:, c * 128:c * 128 + nk], in_=kr_[0:nk, c * 128:(c + 1) * 128],
                                                    identity=identb[0:nk, 0:nk]) for c in range(4)], r=[krr, cR], w=[psTR])
                                                S.op("dve", lambda v, ti=ti, nk=nk: v.tensor_copy(
                                                    out=KT[:, 0:4, ti * 128:ti * 128 + nk],
                                                    in_=psT[:, 0:512].rearrange("p (a b) -> p a b", b=128)[:, :, 0:nk]),
                                                    r=[psTR], w=[KTR])
                                        else:
                                            ntt = max(1, nq // 128)
                                            for qi in range(ntt):
                                                nr = min(128, nq)
                                                r0 = tok0 + qi * 128
                                                tt = r0 // 128
                                                qn_, qnr = small.get()
                                                kf, kfr = kst.get()
                                                S.dma("sp", kf[0:nr, 0:384], qn_scr[r0:r0 + nr, :], r=[rkv("qn", tt)], w=[kfr])
                                                S.op("dve", lambda v: v.tensor_copy(out=qn_[0:nr, 0:384], in_=kf[0:nr, 0:384]), r=[kfr], w=[qnr])
                                                cs, csr = csq.get()
                                                S.dma("sp", cs[0:nr, :], c_cs[r0:r0 + nr, :], w=[csr])
                                                S.mm([lambda pe, qn_=qn_, c=c, nr=nr: pe.transpose(
                                                    out=psT[:, c * 128:c * 128 + nr], in_=qn_[0:nr, c * 128:(c + 1) * 128],
                                                    identity=identb[0:nr, 0:nr]) for c in range(3)], r=[qnr, cR], w=[psTR])
                                                qnT, qnTr = small.get()
                                                S.op("dve", lambda v, qnT=qnT: v.tensor_copy(out=qnT[:, 0:384], in_=psT[:, 0:384]),
                                                     r=[psTR], w=[qnTr])
                                                S.mm([lambda pe, qnT=qnT, c=c, nr=nr: pe.matmul(
                                                    PS[5][0:nr, :], lhsT=qnT[:, c * 128:c * 128 + nr], rhs=wqupb[:, c, 0:512],
                                                    start=(c == 0), stop=(c == 2)) for c in range(3)], r=[qnTr, lR], w=[PSR[5]])
                                                S.mm([lambda pe, qnT=qnT, c=c, nr=nr: pe.matmul(
                                                    PS[6][0:nr, 0:256], lhsT=qnT[:, c * 128:c * 128 + nr], rhs=wqupb[:, c, 512:768],
                                                    start=(c == 0), stop=(c == 2)) for c in range(3)], r=[qnTr, lR], w=[PSR[6]])
                                                qf, qfr = qmf.get()
                                                S.op("act", lambda a, qf=qf, nr=nr: a.copy(out=qf[0:nr, 0:512], in_=PS[5][0:nr, :]),
                                                     r=[PSR[5]], w=[qfr])
                                                S.op("act", lambda a, qf=qf, nr=nr: a.copy(out=qf[0:nr, 512:768], in_=PS[6][0:nr, 0:256]),
                                                     r=[PSR[6]], w=[qfr])
                                                qb_, qbr = small.get()
                                                qf3 = qf[0:nr, :].rearrange("p (h e) -> p h e", e=96)
                                                qb3 = qb_[0:nr, 0:768].rearrange("p (h e) -> p h e", e=96)
                                                tmp, tmpr = wf.get()
                                                ta = tmp[0:nr, 0:128].rearrange("p (h e) -> p h e", e=16)
                                                tb_ = tmp[0:nr, 128:256].rearrange("p (h e) -> p h e", e=16)
                                                cosb = bc_ap(cs[0:nr, 0:16], [[0, 8], [1, 16]])
                                                sinb = bc_ap(cs[0:nr, 16:32], [[0, 8], [1, 16]])
                                                S.op("dve", lambda v: v.tensor_copy(out=qb3[:, :, 0:64], in_=qf3[:, :, 0:64]), r=[qfr], w=[qbr])
                                                S.op("dve", lambda v: v.tensor_tensor(out=ta, in0=qf3[:, :, 64:80], in1=cosb, op=ALU.mult), r=[qfr, csr], w=[tmpr])
                                                S.op("dve", lambda v: v.tensor_tensor(out=tb_, in0=qf3[:, :, 80:96], in1=sinb, op=ALU.mult), r=[qfr, csr], w=[tmpr])
                                                S.op("dve", lambda v: v.tensor_tensor(out=qb3[:, :, 64:80], in0=ta, in1=tb_, op=ALU.subtract), r=[tmpr], w=[qbr, tmpr])
                                                S.op("dve", lambda v: v.tensor_tensor(out=ta, in0=qf3[:, :, 80:96], in1=cosb, op=ALU.mult), r=[qfr, csr], w=[tmpr])
                                                S.op("dve", lambda v: v.tensor_tensor(out=tb_, in0=qf3[:, :, 64:80], in1=sinb, op=ALU.mult), r=[qfr, csr], w=[tmpr])
                                                S.op("dve", lambda v: v.tensor_tensor(out=qb3[:, :, 80:96], in0=ta, in1=tb_, op=ALU.add), r=[tmpr], w=[qbr, tmpr])
                                                S.mm([lambda pe, qb_=qb_, h=h, nr=nr: pe.transpose(
                                                    out=psT[0:96, h * 128:h * 128 + nr], in_=qb_[0:nr, h * 96:(h + 1) * 96],
                                                    identity=identb[0:nr, 0:nr]) for h in range(8)], r=[qbr, cR], w=[psTR])
                                                S.op("dve", lambda v, qi=qi, nr=nr: v.tensor_copy(
                                                    out=qT[0:96, :, qi * 128:qi * 128 + nr],
                                                    in_=psT[0:96, :].rearrange("p (a b) -> p a b", b=128)[:, :, 0:nr]),
                                                    r=[psTR], w=[qTR])
                                            _chk(3.61)
                                            dst_ = {}

                                            def stA(ti):
                                                nk, cap, rap, D0, rs, prv = tiles[ti]
                                                b0 = 5 if ti % 2 == 0 else 3
                                                ck, ckr_ = small.get()
                                                kf, kfr = kst.get()
                                                S.dma("sp", kf[0:nk, 0:128], cap, r=rs, w=[kfr])
                                                S.dma("sp", kf[0:nk, 128:160], rap, r=rs, w=[kfr])
                                                S.op("dve", lambda v: v.tensor_copy(out=ck[0:nk, 0:160], in_=kf[0:nk, 0:160]), r=[kfr], w=[ckr_])
                                                S.mm([lambda pe: pe.transpose(
                                                    out=psT[:, 0:nk], in_=ck[0:nk, 0:128], identity=identb[0:nk, 0:nk])],
                                                    r=[ckr_, cR], w=[psTR])
                                                ckT, ckTr = small.get()
                                                S.op("dve", lambda v: v.tensor_copy(out=ckT[:, 0:nk], in_=psT[:, 0:nk]),
                                                     r=[psTR], w=[ckTr])
                                                S.mm([lambda pe: pe.matmul(
                                                    PS[b0][0:nk, :], lhsT=ckT[:, 0:nk], rhs=wkvupb[:, 0:512], start=True, stop=True)],
                                                    r=[ckTr, lR], w=[PSR[b0]])
                                                S.mm([lambda pe: pe.matmul(
                                                    PS[b0 + 1][0:nk, :], lhsT=ckT[:, 0:nk], rhs=wkvupb[:, 512:1024], start=True, stop=True)],
                                                    r=[ckTr, lR], w=[PSR[b0 + 1]])
                                                dst_[ti] = (ck, ckr_)

                                            def stB(ti):
                                                nk, cap, rap, D0, rs, prv = tiles[ti]
                                                b0 = 5 if ti % 2 == 0 else 3
                                                ck, ckr_ = dst_.pop(ti)
                                                kd, kdr = small.get()
                                                kd3 = kd[0:nk, 0:768].rearrange("p (h e) -> p h e", e=96)
                                                for hf in range(2):
                                                    pv = PS[b0 + hf][0:nk, :].rearrange("p (h e) -> p h e", e=128)
                                                    S.op("act", lambda a: a.copy(out=kd3[:, 4 * hf:4 * hf + 4, 0:64], in_=pv[:, :, 0:64]),
                                                         r=[PSR[b0 + hf]], w=[kdr])
                                                    S.op("dve", lambda v: v.tensor_copy(
                                                        out=Vsb[0:nk, ti, 256 * hf:256 * hf + 256].rearrange("p (h e) -> p h e", e=64),
                                                        in_=pv[:, :, 64:128]), r=[PSR[b0 + hf]], w=[VR])
                                                S.op("dve", lambda v: v.tensor_copy(
                                                    out=kd3[:, :, 64:96], in_=bc_ap(ck[0:nk, 128:160], [[0, 8], [1, 32]])), r=[ckr_], w=[kdr])
                                                S.mm([lambda pe, h=h: pe.transpose(
                                                    out=psT[0:96, h * 128:h * 128 + nk], in_=kd[0:nk, h * 96:(h + 1) * 96],
                                                    identity=identb[0:nk, 0:nk]) for h in range(8)], r=[kdr, cR], w=[psTR])
                                                S.op("dve", lambda v: v.tensor_copy(
                                                    out=KT[0:96, :, ti * 128:ti * 128 + nk],
                                                    in_=psT[0:96, :].rearrange("p (a b) -> p a b", b=128)[:, :, 0:nk]),
                                                    r=[psTR], w=[KTR])
                                                warm(WARM)

                                            stA(0)
                                            for ti in range(len(tiles)):
                                                if ti + 1 < len(tiles):
                                                    stA(ti + 1)
                                                stB(ti)

                                        _chk(3.0 + 0.2 * bi + 0.1)
                                        def mask_ap(kind, D0, nk):
                                            d = D0 // 128
                                            if kind == "A":
                                                return masks[0:nk, d, 0:nq]
                                            if kind == "CD":
                                                return masks[0:nk, 4 + d, 0:nq]
                                            return masks[0:nk, 8 + (D0 + 512) // 128, 0:nq]

                                        isP = G["kind"] == "p"
                                        LOOK = 3

                                        def run_pipe(items, s1, s2):
                                            n = len(items)
                                            for i in range(min(LOOK, n)):
                                                s1(i)
                                            for i in range(n):
                                                s2(i)
                                                if i + LOOK < n:
                                                    s1(i + LOOK)

                                        if br == "A":
                                            ntl = len(tiles)
                                            items = [(2 * hp_ + hf_, idx) for hp_ in range(4) for idx in range(ntl) for hf_ in range(2)]
                                            order = list(enumerate(tiles))[::-1]
                                            st = {}
                                            psbanks = [0, 1, 5]

                                            def s1q(i):
                                                h, idx = items[i]
                                                hp, r0 = h // 2, 64 * (h % 2)
                                                ti, (nk, kap, vap, D0, rs, prv) = order[idx]
                                                pS, pSR = PS[psbanks[i % 3]], PSR[psbanks[i % 3]]
                                                S.mm([lambda pe: pe.matmul(
                                                    pS[0:nk, 0:nq], lhsT=KT[r0:r0 + 64, hp, ti * 128:ti * 128 + nk],
                                                    rhs=qT[r0:r0 + 64, hp, 0:nq], start=True, stop=True)], r=[KTR, qTR], w=[pSR])

                                            def s1(i):
                                                h, idx = items[i]
                                                hp, r0 = h // 2, 64 * (h % 2)
                                                ti, (nk, kap, vap, D0, rs, prv) = order[idx]
                                                pS, pSR = PS[psbanks[i % 3]], PSR[psbanks[i % 3]]
                                                diag = (D0 >= 0)
                                                ez, ezr = wf.get()
                                                bkw = dict(bias=vbt[0:nk, 0:1]) if prv else {}
                                                S.op("act", lambda a: a.activation(
                                                    out=ez[0:nk, 0:nq], in_=pS[0:nk, 0:nq], func=AF.Exp, scale=0.125, **bkw), r=[pSR, cR], w=[ezr])
                                                sp, spr = wb_.get()
                                                S.op("act", lambda a: a.activation(
                                                    out=sp[0:nk, 0:nq], in_=ez[0:nk, 0:nq], func=AF.Ln, bias=1.0), r=[ezr], w=[spr])
                                                t1, t1r = t1p.get()
                                                S.op("dve", lambda v: v.scalar_tensor_tensor(
                                                    out=t1[0:nk, 0:nq], in0=pS[0:nk, 0:nq], scalar=0.125, in1=sp[0:nk, 0:nq],
                                                    op0=ALU.mult, op1=ALU.subtract), r=[pSR, spr], w=[t1r])
                                                if diag:
                                                    spm, spmr = wb_.get()
                                                    S.op("dve", lambda v: v.tensor_tensor(
                                                        out=spm[0:nk, 0:nq], in0=sp[0:nk, 0:nq], in1=mask_ap("A", D0, nk), op=ALU.mult),
                                                        r=[spr, cR], w=[spmr])
                                                else:
                                                    spm, spmr = sp, spr
                                                st[i] = (spm, spmr, t1, t1r)

                                            def s2a(i):
                                                h, idx = items[i]
                                                ti, (nk, kap, vap, D0, rs, prv) = order[idx]
                                                lb = 2 if h % 2 == 0 else 6
                                                psL, psLR = PS[lb], PSR[lb]
                                                spm, spmr, t1, t1r = st[i]
                                                fns = []
                                                rr = [spmr, cR]
                                                if idx > 0:
                                                    psp, pspr, pnk = st[("prev", h % 2)]
                                                    fns.append(lambda pe: pe.matmul(
                                                        psL[:, 0:nq], lhsT=TRIC[0:pnk, :], rhs=psp[0:pnk, 0:nq],
                                                        start=False, stop=False, skip_group_check=True))
                                                    rr.append(pspr)
                                                fns.append(lambda pe: pe.matmul(
                                                    psL[:, 0:nq], lhsT=TRI[0:nk, :], rhs=spm[0:nk, 0:nq],
                                                    start=(idx == 0), stop=True, skip_group_check=True))
                                                S.mm(fns, r=rr, w=[psLR])
                                                st[("prev", h % 2)] = (spm, spmr, nk)

                                            def s2t(i):
                                                h, idx = items[i]
                                                ti, (nk, kap, vap, D0, rs, prv) = order[idx]
                                                lb = 2 if h % 2 == 0 else 6
                                                psL, psLR = PS[lb], PSR[lb]
                                                spm, spmr, t1, t1r = st[i]
                                                S.op("dve", lambda v: v.tensor_tensor(
                                                    out=t1[0:nk, 0:nq], in0=t1[0:nk, 0:nq], in1=psL[0:nk, 0:nq], op=ALU.subtract),
                                                    r=[psLR], w=[t1r])

                                            def s2b(i):
                                                h, idx = items[i]
                                                hp, r0 = h // 2, 64 * (h % 2)
                                                ti, (nk, kap, vap, D0, rs, prv) = order[idx]
                                                diag = (D0 >= 0)
                                                lb = 2 if h % 2 == 0 else 6
                                                psL, psLR = PS[lb], PSR[lb]
                                                psO, psOR = PS[3 + (h % 2)], PSR[3 + (h % 2)]
                                                spm, spmr, t1, t1r = st.pop(i)
                                                wt_, wtr = wb_.get()
                                                bkw = dict(bias=vbt[0:nk, 0:1]) if prv else {}
                                                S.op("act", lambda a: a.activation(
                                                    out=wt_[0:nk, 0:nq], in_=t1[0:nk, 0:nq], func=AF.Exp, **bkw), r=[t1r, cR], w=[wtr])
                                                if diag:
                                                    S.op("dve", lambda v: v.tensor_tensor(
                                                        out=wt_[0:nk, 0:nq], in0=wt_[0:nk, 0:nq], in1=mask_ap("A", D0, nk), op=ALU.mult),
                                                        r=[cR], w=[wtr])
                                                S.mm([lambda pe: pe.matmul(
                                                    psO[:, 0:nq], lhsT=Vsb[0:nk, ti, hp * 128:(hp + 1) * 128], rhs=wt_[0:nk, 0:nq],
                                                    start=(idx == 0), stop=(idx == ntl - 1), skip_group_check=True)],
                                                    r=[wtr, VR], w=[psOR])
                                                if idx == ntl - 1:
                                                    S.op("dve", lambda v: v.tensor_tensor(
                                                        out=gatedT[r0:r0 + 64, hp, col0:col0 + nq], in0=psO[r0:r0 + 64, 0:nq],
                                                        in1=gT[r0:r0 + 64, hp, 0:nq], op=ALU.mult), r=[psOR, gTR], w=[gatedR])

                                            nit = len(items)
                                            for i in range(min(3, nit)):
                                                s1q(i)
                                            s1(0)
                                            if nit > 3:
                                                s1q(3)
                                            if nit > 1:
                                                s1(1)
                                            s2a(0)
                                            for i in range(nit):
                                                if i + 4 < nit:
                                                    s1q(i + 4)
                                                s2t(i)
                                                if i + 1 < nit:
                                                    s2a(i + 1)
                                                if i + 2 < nit:
                                                    s1(i + 2)
                                                s2b(i)
                                        elif br in ("B", "D"):
                                            ntl = len(tiles)
                                            items = [(2 * hp_ + hf_, idx) for hp_ in range(4) for idx in range(ntl) for hf_ in range(2)]
                                            st = {}

                                            def s1(i):
                                                h, idx = items[i]
                                                hp, r0 = h // 2, 64 * (h % 2)
                                                nk, kap, vap, D0, rs, prv = tiles[idx]
                                                ti = idx
                                                pS, pSR = PS[i % 3], PSR[i % 3]
                                                if br == "B":
                                                    S.mm([lambda pe: pe.matmul(
                                                        pS[0:nk, 0:nq], lhsT=KT[r0:r0 + 64, hp, ti * 128:ti * 128 + nk],
                                                        rhs=qT[r0:r0 + 64, hp, 0:nq], start=True, stop=True)], r=[KTR, qTR], w=[pSR])
                                                    scale = 0.125
                                                else:
                                                    S.mm([lambda pe: pe.matmul(
                                                        pS[0:nk, 0:nq], lhsT=KT[0:96, h, ti * 128:ti * 128 + nk],
                                                        rhs=qT[0:96, h, 0:nq], start=True, stop=True)], r=[KTR, qTR], w=[pSR])
                                                    scale = MLA_SCALE
                                                e, er = wb_.get()
                                                bkw = dict(bias=vbt[0:nk, 0:1]) if prv else {}
                                                S.op("act", lambda a: a.activation(
                                                    out=e[0:nk, 0:nq], in_=pS[0:nk, 0:nq], func=AF.Exp, scale=scale, **bkw), r=[pSR, cR], w=[er])
                                                if br == "B":
                                                    c0 = 384 - D0
                                                    S.op("dve", lambda v: v.tensor_tensor(
                                                        out=e[0:nk, 0:nq], in0=e[0:nk, 0:nq], in1=EGB[0:nk, h, c0:c0 + nq], op=ALU.mult),
                                                        r=[EGBR], w=[er])
                                                    if isP:
                                                        S.op("dve", lambda g: g.tensor_tensor(
                                                            out=e[0:nk, 0:nq], in0=e[0:nk, 0:nq], in1=mask_ap("B", D0, nk), op=ALU.mult),
                                                            r=[cR], w=[er])
                                                else:
                                                    if isP and D0 >= 0:
                                                        S.op("dve", lambda v: v.tensor_tensor(
                                                            out=e[0:nk, 0:nq], in0=e[0:nk, 0:nq], in1=mask_ap("CD", D0, nk), op=ALU.mult),
                                                            r=[cR], w=[er])
                                                st[i] = (e, er)

                                            def s2(i):
                                                h, idx = items[i]
                                                hp, r0 = h // 2, 64 * (h % 2)
                                                nk, kap, vap, D0, rs, prv = tiles[idx]
                                                ti = idx
                                                psO, psOR = PS[3 + (h % 2)], PSR[3 + (h % 2)]
                                                psD, psDR = PS[5 + (h % 2)], PSR[5 + (h % 2)]
                                                e, er = st.pop(i)
                                                S.mm([lambda pe: pe.matmul(
                                                    psO[:, 0:nq], lhsT=Vsb[0:nk, ti, hp * 128:(hp + 1) * 128], rhs=e[0:nk, 0:nq],
                                                    start=(idx == 0), stop=(idx == ntl - 1), skip_group_check=True)],
                                                    r=[er, VR], w=[psOR])
                                                S.mm([lambda pe: pe.matmul(
                                                    psD[:, 0:nq], lhsT=ONES[0:nk, :], rhs=e[0:nk, 0:nq],
                                                    start=(idx == 0), stop=(idx == ntl - 1), skip_group_check=True)],
                                                    r=[er, cR], w=[psDR])
                                                if idx == ntl - 1:
                                                    rc, rcr = wf.get()
                                                    S.op("act", lambda a: a.activation(out=rc[r0:r0 + 64, 0:nq], in_=psD[r0:r0 + 64, 0:nq], func=AF.Ln),
                                                         r=[psDR], w=[rcr])
                                                    S.op("act", lambda a: a.activation(out=rc[r0:r0 + 64, 0:nq], in_=rc[r0:r0 + 64, 0:nq], func=AF.Exp, scale=-1.0),
                                                         r=[], w=[rcr])
                                                    S.op("dve", lambda v: v.tensor_tensor(
                                                        out=rc[r0:r0 + 64, 0:nq], in0=psO[r0:r0 + 64, 0:nq], in1=rc[r0:r0 + 64, 0:nq], op=ALU.mult),
                                                        r=[psOR], w=[rcr])
                                                    S.op("pool", lambda g: g.tensor_tensor(
                                                        out=gatedT[r0:r0 + 64, 4 * bi + hp, col0:col0 + nq], in0=rc[r0:r0 + 64, 0:nq],
                                                        in1=gT[r0:r0 + 64, hp, 0:nq], op=ALU.mult), r=[rcr, gTR], w=[gatedR])

                                            run_pipe(items, s1, s2)
                                        else:
                                            ntl = len(tiles)
                                            items = [(h, m, idx) for h in range(4) for idx in range(ntl) for m in range(2)]
                                            st = {}

                                            def s1(i):
                                                h, m, idx = items[i]
                                                r0 = 64 * m
                                                nk, kap, vap, D0, rs, prv = tiles[idx]
                                                ti = idx
                                                pS, pSR = PS[i % 3], PSR[i % 3]
                                                S.mm([lambda pe: pe.matmul(
                                                    pS[0:nk, 0:nq], lhsT=KT[r0:r0 + 64, h, ti * 128:ti * 128 + nk],
                                                    rhs=qT[r0:r0 + 64, h, 0:nq], start=True, stop=True)], r=[KTR, qTR], w=[pSR])
                                                e, er = wb_.get()
                                                if D0 <= -512:
                                                    S.op("act", lambda a: a.activation(
                                                        out=e[0:nk, 0:nq], in_=pS[0:nk, 0:nq], func=AF.Exp, scale=0.125,
                                                        bias=(satCv if prv else satC)[0:nk, h:h + 1]), r=[pSR, cR], w=[er])
                                                else:
                                                    bkw = dict(bias=vbt[0:nk, 0:1]) if prv else {}
                                                    S.op("act", lambda a: a.activation(
                                                        out=e[0:nk, 0:nq], in_=pS[0:nk, 0:nq], func=AF.Exp, scale=0.125, **bkw), r=[pSR, cR], w=[er])
                                                    c0 = 384 - D0
                                                    S.op("dve", lambda v: v.tensor_tensor(
                                                        out=e[0:nk, 0:nq], in0=e[0:nk, 0:nq], in1=EGC[0:nk, h, c0:c0 + nq], op=ALU.mult),
                                                        r=[EGCR], w=[er])
                                                    if isP and D0 >= 0:
                                                        S.op("dve", lambda g: g.tensor_tensor(
                                                            out=e[0:nk, 0:nq], in0=e[0:nk, 0:nq], in1=mask_ap("CD", D0, nk), op=ALU.mult),
                                                            r=[cR], w=[er])
                                                st[i] = (e, er)

                                            def s2(i):
                                                h, m, idx = items[i]
                                                nk, kap, vap, D0, rs, prv = tiles[idx]
                                                ti = idx
                                                psO, psOR = PS[3 + m], PSR[3 + m]
                                                psD, psDR = PS[5 + m], PSR[5 + m]
                                                e, er = st.pop(i)
                                                S.mm([lambda pe: pe.matmul(
                                                    psO[:, 0:nq], lhsT=Vsb[0:nk, ti, h * 128:(h + 1) * 128], rhs=e[0:nk, 0:nq],
                                                    start=(idx == 0), stop=(idx == ntl - 1), skip_group_check=True)],
                                                    r=[er, VR], w=[psOR])
                                                S.mm([lambda pe: pe.matmul(
                                                    psD[:, 0:nq], lhsT=ONES[0:nk, :], rhs=e[0:nk, 0:nq],
                                                    start=(idx == 0), stop=(idx == ntl - 1), skip_group_check=True)],
                                                    r=[er, cR], w=[psDR])
                                                if idx == ntl - 1:
                                                    a_, ar = wf.get()
                                                    S.op("act", lambda a: a.activation(out=a_[:, 0:nq], in_=psD[:, 0:nq], func=AF.Ln), r=[psDR], w=[ar])
                                                    S.op("act", lambda a: a.activation(out=a_[:, 0:nq], in_=a_[:, 0:nq], func=AF.Exp, scale=-1.0), r=[], w=[ar])
                                                    S.op("dve", lambda v: v.tensor_tensor(
                                                        out=a_[:, 0:nq], in0=psO[:, 0:nq], in1=a_[:, 0:nq], op=ALU.mult), r=[psOR], w=[ar])
                                                    st[("acc", m)] = (a_, ar)
                                                    if m == 1:
                                                        a0, a0r = st.pop(("acc", 0))
                                                        a1, a1r = st.pop(("acc", 1))
                                                        S.op("dve", lambda v: v.scalar_tensor_tensor(
                                                            out=a0[:, 0:nq], in0=a1[:, 0:nq], scalar=lcol[:, 0:1], in1=a0[:, 0:nq],
                                                            op0=ALU.mult, op1=ALU.add), r=[a1r, lR], w=[a0r])
                                                        sq, sqr = wb_.get()
                                                        S.op("act", lambda a: a.activation(out=sq[:, 0:nq], in_=a0[:, 0:nq], func=AF.Square),
                                                             r=[a0r], w=[sqr])
                                                        S.mm([lambda pe: pe.matmul(PS[2][:, 0:nq], lhsT=ONES, rhs=sq[:, 0:nq],
                                                                                   start=True, stop=True)], r=[sqr, cR], w=[PSR[2]])
                                                        S.op("dve", lambda v: v.tensor_scalar(
                                                            out=a1[:, 0:nq], in0=PS[2][:, 0:nq], scalar1=1.0 / 128, scalar2=EPS,
                                                            op0=ALU.mult, op1=ALU.add), r=[PSR[2]], w=[a1r])
                                                        S.op("act", lambda a: a.activation(out=a1[:, 0:nq], in_=a1[:, 0:nq], func=AF.Ln),
                                                             r=[], w=[a1r])
                                                        S.op("act", lambda a: a.activation(out=a1[:, 0:nq], in_=a1[:, 0:nq], func=AF.Exp, scale=-0.5),
                                                             r=[], w=[a1r])
                                                        S.op("pool", lambda g: g.tensor_tensor(
                                                            out=a0[:, 0:nq], in0=a0[:, 0:nq], in1=a1[:, 0:nq], op=ALU.mult), r=[a1r], w=[a0r])
                                                        S.op("dve", lambda g: g.scalar_tensor_tensor(
                                                            out=gatedT[:, 8 + h, col0:col0 + nq], in0=a0[:, 0:nq], scalar=lcol[:, 1:2],
                                                            in1=gT[:, h, 0:nq], op0=ALU.mult, op1=ALU.mult), r=[a0r, lR, gTR], w=[gatedR])

                                            run_pipe(items, s1, s2)
                                S.barrier()
                                _chk(4)

                            with contextlib.ExitStack() as s4:
                                mergedT = sb("mergedT", [128, DC, 640], BF16, s4)
                                mrgR = R("merged")
                                with contextlib.ExitStack() as s4a:
                                    wbp = Pool(s4a, nc, "wbp", 2, [128, 16, 512], BF16)
                                    mgp = Pool(s4a, nc, "mgp", 3, [128, 4, 512], BF16)
                                    pfb = Pool(s4a, nc, "pfb", 10, [128, 512], BF16)
                                    tok0 = 512 * blk
                                    pend = [None]
                                    acnt = [0]

                                    def flush_acc():
                                        if pend[0] is None:
                                            return
                                        dcp, prods, cc0, ccn, abi = pend[0]
                                        pend[0] = None
                                        ab = 4 + abi % 2
                                        S.mm([lambda pe, p_=p_, n=n: pe.matmul(
                                            PS[ab][:, 0:ccn], lhsT=identb[:], rhs=p_[:, 0:ccn], start=(n == 0), stop=(n == 3))
                                            for n, (p_, p_r) in enumerate(prods)], r=[p_r for (_, p_r) in prods] + [cR], w=[PSR[ab]])
                                        S.op("act", lambda a: a.copy(out=mergedT[:, dcp, cc0:cc0 + ccn], in_=PS[ab][:, 0:ccn]),
                                             r=[PSR[ab]], w=[mrgR])

                                    for g4 in range(4):
                                        wt, wr = wbp.get()
                                        S.dma("pool", wt[:],
                                              wbr[l, :, :, g4 * 512:(g4 + 1) * 512].rearrange("n (c p) d -> p (n c) d", p=128), w=[wr])
                                        for j4 in range(4):
                                            dc = 4 * g4 + j4
                                            for (cc0, ccn) in CPS:
                                                mg, mgr = mgp.get()
                                                S.dma("sp", mg[:, :, 0:ccn],
                                                      mT_scr[:, tok0 + cc0:tok0 + cc0 + ccn].rearrange("(n c p) t -> p n c t", p=128, c=16)[:, :, dc, :],
                                                      r=[rfm("m", t_) for t_ in MTB], w=[mgr])
                                                prods = []
                                                for n in range(4):
                                                    S.mm([lambda pe, n=n, c=c: pe.matmul(
                                                        PS[n][:, 0:ccn], lhsT=wt[:, 4 * n + c, j4 * 128:(j4 + 1) * 128],
                                                        rhs=gatedT[:, 4 * n + c, cc0:cc0 + ccn],
                                                        start=(c == 0), stop=(c == 3)) for c in range(4)], r=[wr, gatedR], w=[PSR[n]])
                                                    p_, p_r = pfb.get()
                                                    S.op("dve", lambda v, p_=p_, n=n: v.tensor_tensor(
                                                        out=p_[:, 0:ccn], in0=PS[n][:, 0:ccn], in1=mg[:, n, 0:ccn], op=ALU.mult),
                                                        r=[PSR[n], mgr], w=[p_r])
                                                    prods.append((p_, p_r))
                                                flush_acc()
                                                acnt[0] += 1
                                                pend[0] = (dc, prods, cc0, ccn, acnt[0])
                                    flush_acc()
                                    S.barrier()
                                    _chk(5)
                                with contextlib.ExitStack() as s4b:
                                    wop = Pool(s4b, nc, "wop", 2, [128, DC, 512], BF16)
                                    ntk = NQB // 128
                                    ysb = [sb(f"ysb{i}", [128, D], F32, s4b) for i in range(ntk)]
                                    ysR = [R(f"ysb{i}") for i in range(ntk)]
                                    gpost = sb("gpost", [128, D], F32, s4b)
                                    gpR = R("gpost")
                                    S.dma("sp", gpost[:], bass.AP(tensor=norm_post.tensor, offset=l * D, ap=[[0, 128], [1, D]]), w=[gpR])
                                    xin = Pool(s4b, nc, "xin", 2, [128, D], F32)
                                    jk4 = sb("jk4", [128, D], BF16, s4b)
                                    jk4R = R("jk4")
                                    st4 = Pool(s4b, nc, "st4", 2, [128, 4], F32)
                                    for eb in range(4):
                                        wt, wr = wop.get()
                                        S.dma("pool", wt[:], wout[l, :, eb * 512:(eb + 1) * 512].rearrange("(c p) n -> p c n", p=128), w=[wr])
                                        for tk in range(ntk):
                                            pi = 4 + (eb * ntk + tk) % 3
                                            S.mm([lambda pe, wt=wt, c=c, tk=tk, pi=pi: pe.matmul(
                                                PS[pi][:, :], lhsT=mergedT[:, c, tk * 128:(tk + 1) * 128], rhs=wt[:, c, :],
                                                start=(c == 0), stop=(c == DC - 1)) for c in range(DC)], r=[wr, mrgR], w=[PSR[pi]])
                                            S.op("act", lambda a, tk=tk, eb=eb, pi=pi: a.copy(out=ysb[tk][:, eb * 512:(eb + 1) * 512], in_=PS[pi][:, :]),
                                                 r=[PSR[pi]], w=[ysR[tk]])
                                    for tk in range(ntk):
                                        tt = (512 * blk) // 128 + tk
                                        xt, xr = xin.get()
                                        if l == 0:
                                            src = xp[tt * 128:(tt + 1) * 128, :] if tt < NPT else xs[:, :]
                                            S.dma("sp", xt[:], src, w=[xr])
                                        else:
                                            S.dma("sp", xt[:], x1[tt * 128:(tt + 1) * 128, :], r=[RX[tt]], w=[xr])
                                        ss, ssr = st4.get()
                                        S.op("dve", lambda v, ss=ss: v.memset(ss[:], 0.0), w=[ssr])
                                        S.op("act", lambda a, tk=tk, ss=ss: a.activation(out=jk4[:], in_=ysb[tk][:], func=AF.Square,
                                                                                        accum_out=ss[:, 0:1]), r=[ysR[tk]], w=[jk4R, ssr])
                                        S.op("dve", lambda v, ss=ss: v.tensor_scalar(out=ss[:, 1:2], in0=ss[:, 0:1], scalar1=1.0 / D,
                                                                                      scalar2=EPS, op0=ALU.mult, op1=ALU.add), r=[ssr], w=[ssr])
                                        S.op("act", lambda a, ss=ss: a.activation(out=ss[:, 2:3], in_=ss[:, 1:2], func=AF.Sqrt), r=[ssr], w=[ssr])
                                        S.op("dve", lambda v, ss=ss: v.reciprocal(out=ss[:, 3:4], in_=ss[:, 2:3]), r=[ssr], w=[ssr])
                                        S.op("dve", lambda v, tk=tk, ss=ss: v.scalar_tensor_tensor(
                                            out=ysb[tk][:], in0=ysb[tk][:], scalar=ss[:, 3:4], in1=gpost[:], op0=ALU.mult, op1=ALU.mult),
                                            r=[ssr, gpR], w=[ysR[tk]])
                                        S.op("pool", lambda g, tk=tk, xt=xt: g.tensor_tensor(out=ysb[tk][:], in0=ysb[tk][:], in1=xt[:], op=ALU.add),
                                             r=[xr], w=[ysR[tk]])
                                        if last:
                                            dst = yp[tt * 128:(tt + 1) * 128, :] if tt < NPT else ys[:, :]
                                            S.dma("sp", dst, ysb[tk][:], r=[ysR[tk]], w=[R("yo")])
                                        else:
                                            S.dma("sp", x1[tt * 128:(tt + 1) * 128, :], ysb[tk][:], r=[ysR[tk]], w=[RX[tt]])
                                    S.barrier()
                                    _chk(6)
                    S.barrier()
                    _chk(7)
        except _StopBuild:
            pass
        S.finish()
    return nc


def _t5_bucket(rel):
    rel = np.asarray(rel, dtype=np.int64)
    nb = 16
    max_exact = 8
    ret = np.where(rel > 0, nb, 0)
    n = np.abs(rel)
    nf = np.maximum(n, 1).astype(np.float32)
    large = max_exact + (np.log(nf / np.float32(max_exact)) / np.float32(math.log(512 / max_exact))
                         * np.float32(nb - max_exact)).astype(np.int32)
    large = np.minimum(large, nb - 1)
    return ret + np.where(n < max_exact, n, large)


def _consts():
    ident = np.eye(128, dtype=np.float32)
    jm = np.ascontiguousarray(ident[::-1])
    j = np.arange(128)[:, None]
    s = np.arange(128)[None, :]
    tri = np.stack([(j > s), (j <= s), np.ones((128, 128), bool)]).astype(np.float32)
    i = np.arange(128)[:, None]
    q = np.arange(512)[None, :]
    masks = np.zeros((16, 128, 512), np.float32)
    for d in range(4):
        D0 = 128 * d
        masks[d] = (D0 + i < q)
        masks[4 + d] = ((D0 + i) // 64 <= q // 64)
    for m in range(8):
        D0 = -512 + 128 * m
        dd = q // 64 - (D0 + i) // 64
        masks[8 + m] = (dd >= 0) & (dd <= 8)
    t = np.arange(LC)
    b = _t5_bucket(511 - t)
    oh = (b[None, :] == np.arange(32)[:, None]).astype(np.float32)
    inv = (10000.0 ** (-np.arange(0, 32, 2, dtype=np.float32) / np.float32(32))).astype(np.float32)
    css = []
    for p in range(2):
        pos = np.concatenate([TP * p + np.arange(TP), PAST + np.arange(64), np.zeros(64)]).astype(np.float32)
        ang = pos[:, None] * inv[None, :]
        css.append(np.concatenate([np.cos(ang), np.sin(ang)], axis=1).astype(np.float32))
    return dict(c_ident=ident, c_j=jm, c_tri=tri, c_masks=masks, c_oh=oh), css


_NC = None


def kernel(x_prompt, x_sample, cache_sb_k, cache_sb_v, cache_band_k, cache_band_v,
           cache_diff_k, cache_diff_v, cache_mla_ckv, cache_mla_krope,
           norm_pre, norm_post, w_in, band_bias, t5_table, diff_lambda, diff_subln,
           mla_q_norm, mla_w_q_up, mla_kv_norm, mla_w_kv_up, w_branch, w_out):
    global _NC
    f = lambda a: np.ascontiguousarray(np.asarray(a, dtype=np.float32))
    if _NC is None:
        _NC = build_nc()
    nc = _NC
    consts, css = _consts()
    shared = dict(norm_pre=f(norm_pre), norm_post=f(norm_post), w_in=f(w_in), band_bias=f(band_bias), t5=f(t5_table),
                  dlam=f(diff_lambda).reshape(2, 256), subln=f(diff_subln), qnorm=f(mla_q_norm), wqup=f(mla_w_q_up),
                  kvnorm=f(mla_kv_norm), wkvup=f(mla_w_kv_up), wbr=f(w_branch), wout=f(w_out), **consts)
    in_maps = []
    for c in range(8):
        b, p = c // 2, c % 2
        m = dict(shared)
        m["c_cs"] = css[p]
        m["c_vb"] = np.full((128, 1), 0.0 if p == 1 else -30000.0, np.float32)
        m["xp"] = f(x_prompt[b, TP * p:TP * (p + 1)])
        xs_ = np.zeros((128, D), np.float32)
        xs_[:64] = np.asarray(x_sample[c], dtype=np.float32)
        m["xs"] = xs_
        cs_ = slice(c, c + 1)
        m["csbk"] = f(cache_sb_k[:, cs_]).reshape(2, 1, PAST, 512)
        m["csbv"] = f(cache_sb_v[:, cs_]).reshape(2, 1, PAST, 512)
        m["cbk"] = f(cache_band_k[:, cs_]).reshape(2, 1, 512, 512)
        m["cbv"] = f(cache_band_v[:, cs_]).reshape(2, 1, 512, 512)
        m["cdk"] = f(cache_diff_k[:, cs_]).reshape(2, 1, PAST, 512)
        m["cdv"] = f(cache_diff_v[:, cs_]).reshape(2, 1, PAST, 512)
        m["cckv"] = f(cache_mla_ckv[:, cs_])
        m["ckr"] = f(cache_mla_krope[:, cs_])
        in_maps.append(m)
    ncores = _NCORES[0]
    res = run_bass_kernel_spmd(nc, in_maps[:ncores], core_ids=list(range(ncores)))
    rs = [res.results[i if i < ncores else i % 2] for i in range(8)]

    def cat_p(name, shape_tail):
        a = np.stack([np.concatenate([rs[2 * b][name], rs[2 * b + 1][name]], axis=1) for b in range(4)], axis=1)
        return a.reshape((2, 4) + shape_tail).astype(np.float32)

    def cat_b(name, shape_tail):
        a = np.stack([rs[2 * b + 1][name] for b in range(4)], axis=1)
        return a.reshape((2, 4) + shape_tail).astype(np.float32)

    def cat_s(name, shape_tail):
        return np.concatenate([r[name] for r in rs], axis=1).reshape((2, 8) + shape_tail).astype(np.float32)

    y_p = np.stack([np.concatenate([rs[2 * b]["yp"], rs[2 * b + 1]["yp"]], axis=0) for b in range(4)], axis=0).astype(np.float32)
    y_s = np.stack([r["ys"][:64] for r in rs], axis=0).astype(np.float32)
    S2 = 2 * TP
    return (y_p, y_s,
            cat_p("sbk_p", (S2, 8, 64)), cat_p("sbv_p", (S2, 8, 64)),
            cat_b("bk_p", (512, 8, 64)), cat_b("bv_p", (512, 8, 64)),
            cat_p("dk_p", (S2, 4, 2, 64)), cat_p("dv_p", (S2, 4, 128)),
            cat_p("ckv_p", (S2, 128)), cat_p("kr_p", (S2, 32)),
            cat_s("sbk_s", (64, 8, 64)), cat_s("sbv_s", (64, 8, 64)),
            cat_s("bk_s", (512, 8, 64)), cat_s("bv_s", (512, 8, 64)),
            cat_s("dk_s", (64, 4, 2, 64)), cat_s("dv_s", (64, 4, 128)),
            cat_s("ckv_s", (64, 128)), cat_s("kr_s", (64, 32)))
```

```python
import contextlib
import math
import numpy as np
import concourse.bass as bass
import concourse.mybir as mybir
from concourse.bass_utils import run_bass_kernel_spmd

F32 = mybir.dt.float32
BF16 = mybir.dt.bfloat16
AF = mybir.ActivationFunctionType
ALU = mybir.AluOpType

D = 2048
DC = 16
TP = 1024
NS = 1
NPT = TP // 128
T = TP + 128
NT = T // 128
KVW = 3232
KCOL = {'A.k': 0, 'A.v': 512, 'B.k': 1024, 'B.v': 1536, 'C.k': 2048, 'C.v': 2560, 'ckv': 3072, 'kr': 3200}
DIN = 15392
EPS = 1e-6
PAST = 1024
LC = 1408
WC = 1280
LB = 1536
WB = 1408
MLA_SCALE = 96 ** -0.5
DEPTH = 2


class R:
    __slots__ = ("w", "r", "name", "excl")

    def __init__(self, name="", excl=False):
        self.w = None
        self.r = []
        self.name = name
        self.excl = excl


class Sched:
    def __init__(self, nc, es):
        self.nc = nc
        self.eng = {"pe": nc.tensor, "act": nc.scalar, "dve": nc.vector, "pool": nc.gpsimd, "sp": nc.sync}
        self.csem = {}
        self.ccnt = {}
        for e in ("pe", "act", "dve", "pool"):
            self.csem[e] = es.enter_context(nc.semaphore("c_" + e))
            self.ccnt[e] = 0
        self.dsem = {}
        self.dcnt = {}
        self.dnext = {}
        for q, n in (("sp", 40), ("pool", 40)):
            self.dsem[q] = [es.enter_context(nc.semaphore(f"d_{q}{i}")) for i in range(n)]
            self.dcnt[q] = [0] * n
            self.dnext[q] = 0
        self.seen = {e: {} for e in self.eng}
        self.all_dma_tokens = []
        self.cccnt = 0
        self.ccsem = es.enter_context(nc.semaphore("ccsem"))

    def _wait(self, e, tok):
        sem, val, owner = tok
        if e == "pe" and owner == "pe":
            return
        key = id(sem)
        if self.seen[e].get(key, 0) >= val:
            return
        self.eng[e].wait_ge(sem, val)
        self.seen[e][key] = val

    def _deps(self, e, r, w):
        for x in r:
            if x.w is not None:
                self._wait(e, x.w)
            if x.excl:
                for t in x.r:
                    if t[2] != e:
                        self._wait(e, t)
        for x in w:
            if x.w is not None:
                self._wait(e, x.w)
            for t in x.r:
                self._wait(e, t)

    def _commit(self, tok, r, w):
        for x in r:
            x.r.append(tok)
            if len(x.r) > 24:
                x.r = x.r[-24:]
        for x in w:
            x.w = tok
            x.r = []

    def op(self, e, fn, r=(), w=()):
        if _DEAD[0]:
            return None
        self._deps(e, r, w)
        inst = fn(self.eng[e])
        self.ccnt[e] += 1
        tok = (self.csem[e], self.ccnt[e], e)
        inst.then_inc(tok[0], 1)
        self._commit(tok, r, w)
        return tok

    def mm(self, fns, r=(), w=()):
        if _DEAD[0]:
            return None
        self._deps("pe", r, w)
        inst = None
        for fn in fns:
            inst = fn(self.eng["pe"])
        self.ccnt["pe"] += 1
        tok = (self.csem["pe"], self.ccnt["pe"], "pe")
        inst.then_inc(tok[0], 1)
        self._commit(tok, r, w)
        return tok

    def dma(self, q, out, in_, r=(), w=(), **kw):
        if _DEAD[0]:
            return None
        i = self.dnext[q]
        self.dnext[q] = (i + 1) % len(self.dsem[q])
        sem = self.dsem[q][i]
        if self.dcnt[q][i]:
            self._wait(q, (sem, self.dcnt[q][i], "dma"))
        self._deps(q, r, w)
        inst = self.eng[q].dma_start(out=out, in_=in_, **kw)
        self.dcnt[q][i] += 16
        tok = (sem, self.dcnt[q][i], "dma")
        inst.then_inc(sem, 16)
        self._commit(tok, r, w)
        return tok

    def collective(self, ccsem, in_t, out_t, r=(), w=()):
        if _DEAD[0]:
            return None
        self._deps("pool", r, w)
        inst = self.eng["pool"].collective_compute(
            "AllGather", ALU.bypass, replica_groups=[[2 * i_, 2 * i_ + 1] for i_ in range(_NCORES[0] // 2)],
            ins=[in_t.ap().opt()], outs=[out_t.ap().opt()])
        self.cccnt += 1
        tok = (ccsem, self.cccnt, "cc")
        inst.then_inc(ccsem)
        self._commit(tok, r, w)
        return tok

    def barrier(self):
        if _DEAD[0]:
            return
        toks = []
        for e in ("pe", "act", "dve", "pool"):
            if self.ccnt[e]:
                toks.append((self.csem[e], self.ccnt[e], e + "_b"))
        for q in self.dsem:
            for i, s in enumerate(self.dsem[q]):
                if self.dcnt[q][i]:
                    toks.append((s, self.dcnt[q][i], "dma"))
        if self.cccnt:
            toks.append((self.ccsem, self.cccnt, "cc"))
        for e in self.eng:
            for t in toks:
                if e == "pe" and t[2] == "pe_b":
                    continue
                self._wait(e, t)

    def finish(self):
        for q in self.dsem:
            for i, s in enumerate(self.dsem[q]):
                if self.dcnt[q][i]:
                    self._wait("sp", (s, self.dcnt[q][i], "dma"))


_UID = [0]
_STOP = [99]
_NCORES = [8]
WARM = 3


class _StopBuild(Exception):
    pass


_DEAD = [False]


def _chk(n):
    if round(_STOP[0] * 1000) <= round(n * 1000):
        _DEAD[0] = True


class Pool:
    def __init__(self, es, nc, name, n, shape, dtype):
        _UID[0] += 1
        self.t = [es.enter_context(nc.sbuf_tensor(f"{name}{i}_{_UID[0]}", shape, dtype)) for i in range(n)]
        self.res = [R(f"{name}{i}") for i in range(n)]
        self.i = 0

    def get(self):
        i = self.i
        self.i = (i + 1) % len(self.t)
        return self.t[i], self.res[i]


def bc_ap(ap, dims):
    return bass.AP(tensor=ap.tensor, offset=ap.offset, ap=[list(ap.ap[0])] + [list(d) for d in dims])


def build_nc():
    nc = bass.Bass("TRN2", target_bir_lowering=False)
    _DEAD[0] = False
    dt = nc.dram_tensor

    def inp(name, shape, dtype=F32):
        return dt(name, list(shape), dtype, kind="ExternalInput").ap()

    def outp(name, shape):
        return dt(name, list(shape), F32, kind="ExternalOutput").ap()

    def scr(name, shape, dtype=F32):
        return dt(name, list(shape), dtype).ap()

    xp = inp("xp", [TP, D])
    xs = inp("xs", [128, D])
    csbk = inp("csbk", [2, NS, PAST, 512])
    csbv = inp("csbv", [2, NS, PAST, 512])
    cbk = inp("cbk", [2, NS, 512, 512])
    cbv = inp("cbv", [2, NS, 512, 512])
    cdk = inp("cdk", [2, NS, PAST, 512])
    cdv = inp("cdv", [2, NS, PAST, 512])
    cckv = inp("cckv", [2, NS, PAST, 128])
    ckr = inp("ckr", [2, NS, PAST, 32])
    norm_pre = inp("norm_pre", [2, D])
    norm_post = inp("norm_post", [2, D])
    w_in = inp("w_in", [2, D, DIN])
    band_bias = inp("band_bias", [2, 513, 8])
    t5 = inp("t5", [32, 4])
    dlam = inp("dlam", [2, 256])
    subln = inp("subln", [2, 128])
    qnorm = inp("qnorm", [2, 384])
    wqup = inp("wqup", [2, 384, 768])
    kvnorm = inp("kvnorm", [2, 128])
    wkvup = inp("wkvup", [2, 128, 1024])
    wbr = inp("wbr", [2, 4, 512, D])
    wout = inp("wout", [2, D, D])
    c_ident = inp("c_ident", [128, 128])
    c_j = inp("c_j", [128, 128])
    c_tri = inp("c_tri", [3, 128, 128])
    c_masks = inp("c_masks", [16, 128, 512])
    c_oh = inp("c_oh", [32, LC])
    c_cs = inp("c_cs", [T, 32])
    c_vb = inp("c_vb", [128, 1])

    yp = outp("yp", [TP, D])
    ys = outp("ys", [128, D])
    sbk_p = outp("sbk_p", [2, TP, 512])
    sbv_p = outp("sbv_p", [2, TP, 512])
    bk_p = outp("bk_p", [2, 512, 512])
    bv_p = outp("bv_p", [2, 512, 512])
    dk_p = outp("dk_p", [2, TP, 512])
    dv_p = outp("dv_p", [2, TP, 512])
    ckv_p = outp("ckv_p", [2, TP, 128])
    kr_p = outp("kr_p", [2, TP, 32])
    sbk_s = outp("sbk_s", [2, NS, 64, 512])
    sbv_s = outp("sbv_s", [2, NS, 64, 512])
    bk_s = outp("bk_s", [2, NS, 512, 512])
    bv_s = outp("bv_s", [2, NS, 512, 512])
    dk_s = outp("dk_s", [2, NS, 64, 512])
    dv_s = outp("dv_s", [2, NS, 64, 512])
    ckv_s = outp("ckv_s", [2, NS, 64, 128])
    kr_s = outp("kr_s", [2, NS, 64, 32])

    x1 = scr("x1", [T, D])
    kvloc_t = [dt(f"kvloc{t_}", [128, KVW], F32) for t_ in range(NPT)]
    kvloc = [t_.ap() for t_ in kvloc_t]
    kvall_t = [[dt(f"kvall{i}_{t_}", [256, KVW], F32) for t_ in range(NPT)] for i in range(2)]
    kvall = [[t_.ap() for t_ in row] for row in kvall_t]
    qT_scr = scr("qT_scr", [3, 512, T], BF16)
    gT_scr = scr("gT_scr", [2048, T], BF16)
    mT_scr = scr("mT_scr", [8192, T], BF16)
    qn_scr = scr("qn_scr", [T, 384])
    gvC_scr = scr("gvC_scr", [4, LC])
    gvB_scr = scr("gvB_scr", [8, LB])

    with contextlib.ExitStack() as es:
        S = Sched(nc, es)

        try:
            def sb(name, shape, dtype, stack=es):
                _UID[0] += 1
                return stack.enter_context(nc.sbuf_tensor(f"{name}_{_UID[0]}", list(shape), dtype))

            PS = [es.enter_context(nc.psum_tensor(f"ps{i}", [128, 512], F32)) for i in range(7)]
            PSR = [R(f"ps{i}", excl=True) for i in range(7)]
            psT = es.enter_context(nc.psum_tensor("psT", [128, 1024], BF16))
            psTR = R("psT", excl=True)

            identb = sb("identb", [128, 128], BF16)
            jm = sb("jm", [128, 128], F32)
            trib = sb("trib", [128, 3, 128], BF16)
            masks = sb("masks", [128, 16, 512], BF16)
            EGC = sb("EGC", [128, 4, WC], BF16)
            EGB = sb("EGB", [128, 8, WB], BF16)
            satC = sb("satC", [128, 4], F32)
            cR = R("consts")
            EGCR = R("EGC")
            EGBR = R("EGB")
            S.dma("pool", identb[:], c_ident[:, :], w=[cR])
            S.dma("sp", jm[:], c_j[:, :], w=[cR])
            S.dma("pool", trib[:], c_tri.rearrange("m p n -> p m n"), w=[cR])
            for m4 in range(4):
                S.dma("pool", masks[:, 4 * m4:4 * m4 + 4, :],
                      c_masks[4 * m4:4 * m4 + 4].rearrange("m p n -> p m n"), w=[cR])
            S.dma("sp", satC[:], bass.AP(tensor=t5.tensor, offset=15 * 4, ap=[[0, 128], [1, 4]]), w=[cR])
            vbt = sb("vbt", [128, 1], F32)
            satCv = sb("satCv", [128, 4], F32)
            S.dma("sp", vbt[:], c_vb[:, :], w=[cR])
            S.op("dve", lambda v: v.tensor_scalar(out=satCv[:], in0=satC[:], scalar1=vbt[:, 0:1], scalar2=None, op0=ALU.add),
                 r=[cR], w=[cR])
            kvallR = [[R(f"kvall{i}_{t_}") for t_ in range(NPT)] for i in range(2)]
            TRI = trib[:, 0, :]
            TRIC = trib[:, 1, :]
            ONES = trib[:, 2, :]

            def toeplitz_build(gv_scr, nheads, L, W, EG, EGR, stack, defer=None):
                nb = 1 if defer is None else 2
                hks = [sb("hk", [128, W], F32, stack) for _ in range(nb)]
                hkRs = [R("hk") for _ in range(nb)]

                def dma_step(h):
                    src = bass.AP(tensor=gv_scr.tensor, offset=h * L, ap=[[1, 128], [1, W]])
                    S.dma("sp", hks[h % nb][:], src, r=[gvR], w=[hkRs[h % nb]])

                def comp_step(h, bank=None):
                    hk, hkR = hks[h % nb], hkRs[h % nb]
                    c = 0
                    bi = 0
                    while c < W:
                        n = min(512, W - c)
                        if bank is None:
                            ps_, pr_ = PS[bi % 2], PSR[bi % 2]
                        else:
                            ps_, pr_ = bank()
                        S.mm([lambda pe: pe.matmul(ps_[:, 0:n], lhsT=jm[:], rhs=hk[:, c:c + n], start=True, stop=True)],
                             r=[hkR, cR], w=[pr_])
                        S.op("act", lambda a: a.activation(out=EG[:, h, c:c + n], in_=ps_[:, 0:n], func=AF.Exp),
                             r=[pr_], w=[EGR])
                        c += n
                        bi += 1

                if defer is None:
                    for h in range(nheads):
                        dma_step(h)
                        comp_step(h)
                else:
                    defer.append(lambda: dma_step(0))
                    for h in range(nheads):
                        if h + 1 < nheads:
                            defer.append(lambda h=h: dma_step(h + 1))
                        defer.append(lambda h=h: comp_step(h, bank=defer_bank[0]))

            defer_bank = [None]
            gvR = R("gv")
            with contextlib.ExitStack() as st0:
                t5sb = sb("t5sb", [32, 4], F32, st0)
                ohsb = sb("ohsb", [32, LC], F32, st0)
                gvsb = sb("gvsb", [4, LC], F32, st0)
                tr = R("t0")
                S.dma("sp", t5sb[:], t5[:, :], w=[tr])
                S.dma("sp", ohsb[:], c_oh[:, :], w=[tr])
                c = 0
                gvsR = R("gvs")
                while c < LC:
                    n = min(512, LC - c)
                    S.mm([lambda pe, c=c, n=n: pe.matmul(PS[0][0:4, 0:n], lhsT=t5sb[:], rhs=ohsb[:, c:c + n],
                                                         start=True, stop=True)], r=[tr], w=[PSR[0]])
                    S.op("dve", lambda v, c=c, n=n: v.tensor_copy(out=gvsb[:, c:c + n], in_=PS[0][0:4, 0:n]),
                         r=[PSR[0]], w=[gvsR])
                    c += n
                S.dma("sp", gvC_scr[:, :], gvsb[:], r=[gvsR], w=[gvR])
                toeplitz_build(gvC_scr, 4, LC, WC, EGC, EGCR, st0)
                S.barrier()
                _chk(0)

            RX = [R(f"x{t}") for t in range(NT)]
            RKV = {}

            def rkv(name, i):
                k = (name, i)
                if k not in RKV:
                    RKV[k] = R(str(k))
                return RKV[k]

            RFM = {}

            def rfm(name, tb):
                k = (name, tb)
                if k not in RFM:
                    RFM[k] = R(str(k))
                return RFM[k]

            TB = [(0, 512), (512, 512), (1024, 128)]

            for l in range(DEPTH):
                lam_init = 0.8 - 0.6 * math.exp(-0.3 * l)
                last = (l == DEPTH - 1)
                with contextlib.ExitStack() as sl:
                    wqupb = sb("wqupb", [128, 3, 768], BF16, sl)
                    wkvupb = sb("wkvupb", [128, 1024], BF16, sl)
                    lcol = sb("lcol", [128, 8], F32, sl)
                    lR = R("layerconst")
                    S.dma("pool", wqupb[:], wqup[l].rearrange("(c p) n -> p c n", p=128), w=[lR])
                    S.dma("pool", wkvupb[:], wkvup[l], w=[lR])

                    with contextlib.ExitStack() as s2:
                        hT = sb("hT", [128, DC, T], BF16, s2)
                        hTR = [R(f"hT{t}") for t in range(NT)]
                        DEFER = []
                        bb = sb("bb", [8, 513], F32, s2)
                        gvb = sb("gvb", [8, LB], F32, s2)
                        bbR = R("bb")
                        S.dma("sp", bb[:], bass.AP(tensor=band_bias.tensor, offset=l * 513 * 8, ap=[[1, 8], [8, 513]]),
                              w=[bbR], allow_slow_non_contiguous=True)
                        gvbR = R("gvb")
                        S.op("dve", lambda v: v.tensor_copy(out=gvb[:, 0:255], in_=bc_ap(bb[:, 0:1], [[0, 255]])),
                             r=[bbR], w=[gvbR])
                        S.op("dve", lambda v: v.tensor_copy(out=gvb[:, 255:768], in_=bb[:, 0:513]), r=[bbR], w=[gvbR])
                        S.op("dve", lambda v: v.tensor_copy(out=gvb[:, 768:LB], in_=bc_ap(bb[:, 512:513], [[0, LB - 768]])),
                             r=[bbR], w=[gvbR])
                        S.dma("sp", gvB_scr[:, :], gvb[:], r=[gvbR], w=[gvR])
                        toeplitz_build(gvB_scr, 8, LB, WB, EGB, EGBR, s2, defer=DEFER)
                        dl = sb("dl", [128, 256], F32, s2)
                        dj = sb("dj", [128, 128], F32, s2)
                        sg = sb("sg", [128, 128], F32, s2)
                        la = sb("la", [128, 8], F32, s2)
                        dR = R("dl")
                        S.dma("sp", dl[:], bass.AP(tensor=dlam.tensor, offset=l * 256, ap=[[0, 128], [1, 256]]), w=[dR])
                        S.dma("sp", sg[:, 0:1], bass.AP(tensor=subln.tensor, offset=l * 128, ap=[[1, 128], [1, 1]]), w=[dR])
                        S.op("dve", lambda v: v.tensor_tensor(out=dj[:, 0:64], in0=dl[:, 0:64], in1=dl[:, 64:128], op=ALU.mult),
                             r=[dR], w=[dR])
                        S.op("dve", lambda v: v.tensor_tensor(out=dj[:, 64:128], in0=dl[:, 128:192], in1=dl[:, 192:256], op=ALU.mult),
                             r=[dR], w=[dR])
                        S.op("dve", lambda v: v.reduce_sum(out=la[:, 0:1], in_=dj[:, 0:64], axis=mybir.AxisListType.X), r=[dR], w=[dR])
                        S.op("dve", lambda v: v.reduce_sum(out=la[:, 1:2], in_=dj[:, 64:128], axis=mybir.AxisListType.X), r=[dR], w=[dR])
                        S.op("act", lambda a: a.activation(out=la[:, 2:4], in_=la[:, 0:2], func=AF.Exp), r=[dR], w=[dR])
                        S.op("dve", lambda v: v.tensor_tensor(out=la[:, 4:5], in0=la[:, 3:4], in1=la[:, 2:3], op=ALU.subtract),
                             r=[dR], w=[dR])
                        S.op("dve", lambda v: v.tensor_scalar(out=lcol[:, 0:1], in0=la[:, 4:5], scalar1=-lam_init, scalar2=None,
                                                              op0=ALU.add), r=[dR], w=[lR])
                        S.op("dve", lambda v: v.tensor_scalar(out=lcol[:, 1:2], in0=sg[:, 0:1], scalar1=(1.0 - lam_init),
                                                              scalar2=None, op0=ALU.mult), r=[dR], w=[lR])
                        with contextlib.ExitStack() as s1:
                            gpre = sb("gpre", [128, D], F32, s1)
                            gR = R("gpre")
                            S.dma("sp", gpre[:], bass.AP(tensor=norm_pre.tensor, offset=l * D, ap=[[0, 128], [1, D]]), w=[gR])
                            xpool = Pool(s1, nc, "xt", 2, [128, D], F32)
                            hbpool = Pool(s1, nc, "hb", 2, [128, D], BF16)
                            junk = sb("junk", [128, D], BF16, s1)
                            junkR = R("junk")
                            sspool = Pool(s1, nc, "ss", 2, [128, 4], F32)
                            p1st = {}

                            def p1front(tt):
                                xt, xr = xpool.get()
                                if l == 0:
                                    src = xp[tt * 128:(tt + 1) * 128, :] if tt < NPT else xs[:, :]
                                    S.dma("sp", xt[:], src, w=[xr])
                                else:
                                    S.dma("sp", xt[:], x1[tt * 128:(tt + 1) * 128, :], r=[RX[tt]], w=[xr])
                                ss, ssr = sspool.get()
                                S.op("dve", lambda v, ss=ss: v.memset(ss[:], 0.0), w=[ssr])
                                S.op("act", lambda a, xt=xt, ss=ss: a.activation(out=junk[:], in_=xt[:], func=AF.Square,
                                                                                 accum_out=ss[:, 0:1]),
                                     r=[xr], w=[junkR, ssr])
                                S.op("dve", lambda v, ss=ss: v.tensor_scalar(out=ss[:, 1:2], in0=ss[:, 0:1], scalar1=1.0 / D,
                                                                              scalar2=EPS, op0=ALU.mult, op1=ALU.add),
                                     r=[ssr], w=[ssr])
                                S.op("act", lambda a, ss=ss: a.activation(out=ss[:, 2:3], in_=ss[:, 1:2], func=AF.Sqrt),
                                     r=[ssr], w=[ssr])
                                S.op("dve", lambda v, ss=ss: v.reciprocal(out=ss[:, 3:4], in_=ss[:, 2:3]), r=[ssr], w=[ssr])
                                hb, hbr = hbpool.get()
                                S.op("dve", lambda v, hb=hb, xt=xt, ss=ss: v.scalar_tensor_tensor(
                                    out=hb[:], in0=xt[:], scalar=ss[:, 3:4], in1=gpre[:], op0=ALU.mult, op1=ALU.mult),
                                    r=[xr, ssr, gR], w=[hbr])
                                p1st[tt] = (hb, hbr)

                            def p1back(tt):
                                hb, hbr = p1st.pop(tt)
                                for half in range(2):
                                    S.mm([lambda pe, hb=hb, c=c, half=half: pe.transpose(
                                        out=psT[:, c * 128:(c + 1) * 128], in_=hb[:, (half * 8 + c) * 128:(half * 8 + c + 1) * 128],
                                        identity=identb[:]) for c in range(8)], r=[hbr, cR], w=[psTR])
                                    eng = "act" if half == 0 else "dve"
                                    if eng == "act":
                                        S.op("act", lambda a, half=half, tt=tt: a.copy(
                                            out=hT[:, half * 8:half * 8 + 8, tt * 128:(tt + 1) * 128],
                                            in_=psT[:, :].rearrange("p (a b) -> p a b", b=128)), r=[psTR], w=[hTR[tt]])
                                    else:
                                        S.op("dve", lambda v, half=half, tt=tt: v.tensor_copy(
                                            out=hT[:, half * 8:half * 8 + 8, tt * 128:(tt + 1) * 128],
                                            in_=psT[:, :].rearrange("p (a b) -> p a b", b=128)), r=[psTR], w=[hTR[tt]])


                            p1front(0)
                            for tt in range(NT):
                                if tt + 1 < NT:
                                    p1front(tt + 1)
                                p1back(tt)

                        S.barrier()
                        _chk(2)
                        wpool = Pool(s2, nc, "wblk", 3, [128, DC, 512], BF16)
                        w32 = sb("w32", [128, DC, 32], BF16, s2)
                        w32R = R("w32")
                        evf = Pool(s2, nc, "evf", 6, [128, 512], F32)
                        evb = Pool(s2, nc, "evb", 8, [128, 512], BF16)
                        dsb = Pool(s2, nc, "dsb", 2, [128, 544], F32)
                        dwk = Pool(s2, nc, "dwk", 2, [128, 640], F32)
                        csb = Pool(s2, nc, "csb", 2, [128, 32], F32)
                        gq = sb("gq", [128, 384], F32, s2)
                        gkv = sb("gkv", [128, 128], F32, s2)
                        jk2 = sb("jk2", [128, 384], F32, s2)
                        jk2R = R("jk2")
                        g2R = R("g2")
                        S.dma("sp", gq[:], bass.AP(tensor=qnorm.tensor, offset=l * 384, ap=[[0, 128], [1, 384]]), w=[g2R])
                        S.dma("sp", gkv[:], bass.AP(tensor=kvnorm.tensor, offset=l * 128, ap=[[0, 128], [1, 128]]), w=[g2R])
                        w_l = w_in[l].rearrange("(c p) n -> p c n", p=128)
                        psi = [0]

                        def next_ps():
                            i = psi[0]
                            psi[0] = (i + 1) % 6
                            return PS[i], PSR[i]

                        def tm_dests(name, tt):
                            outs = []
                            if tt < NPT:
                                rows = slice(tt * 128, (tt + 1) * 128)
                                kc = KCOL[name]
                                outs.append((kvloc[tt][:, kc:kc + 512], (0, 128), rkv("L" + name, tt)))
                                po = {"A.k": sbk_p, "A.v": sbv_p, "C.k": dk_p, "C.v": dv_p}.get(name)
                                if po is not None:
                                    outs.append((po[l, rows, :], (0, 128), rkv("O" + name, tt)))
                            else:
                                for s in range(NS):
                                    rs = (s * 64, s * 64 + 64)
                                    if name == "A.k":
                                        outs.append((sbk_s[l, s, :, :], rs, rkv("sbk_s", s)))
                                    elif name == "A.v":
                                        outs.append((sbv_s[l, s, :, :], rs, rkv("sbv_s", s)))
                                    elif name == "B.k":
                                        outs.append((bk_s[l, s, 448:512, :], rs, rkv("bk_s", s)))
                                    elif name == "B.v":
                                        outs.append((bv_s[l, s, 448:512, :], rs, rkv("bv_s", s)))
                                    elif name == "C.k":
                                        outs.append((dk_s[l, s, :, :], rs, rkv("dk_s", s)))
                                    elif name == "C.v":
                                        outs.append((dv_s[l, s, :, :], rs, rkv("dv_s", s)))
                            return outs

                        TMB = [("A.k", 512), ("A.v", 1024), ("B.k", 2048), ("B.v", 2560), ("C.k", 3584), ("C.v", 4096)]
                        for name, c0 in TMB:
                            wt, wr = wpool.get()
                            S.dma("pool", wt[:], w_l[:, :, c0:c0 + 512], w=[wr])
                            for tt in range(NT):
                                ps, pr = next_ps()
                                S.mm([lambda pe, ps=ps, wt=wt, dc=dc, tt=tt: pe.matmul(
                                    ps[:, :], lhsT=hT[:, dc, tt * 128:(tt + 1) * 128], rhs=wt[:, dc, :],
                                    start=(dc == 0), stop=(dc == DC - 1)) for dc in range(DC)],
                                    r=[wr, hTR[tt]], w=[pr])
                                ev, er = evf.get()
                                S.op("dve", lambda v, ev=ev, ps=ps: v.tensor_copy(out=ev[:], in_=ps[:, :]), r=[pr], w=[er])
                                for (dap, (ra, rb), dres) in tm_dests(name, tt):
                                    S.dma("sp", dap, ev[ra:rb, :], r=[er], w=[dres])
                        bandR = R("band_out")
                        for t_ in range(4, 8):
                            S.dma("sp", bk_p[l, (t_ - 4) * 128:(t_ - 3) * 128, :], kvloc[t_][:, 1024:1536], r=[rkv("LB.k", t_)], w=[bandR])
                            S.dma("sp", bv_p[l, (t_ - 4) * 128:(t_ - 3) * 128, :], kvloc[t_][:, 1536:2048], r=[rkv("LB.v", t_)], w=[bandR])
                        for s in range(NS):
                            S.dma("sp", bk_s[l, s, 0:448, :], cbk[l, s, 64:512, :], w=[bandR])
                            S.dma("sp", bv_s[l, s, 0:448, :], cbv[l, s, 64:512, :], w=[bandR])

                        wt, wr = wpool.get()
                        S.dma("pool", wt[:], w_l[:, :, 4608:5120], w=[wr])
                        S.dma("pool", w32[:], w_l[:, :, 5120:5152], w=[w32R])
                        for tt in range(NT):
                            ps, pr = next_ps()
                            ps2, pr2 = next_ps()
                            S.mm([lambda pe, ps=ps, wt=wt, dc=dc, tt=tt: pe.matmul(
                                ps[:, :], lhsT=hT[:, dc, tt * 128:(tt + 1) * 128], rhs=wt[:, dc, :],
                                start=(dc == 0), stop=(dc == DC - 1)) for dc in range(DC)], r=[wr, hTR[tt]], w=[pr])
                            S.mm([lambda pe, ps2=ps2, dc=dc, tt=tt: pe.matmul(
                                ps2[:, 0:32], lhsT=hT[:, dc, tt * 128:(tt + 1) * 128], rhs=w32[:, dc, :],
                                start=(dc == 0), stop=(dc == DC - 1)) for dc in range(DC)], r=[w32R, hTR[tt]], w=[pr2])
                            d, dr = dsb.get()
                            S.op("act", lambda a, d=d, ps=ps: a.copy(out=d[:, 0:512], in_=ps[:, :]), r=[pr], w=[dr])
                            S.op("act", lambda a, d=d, ps2=ps2: a.copy(out=d[:, 512:544], in_=ps2[:, 0:32]), r=[pr2], w=[dr])
                            wk, wkr = dwk.get()
                            cs, csr = csb.get()
                            S.dma("sp", cs[:], c_cs[tt * 128:(tt + 1) * 128, :], w=[csr])
                            st = 600
                            S.op("dve", lambda v, wk=wk: v.memset(wk[:, st:st + 8], 0.0), w=[wkr])
                            S.op("act", lambda a, d=d, wk=wk: a.activation(out=jk2[:, 0:384], in_=d[:, 0:384], func=AF.Square,
                                                                           accum_out=wk[:, st:st + 1]),
                                 r=[dr], w=[wkr, jk2R])
                            S.op("act", lambda a, d=d, wk=wk: a.activation(out=jk2[:, 0:128], in_=d[:, 384:512], func=AF.Square,
                                                                           accum_out=wk[:, st + 1:st + 2]),
                                 r=[dr], w=[wkr, jk2R])
                            S.op("dve", lambda v, wk=wk: v.tensor_scalar(out=wk[:, st + 2:st + 3], in0=wk[:, st:st + 1],
                                                                         scalar1=1.0 / 384, scalar2=EPS, op0=ALU.mult, op1=ALU.add),
                                 r=[wkr], w=[wkr])
                            S.op("dve", lambda v, wk=wk: v.tensor_scalar(out=wk[:, st + 3:st + 4], in0=wk[:, st + 1:st + 2],
                                                                         scalar1=1.0 / 128, scalar2=EPS, op0=ALU.mult, op1=ALU.add),
                                 r=[wkr], w=[wkr])
                            S.op("act", lambda a, wk=wk: a.activation(out=wk[:, st + 4:st + 6], in_=wk[:, st + 2:st + 4], func=AF.Sqrt),
                                 r=[wkr], w=[wkr])
                            S.op("dve", lambda v, wk=wk: v.reciprocal(out=wk[:, st + 6:st + 8], in_=wk[:, st + 4:st + 6]),
                                 r=[wkr], w=[wkr])
                            S.op("dve", lambda v, wk=wk, d=d: v.scalar_tensor_tensor(
                                out=wk[:, 0:384], in0=d[:, 0:384], scalar=wk[:, st + 6:st + 7], in1=gq[:], op0=ALU.mult, op1=ALU.mult),
                                r=[dr, wkr, g2R], w=[wkr])
                            S.op("dve", lambda v, wk=wk, d=d: v.scalar_tensor_tensor(
                                out=wk[:, 384:512], in0=d[:, 384:512], scalar=wk[:, st + 7:st + 8], in1=gkv[:], op0=ALU.mult, op1=ALU.mult),
                                r=[dr, wkr, g2R], w=[wkr])
                            x1a, x2a = d[:, 512:528], d[:, 528:544]
                            cosa, sina = cs[:, 0:16], cs[:, 16:32]
                            S.op("dve", lambda v, wk=wk: v.tensor_tensor(out=wk[:, 544:560], in0=x1a, in1=cosa, op=ALU.mult), r=[dr, csr], w=[wkr])
                            S.op("dve", lambda v, wk=wk: v.tensor_tensor(out=wk[:, 560:576], in0=x2a, in1=sina, op=ALU.mult), r=[dr, csr], w=[wkr])
                            S.op("dve", lambda v, wk=wk: v.tensor_tensor(out=wk[:, 512:528], in0=wk[:, 544:560], in1=wk[:, 560:576], op=ALU.subtract), r=[wkr], w=[wkr])
                            S.op("dve", lambda v, wk=wk: v.tensor_tensor(out=wk[:, 544:560], in0=x2a, in1=cosa, op=ALU.mult), r=[dr, csr], w=[wkr])
                            S.op("dve", lambda v, wk=wk: v.tensor_tensor(out=wk[:, 560:576], in0=x1a, in1=sina, op=ALU.mult), r=[dr, csr], w=[wkr])
                            S.op("dve", lambda v, wk=wk: v.tensor_tensor(out=wk[:, 528:544], in0=wk[:, 544:560], in1=wk[:, 560:576], op=ALU.add), r=[wkr], w=[wkr])
                            S.dma("sp", qn_scr[tt * 128:(tt + 1) * 128, :], wk[:, 0:384], r=[wkr], w=[rkv("qn", tt)])
                            if tt < NPT:
                                S.dma("sp", ckv_p[l, tt * 128:(tt + 1) * 128, :], wk[:, 384:512], r=[wkr], w=[rkv("Ockv", tt)])
                                S.dma("sp", kr_p[l, tt * 128:(tt + 1) * 128, :], wk[:, 512:544], r=[wkr], w=[rkv("Okr", tt)])
                                S.dma("sp", kvloc[tt][:, 3072:3200], wk[:, 384:512], r=[wkr], w=[rkv("Lckv", tt)])
                                S.dma("sp", kvloc[tt][:, 3200:3232], wk[:, 512:544], r=[wkr], w=[rkv("Lkr", tt)])
                            else:
                                for s in range(NS):
                                    S.dma("sp", ckv_s[l, s, :, :], wk[s * 64:s * 64 + 64, 384:512], r=[wkr], w=[rkv("ckv_s", s)])
                                    S.dma("sp", kr_s[l, s, :, :], wk[s * 64:s * 64 + 64, 512:544], r=[wkr], w=[rkv("kr_s", s)])

                        FMB = []
                        for bi, c0 in enumerate((0, 1536, 3072)):
                            FMB.append((c0, AF.Copy, qT_scr[bi], "q%d" % bi))
                        for n in range(4):
                            FMB.append((5152 + 512 * n, AF.Silu, gT_scr[512 * n:512 * n + 512], "g"))
                        for j in range(16):
                            FMB.append((7200 + 512 * j, AF.Sigmoid, mT_scr[512 * j:512 * j + 512], "m"))
                        defer_bank[0] = next_ps
                        for fmi, (c0, func, dst, nm) in enumerate(FMB):
                            if DEFER:
                                DEFER.pop(0)()
                            wt, wr = wpool.get()
                            S.dma("pool", wt[:], w_l[:, :, c0:c0 + 512], w=[wr])
                            if fmi == 7:
                                for t_ in range(NPT):
                                    S.collective(S.ccsem, kvloc_t[t_], kvall_t[l][t_],
                                                 r=[rkv("L" + n_, t_) for n_ in KCOL], w=[kvallR[l][t_]])
                            for ch in range(4):
                                for tbi, (t0, nt) in enumerate(TB):
                                    ps, pr = next_ps()
                                    tts = [hTR[t0 // 128 + i] for i in range(nt // 128)]
                                    S.mm([lambda pe, ps=ps, wt=wt, dc=dc, ch=ch, t0=t0, nt=nt: pe.matmul(
                                        ps[:, 0:nt], lhsT=wt[:, dc, ch * 128:(ch + 1) * 128], rhs=hT[:, dc, t0:t0 + nt],
                                        start=(dc == 0), stop=(dc == DC - 1)) for dc in range(DC)], r=[wr] + tts, w=[pr])
                                    ev, er = evb.get()
                                    S.op("act", lambda a, ev=ev, ps=ps, nt=nt, func=func: a.activation(
                                        out=ev[:, 0:nt], in_=ps[:, 0:nt], func=func), r=[pr], w=[er])
                                    S.dma("sp", dst[ch * 128:(ch + 1) * 128, t0:t0 + nt], ev[:, 0:nt], r=[er], w=[rfm(nm, tbi)])
                        while DEFER:
                            DEFER.pop(0)()
                        S.barrier()
                        _chk(3)

                    for blk in range(2):
                        with contextlib.ExitStack() as s34:
                            gatedT = sb("gatedT", [128, 16, 640], BF16, s34)
                            gatedR = R("gated")
                            groups = [dict(kind="p", qb=blk, tok0=512 * blk, nq=512, col0=0, tbi=blk)]
                            NQB = 512
                            CPS = [(0, 512)]
                            MTB = [blk]
                            if blk == 1:
                                groups += [dict(kind="s", s=s, tok0=TP + 64 * s, nq=64, col0=512 + 64 * s, tbi=2) for s in range(NS)]
                                NQB = 640
                                CPS = [(0, 512), (512, 128)]
                                MTB = [1, 2]
                                S.op("dve", lambda v: v.memset(gatedT[:, :, 576:640], 0.0), w=[gatedR])
                            with contextlib.ExitStack() as s3:
                                KT = sb("KT", [128, 8, 2048], BF16, s3)
                                KTR = R("KT")
                                Vsb = sb("Vsb", [128, 16, 512], BF16, s3)
                                VR = R("V")
                                qT = sb("qT", [128, 8, 512], BF16, s3)
                                qTR = R("qT")
                                gT = sb("gT", [128, 4, 512], BF16, s3)
                                gTR = R("gT")
                                kraw = Pool(s3, nc, "kraw", 3, [128, 512], BF16)
                                kst = Pool(s3, nc, "kst", 3, [128, 512], F32)
                                vst = Pool(s3, nc, "vst", 3, [128, 512], F32)

                                def warm(n):
                                    for _ in range(n):
                                        S.mm([lambda pe: pe.matmul(PS[0][:, 0:512], lhsT=identb[:], rhs=masks[:, 0, :],
                                                                   start=True, stop=True)], r=[cR], w=[PSR[0]])
                                wf = Pool(s3, nc, "wf", 4, [128, 512], F32)
                                t1p = Pool(s3, nc, "t1p", 6, [128, 512], F32)
                                wb_ = Pool(s3, nc, "wb", 16, [128, 512], BF16)
                                small = Pool(s3, nc, "small", 7, [128, 1024], BF16)
                                qmf = Pool(s3, nc, "qmf", 2, [128, 768], F32)
                                csq = Pool(s3, nc, "csq", 2, [128, 32], F32)

                                for G in groups:
                                    nq = G["nq"]
                                    tok0 = G["tok0"]
                                    col0 = G["col0"]

                                    def key_tiles(br):
                                        tl = []
                                        if G["kind"] == "p":
                                            qb = G["qb"]
                                            hi = 8 + 4 * qb + 4
                                            lo = (8 + 4 * qb - 4) if br == "B" else 0
                                            for kt in range(lo, hi):
                                                D0 = 128 * kt - (1024 + 512 * qb)
                                                prv = kt < 8
                                                rows = slice(0, 128)
                                                if prv:
                                                    srcd = kvall[l][kt]
                                                    rs_ = [kvallR[l][kt]]
                                                else:
                                                    srcd = kvloc[kt - 8]
                                                if br != "D":
                                                    kc, vc = KCOL[br + ".k"], KCOL[br + ".v"]
                                                    if not prv:
                                                        rs_ = [rkv("L" + br + ".k", kt - 8), rkv("L" + br + ".v", kt - 8)]
                                                    tl.append((128, srcd[rows, kc:kc + 512], srcd[rows, vc:vc + 512], D0, rs_, prv))
                                                else:
                                                    if not prv:
                                                        rs_ = [rkv("Lckv", kt - 8), rkv("Lkr", kt - 8)]
                                                    tl.append((128, srcd[rows, 3072:3200], srcd[rows, 3200:3232], D0, rs_, prv))
                                        else:
                                            s = G["s"]
                                            if br == "B":
                                                for kt in range(4):
                                                    rows = slice(kt * 128, kt * 128 + 128)
                                                    tl.append((128, cbk[l, s, rows, :], cbv[l, s, rows, :], -512 + 128 * kt, [], False))
                                                tl.append((64, bk_s[l, s, 448:512, :], bv_s[l, s, 448:512, :], 0, [rkv("bk_s", s), rkv("bv_s", s)], False))
                                            else:
                                                for kt in range(8):
                                                    rows = slice(kt * 128, kt * 128 + 128)
                                                    D0 = kt * 128 - 1024
                                                    if br == "A":
                                                        tl.append((128, csbk[l, s, rows, :], csbv[l, s, rows, :], D0, [], False))
                                                    elif br == "C":
                                                        tl.append((128, cdk[l, s, rows, :], cdv[l, s, rows, :], D0, [], False))
                                                    else:
                                                        tl.append((128, cckv[l, s, rows, :], ckr[l, s, rows, :], D0, [], False))
                                                if br == "A":
                                                    tl.append((64, sbk_s[l, s, :, :], sbv_s[l, s, :, :], 0, [rkv("sbk_s", s), rkv("sbv_s", s)], False))
                                                elif br == "C":
                                                    tl.append((64, dk_s[l, s, :, :], dv_s[l, s, :, :], 0, [rkv("dk_s", s), rkv("dv_s", s)], False))
                                                else:
                                                    tl.append((64, ckv_s[l, s, :, :], kr_s[l, s, :, :], 0, [rkv("ckv_s", s), rkv("kr_s", s)], False))
                                        return tl

                                    for bi, br in enumerate("ABCD"):
                                        tiles = key_tiles(br)
                                        S.dma("sp", gT[:, :, 0:nq],
                                              gT_scr[512 * bi:512 * bi + 512, tok0:tok0 + nq].rearrange("(c p) t -> p c t", p=128),
                                              r=[rfm("g", G["tbi"])], w=[gTR])
                                        if br != "D":
                                            S.dma("sp", qT[:, 0:4, 0:nq],
                                                  qT_scr[bi, :, tok0:tok0 + nq].rearrange("(c p) t -> p c t", p=128),
                                                  r=[rfm("q%d" % bi, G["tbi"])], w=[qTR])
                                            for ti, (nk, kap, vap, D0, rs, prv) in enumerate(tiles):
                                                kf, kfr = kst.get()
                                                vf, vfr = vst.get()
                                                S.dma("sp", kf[0:nk, :], kap, r=rs, w=[kfr])
                                                S.dma("sp", vf[0:nk, :], vap, r=rs, w=[vfr])
                                                kr_, krr = kraw.get()
                                                S.op("dve", lambda v: v.tensor_copy(out=kr_[0:nk, :], in_=kf[0:nk, :]), r=[kfr], w=[krr])
                                                S.op("act", lambda a: a.copy(out=Vsb[0:nk, ti, :], in_=vf[0:nk, :]), r=[vfr], w=[VR])
                                                S.mm([lambda pe, kr_=kr_, c=c, nk=nk: pe.transpose(
                                                    out=psT[:, c * 128:c * 128 + nk], in_=kr_[0:nk, c * 128:(c + 1) * 128],
                                                    identity=identb[0:nk, 0:nk]) for c in range(4)], r=[krr, cR], w=[psTR])
                                                S.op("dve", lambda v, ti=ti, nk=nk: v.tensor_copy(
                                                    out=KT[:, 0:4, ti * 128:ti * 128 + nk],
                                                    in_=psT[:, 0:512].rearrange("p (a b) -> p a b", b=128)[:, :, 0:nk]),
                                                    r=[psTR], w=[KTR])
                                        else:
                                            ntt = max(1, nq // 128)
                                            for qi in range(ntt):
                                                nr = min(128, nq)
                                                r0 = tok0 + qi * 128
                                                tt = r0 // 128
                                                qn_, qnr = small.get()
                                                kf, kfr = kst.get()
                                                S.dma("sp", kf[0:nr, 0:384], qn_scr[r0:r0 + nr, :], r=[rkv("qn", tt)], w=[kfr])
                                                S.op("dve", lambda v: v.tensor_copy(out=qn_[0:nr, 0:384], in_=kf[0:nr, 0:384]), r=[kfr], w=[qnr])
                                                cs, csr = csq.get()
                                                S.dma("sp", cs[0:nr, :], c_cs[r0:r0 + nr, :], w=[csr])
                                                S.mm([lambda pe, qn_=qn_, c=c, nr=nr: pe.transpose(
                                                    out=psT[:, c * 128:c * 128 + nr], in_=qn_[0:nr, c * 128:(c + 1) * 128],
                                                    identity=identb[0:nr, 0:nr]) for c in range(3)], r=[qnr, cR], w=[psTR])
                                                qnT, qnTr = small.get()
                                                S.op("dve", lambda v, qnT=qnT: v.tensor_copy(out=qnT[:, 0:384], in_=psT[:, 0:384]),
                                                     r=[psTR], w=[qnTr])
                                                S.mm([lambda pe, qnT=qnT, c=c, nr=nr: pe.matmul(
                                                    PS[5][0:nr, :], lhsT=qnT[:, c * 128:c * 128 + nr], rhs=wqupb[:, c, 0:512],
                                                    start=(c == 0), stop=(c == 2)) for c in range(3)], r=[qnTr, lR], w=[PSR[5]])
                                                S.mm([lambda pe, qnT=qnT, c=c, nr=nr: pe.matmul(
                                                    PS[6][0:nr, 0:256], lhsT=qnT[:, c * 128:c * 128 + nr], rhs=wqupb[:, c, 512:768],
                                                    start=(c == 0), stop=(c == 2)) for c in range(3)], r=[qnTr, lR], w=[PSR[6]])
                                                qf, qfr = qmf.get()
                                                S.op("act", lambda a, qf=qf, nr=nr: a.copy(out=qf[0:nr, 0:512], in_=PS[5][0:nr, :]),
                                                     r=[PSR[5]], w=[qfr])
                                                S.op("act", lambda a, qf=qf, nr=nr: a.copy(out=qf[0:nr, 512:768], in_=PS[6][0:nr, 0:256]),
                                                     r=[PSR[6]], w=[qfr])
                                                qb_, qbr = small.get()
                                                qf3 = qf[0:nr, :].rearrange("p (h e) -> p h e", e=96)
                                                qb3 = qb_[0:nr, 0:768].rearrange("p (h e) -> p h e", e=96)
                                                tmp, tmpr = wf.get()
                                                ta = tmp[0:nr, 0:128].rearrange("p (h e) -> p h e", e=16)
                                                tb_ = tmp[0:nr, 128:256].rearrange("p (h e) -> p h e", e=16)
                                                cosb = bc_ap(cs[0:nr, 0:16], [[0, 8], [1, 16]])
                                                sinb = bc_ap(cs[0:nr, 16:32], [[0, 8], [1, 16]])
                                                S.op("dve", lambda v: v.tensor_copy(out=qb3[:, :, 0:64], in_=qf3[:, :, 0:64]), r=[qfr], w=[qbr])
                                                S.op("dve", lambda v: v.tensor_tensor(out=ta, in0=qf3[:, :, 64:80], in1=cosb, op=ALU.mult), r=[qfr, csr], w=[tmpr])
                                                S.op("dve", lambda v: v.tensor_tensor(out=tb_, in0=qf3[:, :, 80:96], in1=sinb, op=ALU.mult), r=[qfr, csr], w=[tmpr])
                                                S.op("dve", lambda v: v.tensor_tensor(out=qb3[:, :, 64:80], in0=ta, in1=tb_, op=ALU.subtract), r=[tmpr], w=[qbr, tmpr])
                                                S.op("dve", lambda v: v.tensor_tensor(out=ta, in0=qf3[:, :, 80:96], in1=cosb, op=ALU.mult), r=[qfr, csr], w=[tmpr])
                                                S.op("dve", lambda v: v.tensor_tensor(out=tb_, in0=qf3[:, :, 64:80], in1=sinb, op=ALU.mult), r=[qfr, csr], w=[tmpr])
                                                S.op("dve", lambda v: v.tensor_tensor(out=qb3[:, :, 80:96], in0=ta, in1=tb_, op=ALU.add), r=[tmpr], w=[qbr, tmpr])
                                                S.mm([lambda pe, qb_=qb_, h=h, nr=nr: pe.transpose(
                                                    out=psT[0:96, h * 128:h * 128 + nr], in_=qb_[0:nr, h * 96:(h + 1) * 96],
                                                    identity=identb[0:nr, 0:nr]) for h in range(8)], r=[qbr, cR], w=[psTR])
                                                S.op("dve", lambda v, qi=qi, nr=nr: v.tensor_copy(
                                                    out=qT[0:96, :, qi * 128:qi * 128 + nr],
                                                    in_=psT[0:96, :].rearrange("p (a b) -> p a b", b=128)[:, :, 0:nr]),
                                                    r=[psTR], w=[qTR])
                                            _chk(3.61)
                                            dst_ = {}

                                            def stA(ti):
                                                nk, cap, rap, D0, rs, prv = tiles[ti]
                                                b0 = 5 if ti % 2 == 0 else 3
                                                ck, ckr_ = small.get()
                                                kf, kfr = kst.get()
                                                S.dma("sp", kf[0:nk, 0:128], cap, r=rs, w=[kfr])
                                                S.dma("sp", kf[0:nk, 128:160], rap, r=rs, w=[kfr])
                                                S.op("dve", lambda v: v.tensor_copy(out=ck[0:nk, 0:160], in_=kf[0:nk, 0:160]), r=[kfr], w=[ckr_])
                                                S.mm([lambda pe: pe.transpose(
                                                    out=psT[:, 0:nk], in_=ck[0:nk, 0:128], identity=identb[0:nk, 0:nk])],
                                                    r=[ckr_, cR], w=[psTR])
                                                ckT, ckTr = small.get()
                                                S.op("dve", lambda v: v.tensor_copy(out=ckT[:, 0:nk], in_=psT[:, 0:nk]),
                                                     r=[psTR], w=[ckTr])
                                                S.mm([lambda pe: pe.matmul(
                                                    PS[b0][0:nk, :], lhsT=ckT[:, 0:nk], rhs=wkvupb[:, 0:512], start=True, stop=True)],
                                                    r=[ckTr, lR], w=[PSR[b0]])
                                                S.mm([lambda pe: pe.matmul(
                                                    PS[b0 + 1][0:nk, :], lhsT=ckT[:, 0:nk], rhs=wkvupb[:, 512:1024], start=True, stop=True)],
                                                    r=[ckTr, lR], w=[PSR[b0 + 1]])
                                                dst_[ti] = (ck, ckr_)

                                            def stB(ti):
                                                nk, cap, rap, D0, rs, prv = tiles[ti]
                                                b0 = 5 if ti % 2 == 0 else 3
                                                ck, ckr_ = dst_.pop(ti)
                                                kd, kdr = small.get()
                                                kd3 = kd[0:nk, 0:768].rearrange("p (h e) -> p h e", e=96)
                                                for hf in range(2):
                                                    pv = PS[b0 + hf][0:nk, :].rearrange("p (h e) -> p h e", e=128)
                                                    S.op("act", lambda a: a.copy(out=kd3[:, 4 * hf:4 * hf + 4, 0:64], in_=pv[:, :, 0:64]),
                                                         r=[PSR[b0 + hf]], w=[kdr])
                                                    S.op("dve", lambda v: v.tensor_copy(
                                                        out=Vsb[0:nk, ti, 256 * hf:256 * hf + 256].rearrange("p (h e) -> p h e", e=64),
                                                        in_=pv[:, :, 64:128]), r=[PSR[b0 + hf]], w=[VR])
                                                S.op("dve", lambda v: v.tensor_copy(
                                                    out=kd3[:, :, 64:96], in_=bc_ap(ck[0:nk, 128:160], [[0, 8], [1, 32]])), r=[ckr_], w=[kdr])
                                                S.mm([lambda pe, h=h: pe.transpose(
                                                    out=psT[0:96, h * 128:h * 128 + nk], in_=kd[0:nk, h * 96:(h + 1) * 96],
                                                    identity=identb[0:nk, 0:nk]) for h in range(8)], r=[kdr, cR], w=[psTR])
                                                S.op("dve", lambda v: v.tensor_copy(
                                                    out=KT[0:96, :, ti * 128:ti * 128 + nk],
                                                    in_=psT[0:96, :].rearrange("p (a b) -> p a b", b=128)[:, :, 0:nk]),
                                                    r=[psTR], w=[KTR])
                                                warm(WARM)

                                            stA(0)
                                            for ti in range(len(tiles)):
                                                if ti + 1 < len(tiles):
                                                    stA(ti + 1)
                                                stB(ti)

                                        _chk(3.0 + 0.2 * bi + 0.1)
                                        def mask_ap(kind, D0, nk):
                                            d = D0 // 128
                                            if kind == "A":
                                                return masks[0:nk, d, 0:nq]
                                            if kind == "CD":
                                                return masks[0:nk, 4 + d, 0:nq]
                                            return masks[0:nk, 8 + (D0 + 512) // 128, 0:nq]

                                        isP = G["kind"] == "p"
                                        LOOK = 3

                                        def run_pipe(items, s1, s2):
                                            n = len(items)
                                            for i in range(min(LOOK, n)):
                                                s1(i)
                                            for i in range(n):
                                                s2(i)
                                                if i + LOOK < n:
                                                    s1(i + LOOK)

                                        if br == "A":
                                            ntl = len(tiles)
                                            items = [(2 * hp_ + hf_, idx) for hp_ in range(4) for idx in range(ntl) for hf_ in range(2)]
                                            order = list(enumerate(tiles))[::-1]
                                            st = {}
                                            psbanks = [0, 1, 5]

                                            def s1q(i):
                                                h, idx = items[i]
                                                hp, r0 = h // 2, 64 * (h % 2)
                                                ti, (nk, kap, vap, D0, rs, prv) = order[idx]
                                                pS, pSR = PS[psbanks[i % 3]], PSR[psbanks[i % 3]]
                                                S.mm([lambda pe: pe.matmul(
                                                    pS[0:nk, 0:nq], lhsT=KT[r0:r0 + 64, hp, ti * 128:ti * 128 + nk],
                                                    rhs=qT[r0:r0 + 64, hp, 0:nq], start=True, stop=True)], r=[KTR, qTR], w=[pSR])

                                            def s1(i):
                                                h, idx = items[i]
                                                hp, r0 = h // 2, 64 * (h % 2)
                                                ti, (nk, kap, vap, D0, rs, prv) = order[idx]
                                                pS, pSR = PS[psbanks[i % 3]], PSR[psbanks[i % 3]]
                                                diag = (D0 >= 0)
                                                ez, ezr = wf.get()
                                                bkw = dict(bias=vbt[0:nk, 0:1]) if prv else {}
                                                S.op("act", lambda a: a.activation(
                                                    out=ez[0:nk, 0:nq], in_=pS[0:nk, 0:nq], func=AF.Exp, scale=0.125, **bkw), r=[pSR, cR], w=[ezr])
                                                sp, spr = wb_.get()
                                                S.op("act", lambda a: a.activation(
                                                    out=sp[0:nk, 0:nq], in_=ez[0:nk, 0:nq], func=AF.Ln, bias=1.0), r=[ezr], w=[spr])
                                                t1, t1r = t1p.get()
                                                S.op("dve", lambda v: v.scalar_tensor_tensor(
                                                    out=t1[0:nk, 0:nq], in0=pS[0:nk, 0:nq], scalar=0.125, in1=sp[0:nk, 0:nq],
                                                    op0=ALU.mult, op1=ALU.subtract), r=[pSR, spr], w=[t1r])
                                                if diag:
                                                    spm, spmr = wb_.get()
                                                    S.op("dve", lambda v: v.tensor_tensor(
                                                        out=spm[0:nk, 0:nq], in0=sp[0:nk, 0:nq], in1=mask_ap("A", D0, nk), op=ALU.mult),
                                                        r=[spr, cR], w=[spmr])
                                                else:
                                                    spm, spmr = sp, spr
                                                st[i] = (spm, spmr, t1, t1r)

                                            def s2a(i):
                                                h, idx = items[i]
                                                ti, (nk, kap, vap, D0, rs, prv) = order[idx]
                                                lb = 2 if h % 2 == 0 else 6
                                                psL, psLR = PS[lb], PSR[lb]
                                                spm, spmr, t1, t1r = st[i]
                                                fns = []
                                                rr = [spmr, cR]
                                                if idx > 0:
                                                    psp, pspr, pnk = st[("prev", h % 2)]
                                                    fns.append(lambda pe: pe.matmul(
                                                        psL[:, 0:nq], lhsT=TRIC[0:pnk, :], rhs=psp[0:pnk, 0:nq],
                                                        start=False, stop=False, skip_group_check=True))
                                                    rr.append(pspr)
                                                fns.append(lambda pe: pe.matmul(
                                                    psL[:, 0:nq], lhsT=TRI[0:nk, :], rhs=spm[0:nk, 0:nq],
                                                    start=(idx == 0), stop=True, skip_group_check=True))
                                                S.mm(fns, r=rr, w=[psLR])
                                                st[("prev", h % 2)] = (spm, spmr, nk)

                                            def s2t(i):
                                                h, idx = items[i]
                                                ti, (nk, kap, vap, D0, rs, prv) = order[idx]
                                                lb = 2 if h % 2 == 0 else 6
                                                psL, psLR = PS[lb], PSR[lb]
                                                spm, spmr, t1, t1r = st[i]
                                                S.op("dve", lambda v: v.tensor_tensor(
                                                    out=t1[0:nk, 0:nq], in0=t1[0:nk, 0:nq], in1=psL[0:nk, 0:nq], op=ALU.subtract),
                                                    r=[psLR], w=[t1r])

                                            def s2b(i):
                                                h, idx = items[i]
                                                hp, r0 = h // 2, 64 * (h % 2)
                                                ti, (nk, kap, vap, D0, rs, prv) = order[idx]
                                                diag = (D0 >= 0)
                                                lb = 2 if h % 2 == 0 else 6
                                                psL, psLR = PS[lb], PSR[lb]
                                                psO, psOR = PS[3 + (h % 2)], PSR[3 + (h % 2)]
                                                spm, spmr, t1, t1r = st.pop(i)
                                                wt_, wtr = wb_.get()
                                                bkw = dict(bias=vbt[0:nk, 0:1]) if prv else {}
                                                S.op("act", lambda a: a.activation(
                                                    out=wt_[0:nk, 0:nq], in_=t1[0:nk, 0:nq], func=AF.Exp, **bkw), r=[t1r, cR], w=[wtr])
                                                if diag:
                                                    S.op("dve", lambda v: v.tensor_tensor(
                                                        out=wt_[0:nk, 0:nq], in0=wt_[0:nk, 0:nq], in1=mask_ap("A", D0, nk), op=ALU.mult),
                                                        r=[cR], w=[wtr])
                                                S.mm([lambda pe: pe.matmul(
                                                    psO[:, 0:nq], lhsT=Vsb[0:nk, ti, hp * 128:(hp + 1) * 128], rhs=wt_[0:nk, 0:nq],
                                                    start=(idx == 0), stop=(idx == ntl - 1), skip_group_check=True)],
                                                    r=[wtr, VR], w=[psOR])
                                                if idx == ntl - 1:
                                                    S.op("dve", lambda v: v.tensor_tensor(
                                                        out=gatedT[r0:r0 + 64, hp, col0:col0 + nq], in0=psO[r0:r0 + 64, 0:nq],
                                                        in1=gT[r0:r0 + 64, hp, 0:nq], op=ALU.mult), r=[psOR, gTR], w=[gatedR])

                                            nit = len(items)
                                            for i in range(min(3, nit)):
                                                s1q(i)
                                            s1(0)
                                            if nit > 3:
                                                s1q(3)
                                            if nit > 1:
                                                s1(1)
                                            s2a(0)
                                            for i in range(nit):
                                                if i + 4 < nit:
                                                    s1q(i + 4)
                                                s2t(i)
                                                if i + 1 < nit:
                                                    s2a(i + 1)
                                                if i + 2 < nit:
                                                    s1(i + 2)
                                                s2b(i)
                                        elif br in ("B", "D"):
                                            ntl = len(tiles)
                                            items = [(2 * hp_ + hf_, idx) for hp_ in range(4) for idx in range(ntl) for hf_ in range(2)]
                                            st = {}

                                            def s1(i):
                                                h, idx = items[i]
                                                hp, r0 = h // 2, 64 * (h % 2)
                                                nk, kap, vap, D0, rs, prv = tiles[idx]
                                                ti = idx
                                                pS, pSR = PS[i % 3], PSR[i % 3]
                                                if br == "B":
                                                    S.mm([lambda pe: pe.matmul(
                                                        pS[0:nk, 0:nq], lhsT=KT[r0:r0 + 64, hp, ti * 128:ti * 128 + nk],
                                                        rhs=qT[r0:r0 + 64, hp, 0:nq], start=True, stop=True)], r=[KTR, qTR], w=[pSR])
                                                    scale = 0.125
                                                else:
                                                    S.mm([lambda pe: pe.matmul(
                                                        pS[0:nk, 0:nq], lhsT=KT[0:96, h, ti * 128:ti * 128 + nk],
                                                        rhs=qT[0:96, h, 0:nq], start=True, stop=True)], r=[KTR, qTR], w=[pSR])
                                                    scale = MLA_SCALE
                                                e, er = wb_.get()
                                                bkw = dict(bias=vbt[0:nk, 0:1]) if prv else {}
                                                S.op("act", lambda a: a.activation(
                                                    out=e[0:nk, 0:nq], in_=pS[0:nk, 0:nq], func=AF.Exp, scale=scale, **bkw), r=[pSR, cR], w=[er])
                                                if br == "B":
                                                    c0 = 384 - D0
                                                    S.op("dve", lambda v: v.tensor_tensor(
                                                        out=e[0:nk, 0:nq], in0=e[0:nk, 0:nq], in1=EGB[0:nk, h, c0:c0 + nq], op=ALU.mult),
                                                        r=[EGBR], w=[er])
                                                    if isP:
                                                        S.op("dve", lambda g: g.tensor_tensor(
                                                            out=e[0:nk, 0:nq], in0=e[0:nk, 0:nq], in1=mask_ap("B", D0, nk), op=ALU.mult),
                                                            r=[cR], w=[er])
                                                else:
                                                    if isP and D0 >= 0:
                                                        S.op("dve", lambda v: v.tensor_tensor(
                                                            out=e[0:nk, 0:nq], in0=e[0:nk, 0:nq], in1=mask_ap("CD", D0, nk), op=ALU.mult),
                                                            r=[cR], w=[er])
                                                st[i] = (e, er)

                                            def s2(i):
                                                h, idx = items[i]
                                                hp, r0 = h // 2, 64 * (h % 2)
                                                nk, kap, vap, D0, rs, prv = tiles[idx]
                                                ti = idx
                                                psO, psOR = PS[3 + (h % 2)], PSR[3 + (h % 2)]
                                                psD, psDR = PS[5 + (h % 2)], PSR[5 + (h % 2)]
                                                e, er = st.pop(i)
                                                S.mm([lambda pe: pe.matmul(
                                                    psO[:, 0:nq], lhsT=Vsb[0:nk, ti, hp * 128:(hp + 1) * 128], rhs=e[0:nk, 0:nq],
                                                    start=(idx == 0), stop=(idx == ntl - 1), skip_group_check=True)],
                                                    r=[er, VR], w=[psOR])
                                                S.mm([lambda pe: pe.matmul(
                                                    psD[:, 0:nq], lhsT=ONES[0:nk, :], rhs=e[0:nk, 0:nq],
                                                    start=(idx == 0), stop=(idx == ntl - 1), skip_group_check=True)],
                                                    r=[er, cR], w=[psDR])
                                                if idx == ntl - 1:
                                                    rc, rcr = wf.get()
                                                    S.op("act", lambda a: a.activation(out=rc[r0:r0 + 64, 0:nq], in_=psD[r0:r0 + 64, 0:nq], func=AF.Ln),
                                                         r=[psDR], w=[rcr])
                                                    S.op("act", lambda a: a.activation(out=rc[r0:r0 + 64, 0:nq], in_=rc[r0:r0 + 64, 0:nq], func=AF.Exp, scale=-1.0),
                                                         r=[], w=[rcr])
                                                    S.op("dve", lambda v: v.tensor_tensor(
                                                        out=rc[r0:r0 + 64, 0:nq], in0=psO[r0:r0 + 64, 0:nq], in1=rc[r0:r0 + 64, 0:nq], op=ALU.mult),
                                                        r=[psOR], w=[rcr])
                                                    S.op("pool", lambda g: g.tensor_tensor(
                                                        out=gatedT[r0:r0 + 64, 4 * bi + hp, col0:col0 + nq], in0=rc[r0:r0 + 64, 0:nq],
                                                        in1=gT[r0:r0 + 64, hp, 0:nq], op=ALU.mult), r=[rcr, gTR], w=[gatedR])

                                            run_pipe(items, s1, s2)
                                        else:
                                            ntl = len(tiles)
                                            items = [(h, m, idx) for h in range(4) for idx in range(ntl) for m in range(2)]
                                            st = {}

                                            def s1(i):
                                                h, m, idx = items[i]
                                                r0 = 64 * m
                                                nk, kap, vap, D0, rs, prv = tiles[idx]
                                                ti = idx
                                                pS, pSR = PS[i % 3], PSR[i % 3]
                                                S.mm([lambda pe: pe.matmul(
                                                    pS[0:nk, 0:nq], lhsT=KT[r0:r0 + 64, h, ti * 128:ti * 128 + nk],
                                                    rhs=qT[r0:r0 + 64, h, 0:nq], start=True, stop=True)], r=[KTR, qTR], w=[pSR])
                                                e, er = wb_.get()
                                                if D0 <= -512:
                                                    S.op("act", lambda a: a.activation(
                                                        out=e[0:nk, 0:nq], in_=pS[0:nk, 0:nq], func=AF.Exp, scale=0.125,
                                                        bias=(satCv if prv else satC)[0:nk, h:h + 1]), r=[pSR, cR], w=[er])
                                                else:
                                                    bkw = dict(bias=vbt[0:nk, 0:1]) if prv else {}
                                                    S.op("act", lambda a: a.activation(
                                                        out=e[0:nk, 0:nq], in_=pS[0:nk, 0:nq], func=AF.Exp, scale=0.125, **bkw), r=[pSR, cR], w=[er])
                                                    c0 = 384 - D0
                                                    S.op("dve", lambda v: v.tensor_tensor(
                                                        out=e[0:nk, 0:nq], in0=e[0:nk, 0:nq], in1=EGC[0:nk, h, c0:c0 + nq], op=ALU.mult),
                                                        r=[EGCR], w=[er])
                                                    if isP and D0 >= 0:
                                                        S.op("dve", lambda g: g.tensor_tensor(
                                                            out=e[0:nk, 0:nq], in0=e[0:nk, 0:nq], in1=mask_ap("CD", D0, nk), op=ALU.mult),
                                                            r=[cR], w=[er])
                                                st[i] = (e, er)

                                            def s2(i):
                                                h, m, idx = items[i]
                                                nk, kap, vap, D0, rs, prv = tiles[idx]
                                                ti = idx
                                                psO, psOR = PS[3 + m], PSR[3 + m]
                                                psD, psDR = PS[5 + m], PSR[5 + m]
                                                e, er = st.pop(i)
                                                S.mm([lambda pe: pe.matmul(
                                                    psO[:, 0:nq], lhsT=Vsb[0:nk, ti, h * 128:(h + 1) * 128], rhs=e[0:nk, 0:nq],
                                                    start=(idx == 0), stop=(idx == ntl - 1), skip_group_check=True)],
                                                    r=[er, VR], w=[psOR])
                                                S.mm([lambda pe: pe.matmul(
                                                    psD[:, 0:nq], lhsT=ONES[0:nk, :], rhs=e[0:nk, 0:nq],
                                                    start=(idx == 0), stop=(idx == ntl - 1), skip_group_check=True)],
                                                    r=[er, cR], w=[psDR])
                                                if idx == ntl - 1:
                                                    a_, ar = wf.get()
                                                    S.op("act", lambda a: a.activation(out=a_[:, 0:nq], in_=psD[:, 0:nq], func=AF.Ln), r=[psDR], w=[ar])
                                                    S.op("act", lambda a: a.activation(out=a_[:, 0:nq], in_=a_[:, 0:nq], func=AF.Exp, scale=-1.0), r=[], w=[ar])
                                                    S.op("dve", lambda v: v.tensor_tensor(
                                                        out=a_[:, 0:nq], in0=psO[:, 0:nq], in1=a_[:, 0:nq], op=ALU.mult), r=[psOR], w=[ar])
                                                    st[("acc", m)] = (a_, ar)
                                                    if m == 1:
                                                        a0, a0r = st.pop(("acc", 0))
                                                        a1, a1r = st.pop(("acc", 1))
                                                        S.op("dve", lambda v: v.scalar_tensor_tensor(
                                                            out=a0[:, 0:nq], in0=a1[:, 0:nq], scalar=lcol[:, 0:1], in1=a0[:, 0:nq],
                                                            op0=ALU.mult, op1=ALU.add), r=[a1r, lR], w=[a0r])
                                                        sq, sqr = wb_.get()
                                                        S.op("act", lambda a: a.activation(out=sq[:, 0:nq], in_=a0[:, 0:nq], func=AF.Square),
                                                             r=[a0r], w=[sqr])
                                                        S.mm([lambda pe: pe.matmul(PS[2][:, 0:nq], lhsT=ONES, rhs=sq[:, 0:nq],
                                                                                   start=True, stop=True)], r=[sqr, cR], w=[PSR[2]])
                                                        S.op("dve", lambda v: v.tensor_scalar(
                                                            out=a1[:, 0:nq], in0=PS[2][:, 0:nq], scalar1=1.0 / 128, scalar2=EPS,
                                                            op0=ALU.mult, op1=ALU.add), r=[PSR[2]], w=[a1r])
                                                        S.op("act", lambda a: a.activation(out=a1[:, 0:nq], in_=a1[:, 0:nq], func=AF.Ln),
                                                             r=[], w=[a1r])
                                                        S.op("act", lambda a: a.activation(out=a1[:, 0:nq], in_=a1[:, 0:nq], func=AF.Exp, scale=-0.5),
                                                             r=[], w=[a1r])
                                                        S.op("pool", lambda g: g.tensor_tensor(
                                                            out=a0[:, 0:nq], in0=a0[:, 0:nq], in1=a1[:, 0:nq], op=ALU.mult), r=[a1r], w=[a0r])
                                                        S.op("dve", lambda g: g.scalar_tensor_tensor(
                                                            out=gatedT[:, 8 + h, col0:col0 + nq], in0=a0[:, 0:nq], scalar=lcol[:, 1:2],
                                                            in1=gT[:, h, 0:nq], op0=ALU.mult, op1=ALU.mult), r=[a0r, lR, gTR], w=[gatedR])

                                            run_pipe(items, s1, s2)
                                S.barrier()
                                _chk(4)

                            with contextlib.ExitStack() as s4:
                                mergedT = sb("mergedT", [128, DC, 640], BF16, s4)
                                mrgR = R("merged")
                                with contextlib.ExitStack() as s4a:
                                    wbp = Pool(s4a, nc, "wbp", 2, [128, 16, 512], BF16)
                                    mgp = Pool(s4a, nc, "mgp", 3, [128, 4, 512], BF16)
                                    pfb = Pool(s4a, nc, "pfb", 10, [128, 512], BF16)
                                    tok0 = 512 * blk
                                    pend = [None]
                                    acnt = [0]

                                    def flush_acc():
                                        if pend[0] is None:
                                            return
                                        dcp, prods, cc0, ccn, abi = pend[0]
                                        pend[0] = None
                                        ab = 4 + abi % 2
                                        S.mm([lambda pe, p_=p_, n=n: pe.matmul(
                                            PS[ab][:, 0:ccn], lhsT=identb[:], rhs=p_[:, 0:ccn], start=(n == 0), stop=(n == 3))
                                            for n, (p_, p_r) in enumerate(prods)], r=[p_r for (_, p_r) in prods] + [cR], w=[PSR[ab]])
                                        S.op("act", lambda a: a.copy(out=mergedT[:, dcp, cc0:cc0 + ccn], in_=PS[ab][:, 0:ccn]),
                                             r=[PSR[ab]], w=[mrgR])

                                    for g4 in range(4):
                                        wt, wr = wbp.get()
                                        S.dma("pool", wt[:],
                                              wbr[l, :, :, g4 * 512:(g4 + 1) * 512].rearrange("n (c p) d -> p (n c) d", p=128), w=[wr])
                                        for j4 in range(4):
                                            dc = 4 * g4 + j4
                                            for (cc0, ccn) in CPS:
                                                mg, mgr = mgp.get()
                                                S.dma("sp", mg[:, :, 0:ccn],
                                                      mT_scr[:, tok0 + cc0:tok0 + cc0 + ccn].rearrange("(n c p) t -> p n c t", p=128, c=16)[:, :, dc, :],
                                                      r=[rfm("m", t_) for t_ in MTB], w=[mgr])
                                                prods = []
                                                for n in range(4):
                                                    S.mm([lambda pe, n=n, c=c: pe.matmul(
                                                        PS[n][:, 0:ccn], lhsT=wt[:, 4 * n + c, j4 * 128:(j4 + 1) * 128],
                                                        rhs=gatedT[:, 4 * n + c, cc0:cc0 + ccn],
                                                        start=(c == 0), stop=(c == 3)) for c in range(4)], r=[wr, gatedR], w=[PSR[n]])
                                                    p_, p_r = pfb.get()
                                                    S.op("dve", lambda v, p_=p_, n=n: v.tensor_tensor(
                                                        out=p_[:, 0:ccn], in0=PS[n][:, 0:ccn], in1=mg[:, n, 0:ccn], op=ALU.mult),
                                                        r=[PSR[n], mgr], w=[p_r])
                                                    prods.append((p_, p_r))
                                                flush_acc()
                                                acnt[0] += 1
                                                pend[0] = (dc, prods, cc0, ccn, acnt[0])
                                    flush_acc()
                                    S.barrier()
                                    _chk(5)
                                with contextlib.ExitStack() as s4b:
                                    wop = Pool(s4b, nc, "wop", 2, [128, DC, 512], BF16)
                                    ntk = NQB // 128
                                    ysb = [sb(f"ysb{i}", [128, D], F32, s4b) for i in range(ntk)]
                                    ysR = [R(f"ysb{i}") for i in range(ntk)]
                                    gpost = sb("gpost", [128, D], F32, s4b)
                                    gpR = R("gpost")
                                    S.dma("sp", gpost[:], bass.AP(tensor=norm_post.tensor, offset=l * D, ap=[[0, 128], [1, D]]), w=[gpR])
                                    xin = Pool(s4b, nc, "xin", 2, [128, D], F32)
                                    jk4 = sb("jk4", [128, D], BF16, s4b)
                                    jk4R = R("jk4")
                                    st4 = Pool(s4b, nc, "st4", 2, [128, 4], F32)
                                    for eb in range(4):
                                        wt, wr = wop.get()
                                        S.dma("pool", wt[:], wout[l, :, eb * 512:(eb + 1) * 512].rearrange("(c p) n -> p c n", p=128), w=[wr])
                                        for tk in range(ntk):
                                            pi = 4 + (eb * ntk + tk) % 3
                                            S.mm([lambda pe, wt=wt, c=c, tk=tk, pi=pi: pe.matmul(
                                                PS[pi][:, :], lhsT=mergedT[:, c, tk * 128:(tk + 1) * 128], rhs=wt[:, c, :],
                                                start=(c == 0), stop=(c == DC - 1)) for c in range(DC)], r=[wr, mrgR], w=[PSR[pi]])
                                            S.op("act", lambda a, tk=tk, eb=eb, pi=pi: a.copy(out=ysb[tk][:, eb * 512:(eb + 1) * 512], in_=PS[pi][:, :]),
                                                 r=[PSR[pi]], w=[ysR[tk]])
                                    for tk in range(ntk):
                                        tt = (512 * blk) // 128 + tk
                                        xt, xr = xin.get()
                                        if l == 0:
                                            src = xp[tt * 128:(tt + 1) * 128, :] if tt < NPT else xs[:, :]
                                            S.dma("sp", xt[:], src, w=[xr])
                                        else:
                                            S.dma("sp", xt[:], x1[tt * 128:(tt + 1) * 128, :], r=[RX[tt]], w=[xr])
                                        ss, ssr = st4.get()
                                        S.op("dve", lambda v, ss=ss: v.memset(ss[:], 0.0), w=[ssr])
                                        S.op("act", lambda a, tk=tk, ss=ss: a.activation(out=jk4[:], in_=ysb[tk][:], func=AF.Square,
                                                                                        accum_out=ss[:, 0:1]), r=[ysR[tk]], w=[jk4R, ssr])
                                        S.op("dve", lambda v, ss=ss: v.tensor_scalar(out=ss[:, 1:2], in0=ss[:, 0:1], scalar1=1.0 / D,
                                                                                      scalar2=EPS, op0=ALU.mult, op1=ALU.add), r=[ssr], w=[ssr])
                                        S.op("act", lambda a, ss=ss: a.activation(out=ss[:, 2:3], in_=ss[:, 1:2], func=AF.Sqrt), r=[ssr], w=[ssr])
                                        S.op("dve", lambda v, ss=ss: v.reciprocal(out=ss[:, 3:4], in_=ss[:, 2:3]), r=[ssr], w=[ssr])
                                        S.op("dve", lambda v, tk=tk, ss=ss: v.scalar_tensor_tensor(
                                            out=ysb[tk][:], in0=ysb[tk][:], scalar=ss[:, 3:4], in1=gpost[:], op0=ALU.mult, op1=ALU.mult),
                                            r=[ssr, gpR], w=[ysR[tk]])
                                        S.op("pool", lambda g, tk=tk, xt=xt: g.tensor_tensor(out=ysb[tk][:], in0=ysb[tk][:], in1=xt[:], op=ALU.add),
                                             r=[xr], w=[ysR[tk]])
                                        if last:
                                            dst = yp[tt * 128:(tt + 1) * 128, :] if tt < NPT else ys[:, :]
                                            S.dma("sp", dst, ysb[tk][:], r=[ysR[tk]], w=[R("yo")])
                                        else:
                                            S.dma("sp", x1[tt * 128:(tt + 1) * 128, :], ysb[tk][:], r=[ysR[tk]], w=[RX[tt]])
                                    S.barrier()
                                    _chk(6)
                    S.barrier()
                    _chk(7)
        except _StopBuild:
            pass
        S.finish()
    return nc


def _t5_bucket(rel):
    rel = np.asarray(rel, dtype=np.int64)
    nb = 16
    max_exact = 8
    ret = np.where(rel > 0, nb, 0)
    n = np.abs(rel)
    nf = np.maximum(n, 1).astype(np.float32)
    large = max_exact + (np.log(nf / np.float32(max_exact)) / np.float32(math.log(512 / max_exact))
                         * np.float32(nb - max_exact)).astype(np.int32)
    large = np.minimum(large, nb - 1)
    return ret + np.where(n < max_exact, n, large)


def _consts():
    ident = np.eye(128, dtype=np.float32)
    jm = np.ascontiguousarray(ident[::-1])
    j = np.arange(128)[:, None]
    s = np.arange(128)[None, :]
    tri = np.stack([(j > s), (j <= s), np.ones((128, 128), bool)]).astype(np.float32)
    i = np.arange(128)[:, None]
    q = np.arange(512)[None, :]
    masks = np.zeros((16, 128, 512), np.float32)
    for d in range(4):
        D0 = 128 * d
        masks[d] = (D0 + i < q)
        masks[4 + d] = ((D0 + i) // 64 <= q // 64)
    for m in range(8):
        D0 = -512 + 128 * m
        dd = q // 64 - (D0 + i) // 64
        masks[8 + m] = (dd >= 0) & (dd <= 8)
    t = np.arange(LC)
    b = _t5_bucket(511 - t)
    oh = (b[None, :] == np.arange(32)[:, None]).astype(np.float32)
    inv = (10000.0 ** (-np.arange(0, 32, 2, dtype=np.float32) / np.float32(32))).astype(np.float32)
    css = []
    for p in range(2):
        pos = np.concatenate([TP * p + np.arange(TP), PAST + np.arange(64), np.zeros(64)]).astype(np.float32)
        ang = pos[:, None] * inv[None, :]
        css.append(np.concatenate([np.cos(ang), np.sin(ang)], axis=1).astype(np.float32))
    return dict(c_ident=ident, c_j=jm, c_tri=tri, c_masks=masks, c_oh=oh), css


_NC = None


def kernel(x_prompt, x_sample, cache_sb_k, cache_sb_v, cache_band_k, cache_band_v,
           cache_diff_k, cache_diff_v, cache_mla_ckv, cache_mla_krope,
           norm_pre, norm_post, w_in, band_bias, t5_table, diff_lambda, diff_subln,
           mla_q_norm, mla_w_q_up, mla_kv_norm, mla_w_kv_up, w_branch, w_out):
    global _NC
    f = lambda a: np.ascontiguousarray(np.asarray(a, dtype=np.float32))
    if _NC is None:
        _NC = build_nc()
    nc = _NC
    consts, css = _consts()
    shared = dict(norm_pre=f(norm_pre), norm_post=f(norm_post), w_in=f(w_in), band_bias=f(band_bias), t5=f(t5_table),
                  dlam=f(diff_lambda).reshape(2, 256), subln=f(diff_subln), qnorm=f(mla_q_norm), wqup=f(mla_w_q_up),
                  kvnorm=f(mla_kv_norm), wkvup=f(mla_w_kv_up), wbr=f(w_branch), wout=f(w_out), **consts)
    in_maps = []
    for c in range(8):
        b, p = c // 2, c % 2
        m = dict(shared)
        m["c_cs"] = css[p]
        m["c_vb"] = np.full((128, 1), 0.0 if p == 1 else -30000.0, np.float32)
        m["xp"] = f(x_prompt[b, TP * p:TP * (p + 1)])
        xs_ = np.zeros((128, D), np.float32)
        xs_[:64] = np.asarray(x_sample[c], dtype=np.float32)
        m["xs"] = xs_
        cs_ = slice(c, c + 1)
        m["csbk"] = f(cache_sb_k[:, cs_]).reshape(2, 1, PAST, 512)
        m["csbv"] = f(cache_sb_v[:, cs_]).reshape(2, 1, PAST, 512)
        m["cbk"] = f(cache_band_k[:, cs_]).reshape(2, 1, 512, 512)
        m["cbv"] = f(cache_band_v[:, cs_]).reshape(2, 1, 512, 512)
        m["cdk"] = f(cache_diff_k[:, cs_]).reshape(2, 1, PAST, 512)
        m["cdv"] = f(cache_diff_v[:, cs_]).reshape(2, 1, PAST, 512)
        m["cckv"] = f(cache_mla_ckv[:, cs_])
        m["ckr"] = f(cache_mla_krope[:, cs_])
        in_maps.append(m)
    ncores = _NCORES[0]
    res = run_bass_kernel_spmd(nc, in_maps[:ncores], core_ids=list(range(ncores)))
    rs = [res.results[i if i < ncores else i % 2] for i in range(8)]

    def cat_p(name, shape_tail):
        a = np.stack([np.concatenate([rs[2 * b][name], rs[2 * b + 1][name]], axis=1) for b in range(4)], axis=1)
        return a.reshape((2, 4) + shape_tail).astype(np.float32)

    def cat_b(name, shape_tail):
        a = np.stack([rs[2 * b + 1][name] for b in range(4)], axis=1)
        return a.reshape((2, 4) + shape_tail).astype(np.float32)

    def cat_s(name, shape_tail):
        return np.concatenate([r[name] for r in rs], axis=1).reshape((2, 8) + shape_tail).astype(np.float32)

    y_p = np.stack([np.concatenate([rs[2 * b]["yp"], rs[2 * b + 1]["yp"]], axis=0) for b in range(4)], axis=0).astype(np.float32)
    y_s = np.stack([r["ys"][:64] for r in rs], axis=0).astype(np.float32)
    S2 = 2 * TP
    return (y_p, y_s,
            cat_p("sbk_p", (S2, 8, 64)), cat_p("sbv_p", (S2, 8, 64)),
            cat_b("bk_p", (512, 8, 64)), cat_b("bv_p", (512, 8, 64)),
            cat_p("dk_p", (S2, 4, 2, 64)), cat_p("dv_p", (S2, 4, 128)),
            cat_p("ckv_p", (S2, 128)), cat_p("kr_p", (S2, 32)),
            cat_s("sbk_s", (64, 8, 64)), cat_s("sbv_s", (64, 8, 64)),
            cat_s("bk_s", (512, 8, 64)), cat_s("bv_s", (512, 8, 64)),
            cat_s("dk_s", (64, 4, 2, 64)), cat_s("dv_s", (64, 4, 128)),
            cat_s("ckv_s", (64, 128)), cat_s("kr_s", (64, 32)))
```
